# Optimizing a Trainium2 kernel written in Bass

```python
import math
import numpy as np
import jax
import jax.numpy as jnp
from jax import lax

D_MODEL = 1024
BATCH = 8
SEQ = 4096
DEPTH = 4

N_GROUPS = 4
GROUP_W = D_MODEL // N_GROUPS
MIX_W = N_GROUPS * GROUP_W
HEADS = 4
HEAD_DIM = GROUP_W // HEADS
Q_BLOCK = 128
EPS = 1e-6

NSA_KV_DIM = HEAD_DIM
CMP_LEN = 32
CMP_STRIDE = 16
CMP_HIDDEN = 256
SLC_BLOCK = 64
SLC_TOPK = 16
WINDOW = 512
N_NSA_BRANCH = 3

DIFF_QK_DIM = HEAD_DIM // 2
DIFF_V_DIM = HEAD_DIM

RET_CHUNK = 128

SSM_HEAD_DIM = HEAD_DIM
SSM_HEADS = GROUP_W // SSM_HEAD_DIM
SSM_GROUPS = 2
SSM_STATE = 128
CONV_W = 4
CONV_CH = GROUP_W + 2 * SSM_GROUPS * SSM_STATE
SSM_CHUNK = 128

N_ALIBI_HEADS = 2 * HEADS

IN_LAYOUT = (
    ("nsa_q", GROUP_W), ("nsa_k_cmp", NSA_KV_DIM), ("nsa_v_cmp", NSA_KV_DIM),
    ("nsa_k_slc", NSA_KV_DIM), ("nsa_v_slc", NSA_KV_DIM), ("nsa_k_win", NSA_KV_DIM),
    ("nsa_v_win", NSA_KV_DIM), ("nsa_gate", HEADS * N_NSA_BRANCH), ("nsa_z", GROUP_W),
    ("diff_q", GROUP_W), ("diff_k", GROUP_W), ("diff_v", GROUP_W), ("diff_z", GROUP_W),
    ("ret_q", GROUP_W), ("ret_k", GROUP_W), ("ret_v", GROUP_W), ("ret_z", GROUP_W),
    ("ssm_z", GROUP_W), ("ssm_xbc", CONV_CH), ("ssm_dt", SSM_HEADS),
)
IN_W = sum(w for _, w in IN_LAYOUT)

kernel_name = "hymba_nsa_diff_retnet_ssd_trunk"


def rms_norm(x, w):
    xf = x.astype(jnp.float32)
    y = xf * lax.rsqrt(jnp.mean(xf * xf, axis=-1, keepdims=True) + EPS)
    return (y * w.astype(jnp.float32)).astype(x.dtype)


def masked_softmax(s, mask):
    s = jnp.where(mask, s, -jnp.inf)
    m = jnp.max(s, axis=-1, keepdims=True)
    p = jnp.exp(s - jnp.where(jnp.isfinite(m), m, 0.0))
    return p / jnp.maximum(jnp.sum(p, axis=-1, keepdims=True), 1e-30)


def alibi_slopes(n):
    return jnp.asarray(np.array([2.0 ** (-8.0 * (i + 1) / n) for i in range(n)], dtype=np.float32))


def split_columns(h):
    out = {}
    off = 0
    for name, w in IN_LAYOUT:
        out[name] = h[..., off:off + w]
        off += w
    return out


def nsa_mixer(q, k_cmp, v_cmp, k_slc, v_slc, k_win, v_win, gate_logits,
              pe_k, pe_v, w_ck1, w_ck2, w_cv1, w_cv2, slopes):
    f32 = jnp.float32
    B, S, _ = q.shape
    H, dh = HEADS, HEAD_DIM
    scale = dh ** -0.5
    q = q.reshape(B, S, H, dh)
    n_qb = S // Q_BLOCK
    tpos = jnp.arange(S)

    nc = (S - CMP_LEN) // CMP_STRIDE + 1
    blk = np.arange(nc)[:, None] * CMP_STRIDE + np.arange(CMP_LEN)[None, :]
    c_start, c_end = blk[:, 0], blk[:, -1]

    def compress(kv, pe, w1, w2):
        blocks = (kv[:, blk] + pe).reshape(B, nc, CMP_LEN * NSA_KV_DIM)
        return jax.nn.silu(blocks @ w1) @ w2

    kc = compress(k_cmp, pe_k, w_ck1, w_ck2)
    vc = compress(v_cmp, pe_v, w_cv1, w_cv2)
    c_center = jnp.asarray(((c_start + c_end) / 2.0).astype(np.float32))
    dist_c = tpos.astype(f32)[:, None] - c_center[None, :]
    s_c = jnp.einsum('bqhd,bnd->bhqn', q, kc).astype(f32) * scale - slopes[:, None, None] * dist_c
    p_c = masked_softmax(s_c, jnp.asarray(c_end)[None, :] <= tpos[:, None])
    o_cmp = jnp.einsum('bhqn,bnd->bqhd', p_c.astype(vc.dtype), vc)

    ns = S // SLC_BLOCK
    top = min(SLC_TOPK, ns)
    s_start = np.arange(ns) * SLC_BLOCK
    s_end = s_start + SLC_BLOCK - 1
    overlap = ((c_start[:, None] <= s_end[None, :]) & (c_end[:, None] >= s_start[None, :])).astype(np.float32)
    imp = jnp.einsum('bhqn,nj->bqj', p_c, jnp.asarray(overlap))
    jb = jnp.arange(ns)
    cur = tpos // SLC_BLOCK
    forced = (jb[None, :] == 0) | (jb[None, :] == cur[:, None]) | (jb[None, :] == cur[:, None] - 1)
    valid = jb[None, :] * SLC_BLOCK <= tpos[:, None]
    score = jnp.where(forced, jnp.inf, jnp.where(valid, imp, -jnp.inf))
    _, sel = lax.top_k(score, top)

    def select_block(args):
        qb, selb, tb = args
        tok = (selb[..., None] * SLC_BLOCK + jnp.arange(SLC_BLOCK)).reshape(B, Q_BLOCK, top * SLC_BLOCK)
        kg = jax.vmap(lambda kk, ii: kk[ii])(k_slc, tok)
        vg = jax.vmap(lambda vv, ii: vv[ii])(v_slc, tok)
        dist = tb[None, :, None] - tok
        s = (jnp.einsum('bqhd,bqkd->bhqk', qb, kg).astype(f32) * scale
             - slopes[None, :, None, None] * dist[:, None].astype(f32))
        p = masked_softmax(s, (dist >= 0)[:, None])
        return jnp.einsum('bhqk,bqkd->bqhd', p.astype(vg.dtype), vg)

    q_blocks = q.reshape(B, n_qb, Q_BLOCK, H, dh).transpose(1, 0, 2, 3, 4)
    sel_blocks = sel.reshape(B, n_qb, Q_BLOCK, top).transpose(1, 0, 2, 3)
    t_blocks = tpos.reshape(n_qb, Q_BLOCK)
    o_slc = lax.map(select_block, (q_blocks, sel_blocks, t_blocks))
    o_slc = o_slc.transpose(1, 0, 2, 3, 4).reshape(B, S, H, dh)

    n_pre = WINDOW // Q_BLOCK
    kw_len = (n_pre + 1) * Q_BLOCK

    def band(kv):
        kp = jnp.pad(kv, ((0, 0), (WINDOW, 0), (0, 0))).reshape(B, n_qb + n_pre, Q_BLOCK, NSA_KV_DIM)
        return jnp.concatenate([kp[:, i:i + n_qb] for i in range(n_pre + 1)], axis=2)

    kw, vw = band(k_win), band(v_win)
    tq = tpos.reshape(n_qb, Q_BLOCK)
    tk = jnp.arange(n_qb)[:, None] * Q_BLOCK - WINDOW + jnp.arange(kw_len)[None, :]
    dist = tq[:, :, None] - tk[:, None, :]
    mask = (dist >= 0) & (dist < WINDOW) & (tk[:, None, :] >= 0)
    s_w = (jnp.einsum('bnqhd,bnkd->bnhqk', q.reshape(B, n_qb, Q_BLOCK, H, dh), kw).astype(f32) * scale
           - slopes[None, None, :, None, None] * dist[None, :, None].astype(f32))
    p_w = masked_softmax(s_w, mask[None, :, None])
    o_win = jnp.einsum('bnhqk,bnkd->bnqhd', p_w.astype(vw.dtype), vw).reshape(B, S, H, dh)

    g = jax.nn.sigmoid(gate_logits.astype(f32)).reshape(B, S, H, N_NSA_BRANCH)
    o = g[..., 0:1] * o_cmp + g[..., 1:2] * o_slc + g[..., 2:3] * o_win
    return o.reshape(B, S, H * dh)


def diff_mixer(q, k, v, lam_q1, lam_k1, lam_q2, lam_k2, subln_w, slopes, layer_idx):
    f32 = jnp.float32
    B, S, _ = q.shape
    H, dq, dv = HEADS, DIFF_QK_DIM, DIFF_V_DIM
    q = q.reshape(B, S, H, 2, dq)
    k = k.reshape(B, S, H, 2, dq)
    v = v.reshape(B, S, H, dv)
    scale = dq ** -0.5
    lam_init = 0.8 - 0.6 * math.exp(-0.3 * layer_idx)
    lam = (jnp.exp(jnp.sum(lam_q1.astype(f32) * lam_k1.astype(f32)))
           - jnp.exp(jnp.sum(lam_q2.astype(f32) * lam_k2.astype(f32))) + lam_init)
    n_qb = S // Q_BLOCK
    tk = jnp.arange(S)

    def attend(args):
        qb, tb = args
        s = jnp.einsum('bqhid,bkhid->bhiqk', qb, k).astype(f32) * scale
        dist = tb[:, None] - tk[None, :]
        s = s - slopes[None, :, None, None, None] * dist.astype(f32)
        p = masked_softmax(s, dist >= 0)
        a = p[:, :, 0] - lam * p[:, :, 1]
        return jnp.einsum('bhqk,bkhd->bqhd', a.astype(v.dtype), v)

    qb = q.reshape(B, n_qb, Q_BLOCK, H, 2, dq).transpose(1, 0, 2, 3, 4, 5)
    o = lax.map(attend, (qb, tk.reshape(n_qb, Q_BLOCK)))
    o = o.transpose(1, 0, 2, 3, 4).reshape(B, S, H, dv)
    o = rms_norm(o, subln_w).astype(f32) * (1.0 - lam_init)
    return o.reshape(B, S, H * dv)


def retention_mixer(q, k, v, gn_w):
    f32 = jnp.float32
    B, S, _ = q.shape
    H, dh, C = HEADS, HEAD_DIM, RET_CHUNK
    n = S // C
    q = q.astype(f32).reshape(B, n, C, H, dh) * dh ** -0.5
    k = k.astype(f32).reshape(B, n, C, H, dh)
    v = v.astype(f32).reshape(B, n, C, H, dh)
    log_g = jnp.log(1.0 - 2.0 ** (-5.0 - jnp.arange(H, dtype=f32)))
    pos = jnp.arange(C, dtype=f32)
    rel = pos[:, None] - pos[None, :]
    decay = jnp.where(rel >= 0, jnp.exp(log_g[:, None, None] * jnp.maximum(rel, 0.0)), 0.0)
    inner = jnp.einsum('bnqhd,bnkhd->bnhqk', q, k) * decay
    inner = jnp.einsum('bnhqk,bnkhe->bnqhe', inner, v)
    xi = jnp.exp(log_g[:, None] * (pos + 1.0))
    zeta = jnp.exp(log_g[:, None] * (C - 1.0 - pos))
    chunk_decay = jnp.exp(log_g * C)
    kv = jnp.einsum('bnkhd,hk,bnkhe->bnhde', k, zeta, v)

    def step(state, kv_c):
        return state * chunk_decay[None, :, None, None] + kv_c, state

    _, prev = lax.scan(step, jnp.zeros((B, H, dh, dh), f32), jnp.moveaxis(kv, 1, 0))
    prev = jnp.moveaxis(prev, 0, 1)
    cross = jnp.einsum('bnqhd,hq,bnhde->bnqhe', q, xi, prev)
    o = (inner + cross).reshape(B, S, H, dh)
    mu = jnp.mean(o, axis=-1, keepdims=True)
    var = jnp.mean((o - mu) ** 2, axis=-1, keepdims=True)
    o = ((o - mu) * lax.rsqrt(var + EPS)).reshape(B, S, H * dh)
    return o * gn_w.astype(f32)


def ssd_mixer(xbc, dt_raw, conv_w, conv_b, dt_bias, A_log, D_skip):
    f32 = jnp.float32
    B, S, _ = xbc.shape
    H, P, G, N, L = SSM_HEADS, SSM_HEAD_DIM, SSM_GROUPS, SSM_STATE, SSM_CHUNK
    HG = H // G
    n = S // L
    xbc = lax.conv_general_dilated(xbc, conv_w[:, None, :].astype(xbc.dtype), (1,), [(CONV_W - 1, 0)],
                                   dimension_numbers=('NWC', 'WIO', 'NWC'),
                                   feature_group_count=CONV_CH) + conv_b
    xbc = jax.nn.silu(xbc).astype(f32)
    x = xbc[..., :GROUP_W].reshape(B, n, L, G, HG, P)
    Bm = xbc[..., GROUP_W:GROUP_W + G * N].reshape(B, n, L, G, N)
    Cm = xbc[..., GROUP_W + G * N:].reshape(B, n, L, G, N)
    dt = jax.nn.softplus(dt_raw.astype(f32) + dt_bias.astype(f32))
    A = -jnp.exp(A_log.astype(f32))
    dA = (dt * A).reshape(B, n, L, G, HG)
    cs = jnp.cumsum(dA, axis=2).transpose(0, 1, 3, 4, 2)
    xdt = x * dt.reshape(B, n, L, G, HG)[..., None]
    causal = jnp.asarray(np.tril(np.ones((L, L), dtype=bool)))
    seg = jnp.exp(jnp.where(causal, cs[..., :, None] - cs[..., None, :], -jnp.inf))
    cb = jnp.einsum('bclgn,bcsgn->bcgls', Cm, Bm)
    y_diag = jnp.einsum('bcgls,bcghls,bcsghp->bclghp', cb, seg, xdt)
    decay_states = jnp.exp(cs[..., -1:] - cs)
    states = jnp.einsum('bclgn,bcghl,bclghp->bcghpn', Bm, decay_states, xdt)
    chunk_decay = jnp.exp(cs[..., -1])

    def step(h, inp):
        st, dec = inp
        return h * dec[..., None, None] + st, h

    _, prev = lax.scan(step, jnp.zeros((B, G, HG, P, N), f32),
                       (jnp.moveaxis(states, 1, 0), jnp.moveaxis(chunk_decay, 1, 0)))
    prev = jnp.moveaxis(prev, 0, 1)
    y_off = jnp.einsum('bclgn,bcghpn,bcghl->bclghp', Cm, prev, jnp.exp(cs))
    y = y_diag + y_off + D_skip.astype(f32).reshape(G, HG)[:, :, None] * x
    return y.reshape(B, S, H * P)


def setup_inputs(seed: int = 0) -> dict:
    key = jax.random.key(seed)
    ks = jax.random.split(key, 24)
    f32 = jnp.float32

    def nrm(k, shape, scale):
        return jax.random.normal(k, shape, f32) * scale

    Ld = DEPTH
    dt = jnp.exp(jax.random.uniform(ks[18], (Ld, SSM_HEADS), f32, math.log(1e-3), math.log(1e-1)))
    return {
        "x": nrm(ks[0], (BATCH, SEQ, D_MODEL), 1.0),
        "norm_w": 1.0 + nrm(ks[1], (Ld, D_MODEL), 0.02),
        "w_in": nrm(ks[2], (Ld, D_MODEL, IN_W), D_MODEL ** -0.5),
        "w_out": nrm(ks[3], (Ld, MIX_W, D_MODEL), 0.5 * MIX_W ** -0.5),
        "nsa_pe_k": nrm(ks[4], (Ld, CMP_LEN, NSA_KV_DIM), 0.1),
        "nsa_pe_v": nrm(ks[5], (Ld, CMP_LEN, NSA_KV_DIM), 0.1),
        "nsa_w_ck1": nrm(ks[6], (Ld, CMP_LEN * NSA_KV_DIM, CMP_HIDDEN), (CMP_LEN * NSA_KV_DIM) ** -0.5),
        "nsa_w_ck2": nrm(ks[7], (Ld, CMP_HIDDEN, NSA_KV_DIM), CMP_HIDDEN ** -0.5),
        "nsa_w_cv1": nrm(ks[8], (Ld, CMP_LEN * NSA_KV_DIM, CMP_HIDDEN), (CMP_LEN * NSA_KV_DIM) ** -0.5),
        "nsa_w_cv2": nrm(ks[9], (Ld, CMP_HIDDEN, NSA_KV_DIM), CMP_HIDDEN ** -0.5),
        "diff_lam_q1": nrm(ks[10], (Ld, DIFF_QK_DIM), 0.1),
        "diff_lam_k1": nrm(ks[11], (Ld, DIFF_QK_DIM), 0.1),
        "diff_lam_q2": nrm(ks[12], (Ld, DIFF_QK_DIM), 0.1),
        "diff_lam_k2": nrm(ks[13], (Ld, DIFF_QK_DIM), 0.1),
        "diff_subln_w": 1.0 + nrm(ks[14], (Ld, DIFF_V_DIM), 0.02),
        "ret_gn_w": 1.0 + nrm(ks[15], (Ld, GROUP_W), 0.02),
        "ssm_conv_w": nrm(ks[16], (Ld, CONV_W, CONV_CH), CONV_W ** -0.5),
        "ssm_conv_b": nrm(ks[17], (Ld, CONV_CH), 0.01),
        "ssm_dt_bias": dt + jnp.log(-jnp.expm1(-dt)),
        "ssm_A_log": jnp.log(jax.random.uniform(ks[19], (Ld, SSM_HEADS), f32, 1.0, 16.0)),
        "ssm_D": 1.0 + nrm(ks[20], (Ld, SSM_HEADS), 0.1),
        "ssm_norm_w": 1.0 + nrm(ks[21], (Ld, GROUP_W), 0.02),
        "final_norm_w": 1.0 + nrm(ks[22], (D_MODEL,), 0.02),
    }


def reference(x, norm_w, w_in, w_out, nsa_pe_k, nsa_pe_v, nsa_w_ck1, nsa_w_ck2, nsa_w_cv1, nsa_w_cv2,
              diff_lam_q1, diff_lam_k1, diff_lam_q2, diff_lam_k2, diff_subln_w, ret_gn_w,
              ssm_conv_w, ssm_conv_b, ssm_dt_bias, ssm_A_log, ssm_D, ssm_norm_w, final_norm_w):
    slopes = alibi_slopes(N_ALIBI_HEADS)
    nsa_slopes = slopes[0::2]
    diff_slopes = slopes[1::2]
    for i in range(DEPTH):
        h = rms_norm(x, norm_w[i])
        c = split_columns(h @ w_in[i])
        y_nsa = nsa_mixer(c["nsa_q"], c["nsa_k_cmp"], c["nsa_v_cmp"], c["nsa_k_slc"], c["nsa_v_slc"],
                          c["nsa_k_win"], c["nsa_v_win"], c["nsa_gate"], nsa_pe_k[i], nsa_pe_v[i],
                          nsa_w_ck1[i], nsa_w_ck2[i], nsa_w_cv1[i], nsa_w_cv2[i], nsa_slopes)
        y_nsa = y_nsa * jax.nn.silu(c["nsa_z"])
        y_diff = diff_mixer(c["diff_q"], c["diff_k"], c["diff_v"], diff_lam_q1[i], diff_lam_k1[i],
                            diff_lam_q2[i], diff_lam_k2[i], diff_subln_w[i], diff_slopes, i)
        y_diff = y_diff * jax.nn.silu(c["diff_z"])
        y_ret = retention_mixer(c["ret_q"], c["ret_k"], c["ret_v"], ret_gn_w[i]) * jax.nn.silu(c["ret_z"])
        y_ssm = ssd_mixer(c["ssm_xbc"], c["ssm_dt"], ssm_conv_w[i], ssm_conv_b[i], ssm_dt_bias[i],
                          ssm_A_log[i], ssm_D[i])
        y_ssm = rms_norm(y_ssm * jax.nn.silu(c["ssm_z"]), ssm_norm_w[i])
        mixed = jnp.concatenate([y_nsa.astype(x.dtype), y_diff.astype(x.dtype),
                                 y_ret.astype(x.dtype), y_ssm.astype(x.dtype)], axis=-1)
        x = x + mixed @ w_out[i]
    return rms_norm(x, final_norm_w)
```

```python
import math
from contextlib import ExitStack

import numpy as np
import ml_dtypes

import concourse.bass as bass
import concourse.mybir as mybir
from concourse.bass_utils import run_bass_kernel_spmd

F32 = mybir.dt.float32
BF16 = mybir.dt.bfloat16
I32 = mybir.dt.int32
AF = mybir.ActivationFunctionType
ALU = mybir.AluOpType
AX = mybir.AxisListType

D_MODEL = 1024
DEPTH = 4
SEQ = 4096
IN_W = 3984
EPS = 1e-6
BIG = 30000.0
STORE_Q = "pool"
NFM = 18
FMW = NFM * 128
TMW = 1936
TM_VSLC, TM_VWIN, TM_GATE, TM_NSAZ, TM_DV, TM_DZ, TM_RK, TM_RV, TM_RZ, TM_SZ, TM_DT = (
    0, 64, 128, 140, 396, 652, 908, 1164, 1420, 1676, 1932)
FM_NQ, FM_KVC, FM_KSW, FM_DQ, FM_DK, FM_RQ, FM_RK, FM_XBC = 0, 2, 3, 4, 6, 8, 10, 12

W_PIECES = [
    (0, 0, 256),
    (256, 256, 128),
    (384, 384, 64),
    (448, 512, 64),
    (512, 908, 512),
    (1024, 1932, 512),
    (1536, 3212, 768),
    (FMW + 0, 448, 64),
    (FMW + 64, 576, 332),
    (FMW + 396, 1420, 512),
    (FMW + 908, 2188, 1024),
    (FMW + 1932, 3980, 4),
]
WSBW = FMW + TMW


class Prog:
    ENGS = ("pe", "act", "dve", "pool", "sp")
    DMA_RING = 8

    def __init__(self, nc):
        self.nc = nc
        self.q = {e: [] for e in self.ENGS}
        self.buf = {}
        self.seen_e = {e: {} for e in self.ENGS}
        self.seen_d = {e: {} for e in self.ENGS}
        self.ring_uses = {e: [0] * self.DMA_RING for e in self.ENGS}
        self.ring_next = {e: 0 for e in self.ENGS}
        self.ring_last = {e: [None] * self.DMA_RING for e in self.ENGS}
        self.all_dma = []

    def _bs(self, k):
        s = self.buf.get(k)
        if s is None:
            s = {"w": None, "r_e": {}, "r_d": []}
            self.buf[k] = s
        return s

    def _need(self, eng, tok, waits):
        if tok is None:
            return
        if tok[0] == "e":
            _, pe, idx = tok
            if pe == eng and eng in ("pe", "sp"):
                return
            if self.seen_e[eng].get(pe, -1) >= idx:
                return
            self.seen_e[eng][pe] = idx
            self.q[pe][idx]["flag"] = True
            waits.append(tok)
        else:
            _, sid, val = tok
            if self.seen_d[eng].get(sid, 0) >= val:
                return
            self.seen_d[eng][sid] = val
            waits.append(tok)

    def _deps(self, eng, reads, writes):
        waits = []
        for r in reads:
            self._need(eng, self._bs(r)["w"], waits)
        for w in writes:
            s = self._bs(w)
            self._need(eng, s["w"], waits)
            for pe, idx in s["r_e"].items():
                self._need(eng, ("e", pe, idx), waits)
            for t in s["r_d"]:
                self._need(eng, t, waits)
        return waits

    def _commit(self, tok, reads, writes):
        for r in reads:
            s = self._bs(r)
            if tok[0] == "e":
                s["r_e"][tok[1]] = tok[2]
            else:
                s["r_d"].append(tok)
        for w in writes:
            s = self._bs(w)
            s["w"] = tok
            s["r_e"] = {}
            s["r_d"] = []

    def op(self, eng, fn, reads=(), writes=()):
        if eng != "pe":
            extra = [r for r in reads if r.startswith("ps") and r not in writes]
            if extra:
                writes = list(writes) + extra
        waits = self._deps(eng, reads, writes)
        idx = len(self.q[eng])
        self.q[eng].append({"fn": fn, "waits": waits, "flag": False, "dma": None})
        self._commit(("e", eng, idx), reads, writes)

    def dma(self, eng, out, in_, reads=(), writes=()):
        waits = self._deps(eng, reads, writes)
        slot = self.ring_next[eng]
        self.ring_next[eng] = (slot + 1) % self.DMA_RING
        self._need(eng, self.ring_last[eng][slot], waits)
        self.ring_uses[eng][slot] += 1
        tok = ("d", (eng, slot), 16 * self.ring_uses[eng][slot])
        self.ring_last[eng][slot] = tok
        self.all_dma.append(tok)
        self.q[eng].append({"fn": lambda e: e.dma_start(out=out, in_=in_), "waits": waits,
                            "flag": False, "dma": (eng, slot)})
        self._commit(tok, reads, writes)

    def barrier(self):
        lasts = {e: len(self.q[e]) - 1 for e in self.ENGS}
        dmas = []
        for e in self.ENGS:
            dmas += [t for t in self.ring_last[e] if t is not None]
        for e in self.ENGS:
            waits = []
            for pe, idx in lasts.items():
                if pe == e:
                    continue
                j = idx
                while j >= 0 and (self.q[pe][j]["dma"] is not None or self.q[pe][j]["fn"] is None):
                    j -= 1
                if j >= 0:
                    self._need(e, ("e", pe, j), waits)
            for t in dmas:
                self._need(e, t, waits)
            self.q[e].append({"fn": None, "waits": waits, "flag": False, "dma": None})
        self.buf = {}

    def emit(self, es):
        nc = self.nc
        esem = {e: es.enter_context(nc.semaphore("s_" + e)) for e in self.ENGS}
        dsem = {}
        for e in self.ENGS:
            for s in range(self.DMA_RING):
                if self.ring_uses[e][s]:
                    dsem[(e, s)] = es.enter_context(nc.semaphore("d_%s%d" % (e, s)))
        cnt = {}
        for e in self.ENGS:
            c = 0
            arr = []
            for r in self.q[e]:
                if r["flag"]:
                    c += 1
                arr.append(c)
            cnt[e] = arr
        handles = {"pe": "tensor", "act": "scalar", "dve": "vector", "pool": "gpsimd", "sp": "sync"}
        block = es.enter_context(nc.Block())

        def run(ename, e):
            for r in self.q[ename]:
                for t in r["waits"]:
                    if t[0] == "e":
                        e.wait_ge(esem[t[1]], cnt[t[1]][t[2]])
                    else:
                        e.wait_ge(dsem[t[1]], t[2])
                if r["fn"] is None:
                    continue
                ins = r["fn"](e)
                if r["dma"] is not None:
                    ins.then_inc(dsem[r["dma"]], 16)
                elif r["flag"]:
                    ins.then_inc(esem[ename], 1)

        for ename in self.ENGS:
            getattr(block, handles[ename])(lambda e, ename=ename: run(ename, e))


class Arena:
    def __init__(self, t, width):
        self.t = t
        self.width = width
        self.off = 0
        self.marks = []

    def alloc(self, n, shape=None):
        assert self.off + n <= self.width, ("arena overflow", self.off, n, self.width)
        ap = self.t[:, self.off:self.off + n]
        self.off += n
        if shape is not None:
            names = "abcdef"[:len(shape)]
            ap = ap.rearrange("p (%s) -> p %s" % (" ".join(names), " ".join(names)),
                              **{nm: s for nm, s in zip(names, shape)})
        return ap

    def mark(self):
        self.marks.append(self.off)

    def release(self):
        self.off = self.marks.pop()


class Stream:
    def __init__(self):
        self.ops = []

    def op(self, eng, fn, reads=(), writes=()):
        self.ops.append(("op", eng, fn, tuple(reads), tuple(writes)))

    def dma(self, eng, out, in_, reads=(), writes=()):
        self.ops.append(("dma", eng, out, in_, tuple(reads), tuple(writes)))


def merge_streams(P, streams):
    idx = [0] * len(streams)
    while True:
        best, bf = -1, 2.0
        pending = False
        for i, s in enumerate(streams):
            if idx[i] < len(s.ops):
                pending = True
                o = s.ops[idx[i]]
                rds = o[3] if o[0] == "op" else o[4]
                if any(r.startswith("mixR") and (r not in P.buf or P.buf[r]["w"] is None) for r in rds) and len(streams) > 1:
                    continue
                f = idx[i] / len(s.ops)
                if f < bf:
                    best, bf = i, f
        if best < 0:
            assert not pending, "merge deadlock"
            break
        o = streams[best].ops[idx[best]]
        idx[best] += 1
        if o[0] == "op":
            P.op(o[1], o[2], reads=o[3], writes=o[4])
        else:
            P.dma(o[1], o[2], o[3], reads=o[4], writes=o[5])


def build_program(S=SEQ, L=DEPTH, debug=False, phases=("A", "F"), final_norm=True, stop=99):
    T = S // 128
    NSB = S // 512
    nc = bass.Bass("TRN2", target_bir_lowering=False)
    es = ExitStack()
    dkind = "ExternalOutput" if debug else "Internal"

    def din(name, shape, dt=F32):
        return nc.dram_tensor(name, list(shape), dt, kind="ExternalInput").ap()

    x_in = din("x", [S, D_MODEL])
    norm_w = din("norm_w", [L, D_MODEL])
    w_in = din("w_in", [L, D_MODEL, IN_W])
    w_out = din("w_out", [L, D_MODEL, D_MODEL])
    final_norm_w = din("final_norm_w", [1, D_MODEL])
    ident_d = din("c_ident", [128, 128])
    ret_gn_w = din("ret_gn_w", [L, 256])
    ssm_norm_w = din("ssm_norm_w", [L, 256])
    conv_wT = din("conv_wT", [L, 128, 24])
    conv_bT = din("conv_bT", [L, 128, 6])
    ssm_vec = din("ssm_vec", [L, 12])
    c_ret_decT = din("c_ret_decT", [128, 512])
    c_ret_xiT = din("c_ret_xiT", [128, 256])
    c_ret_zeta = din("c_ret_zeta", [128, 256])
    c_ret_cd = din("c_ret_cd", [128, 4])
    c_triu = din("c_triu", [128, 128])
    mixed_d = nc.dram_tensor("mixed_d", [S, D_MODEL], BF16, kind=dkind).ap()
    NC = (S - 32) // 16 + 1
    NBK = (NC + 127) // 128
    NS = S // 64
    nsa_peT = din("nsa_peT", [L, 128, 32])
    w_ck1 = din("nsa_w_ck1", [L, 2048, 256])
    w_cv1 = din("nsa_w_cv1", [L, 2048, 256])
    w_ck2 = din("nsa_w_ck2", [L, 256, 64])
    w_cv2 = din("nsa_w_cv2", [L, 256, 64])
    c_gneg = din("c_gneg", [128, 4096], BF16)
    c_ex2 = din("c_ex2", [128, T * 128], BF16)
    c_low01 = din("c_low01", [128, 128], BF16)
    c_ovl = din("c_ovl", [128, 2 * 64], BF16)
    c_alibi_cmp = din("c_alibi_cmp", [128, 4 * 2 * 32])
    c_topk_add = din("c_topk_add", [128, T * 64])
    diff_vec = din("diff_vec", [L, 192])
    c_alibi = din("c_alibi", [128, 2 * 4 * 35])
    c_tri01 = din("c_tri01", [128, 128])
    y_out = nc.dram_tensor("y", [S, D_MODEL], F32, kind="ExternalOutput").ap()
    xres = nc.dram_tensor("xres", [S, D_MODEL], F32, kind=dkind).ap()
    fm_d = nc.dram_tensor("fm", [FMW, S], BF16, kind=dkind).ap()
    tm_d = nc.dram_tensor("tm", [S, TMW], BF16, kind=dkind).ap()

    A32W = 14000
    A16W = 66000
    a32 = Arena(es.enter_context(nc.sbuf_tensor("a32", [128, A32W], F32)), A32W)
    a16 = Arena(es.enter_context(nc.sbuf_tensor("a16", [128, A16W], BF16)), A16W)
    pbig = es.enter_context(nc.psum_tensor("pbig", [128, 8 * 512], F32))
    psb = [pbig[:, i * 512:(i + 1) * 512] for i in range(8)]

    P = Prog(nc)

    ident_f = a32.alloc(128)
    ident_b = a16.alloc(128)
    small = a32.alloc(T * 16, (T, 16))
    P.dma("sp", ident_f, ident_d, writes=["ident_f"])
    P.op("dve", lambda e: e.tensor_copy(out=ident_b, in_=ident_f), reads=["ident_f"], writes=["ident_b"])

    cp_rr = [0]

    def evac(out, in_, reads, writes, force=None):
        cp_rr[0] ^= 1
        if force == "act" or (force is None and cp_rr[0]):
            P.op("act", lambda e: e.copy(out=out, in_=in_), reads=reads, writes=writes)
        else:
            P.op("dve", lambda e: e.tensor_copy(out=out, in_=in_), reads=reads, writes=writes)

    def rms_rstd(xt, ss, junk, rstd, key):
        P.op("dve", lambda e: e.scalar_tensor_tensor(out=junk, in0=xt, scalar=1.0, in1=xt,
                                                     op0=ALU.mult, op1=ALU.mult, accum_out=ss),
             reads=[key + "x"], writes=[key + "junk", key + "ss"])
        P.op("dve", lambda e: e.tensor_scalar(out=ss, in0=ss, scalar1=1.0 / D_MODEL, scalar2=EPS,
                                              op0=ALU.mult, op1=ALU.add),
             reads=[key + "ss"], writes=[key + "ss"])
        P.op("act", lambda e: e.activation(out=ss, in_=ss, func=AF.Sqrt), reads=[key + "ss"], writes=[key + "ss"])
        P.op("dve", lambda e: e.reciprocal(out=rstd, in_=ss), reads=[key + "ss"], writes=[key + "rstd"])

    def phase_A(l):
        x_src = x_in if l == 0 else xres
        a32.mark()
        a16.mark()
        w_sb = a16.alloc(8 * WSBW, (8, WSBW))
        stage = [a32.alloc(2048) for _ in range(4)]
        nwb = a32.alloc(D_MODEL)
        xts = [a32.alloc(D_MODEL) for _ in range(2)]
        junk = a16.alloc(D_MODEL)
        sss = [a32.alloc(1) for _ in range(2)]
        rstds = [a32.alloc(1) for _ in range(2)]
        hbs = [a16.alloc(D_MODEL) for _ in range(2)]
        hTs = [a16.alloc(8 * 512, (8, 512)) for _ in range(2)]
        tmts = [a16.alloc(TMW) for _ in range(2)]
        fmts = [a16.alloc(512) for _ in range(4)]

        P.dma("sp", nwb, norm_w[l:l + 1, :].partition_broadcast(128), writes=["nwb"])
        pieces = []
        for (dc, sc, wd) in W_PIECES:
            if sc < 2048 < sc + wd:
                pieces.append((dc, sc, 2048 - sc))
                pieces.append((dc + 2048 - sc, 2048, wd - (2048 - sc)))
            else:
                pieces.append((dc, sc, wd))
        for kc in range(8):
            sp_ = (kc % 2) * 2
            P.dma("sp", stage[sp_], w_in[l, kc * 128:(kc + 1) * 128, 0:2048], writes=["stage%d" % sp_])
            P.dma("sp", stage[sp_ + 1][:, 0:IN_W - 2048], w_in[l, kc * 128:(kc + 1) * 128, 2048:IN_W],
                  writes=["stage%d" % (sp_ + 1)])
            for pi_, (dc, sc, wd) in enumerate(pieces):
                hf = 0 if sc < 2048 else 1
                si = sp_ + hf
                ce = ("pool", "dve", "act")[(pi_ + kc) % 3]
                if ce == "act":
                    P.op("act", lambda e, kc=kc, si=si, hf=hf, dc=dc, sc=sc, wd=wd:
                         e.copy(out=w_sb[:, kc, dc:dc + wd], in_=stage[si][:, sc - 2048 * hf:sc - 2048 * hf + wd]),
                         reads=["stage%d" % si], writes=["w_sb%d" % (pi_ % 4)])
                else:
                    P.op(ce, lambda e, kc=kc, si=si, hf=hf, dc=dc, sc=sc, wd=wd:
                         e.tensor_copy(out=w_sb[:, kc, dc:dc + wd], in_=stage[si][:, sc - 2048 * hf:sc - 2048 * hf + wd]),
                         reads=["stage%d" % si], writes=["w_sb%d" % (pi_ % 4)])

        fm_i = 0
        if stop <= 1:
            P.barrier(); a32.release(); a16.release(); return
        def a_stage1(t):
            j = t % 2
            xt, ss, rstd, hb = xts[j], sss[j], rstds[j], hbs[j]
            k = "A%d_" % j
            P.dma("sp", xt, x_src[t * 128:(t + 1) * 128, :],
                  reads=["xres_t%d" % t], writes=[k + "x"])
            rms_rstd(xt, ss, junk, rstd, k)
            P.op("dve", lambda e, xt=xt, rstd=rstd, hb=hb:
                 e.scalar_tensor_tensor(out=hb, in0=xt, scalar=rstd, in1=nwb, op0=ALU.mult, op1=ALU.mult),
                 reads=[k + "x", k + "rstd", "nwb"], writes=[k + "hb"])

        a_stage1(0)
        for sb in range(NSB):
            hT = hTs[sb % 2]
            hk = "hT%d" % (sb % 2)
            for ti in range(4):
                t = sb * 4 + ti
                j = t % 2
                xt, ss, rstd, hb, tmt = xts[j], sss[j], rstds[j], hbs[j], tmts[j]
                k = "A%d_" % j
                pst = psb[j]
                pstb = pst[:, 0:512].bitcast(BF16)
                for kc in range(8):
                    P.op("pe", lambda e, kc=kc, hb=hb, pstb=pstb:
                         e.transpose(out=pstb[:, kc * 128:(kc + 1) * 128], in_=hb[:, kc * 128:(kc + 1) * 128],
                                     identity=ident_b),
                         reads=[k + "hb", "ident_b"], writes=["ps%d" % j])
                evac(hT[:, :, ti * 128:(ti + 1) * 128], pstb.rearrange("p (k t) -> p k t", k=8),
                     reads=["ps%d" % j], writes=[hk])
                if t + 1 < T:
                    a_stage1(t + 1)
                if stop <= 2:
                    continue
                for c in range(4):
                    c0 = c * 512
                    n = min(512, TMW - c0)
                    pi = 2 + (t * 4 + c) % 3
                    ps = psb[pi]
                    for kc in range(8):
                        P.op("pe", lambda e, kc=kc, ps=ps, hT=hT, ti=ti, c0=c0, n=n:
                             e.matmul(ps[:, 0:n], lhsT=hT[:, kc, ti * 128:(ti + 1) * 128],
                                      rhs=w_sb[:, kc, FMW + c0:FMW + c0 + n], start=(kc == 0), stop=(kc == 7)),
                             reads=[hk, "w_sb0", "w_sb1", "w_sb2", "w_sb3"], writes=["ps%d" % pi])
                    evac(tmt[:, c0:c0 + n], ps[:, 0:n], reads=["ps%d" % pi], writes=[k + "tm"],
                         force=("act" if c in (0, 3) else None))
                    if c == 0 and stop != 31:
                        P.op("act", lambda e, ps=ps, t=t:
                             e.copy(out=small[:, t, 0:12], in_=ps[:, TM_GATE:TM_GATE + 12]),
                             reads=["ps%d" % pi], writes=["small"])
                    if c == 3 and stop != 31:
                        P.op("act", lambda e, ps=ps, t=t, c0=c0:
                             e.copy(out=small[:, t, 12:16], in_=ps[:, TM_DT - c0:TM_DT - c0 + 4]),
                             reads=["ps%d" % pi], writes=["small"])
                if stop != 32:
                    P.dma(STORE_Q, tm_d[t * 128:(t + 1) * 128, :], tmt, reads=[k + "tm"], writes=["tm_t%d" % t])
            for r in range(NFM if stop > 3 else 0):
                pi = 5 + r % 3
                ps = psb[pi]
                for kc in range(8):
                    P.op("pe", lambda e, kc=kc, ps=ps, hT=hT, r=r:
                         e.matmul(ps[:, :], lhsT=w_sb[:, kc, r * 128:(r + 1) * 128], rhs=hT[:, kc, :],
                                  start=(kc == 0), stop=(kc == 7)),
                         reads=[hk, "w_sb0", "w_sb1", "w_sb2", "w_sb3"], writes=["ps%d" % pi])
                fmt = fmts[fm_i % 4]
                fk = "fmt%d" % (fm_i % 4)
                fm_i += 1
                evac(fmt, ps[:, :], reads=["ps%d" % pi], writes=[fk])
                P.dma(STORE_Q, fm_d[r * 128:(r + 1) * 128, sb * 512:(sb + 1) * 512], fmt,
                      reads=[fk], writes=["fm_r%d" % r])
        P.barrier()
        a32.release()
        a16.release()

    def load_bcast(dst, src_row, key):
        P.dma("sp", dst, src_row.partition_broadcast(128), writes=[key])

    def head_norm_stats(o_sb, H, Dh, s1, s2, sq, key):
        o3 = o_sb.rearrange("p (h e) -> p h e", h=H)
        P.op("dve", lambda e: e.tensor_reduce(out=s1, in_=o3, axis=AX.X, op=ALU.add),
             reads=[key + "o"], writes=[key + "s1"])
        P.op("dve", lambda e: e.tensor_tensor(out=sq, in0=o_sb, in1=o_sb, op=ALU.mult),
             reads=[key + "o"], writes=[key + "sq"])
        P.op("dve", lambda e: e.tensor_reduce(out=s2, in_=sq.rearrange("p (h e) -> p h e", h=H), axis=AX.X, op=ALU.add),
             reads=[key + "sq"], writes=[key + "s2"])

    def phase_ret(l):
        decT = a32.alloc(512)
        xiT = a32.alloc(256, (2, 128))
        zeta = a32.alloc(256)
        cdt = a32.alloc(4)
        gnw = a32.alloc(256)
        state = a32.alloc(128, (2, 64))
        state_b = a16.alloc(128, (2, 64))
        P.dma("sp", decT, c_ret_decT, writes=["decT"])
        P.dma("sp", xiT, c_ret_xiT.rearrange("p (r q) -> p r q", r=2), writes=["xiT"])
        P.dma("sp", zeta, c_ret_zeta, writes=["zeta"])
        P.dma("sp", cdt, c_ret_cd, writes=["cdt"])
        load_bcast(gnw, ret_gn_w[l:l + 1, :], "gnw")
        P.op("dve", lambda e: e.memset(state, 0.0), writes=["state"])
        P.op("dve", lambda e: e.memset(state_b, 0.0), writes=["state_b"])
        NB = 2
        qTs = [a16.alloc(256, (2, 128)) for _ in range(NB)]
        kTs = [a16.alloc(256, (2, 128)) for _ in range(NB)]
        tms = [a16.alloc(768) for _ in range(NB)]
        ATs = [a16.alloc(512) for _ in range(NB)]
        qxs = [a16.alloc(256, (2, 128)) for _ in range(NB)]
        kzs = [a16.alloc(256) for _ in range(NB)]
        osb = [a32.alloc(256) for _ in range(NB)]
        sqs = [a32.alloc(256) for _ in range(NB)]
        szs = [a32.alloc(256) for _ in range(NB)]
        st1 = [a32.alloc(4) for _ in range(NB)]
        st2 = [a32.alloc(4) for _ in range(NB)]
        mos = [a16.alloc(256) for _ in range(NB)]
        for t in range(T):
            j = t % NB
            k = "R%d_" % j
            qT, kT, tm, AT, qx, kz, o_sb, sq, sz, s1, s2 = (qTs[j], kTs[j], tms[j], ATs[j], qxs[j], kzs[j],
                                                          osb[j], sqs[j], szs[j], st1[j], st2[j])
            c0 = t * 128
            P.dma("sp", qT, fm_d[FM_RQ * 128:(FM_RQ + 2) * 128, c0:c0 + 128].rearrange("(r p) s -> p r s", p=128),
                  writes=[k + "qT"])
            P.dma("sp", kT, fm_d[FM_RK * 128:(FM_RK + 2) * 128, c0:c0 + 128].rearrange("(r p) s -> p r s", p=128),
                  writes=[k + "kT"])
            P.dma("sp", tm, tm_d[c0:c0 + 128, TM_RK:TM_RK + 768], writes=[k + "tm"])
            ps_sg = [psb[6][:, 0:256], psb[7][:, 0:256]]
            ps_xg = [psb[6][:, 256:384], psb[7][:, 256:384]]
            ps_o, ps_kv = psb[6][:, 0:256], psb[7][:, 0:256]
            ks_o, ks_kv = "ps6", "ps7"
            yield
            for h in range(4):
                hp, hr, g = (h % 2) * 64, h // 2, h % 2
                P.op("pe", lambda e, hp=hp, hr=hr, g=g, kT=kT, qT=qT:
                     e.matmul(ps_sg[g][:, hr * 128:(hr + 1) * 128], lhsT=kT[hp:hp + 64, hr, :], rhs=qT[hp:hp + 64, hr, :],
                              start=True, stop=True),
                     reads=[k + "qT", k + "kT"], writes=["ps%d" % (6 + g)])
            yield
            for g in range(2):
                P.op("dve", lambda e, g=g, AT=AT: e.tensor_tensor(out=AT[:, g * 256:(g + 1) * 256], in0=ps_sg[g][:, 0:256],
                                                                  in1=decT[:, g * 256:(g + 1) * 256], op=ALU.mult),
                     reads=["ps%d" % (6 + g), "decT"], writes=[k + "AT"])
            P.op("dve", lambda e, qx=qx, qT=qT: e.tensor_tensor(out=qx, in0=qT, in1=xiT, op=ALU.mult),
                 reads=[k + "qT", "xiT"], writes=[k + "qx"])
            P.op("dve", lambda e, kz=kz, tm=tm: e.tensor_tensor(out=kz, in0=tm[:, 0:256], in1=zeta, op=ALU.mult),
                 reads=[k + "tm", "zeta"], writes=[k + "kz"])
            for h in range(4):
                hp, hr, g = (h % 2) * 64, h // 2, h % 2
                ai = g * 2 + hr
                P.op("pe", lambda e, h=h, ai=ai, ps_o=ps_o, AT=AT, tm=tm:
                     e.matmul(ps_o[:, h * 64:(h + 1) * 64], lhsT=AT[:, ai * 128:(ai + 1) * 128],
                              rhs=tm[:, 256 + h * 64:256 + (h + 1) * 64], start=True, stop=True),
                     reads=[k + "AT", k + "tm"], writes=[ks_o])
                if t > 0:
                    P.op("pe", lambda e, hp=hp, hr=hr, g=g, qx=qx:
                         e.matmul(ps_xg[g][:, hr * 64:(hr + 1) * 64], lhsT=qx[hp:hp + 64, hr, :],
                                  rhs=state_b[hp:hp + 64, hr, :], start=True, stop=True),
                         reads=[k + "qx", "state_b"], writes=["ps%d" % (6 + g)])
            yield
            if t < T - 1:
                for r in range(2):
                    P.op("pe", lambda e, r=r, ps_kv=ps_kv, kz=kz, tm=tm:
                         e.matmul(ps_kv[:, r * 128:(r + 1) * 128], lhsT=kz[:, r * 128:(r + 1) * 128],
                                  rhs=tm[:, 256 + r * 128:256 + (r + 1) * 128], start=True, stop=True),
                         reads=[k + "kz", k + "tm"], writes=[ks_kv])
                for h in range(4):
                    hp, hr = (h % 2) * 64, h // 2
                    P.op("dve", lambda e, h=h, hp=hp, hr=hr, ps_kv=ps_kv:
                         e.scalar_tensor_tensor(out=state[hp:hp + 64, hr, :], in0=state[hp:hp + 64, hr, :],
                                                scalar=cdt[hp:hp + 64, h:h + 1],
                                                in1=ps_kv[hp:hp + 64, hr * 128 + (h % 2) * 64:hr * 128 + (h % 2) * 64 + 64],
                                                op0=ALU.mult, op1=ALU.add),
                         reads=[ks_kv, "cdt", "state"], writes=["state"])
                P.op("dve", lambda e: e.tensor_copy(out=state_b, in_=state), reads=["state"], writes=["state_b"])
            yield
            P.op("act", lambda e, o_sb=o_sb, ps_o=ps_o: e.copy(out=o_sb, in_=ps_o[:, 0:256]),
                 reads=[ks_o], writes=[k + "o"])
            if t > 0:
                for h in range(4):
                    hr, g = h // 2, h % 2
                    P.op("dve", lambda e, h=h, hr=hr, g=g, o_sb=o_sb:
                         e.tensor_tensor(out=o_sb[:, h * 64:(h + 1) * 64], in0=ps_xg[g][:, hr * 64:(hr + 1) * 64],
                                         in1=o_sb[:, h * 64:(h + 1) * 64], op=ALU.add),
                         reads=["ps%d" % (6 + g), k + "o"], writes=[k + "o"])
            P.op("act", lambda e, sz=sz, tm=tm: e.activation(out=sz, in_=tm[:, 512:768], func=AF.Silu),
                 reads=[k + "tm"], writes=[k + "sz"])
            head_norm_stats(o_sb, 4, 64, s1, s2, sq, k)
            P.op("dve", lambda e, s1=s1: e.tensor_scalar(out=s1, in0=s1, scalar1=1.0 / 64, scalar2=None, op0=ALU.mult),
                 reads=[k + "s1"], writes=[k + "s1"])
            P.op("dve", lambda e, s1=s1, s2=s2, sq=sq: e.tensor_tensor(out=sq[:, 0:4], in0=s1, in1=s1, op=ALU.mult),
                 reads=[k + "s1"], writes=[k + "sq"])
            P.op("dve", lambda e, s2=s2, sq=sq: e.scalar_tensor_tensor(out=s2, in0=s2, scalar=1.0 / 64, in1=sq[:, 0:4],
                                                                       op0=ALU.mult, op1=ALU.subtract),
                 reads=[k + "s2", k + "sq"], writes=[k + "s2"])
            P.op("dve", lambda e, s2=s2: e.tensor_scalar(out=s2, in0=s2, scalar1=EPS, scalar2=None, op0=ALU.add),
                 reads=[k + "s2"], writes=[k + "s2"])
            P.op("act", lambda e, s2=s2: e.activation(out=s2, in_=s2, func=AF.Sqrt), reads=[k + "s2"], writes=[k + "s2"])
            P.op("dve", lambda e, s2=s2: e.reciprocal(out=s2, in_=s2), reads=[k + "s2"], writes=[k + "s2"])
            for h in range(4):
                P.op("dve", lambda e, h=h, o_sb=o_sb, s1=s1, s2=s2:
                     e.tensor_scalar(out=o_sb[:, h * 64:(h + 1) * 64], in0=o_sb[:, h * 64:(h + 1) * 64],
                                     scalar1=s1[:, h:h + 1], scalar2=s2[:, h:h + 1], op0=ALU.subtract, op1=ALU.mult),
                     reads=[k + "o", k + "s1", k + "s2"], writes=[k + "o"])
            P.op("dve", lambda e, sz=sz: e.tensor_tensor(out=sz, in0=sz, in1=gnw, op=ALU.mult),
                 reads=[k + "sz", "gnw"], writes=[k + "sz"])
            mo = mos[j]
            P.op("dve", lambda e, mo=mo, o_sb=o_sb, sz=sz:
                 e.tensor_tensor(out=mo, in0=o_sb, in1=sz, op=ALU.mult),
                 reads=[k + "o", k + "sz"], writes=[k + "mo"])
            P.dma(STORE_Q, mixed_d[t * 128:(t + 1) * 128, 512:768], mo, reads=[k + "mo"], writes=["mixR%d" % t])
            yield

    def phase_ssd(l):
        triu = a32.alloc(128)
        ones_f = a32.alloc(128)
        cw = a32.alloc(24)
        cb = a32.alloc(6)
        vec = a32.alloc(12)
        snw = a32.alloc(256)
        dtt = a32.alloc(T * 4)
        dAt = a32.alloc(T * 4)
        cst = a32.alloc(T * 4)
        ecs = a32.alloc(T * 4)
        dsd = a32.alloc(T * 4)
        ecl = a32.alloc(T * 4)
        Sst = a32.alloc(256, (4, 64))
        Sst_b = a16.alloc(256, (4, 64))
        maskT = a32.alloc(128)
        P.dma("sp", triu, c_triu, writes=["triu"])
        P.dma("sp", cw, conv_wT[l], writes=["cw"])
        P.dma("sp", cb, conv_bT[l], writes=["cb"])
        load_bcast(vec, ssm_vec[l:l + 1, :], "vec")
        load_bcast(snw, ssm_norm_w[l:l + 1, :], "snw")
        P.op("dve", lambda e: e.memset(ones_f, 1.0), writes=["ones_f"])
        P.op("dve", lambda e: e.memset(Sst, 0.0), writes=["Sst"])
        P.op("dve", lambda e: e.memset(Sst_b, 0.0), writes=["Sst_b"])
        P.op("dve", lambda e: e.tensor_copy(out=maskT, in_=triu), reads=["triu"], writes=["maskT"])
        sm3 = small[:, :, 12:16]
        dt3 = dtt.rearrange("p (t h) -> p t h", h=4)
        dA3 = dAt.rearrange("p (t h) -> p t h", h=4)
        for h in range(4):
            P.op("dve", lambda e, h=h: e.tensor_scalar(out=dt3[:, :, h], in0=sm3[:, :, h], scalar1=vec[:, h:h + 1],
                                                       scalar2=None, op0=ALU.add),
                 reads=["small", "vec"], writes=["dtt"])
        P.op("act", lambda e: e.activation(out=dtt, in_=dtt, func=AF.Exp), reads=["dtt"], writes=["dtt"])
        P.op("act", lambda e: e.activation(out=dtt, in_=dtt, func=AF.Ln, bias=1.0), reads=["dtt"], writes=["dtt"])
        P.op("act", lambda e: e.activation(out=vec[:, 4:8], in_=vec[:, 4:8], func=AF.Exp), reads=["vec"], writes=["vec"])
        for h in range(4):
            P.op("dve", lambda e, h=h: e.tensor_scalar(out=dA3[:, :, h], in0=dt3[:, :, h], scalar1=vec[:, 4 + h:5 + h],
                                                       scalar2=-1.0, op0=ALU.mult, op1=ALU.mult),
                 reads=["dtt", "vec"], writes=["dAt"])
        psA, psB = psb[6], psb[7]
        P.op("pe", lambda e: e.matmul(psA[:, 0:T * 4], lhsT=triu, rhs=dAt, start=True, stop=True),
             reads=["triu", "dAt"], writes=["ps6"])
        P.op("pe", lambda e: e.matmul(psB[:, 0:T * 4], lhsT=ones_f, rhs=dAt, start=True, stop=True),
             reads=["ones_f", "dAt"], writes=["ps7"])
        P.op("act", lambda e: e.copy(out=cst, in_=psA[:, 0:T * 4]), reads=["ps6"], writes=["cst"])
        P.op("act", lambda e: e.activation(out=ecs, in_=cst, func=AF.Exp), reads=["cst"], writes=["ecs"])
        P.op("act", lambda e: e.activation(out=ecl, in_=psB[:, 0:T * 4], func=AF.Exp), reads=["ps7"], writes=["ecl"])
        P.op("dve", lambda e: e.tensor_tensor(out=dsd, in0=psB[:, 0:T * 4], in1=cst, op=ALU.subtract),
             reads=["ps7", "cst"], writes=["dsd"])
        P.op("act", lambda e: e.activation(out=dsd, in_=dsd, func=AF.Exp), reads=["dsd"], writes=["dsd"])
        P.op("dve", lambda e: e.tensor_tensor(out=dsd, in0=dsd, in1=dtt, op=ALU.mult),
             reads=["dsd", "dtt"], writes=["dsd"])

        xin = [a16.alloc(6 * 516, (6, 516)) for _ in range(2)]
        acc = a32.alloc(512)
        cvo = [a16.alloc(6 * 512, (6, 512)) for _ in range(2)]
        NB = 2
        xtm = [a32.alloc(256) for _ in range(NB)]
        btm = [a16.alloc(256) for _ in range(NB)]
        xdt = [a16.alloc(256) for _ in range(NB)]
        xdd = [a16.alloc(256) for _ in range(NB)]
        cbm = [a32.alloc(256) for _ in range(NB)]
        uda = [a32.alloc(128) for _ in range(NB)]
        dif = [a32.alloc(128) for _ in range(NB)]
        MTs = [a16.alloc(512) for _ in range(NB)]
        ysb = [a32.alloc(256) for _ in range(NB)]
        zts = [a16.alloc(256) for _ in range(NB)]
        szs = [a32.alloc(256) for _ in range(NB)]
        sqs = [a32.alloc(256) for _ in range(NB)]
        sss = [a32.alloc(1) for _ in range(NB)]
        mos = [a16.alloc(256) for _ in range(NB)]
        for sb in range(NSB):
            xi = xin[sb % 2]
            co = cvo[sb % 2]
            kx = "xin%d" % (sb % 2)
            kc_ = "cvo%d" % (sb % 2)
            src = fm_d[FM_XBC * 128:(FM_XBC + 6) * 128, :].rearrange("(r p) s -> p r s", p=128)
            if sb == 0:
                P.op("pool", lambda e, xi=xi: e.memset(xi[:, :, 0:3], 0.0), writes=[kx])
                P.dma("sp", xi[:, :, 3:515], src[:, :, 0:512], writes=[kx])
            else:
                P.dma("sp", xi[:, :, 0:515], src[:, :, sb * 512 - 3:sb * 512 + 512], writes=[kx])
            for r in range(6):
                for jj in range(4):
                    if jj == 0:
                        P.op("pool", lambda e, r=r, xi=xi: e.tensor_scalar(
                            out=acc, in0=xi[:, r, 0:512], scalar1=cw[:, r * 4:r * 4 + 1], scalar2=None, op0=ALU.mult),
                            reads=[kx, "cw"], writes=["acc"])
                    else:
                        P.op("dve", lambda e, r=r, jj=jj, xi=xi: e.scalar_tensor_tensor(
                            out=acc, in0=xi[:, r, jj:jj + 512], scalar=cw[:, r * 4 + jj:r * 4 + jj + 1], in1=acc,
                            op0=ALU.mult, op1=ALU.add),
                            reads=[kx, "cw", "acc"], writes=["acc"])
                P.op("act", lambda e, r=r, co=co: e.activation(out=co[:, r, :], in_=acc, func=AF.Silu, bias=cb[:, r:r + 1]),
                     reads=["acc", "cb"], writes=[kc_])
                yield
            for ti in range(4):
                t = sb * 4 + ti
                j = t % NB
                k = "S%d_" % j
                cs0 = ti * 128
                x_tm, b_tm, xd, xdd_, cbm_, ud, df, MT, y_sb, zt, sz, sq, ss = (
                    xtm[j], btm[j], xdt[j], xdd[j], cbm[j], uda[j], dif[j], MTs[j], ysb[j], zts[j], szs[j], sqs[j], sss[j])
                P.dma("sp", zt, tm_d[t * 128:(t + 1) * 128, TM_SZ:TM_SZ + 256], writes=[k + "z"])
                pTb = psb[6][:, 0:256].bitcast(BF16)
                for r in range(4):
                    P.op("pe", lambda e, r=r, pTb=pTb, co=co, cs0=cs0:
                         e.transpose(out=pTb[:, r * 128:(r + 1) * 128], in_=co[:, r, cs0:cs0 + 128], identity=ident_b),
                         reads=[kc_, "ident_b"], writes=["ps6"])
                yield
                P.op("act", lambda e, x_tm=x_tm, pTb=pTb: e.copy(out=x_tm, in_=pTb[:, 0:256]),
                     reads=["ps6"], writes=[k + "x"])
                P.op("act", lambda e, b_tm=b_tm, pTb=pTb: e.copy(out=b_tm, in_=pTb[:, 256:512]),
                     reads=["ps6"], writes=[k + "b"])
                for h in range(4):
                    P.op("dve", lambda e, h=h, xd=xd, x_tm=x_tm, t=t: e.tensor_scalar(
                        out=xd[:, h * 64:(h + 1) * 64], in0=x_tm[:, h * 64:(h + 1) * 64],
                        scalar1=dtt[:, t * 4 + h:t * 4 + h + 1], scalar2=None, op0=ALU.mult),
                        reads=[k + "x", "dtt"], writes=[k + "xd"])
                    P.op("dve", lambda e, h=h, xdd_=xdd_, x_tm=x_tm, t=t: e.tensor_scalar(
                        out=xdd_[:, h * 64:(h + 1) * 64], in0=x_tm[:, h * 64:(h + 1) * 64],
                        scalar1=dsd[:, t * 4 + h:t * 4 + h + 1], scalar2=None, op0=ALU.mult),
                        reads=[k + "x", "dsd"], writes=[k + "xdd"])
                yield
                pcb = psb[6][:, 256:512]
                kcb = "ps6"
                for g in range(2):
                    P.op("pe", lambda e, g=g, pcb=pcb, co=co, cs0=cs0:
                         e.matmul(pcb[:, g * 128:(g + 1) * 128], lhsT=co[:, 2 + g, cs0:cs0 + 128],
                                  rhs=co[:, 4 + g, cs0:cs0 + 128], start=True, stop=True),
                         reads=[kc_], writes=[kcb])
                for g in range(2):
                    P.op("dve", lambda e, g=g, cbm_=cbm_, pcb=pcb:
                         e.tensor_tensor(out=cbm_[:, g * 128:(g + 1) * 128], in0=pcb[:, g * 128:(g + 1) * 128],
                                         in1=maskT, op=ALU.mult),
                         reads=[kcb, "maskT"], writes=[k + "cbm"])
                pcs = psb[7]
                kcs = "ps7"
                for h in range(4):
                    yield
                    P.op("dve", lambda e, h=h, ud=ud, t=t: e.tensor_scalar(
                        out=ud, in0=triu, scalar1=dAt[:, t * 4 + h:t * 4 + h + 1], scalar2=None, op0=ALU.mult),
                        reads=["triu", "dAt"], writes=[k + "ud"])
                    P.op("pe", lambda e, h=h, pcs=pcs, ud=ud:
                         e.matmul(pcs[:, h * 128:(h + 1) * 128], lhsT=ones_f, rhs=ud, start=True, stop=True),
                         reads=["ones_f", k + "ud"], writes=[kcs])
                    P.op("dve", lambda e, h=h, df=df, pcs=pcs, t=t: e.tensor_scalar(
                        out=df, in0=pcs[:, h * 128:(h + 1) * 128], scalar1=cst[:, t * 4 + h:t * 4 + h + 1],
                        scalar2=0.0, op0=ALU.subtract, op1=ALU.min),
                        reads=[kcs, "cst"], writes=[k + "df"])
                    P.op("act", lambda e, df=df: e.activation(out=df, in_=df, func=AF.Exp),
                         reads=[k + "df"], writes=[k + "df"])
                    P.op("dve", lambda e, h=h, MT=MT, df=df, cbm_=cbm_: e.tensor_tensor(
                        out=MT[:, h * 128:(h + 1) * 128], in0=df, in1=cbm_[:, (h // 2) * 128:(h // 2 + 1) * 128],
                        op=ALU.mult),
                        reads=[k + "df", k + "cbm"], writes=[k + "MT"])
                yield
                py, pyo, pst = psb[6][:, 0:256], psb[6][:, 256:512], psb[7][:, 0:256]
                for h in range(4):
                    P.op("pe", lambda e, h=h, py=py, MT=MT, xd=xd:
                         e.matmul(py[:, h * 64:(h + 1) * 64], lhsT=MT[:, h * 128:(h + 1) * 128],
                                  rhs=xd[:, h * 64:(h + 1) * 64], start=True, stop=True),
                         reads=[k + "MT", k + "xd"], writes=["ps6"])
                if t > 0:
                    for h in range(4):
                        P.op("pe", lambda e, h=h, pyo=pyo, co=co, cs0=cs0:
                             e.matmul(pyo[:, h * 64:(h + 1) * 64], lhsT=co[:, 4 + h // 2, cs0:cs0 + 128],
                                      rhs=Sst_b[:, h, :], start=True, stop=True),
                             reads=[kc_, "Sst_b"], writes=["ps6"])
                yield
                P.op("act", lambda e, y_sb=y_sb, py=py: e.copy(out=y_sb, in_=py[:, 0:256]),
                     reads=["ps6"], writes=[k + "y"])
                for h in range(4):
                    if t > 0:
                        P.op("dve", lambda e, h=h, y_sb=y_sb, pyo=pyo, t=t: e.scalar_tensor_tensor(
                            out=y_sb[:, h * 64:(h + 1) * 64], in0=pyo[:, h * 64:(h + 1) * 64],
                            scalar=ecs[:, t * 4 + h:t * 4 + h + 1], in1=y_sb[:, h * 64:(h + 1) * 64],
                            op0=ALU.mult, op1=ALU.add),
                            reads=["ps6", "ecs", k + "y"], writes=[k + "y"])
                    P.op("dve", lambda e, h=h, y_sb=y_sb, x_tm=x_tm: e.scalar_tensor_tensor(
                        out=y_sb[:, h * 64:(h + 1) * 64], in0=x_tm[:, h * 64:(h + 1) * 64],
                        scalar=vec[:, 8 + h:9 + h], in1=y_sb[:, h * 64:(h + 1) * 64], op0=ALU.mult, op1=ALU.add),
                        reads=[k + "x", "vec", k + "y"], writes=[k + "y"])
                yield
                if t < T - 1:
                    kst = "ps7"
                    for h in range(4):
                        P.op("pe", lambda e, h=h, pst=pst, b_tm=b_tm, xdd_=xdd_:
                             e.matmul(pst[:, h * 64:(h + 1) * 64], lhsT=b_tm[:, (h // 2) * 128:(h // 2 + 1) * 128],
                                      rhs=xdd_[:, h * 64:(h + 1) * 64], start=True, stop=True),
                             reads=[k + "b", k + "xdd"], writes=[kst])
                    for h in range(4):
                        P.op("dve", lambda e, h=h, pst=pst, t=t: e.scalar_tensor_tensor(
                            out=Sst[:, h, :], in0=Sst[:, h, :], scalar=ecl[:, t * 4 + h:t * 4 + h + 1],
                            in1=pst[:, h * 64:(h + 1) * 64], op0=ALU.mult, op1=ALU.add),
                            reads=[kst, "ecl", "Sst"], writes=["Sst"])
                    P.op("dve", lambda e: e.tensor_copy(out=Sst_b, in_=Sst), reads=["Sst"], writes=["Sst_b"])
                yield
                P.op("act", lambda e, sz=sz, zt=zt: e.activation(out=sz, in_=zt, func=AF.Silu),
                     reads=[k + "z"], writes=[k + "sz"])
                P.op("dve", lambda e, y_sb=y_sb, sz=sz: e.tensor_tensor(out=y_sb, in0=y_sb, in1=sz, op=ALU.mult),
                     reads=[k + "y", k + "sz"], writes=[k + "y"])
                P.op("dve", lambda e, y_sb=y_sb, sq=sq, ss=ss: e.scalar_tensor_tensor(
                    out=sq, in0=y_sb, scalar=1.0, in1=y_sb, op0=ALU.mult, op1=ALU.mult, accum_out=ss),
                    reads=[k + "y"], writes=[k + "sq", k + "ss"])
                P.op("dve", lambda e, ss=ss: e.tensor_scalar(out=ss, in0=ss, scalar1=1.0 / 256, scalar2=EPS,
                                                             op0=ALU.mult, op1=ALU.add),
                     reads=[k + "ss"], writes=[k + "ss"])
                P.op("act", lambda e, ss=ss: e.activation(out=ss, in_=ss, func=AF.Sqrt), reads=[k + "ss"], writes=[k + "ss"])
                P.op("dve", lambda e, ss=ss: e.reciprocal(out=ss, in_=ss), reads=[k + "ss"], writes=[k + "ss"])
                mo = mos[j]
                P.op("dve", lambda e, mo=mo, y_sb=y_sb, ss=ss: e.scalar_tensor_tensor(
                    out=mo, in0=y_sb, scalar=ss, in1=snw, op0=ALU.mult, op1=ALU.mult),
                    reads=[k + "y", k + "ss", "snw"], writes=[k + "mo"])
                P.dma(STORE_Q, mixed_d[t * 128:(t + 1) * 128, 768:1024], mo, reads=[k + "mo"], writes=["mixS%d" % t])
                yield

    def phase_diff(l):
        scale = 32 ** -0.5
        lam_init = 0.8 - 0.6 * math.exp(-0.3 * l)
        dslopes = [2.0 ** (-8.0 * (i + 1) / 8) for i in range(8)][1::2]
        alib = a32.alloc(2 * 4 * 35, (2, 4, 35))
        tri01 = a16.alloc(128)
        tri_f = a32.alloc(128)
        dv = a32.alloc(192)
        lamt = a32.alloc(4)
        sw = a32.alloc(256)
        P.dma("sp", alib, c_alibi.rearrange("p (s h d) -> p s h d", s=2, h=4), writes=["alib"])
        P.dma("sp", tri_f, c_tri01, writes=["tri_f"])
        P.op("dve", lambda e: e.tensor_copy(out=tri01, in_=tri_f), reads=["tri_f"], writes=["tri01"])
        load_bcast(dv, diff_vec[l:l + 1, :], "dv")
        P.op("dve", lambda e: e.scalar_tensor_tensor(out=dv[:, 0:32], in0=dv[:, 0:32], scalar=1.0, in1=dv[:, 32:64],
                                                     op0=ALU.mult, op1=ALU.mult, accum_out=lamt[:, 0:1]),
             reads=["dv"], writes=["dv", "lamt"])
        P.op("dve", lambda e: e.scalar_tensor_tensor(out=dv[:, 64:96], in0=dv[:, 64:96], scalar=1.0, in1=dv[:, 96:128],
                                                     op0=ALU.mult, op1=ALU.mult, accum_out=lamt[:, 1:2]),
             reads=["dv", "lamt"], writes=["dv", "lamt"])
        P.op("act", lambda e: e.activation(out=lamt[:, 0:2], in_=lamt[:, 0:2], func=AF.Exp), reads=["lamt"], writes=["lamt"])
        P.op("dve", lambda e: e.tensor_tensor(out=lamt[:, 2:3], in0=lamt[:, 1:2], in1=lamt[:, 0:1], op=ALU.subtract),
             reads=["lamt"], writes=["lamt"])
        P.op("dve", lambda e: e.tensor_scalar(out=lamt[:, 2:3], in0=lamt[:, 2:3], scalar1=-lam_init, scalar2=None,
                                              op0=ALU.add),
             reads=["lamt"], writes=["lamt"])
        for h in range(4):
            P.op("dve", lambda e, h=h: e.tensor_scalar(out=sw[:, h * 64:(h + 1) * 64], in0=dv[:, 128:192],
                                                       scalar1=1.0 - lam_init, scalar2=None, op0=ALU.mult),
                 reads=["dv"], writes=["sw"])
        qT = a16.alloc(2 * S, (2, S))
        kT = a16.alloc(2 * S, (2, S))
        va = a16.alloc(T * 4 * 65, (T, 4, 65))
        for r in range(2):
            P.dma("sp", qT[:, r, :], fm_d[(FM_DQ + r) * 128:(FM_DQ + r + 1) * 128, :], writes=["dqT"])
            P.dma("sp", kT[:, r, :], fm_d[(FM_DK + r) * 128:(FM_DK + r + 1) * 128, :], writes=["dkT"])
        P.op("pool", lambda e: e.memset(va, 1.0), writes=["va%d" % t_ for t_ in range(T)])
        for t in range(T):
            P.dma("sp", va[:, t, :, 0:64], tm_d[t * 128:(t + 1) * 128, TM_DV:TM_DV + 256].rearrange("p (h e) -> p h e", h=4),
                  writes=["va%d" % t])
        ETp = [a16.alloc(1024, (2, 512)) for _ in range(2)]
        et_i = [0]
        oTs = [a32.alloc(512) for _ in range(2)]
        dso = a32.alloc(4 * 256, (4, 256))
        rz = a32.alloc(2)
        zts = [a16.alloc(256) for _ in range(2)]
        szs = [a32.alloc(256) for _ in range(2)]
        sq = a32.alloc(256)
        s2 = a32.alloc(4)
        mos = [a16.alloc(256) for _ in range(2)]
        ot_i = 0
        yield
        for sb in range(NSB):
            q0 = sb * 512
            for h in range(4):
                r, hl = h // 2, h % 2
                nkb = 4 * sb + 4
                W = 256 if dslopes[h] * 511 > 80 else 512
                def qk(kb):
                    par = kb % 2
                    qlo = max(0, kb - 4 * sb) * 128
                    for i in range(2):
                        rg = hl * 2 + i
                        bk = par * 2 + i
                        P.op("pe", lambda e, bk=bk, rg=rg, kb=kb, qlo=qlo, r=r, q0=q0:
                             e.matmul(psb[bk][:, qlo:512], lhsT=kT[rg * 32:(rg + 1) * 32, r, kb * 128:(kb + 1) * 128],
                                      rhs=qT[rg * 32:(rg + 1) * 32, r, q0 + qlo:q0 + 512], start=True, stop=True,
                                      tile_position=(rg * 32, 0)),
                             reads=["dqT", "dkT"], writes=["ps%d" % bk])

                def rest(kb):
                    par = kb % 2
                    rel = kb - 4 * sb
                    qlo = max(0, rel) * 128
                    ET = ETp[et_i[0] % 2]
                    ek = "ETp%d" % (et_i[0] % 2)
                    et_i[0] += 1
                    pair = pbig[:, par * 1024:(par + 1) * 1024].rearrange("p (i c) -> p i c", i=2)
                    for sub in range(512 // W):
                        a, b = max(qlo, sub * W), (sub + 1) * W
                        if a >= b:
                            continue
                        dd = (kb * 128 - (q0 + sub * W)) // 128
                        P.op("act", lambda e, ET=ET, a=a, b=b, dd=dd, pair=pair, h=h:
                             e.activation(out=ET[:, :, a:b], in_=pair[:, :, a:b], func=AF.Exp,
                                          bias=alib[:, 1, h, dd + 31:dd + 32], scale=scale),
                             reads=["ps%d" % (par * 2), "ps%d" % (par * 2 + 1), "alib"], writes=[ek])
                    if rel >= 0:
                        for i in range(2):
                            P.op("pool", lambda e, i=i, ET=ET, qlo=qlo:
                                 e.tensor_tensor(out=ET[:, i, qlo:qlo + 128], in0=ET[:, i, qlo:qlo + 128], in1=tri01, op=ALU.mult),
                                 reads=[ek, "tri01"], writes=[ek])
                    for i in range(2):
                        P.op("pe", lambda e, i=i, ET=ET, kb=kb, qlo=qlo, h=h, nkb=nkb:
                             e.matmul(psb[4 + i][0:65, qlo:512], lhsT=va[:, kb, h, :], rhs=ET[:, i, qlo:512],
                                      start=(kb == 0), stop=(kb == nkb - 1)),
                             reads=[ek, "va%d" % kb], writes=["ps%d" % (4 + i)])

                qk(0)
                for kb in range(nkb):
                    if kb + 1 < nkb:
                        qk(kb + 1)
                    rest(kb)
                    yield
                for i in range(2):
                    oT = oTs[ot_i % 2]
                    ok = "oT%d" % (ot_i % 2)
                    ot_i += 1
                    evac(oT[0:65, :], psb[4 + i][0:65, :], reads=["ps%d" % (4 + i)], writes=[ok])
                    for qt in range(4):
                        cbi = ((qt % 2) * 2 + i) * 65
                        P.op("pe", lambda e, qt=qt, cbi=cbi, oT=oT:
                             e.transpose(out=psb[qt // 2][:, cbi:cbi + 65], in_=oT[0:65, qt * 128:(qt + 1) * 128],
                                         identity=ident_f[0:65, 0:65]),
                             reads=[ok, "ident_f"], writes=["ps%d" % (qt // 2)])
                yield
                for qt in range(4):
                    pq = psb[qt // 2]
                    kq = "ps%d" % (qt // 2)
                    cb0 = (qt % 2) * 130
                    P.op("dve", lambda e, pq=pq, cb0=cb0: e.tensor_scalar(
                        out=rz, in0=pq[:, cb0:cb0 + 130].rearrange("p (s c) -> p s c", c=65)[:, :, 64], scalar1=1e-30,
                        scalar2=None, op0=ALU.max),
                        reads=[kq], writes=["rz"])
                    P.op("dve", lambda e: e.reciprocal(out=rz, in_=rz), reads=["rz"], writes=["rz"])
                    P.op("dve", lambda e: e.tensor_tensor(out=rz[:, 1:2], in0=rz[:, 1:2], in1=lamt[:, 2:3], op=ALU.mult),
                         reads=["rz", "lamt"], writes=["rz"])
                    P.op("dve", lambda e, h=h, qt=qt, pq=pq, cb0=cb0: e.tensor_scalar(
                        out=dso[:, qt, h * 64:(h + 1) * 64], in0=pq[:, cb0:cb0 + 64],
                        scalar1=rz[:, 0:1], scalar2=None, op0=ALU.mult),
                        reads=[kq, "rz"], writes=["dso"])
                    P.op("dve", lambda e, h=h, qt=qt, pq=pq, cb0=cb0: e.scalar_tensor_tensor(
                        out=dso[:, qt, h * 64:(h + 1) * 64], in0=pq[:, cb0 + 65:cb0 + 129],
                        scalar=rz[:, 1:2], in1=dso[:, qt, h * 64:(h + 1) * 64],
                        op0=ALU.mult, op1=ALU.add),
                        reads=[kq, "rz", "dso"], writes=["dso"])
                yield
            for qt in range(4):
                t = sb * 4 + qt
                j = t % 2
                zt, sz = zts[j], szs[j]
                k = "D%d_" % j
                P.dma("sp", zt, tm_d[t * 128:(t + 1) * 128, TM_DZ:TM_DZ + 256], writes=[k + "z"])
                P.op("act", lambda e, sz=sz, zt=zt: e.activation(out=sz, in_=zt, func=AF.Silu), reads=[k + "z"], writes=[k + "sz"])
                P.op("dve", lambda e, sz=sz: e.tensor_tensor(out=sz, in0=sz, in1=sw, op=ALU.mult),
                     reads=[k + "sz", "sw"], writes=[k + "sz"])
                P.op("dve", lambda e, qt=qt: e.tensor_tensor(out=sq, in0=dso[:, qt, :], in1=dso[:, qt, :], op=ALU.mult),
                     reads=["dso"], writes=["dsq"])
                P.op("dve", lambda e: e.tensor_reduce(out=s2, in_=sq.rearrange("p (h e) -> p h e", h=4), axis=AX.X, op=ALU.add),
                     reads=["dsq"], writes=["ds2"])
                P.op("dve", lambda e: e.tensor_scalar(out=s2, in0=s2, scalar1=1.0 / 64, scalar2=EPS, op0=ALU.mult, op1=ALU.add),
                     reads=["ds2"], writes=["ds2"])
                P.op("act", lambda e: e.activation(out=s2, in_=s2, func=AF.Sqrt), reads=["ds2"], writes=["ds2"])
                P.op("dve", lambda e: e.reciprocal(out=s2, in_=s2), reads=["ds2"], writes=["ds2"])
                mo = mos[j]
                for hh in range(4):
                    P.op("dve", lambda e, hh=hh, qt=qt, mo=mo, sz=sz: e.scalar_tensor_tensor(
                        out=mo[:, hh * 64:(hh + 1) * 64], in0=dso[:, qt, hh * 64:(hh + 1) * 64],
                        scalar=s2[:, hh:hh + 1], in1=sz[:, hh * 64:(hh + 1) * 64], op0=ALU.mult, op1=ALU.mult),
                        reads=["dso", "ds2", k + "sz"], writes=[k + "mo"])
                P.dma(STORE_Q, mixed_d[t * 128:(t + 1) * 128, 256:512], mo, reads=[k + "mo"], writes=["mixD%d" % t])
                yield

    def phase_nsa(l):
        a32.mark()
        a16.mark()
        scale = 64 ** -0.5
        nslopes = [2.0 ** (-8.0 * (i + 1) / 8) for i in range(8)][0::2]

        def subw(sl):
            return 128 if sl * 255 > 80 else (256 if sl * 511 > 80 else 512)
        alib = a32.alloc(2 * 4 * 35, (2, 4, 35))
        alic = a32.alloc(256, (4, 2, 32))
        tri01 = a16.alloc(128)
        low01 = a16.alloc(128)
        gneg = a16.alloc(4096)
        ex2 = a16.alloc(T * 128, (T, 128))
        ovl = a16.alloc(128, (2, 64))
        tri_f = a32.alloc(128)
        P.dma("sp", alib, c_alibi.rearrange("p (s h d) -> p s h d", s=2, h=4), writes=["alib"])
        P.dma("sp", alic, c_alibi_cmp.rearrange("p (h n q) -> p h n q", h=4, n=2), writes=["alic"])
        P.dma("sp", tri_f, c_tri01, writes=["tri_f"])
        P.op("dve", lambda e: e.tensor_copy(out=tri01, in_=tri_f), reads=["tri_f"], writes=["tri01"])
        P.dma("sp", low01, c_low01, writes=["low01"])
        P.dma("sp", gneg, c_gneg, writes=["gneg"])
        P.dma("sp", ex2, c_ex2.rearrange("p (t k) -> p t k", k=128), writes=["ex2"])
        P.dma("sp", ovl, c_ovl.rearrange("p (n j) -> p n j", n=2), writes=["ovl"])
        if NS > 16:
            tka = a32.alloc(T * 64, (T, 64))
            P.dma("sp", tka, c_topk_add.rearrange("p (t j) -> p t j", j=64), writes=["tka"])
        qT = a16.alloc(2 * S, (2, S))
        ksl = a16.alloc(S)
        kwn = a16.alloc(S)
        vsl = a16.alloc(T * 65, (T, 65))
        vwn = a16.alloc(T * 65, (T, 65))
        for r in range(2):
            P.dma("sp", qT[:, r, :], fm_d[(FM_NQ + r) * 128:(FM_NQ + r + 1) * 128, :], writes=["nqT"])
        for g in range(2):
            P.dma("sp", ksl[g * 64:(g + 1) * 64, :], fm_d[FM_KSW * 128:FM_KSW * 128 + 64, :], writes=["ksl"])
            P.dma("sp", kwn[g * 64:(g + 1) * 64, :], fm_d[FM_KSW * 128 + 64:FM_KSW * 128 + 128, :], writes=["kwn"])
        P.op("pool", lambda e: e.memset(vsl, 1.0), writes=["vsl%d" % t_ for t_ in range(T)])
        P.op("pool", lambda e: e.memset(vwn, 1.0), writes=["vwn%d" % t_ for t_ in range(T)])
        for t in range(T):
            P.dma("sp", vsl[:, t, 0:64], tm_d[t * 128:(t + 1) * 128, TM_VSLC:TM_VSLC + 64], writes=["vsl%d" % t])
            P.dma("sp", vwn[:, t, 0:64], tm_d[t * 128:(t + 1) * 128, TM_VWIN:TM_VWIN + 64], writes=["vwn%d" % t])

        kcT2 = a16.alloc(NBK * 128)
        vca = a16.alloc(NBK * 65, (NBK, 65))
        a32.mark()
        a16.mark()
        kvc = a16.alloc(S)
        P.dma("sp", kvc, fm_d[FM_KVC * 128:(FM_KVC + 1) * 128, :], writes=["kvc"])
        W1 = a16.alloc(32 * 256, (32, 256))
        W2k = a16.alloc(2 * 128, (2, 128))
        W2v = a16.alloc(2 * 64, (2, 64))
        peT = a16.alloc(32)
        hsk = a16.alloc(2 * NC, (2, NC))
        hsv = a16.alloc(2 * NC, (2, NC))
        stg = a32.alloc(8 * 256, (8, 256))
        stg2 = a32.alloc(2 * 64, (2, 64))
        pef = a32.alloc(32)
        hb = a32.alloc(4)
        for ch in range(4):
            P.dma("sp", stg[0:64], w_ck1[l, ch * 512:(ch + 1) * 512, :].rearrange("(l c) h -> c l h", c=64), writes=["stgA"])
            P.dma("sp", stg[64:128], w_cv1[l, ch * 512:(ch + 1) * 512, :].rearrange("(l c) h -> c l h", c=64), writes=["stgB"])
            P.op("pool", lambda e, ch=ch: e.tensor_copy(out=W1[:, ch * 8:(ch + 1) * 8, :], in_=stg), reads=["stgA", "stgB"], writes=["W1"])
        P.dma("sp", stg2, w_ck2[l].rearrange("(a p) d -> p a d", p=128), writes=["stg2"])
        for dup in range(2):
            P.op("pool", lambda e, dup=dup: e.tensor_copy(out=W2k[:, :, dup * 64:(dup + 1) * 64], in_=stg2), reads=["stg2"], writes=["W2k"])
        P.dma("sp", stg2, w_cv2[l].rearrange("(a p) d -> p a d", p=128), writes=["stg2"])
        P.op("pool", lambda e: e.tensor_copy(out=W2v, in_=stg2), reads=["stg2"], writes=["W2v"])
        P.dma("sp", pef, nsa_peT[l], writes=["pef"])
        P.op("dve", lambda e: e.tensor_copy(out=peT, in_=pef), reads=["pef"], writes=["peT"])
        P.op("pool", lambda e: e.memset(vca, 1.0), writes=["vca"])
        P.op("pool", lambda e: e.memset(kcT2, 0.0), writes=["kcT2"])
        for g, hs in ((0, hsk), (1, hsv)):
            for hh in range(2):
                ph, pbb = psb[2 * g + hh], psb[4 + 2 * g + hh]
                for li in range(32):
                    P.op("pe", lambda e, g=g, hh=hh, li=li, pbb=pbb:
                         e.matmul(pbb[:, 0:1], lhsT=W1[g * 64:(g + 1) * 64, li, hh * 128:(hh + 1) * 128],
                                  rhs=peT[g * 64:(g + 1) * 64, li:li + 1], start=(li == 0), stop=(li == 31)),
                         reads=["W1", "peT"], writes=["ps%d" % (4 + 2 * g + hh)])
                P.op("dve", lambda e, g=g, hh=hh, pbb=pbb: e.tensor_copy(out=hb[:, 2 * g + hh:2 * g + hh + 1], in_=pbb[:, 0:1]),
                     reads=["ps%d" % (4 + 2 * g + hh)], writes=["hb"])
                for li in range(32):
                    P.op("pe", lambda e, g=g, hh=hh, li=li, ph=ph:
                         e.matmul(ph[:, 0:NC], lhsT=W1[g * 64:(g + 1) * 64, li, hh * 128:(hh + 1) * 128],
                                  rhs=kvc[g * 64:(g + 1) * 64, li:li + 16 * (NC - 1) + 1:16], start=(li == 0), stop=(li == 31)),
                         reads=["W1", "kvc"], writes=["ps%d" % (2 * g + hh)])
                P.op("act", lambda e, g=g, hh=hh, ph=ph, hs=hs:
                     e.activation(out=hs[:, hh, :], in_=ph[:, 0:NC], func=AF.Silu, bias=hb[:, 2 * g + hh:2 * g + hh + 1]),
                     reads=["ps%d" % (2 * g + hh), "hb"], writes=["hs%d" % g])
        for hh in range(2):
            P.op("pe", lambda e, hh=hh: e.matmul(psb[6][:, 0:NC], lhsT=W2k[:, hh, :], rhs=hsk[:, hh, :],
                                                 start=(hh == 0), stop=(hh == 1)),
                 reads=["W2k", "hs0"], writes=["ps6"])
        P.op("act", lambda e: e.copy(out=kcT2[:, 0:NC], in_=psb[6][:, 0:NC]), reads=["ps6"], writes=["kcT2"])
        for nb in range(NBK):
            nn = min(128, NC - nb * 128)
            for hh in range(2):
                P.op("pe", lambda e, hh=hh, nb=nb, nn=nn:
                     e.matmul(psb[7][0:nn, 0:64], lhsT=hsv[:, hh, nb * 128:nb * 128 + nn], rhs=W2v[:, hh, :],
                              start=(hh == 0), stop=(hh == 1)),
                     reads=["W2v", "hs1"], writes=["ps7"])
            P.op("act", lambda e, nb=nb, nn=nn: e.copy(out=vca[0:nn, nb, 0:64], in_=psb[7][0:nn, 0:64]),
                 reads=["ps7"], writes=["vca"])
        P.barrier()
        a32.release()
        a16.release()

        ETs = [[a16.alloc(512) for _ in range(2)] for _ in range(4)]
        et_i = [0, 0, 0, 0]
        smf = [a32.alloc(512) for _ in range(2)]
        oTs = [a32.alloc(512) for _ in range(2)]
        acc = a32.alloc(4 * 256, (4, 256))
        imp = a32.alloc(4 * 64, (4, 64))
        gs = a32.alloc(4 * 12, (4, 12))
        rz = a32.alloc(4)
        coef = a32.alloc(4)
        top8 = a32.alloc(8)
        tmpk = a32.alloc(64)
        nsel = a16.alloc(128)
        negT2 = a16.alloc(512)
        zts = [a16.alloc(256) for _ in range(2)]
        szs = [a32.alloc(256) for _ in range(2)]
        mos = [a16.alloc(256) for _ in range(2)]
        ot_i = [0]
        sm_i = [0]

        def epilogue(sb, b, first):
            for h in range(4):
                oT = oTs[ot_i[0] % 2]
                ok = "noT%d" % (ot_i[0] % 2)
                ot_i[0] += 1
                evac(oT[0:65, :], psb[4 + h][0:65, :], reads=["ps%d" % (4 + h)], writes=[ok])
                for qt in range(4):
                    P.op("pe", lambda e, h=h, qt=qt, oT=oT:
                         e.transpose(out=psb[qt][:, h * 65:(h + 1) * 65], in_=oT[0:65, qt * 128:(qt + 1) * 128],
                                     identity=ident_f[0:65, 0:65]),
                         reads=[ok, "ident_f"], writes=["ps%d" % qt])
            for qt in range(4):
                pq = psb[qt]
                kq = "ps%d" % qt
                P.op("dve", lambda e, pq=pq: e.tensor_scalar(
                    out=rz, in0=pq[:, 0:260].rearrange("p (s c) -> p s c", c=65)[:, :, 64], scalar1=1e-30, scalar2=None,
                    op0=ALU.max), reads=[kq], writes=["nrz"])
                P.op("dve", lambda e: e.reciprocal(out=rz, in_=rz), reads=["nrz"], writes=["nrz"])
                if b == 0:
                    P.op("dve", lambda e, qt=qt: e.tensor_copy(out=rzc[:, qt, :], in_=rz), reads=["nrz"], writes=["rzc"])
                P.op("dve", lambda e, qt=qt: e.tensor_tensor(
                    out=coef, in0=rz, in1=gs[:, qt, :].rearrange("p (h b) -> p h b", b=3)[:, :, b], op=ALU.mult),
                    reads=["nrz", "gs"], writes=["coef"])
                for h in range(4):
                    if first:
                        P.op("dve", lambda e, h=h, qt=qt, pq=pq: e.tensor_scalar(
                            out=acc[:, qt, h * 64:(h + 1) * 64], in0=pq[:, h * 65:h * 65 + 64], scalar1=coef[:, h:h + 1],
                            scalar2=None, op0=ALU.mult), reads=[kq, "coef"], writes=["nacc"])
                    else:
                        P.op("dve", lambda e, h=h, qt=qt, pq=pq: e.scalar_tensor_tensor(
                            out=acc[:, qt, h * 64:(h + 1) * 64], in0=pq[:, h * 65:h * 65 + 64], scalar=coef[:, h:h + 1],
                            in1=acc[:, qt, h * 64:(h + 1) * 64], op0=ALU.mult, op1=ALU.add),
                            reads=[kq, "coef", "nacc"], writes=["nacc"])

        def exp_tile(h, ET, ek, src, src_key, a_lo, a_hi, kb, q0, rows=128):
            W = subw(nslopes[h])
            for sub in range(512 // W):
                a, bb = max(a_lo, sub * W), min(a_hi, (sub + 1) * W)
                if a >= bb:
                    continue
                dd = (kb * 128 - (q0 + sub * W)) // 128
                P.op("act", lambda e, a=a, bb=bb, dd=dd: e.activation(
                    out=ET[0:rows, a:bb], in_=src[0:rows, a:bb], func=AF.Exp, bias=alib[0:rows, 0, h, dd + 31:dd + 32], scale=scale),
                    reads=[src_key, "alib"], writes=[ek])

        rzc = a32.alloc(16, (4, 4))
        for sb in range(NSB):
            q0 = sb * 512
            P.op("act", lambda e, sb=sb: e.activation(out=gs, in_=small[:, sb * 4:sb * 4 + 4, 0:12], func=AF.Sigmoid),
                 reads=["small"], writes=["gs"])
            nbs = []
            for nb in range(NBK):
                nn = min(128, NC - 128 * nb, 32 * sb + 31 - 128 * nb)
                if nn > 0:
                    nbs.append((nb, nn))
            cET = {}
            if nbs:
                for h in range(4):
                    g, r = h % 2, h // 2
                    for (nb, nn) in nbs:
                        P.op("pe", lambda e, h=h, g=g, r=r, nb=nb, nn=nn, q0=q0:
                             e.matmul(psb[h][0:nn, :], lhsT=kcT2[g * 64:(g + 1) * 64, nb * 128:nb * 128 + nn],
                                      rhs=qT[g * 64:(g + 1) * 64, r, q0:q0 + 512], start=True, stop=True),
                             reads=["kcT2", "nqT"], writes=["ps%d" % h])
                        sm = smf[sm_i[0] % 2]
                        sk = "smf%d" % (sm_i[0] % 2)
                        sm_i[0] += 1
                        c0 = q0 - 2048 * nb
                        P.op("dve", lambda e, h=h, nn=nn, sm=sm, c0=c0: e.tensor_tensor(
                            out=sm[0:nn, :], in0=psb[h][0:nn, :], in1=gneg[0:nn, c0:c0 + 512], op=ALU.add),
                            reads=["ps%d" % h, "gneg"], writes=[sk])
                        ET = ETs[h][nb]
                        ek = "ET%d_%d" % (h, nb)
                        cET[(h, nb)] = (ET, ek, nn)
                        W = subw(nslopes[h])
                        for sub in range(512 // W):
                            a, bb = sub * W, (sub + 1) * W
                            qi = (q0 + a) // 128
                            P.op("act", lambda e, h=h, nb=nb, nn=nn, sm=sm, ET=ET, a=a, bb=bb, qi=qi: e.activation(
                                out=ET[0:nn, a:bb], in_=sm[0:nn, a:bb], func=AF.Exp, bias=alic[0:nn, h, nb, qi:qi + 1], scale=scale),
                                reads=[sk, "alic"], writes=[ek])
                    for i, (nb, nn) in enumerate(nbs):
                        ET, ek, _ = cET[(h, nb)]
                        P.op("pe", lambda e, h=h, nb=nb, nn=nn, ET=ET, i=i, n=len(nbs):
                             e.matmul(psb[4 + h][0:65, :], lhsT=vca[0:nn, nb, :], rhs=ET[0:nn, :],
                                      start=(i == 0), stop=(i == n - 1)),
                             reads=[ek, "vca"], writes=["ps%d" % (4 + h)])
                epilogue(sb, 0, True)
                for h in range(4):
                    for qt in range(4):
                        for i, (nb, nn) in enumerate(nbs):
                            ET, ek, _ = cET[(h, nb)]
                            P.op("pe", lambda e, h=h, qt=qt, nb=nb, nn=nn, ET=ET, i=i, n=len(nbs):
                                 e.matmul(psb[h][:, qt * 64:(qt + 1) * 64], lhsT=ET[0:nn, qt * 128:(qt + 1) * 128],
                                          rhs=ovl[0:nn, nb, :], start=(i == 0), stop=(i == n - 1)),
                                 reads=[ek, "ovl"], writes=["ps%d" % h])
                for h in range(4):
                    for qt in range(4):
                        if h == 0:
                            P.op("dve", lambda e, h=h, qt=qt: e.tensor_scalar(
                                out=imp[:, qt, :], in0=psb[h][:, qt * 64:(qt + 1) * 64], scalar1=rzc[:, qt, h:h + 1],
                                scalar2=None, op0=ALU.mult), reads=["ps%d" % h, "rzc"], writes=["imp"])
                        else:
                            P.op("dve", lambda e, h=h, qt=qt: e.scalar_tensor_tensor(
                                out=imp[:, qt, :], in0=psb[h][:, qt * 64:(qt + 1) * 64], scalar=rzc[:, qt, h:h + 1],
                                in1=imp[:, qt, :], op0=ALU.mult, op1=ALU.add), reads=["ps%d" % h, "rzc", "imp"], writes=["imp"])
            else:
                P.op("dve", lambda e: e.memset(acc, 0.0), writes=["nacc"])
                P.op("dve", lambda e: e.memset(imp, 0.0), writes=["imp"])
            for qt in range(4):
                t = sb * 4 + qt
                if NS > 16:
                    P.op("dve", lambda e, qt=qt, t=t: e.tensor_tensor(out=imp[:, qt, :], in0=imp[:, qt, :], in1=tka[:, t, :], op=ALU.add),
                         reads=["imp", "tka"], writes=["imp"])
                    P.op("dve", lambda e, qt=qt: e.max(out=top8, in_=imp[:, qt, :]), reads=["imp"], writes=["top8"])
                    P.op("dve", lambda e, qt=qt: e.match_replace(out=tmpk, in_to_replace=top8, in_values=imp[:, qt, :], imm_value=-1e9),
                         reads=["imp", "top8"], writes=["tmpk"])
                    P.op("dve", lambda e: e.max(out=top8, in_=tmpk), reads=["tmpk"], writes=["top8"])
                    P.op("dve", lambda e, qt=qt: e.tensor_scalar(out=tmpk, in0=imp[:, qt, :], scalar1=top8[:, 7:8], scalar2=-1.0,
                                                                 op0=ALU.is_ge, op1=ALU.add),
                         reads=["imp", "top8"], writes=["tmpk"])
                    for dup in range(2):
                        P.op("dve", lambda e, dup=dup: e.tensor_scalar(out=nsel[:, dup * 64:(dup + 1) * 64], in0=tmpk, scalar1=BIG,
                                                                       scalar2=None, op0=ALU.mult),
                             reads=["tmpk"], writes=["nsel"])
                else:
                    P.op("dve", lambda e: e.memset(nsel, 0.0), writes=["nsel"])
                pt = psb[qt][:, 0:64].bitcast(BF16)
                P.op("pe", lambda e, pt=pt: e.transpose(out=pt, in_=nsel, identity=ident_b), reads=["nsel", "ident_b"], writes=["ps%d" % qt])
                P.op("act", lambda e, qt=qt, pt=pt: e.copy(out=negT2[:, qt * 128:(qt + 1) * 128], in_=pt), reads=["ps%d" % qt], writes=["negT2"])
            for b, kT2, va_, kbs in ((1, ksl, vsl, list(range(0, 4 * sb + 4))),
                                     (2, kwn, vwn, list(range(max(0, 4 * sb - 4), 4 * sb + 4)))):
                kname = "ksl" if b == 1 else "kwn"
                vname = "vsl" if b == 1 else "vwn"
                n_kb = len(kbs)
                for hp in range(2):
                    heads = (2 * hp, 2 * hp + 1)

                    def geom(i, kbs=kbs, b=b, sb=sb):
                        kb = kbs[i]
                        rel = kb - 4 * sb
                        qlo = max(0, rel) * 128
                        qhi = 512 if (b == 1 or rel >= 0) else 128 * (rel + 5)
                        return kb, rel, qlo, qhi

                    def qk(i, heads=heads, kT2=kT2, b=b, q0=q0, kname=kname):
                        kb, rel, qlo, qhi = geom(i)
                        par = i % 2
                        for h in heads:
                            g, r = h % 2, h // 2
                            bk = par * 2 + g
                            P.op("pe", lambda e, bk=bk, g=g, r=r, kb=kb, qlo=qlo, qhi=qhi, kT2=kT2, b=b, q0=q0:
                                 e.matmul(psb[bk][:, qlo:qhi], lhsT=kT2[g * 64:(g + 1) * 64, kb * 128:(kb + 1) * 128],
                                          rhs=qT[g * 64:(g + 1) * 64, r, q0 + qlo:q0 + qhi], start=True, stop=(b != 1)),
                                 reads=[kname, "nqT"], writes=["ps%d" % bk])
                            if b == 1:
                                P.op("pe", lambda e, bk=bk, g=g, kb=kb, qlo=qlo, qhi=qhi:
                                     e.matmul(psb[bk][:, qlo:qhi], lhsT=ex2[g * 64:(g + 1) * 64, kb, :],
                                              rhs=negT2[g * 64:(g + 1) * 64, qlo:qhi], start=False, stop=True),
                                     reads=["ex2", "negT2"], writes=["ps%d" % bk])

                    def rest(i, heads=heads, b=b, q0=q0, va_=va_, vname=vname, n_kb=n_kb):
                        kb, rel, qlo, qhi = geom(i)
                        par = i % 2
                        for h in heads:
                            bk = par * 2 + h % 2
                            ET = ETs[h][et_i[h] % 2]
                            ek = "ET%d_%d" % (h, et_i[h] % 2)
                            et_i[h] += 1
                            exp_tile(h, ET, ek, psb[bk], "ps%d" % bk, qlo, qhi, kb, q0)
                            if rel >= 0:
                                P.op("pool", lambda e, ET=ET, qlo=qlo: e.tensor_tensor(
                                    out=ET[:, qlo:qlo + 128], in0=ET[:, qlo:qlo + 128], in1=tri01, op=ALU.mult),
                                    reads=[ek, "tri01"], writes=[ek])
                            elif b == 2:
                                P.op("pool", lambda e, ET=ET, qhi=qhi: e.tensor_tensor(
                                    out=ET[:, qhi - 128:qhi], in0=ET[:, qhi - 128:qhi], in1=low01, op=ALU.mult),
                                    reads=[ek, "low01"], writes=[ek])
                            P.op("pe", lambda e, h=h, ET=ET, kb=kb, qlo=qlo, qhi=qhi, va_=va_, i=i, n=n_kb:
                                 e.matmul(psb[4 + h][0:65, qlo:qhi], lhsT=va_[:, kb, :], rhs=ET[:, qlo:qhi],
                                          start=(i == 0), stop=(i == n - 1), skip_group_check=True),
                                 reads=[ek, "%s%d" % (vname, kb)], writes=["ps%d" % (4 + h)])

                    qk(0)
                    for i in range(n_kb):
                        if i + 1 < n_kb:
                            qk(i + 1)
                        rest(i)
                epilogue(sb, b, False)
            for qt in range(4):
                t = sb * 4 + qt
                j = t % 2
                zt, sz = zts[j], szs[j]
                k = "N%d_" % j
                P.dma("sp", zt, tm_d[t * 128:(t + 1) * 128, TM_NSAZ:TM_NSAZ + 256], writes=[k + "z"])
                P.op("act", lambda e, sz=sz, zt=zt: e.activation(out=sz, in_=zt, func=AF.Silu), reads=[k + "z"], writes=[k + "sz"])
                mo = mos[j]
                P.op("dve", lambda e, qt=qt, mo=mo, sz=sz: e.tensor_tensor(out=mo, in0=acc[:, qt, :], in1=sz, op=ALU.mult),
                     reads=["nacc", k + "sz"], writes=[k + "mo"])
                P.dma(STORE_Q, mixed_d[t * 128:(t + 1) * 128, 0:256], mo, reads=[k + "mo"], writes=["mixN%d" % t])
        P.barrier()
        a32.release()
        a16.release()

    def phase_dump(l):
        P.barrier()

    def phase_stub(l):
        if True:
            a16.mark()
            tl = [a16.alloc(1024) for _ in range(2)]
            for t in range(T):
                P.dma("sp", tl[t % 2], tm_d[t * 128:(t + 1) * 128, 396:396 + 1024],
                      reads=[], writes=["tl%d" % (t % 2)])
                if stop == 98:
                    continue
                P.dma(STORE_Q, mixed_d[t * 128:(t + 1) * 128, :], tl[t % 2], reads=["tl%d" % (t % 2)], writes=["mixX%d" % t])
            P.barrier()
            a16.release()

    def phase_F(l):
        x_src = x_in if l == 0 else xres
        wo_sb = a16.alloc(8 * 1024, (8, 1024))
        stage = [a32.alloc(1024) for _ in range(2)]
        xts = [a32.alloc(D_MODEL) for _ in range(2)]
        xns = [a32.alloc(D_MODEL) for _ in range(2)]
        mTs = [a16.alloc(8 * 128, (8, 128)) for _ in range(2)]
        mxs = [a16.alloc(1024) for _ in range(2)]
        last = (l == L - 1) and final_norm
        if last:
            fnwb = a32.alloc(D_MODEL)
            junk = a16.alloc(D_MODEL)
            sss = [a32.alloc(1) for _ in range(2)]
            rstds = [a32.alloc(1) for _ in range(2)]
            yts = [a32.alloc(D_MODEL) for _ in range(2)]
            P.dma("sp", fnwb, final_norm_w[0:1, :].partition_broadcast(128), writes=["fnwb"])
        for kc in range(8):
            st = stage[kc % 2]
            P.dma("sp", st, w_out[l, kc * 128:(kc + 1) * 128, :], writes=["stage%d" % (kc % 2)])
            P.op(("pool", "dve")[kc % 2], lambda e, kc=kc, st=st: e.tensor_copy(out=wo_sb[:, kc, :], in_=st),
                 reads=["stage%d" % (kc % 2)], writes=["wo_sb%d" % (kc % 2)])
        def f_loads(t):
            j = t % 2
            k = "F%d_" % j
            P.dma("sp", mxs[j], mixed_d[t * 128:(t + 1) * 128, :], reads=["mixR%d" % t], writes=[k + "mx"])
            P.dma("sp", xts[j], x_src[t * 128:(t + 1) * 128, :], reads=["xres_t%d" % t], writes=[k + "x"])
        f_loads(0)
        for t in range(T):
            j = t % 2
            k = "F%d_" % j
            xt, xn, mT = xts[j], xns[j], mTs[j]
            mx = mxs[j]
            pst = psb[j]
            pstb = pst[:, 0:512].bitcast(BF16)
            for kc in range(8):
                P.op("pe", lambda e, kc=kc, mx=mx, pstb=pstb:
                     e.transpose(out=pstb[:, kc * 128:(kc + 1) * 128], in_=mx[:, kc * 128:(kc + 1) * 128],
                                 identity=ident_b),
                     reads=[k + "mx", "ident_b"], writes=["ps%d" % j])
            evac(mT, pstb.rearrange("p (k t) -> p k t", k=8), reads=["ps%d" % j], writes=[k + "mT"])
            if t + 1 < T:
                f_loads(t + 1)
            for c in range(2):
                pi = 2 + (t * 2 + c) % 4
                ps = psb[pi]
                for kc in range(8):
                    P.op("pe", lambda e, kc=kc, ps=ps, mT=mT, c=c:
                         e.matmul(ps[:, :], lhsT=mT[:, kc, :], rhs=wo_sb[:, kc, c * 512:(c + 1) * 512],
                                  start=(kc == 0), stop=(kc == 7)),
                         reads=[k + "mT", "wo_sb0", "wo_sb1"], writes=["ps%d" % pi])
                P.op("dve", lambda e, ps=ps, xt=xt, xn=xn, c=c:
                     e.tensor_tensor(out=xn[:, c * 512:(c + 1) * 512], in0=ps[:, :],
                                     in1=xt[:, c * 512:(c + 1) * 512], op=ALU.add),
                     reads=["ps%d" % pi, k + "x"], writes=[k + "xn"])
            if not last:
                P.dma(STORE_Q, xres[t * 128:(t + 1) * 128, :], xn, reads=[k + "xn"], writes=["xres_t%d" % t])
            else:
                ss, rstd, yt = sss[j], rstds[j], yts[j]
                P.op("dve", lambda e, xn=xn, ss=ss: e.scalar_tensor_tensor(
                    out=junk, in0=xn, scalar=1.0, in1=xn, op0=ALU.mult, op1=ALU.mult, accum_out=ss),
                    reads=[k + "xn"], writes=["Fjunk", k + "ss"])
                P.op("dve", lambda e, ss=ss: e.tensor_scalar(out=ss, in0=ss, scalar1=1.0 / D_MODEL, scalar2=EPS,
                                                             op0=ALU.mult, op1=ALU.add),
                     reads=[k + "ss"], writes=[k + "ss"])
                P.op("act", lambda e, ss=ss: e.activation(out=ss, in_=ss, func=AF.Sqrt),
                     reads=[k + "ss"], writes=[k + "ss"])
                P.op("dve", lambda e, ss=ss, rstd=rstd: e.reciprocal(out=rstd, in_=ss),
                     reads=[k + "ss"], writes=[k + "rstd"])
                P.op("dve", lambda e, xn=xn, rstd=rstd, yt=yt: e.scalar_tensor_tensor(
                    out=yt, in0=xn, scalar=rstd, in1=fnwb, op0=ALU.mult, op1=ALU.mult),
                    reads=[k + "xn", k + "rstd", "fnwb"], writes=[k + "y"])
                P.dma(STORE_Q, y_out[t * 128:(t + 1) * 128, :], yt, reads=[k + "y"], writes=["y_t%d" % t])

    for l in range(L):
        phase_A(l)
        if "STUB" in phases:
            phase_stub(l)
        realP = P
        for group in ((("DIFF", phase_diff), ("SSD", phase_ssd)),):
            a32.mark()
            a16.mark()
            streams = []
            for nm, ph in group:
                if nm in phases:
                    st_ = Stream()
                    P = st_
                    for _ in ph(l):
                        pass
                    streams.append(st_)
            P = realP
            merge_streams(P, streams)
            P.barrier()
            a32.release()
            a16.release()
        if "NSA" in phases:
            phase_nsa(l)
        if "DUMP" in phases:
            phase_dump(l)
        a32.mark()
        a16.mark()
        streams = []
        if "F" in phases:
            st_ = Stream()
            P = st_
            phase_F(l)
            streams.append(st_)
        if "RET" in phases:
            st_ = Stream()
            P = st_
            for _ in phase_ret(l):
                pass
            streams.append(st_)
        P = realP
        merge_streams(P, streams)
        P.barrier()
        a32.release()
        a16.release()

    P.emit(es)
    es.close()
    return nc


def make_consts(S=SEQ):
    c = {"c_ident": np.eye(128, dtype=np.float32)}
    bf = ml_dtypes.bfloat16
    T = S // 128
    NC = (S - 32) // 16 + 1
    NS = S // 64
    p_ = np.arange(128)
    cc = np.arange(4096)
    c["c_gneg"] = np.where(cc[None, :] - 16 * p_[:, None] >= 31, 0.0, -BIG).astype(bf)
    ex = np.zeros((128, T, 128), np.float32)
    for kb in range(T):
        for pp in range(128):
            j = 2 * kb + pp // 64
            if j < 64:
                ex[j, kb, pp] = 1.0
                ex[64 + j, kb, pp] = 1.0
    c["c_ex2"] = ex.reshape(128, -1).astype(bf)
    c["c_low01"] = (p_[None, :] < p_[:, None]).astype(np.float32).astype(bf)
    n_ = np.arange(256)
    c_start, c_end = n_ * 16, n_ * 16 + 31
    s_start = np.arange(64) * 64
    s_end = s_start + 63
    ov = ((c_start[:, None] <= s_end[None, :]) & (c_end[:, None] >= s_start[None, :]) & (n_[:, None] < NC)).astype(np.float32)
    c["c_ovl"] = ov.reshape(2, 128, 64).transpose(1, 0, 2).reshape(128, 128).astype(bf)
    nsl = np.array([2.0 ** (-8.0 * (i + 1) / 8) for i in range(8)], np.float64)[0::2]
    ac = np.zeros((128, 4, 2, 32), np.float64)
    for h in range(4):
        for nb in range(2):
            for qi in range(32):
                ac[:, h, nb, qi] = nsl[h] * (16.0 * (128 * nb + p_) + 15.5 - 128.0 * qi)
    c["c_alibi_cmp"] = ac.reshape(128, -1).astype(np.float32)
    ta = np.zeros((128, T, 64), np.float32)
    jb = np.arange(64)
    for t in range(T):
        tq = t * 128 + p_
        cur = tq // 64
        forced = (jb[None, :] == 0) | (jb[None, :] == cur[:, None]) | (jb[None, :] == cur[:, None] - 1)
        valid = (jb[None, :] * 64 <= tq[:, None]) & (jb[None, :] < NS)
        ta[:, t, :] = np.where(forced, 1000.0, np.where(valid, 0.0, -1000.0))
    c["c_topk_add"] = ta.reshape(128, -1)
    H, C, dh = 4, 128, 64
    scale = dh ** -0.5
    log_g = np.log(1.0 - 2.0 ** (-5.0 - np.arange(H, dtype=np.float64)))
    pos = np.arange(C, dtype=np.float64)
    rel = pos[None, :] - pos[:, None]
    decT = np.zeros((128, 4 * 128), np.float64)
    xiT = np.zeros((128, 2 * 128), np.float64)
    zeta = np.zeros((128, 256), np.float64)
    cd = np.zeros((128, 4), np.float64)
    for h in range(H):
        ai = (h % 2) * 2 + h // 2
        decT[:, ai * 128:(ai + 1) * 128] = np.where(rel >= 0, np.exp(log_g[h] * np.maximum(rel, 0.0)), 0.0) * scale
        hp, hr = (h % 2) * 64, h // 2
        xiT[hp:hp + 64, hr * 128:(hr + 1) * 128] = (np.exp(log_g[h] * (pos + 1.0)) * scale)[None, :]
        zeta[:, h * 64:(h + 1) * 64] = np.exp(log_g[h] * (C - 1.0 - pos))[:, None]
        cd[:, h] = np.exp(log_g[h] * C)
    c["c_ret_decT"] = decT.astype(np.float32)
    c["c_ret_xiT"] = xiT.astype(np.float32)
    c["c_ret_zeta"] = zeta.astype(np.float32)
    c["c_ret_cd"] = cd.astype(np.float32)
    c["c_triu"] = np.triu(np.ones((128, 128), np.float32))
    c["c_tri01"] = np.triu(np.ones((128, 128), np.float32))
    slopes = np.array([2.0 ** (-8.0 * (i + 1) / 8) for i in range(8)], np.float64)
    al = np.zeros((128, 2, 4, 35), np.float64)
    p = np.arange(128, dtype=np.float64)
    for s, sl in enumerate((slopes[0::2], slopes[1::2])):
        for h in range(4):
            for dd in range(-31, 4):
                al[:, s, h, dd + 31] = sl[h] * (128.0 * dd + p)
    c["c_alibi"] = al.reshape(128, -1).astype(np.float32)
    return c


def layout_params(inputs):
    L = inputs["w_in"].shape[0]
    cw = np.asarray(inputs["ssm_conv_w"])
    conv_wT = np.ascontiguousarray(cw.reshape(L, 4, 6, 128).transpose(0, 3, 2, 1).reshape(L, 128, 24))
    conv_bT = np.ascontiguousarray(np.asarray(inputs["ssm_conv_b"]).reshape(L, 6, 128).transpose(0, 2, 1))
    ssm_vec = np.ascontiguousarray(np.concatenate([np.asarray(inputs["ssm_dt_bias"]), np.asarray(inputs["ssm_A_log"]),
                                                   np.asarray(inputs["ssm_D"])], axis=1))
    return {"conv_wT": conv_wT.astype(np.float32), "conv_bT": conv_bT.astype(np.float32),
            "ssm_vec": ssm_vec.astype(np.float32),
            "nsa_peT": np.ascontiguousarray(np.concatenate(
                [np.asarray(inputs["nsa_pe_k"]).transpose(0, 2, 1), np.asarray(inputs["nsa_pe_v"]).transpose(0, 2, 1)],
                axis=1)).astype(np.float32),
            "diff_vec": np.ascontiguousarray(np.concatenate(
                [np.asarray(inputs[k]) for k in ("diff_lam_q1", "diff_lam_k1", "diff_lam_q2", "diff_lam_k2",
                                                 "diff_subln_w")], axis=1)).astype(np.float32),
            "ret_gn_w": np.ascontiguousarray(inputs["ret_gn_w"]).astype(np.float32),
            "ssm_norm_w": np.ascontiguousarray(inputs["ssm_norm_w"]).astype(np.float32)}


def make_inmap(inputs, b):
    m = {
        "x": np.ascontiguousarray(inputs["x"][b]).astype(np.float32),
        "norm_w": np.ascontiguousarray(inputs["norm_w"]).astype(np.float32),
        "w_in": np.ascontiguousarray(inputs["w_in"]).astype(np.float32),
        "w_out": np.ascontiguousarray(inputs["w_out"]).astype(np.float32),
        "final_norm_w": np.ascontiguousarray(inputs["final_norm_w"]).reshape(1, -1).astype(np.float32),
    }
    m.update(make_consts(S=m["x"].shape[0]))
    m.update(layout_params(inputs))
    for kk in ("nsa_w_ck1", "nsa_w_cv1", "nsa_w_ck2", "nsa_w_cv2"):
        m[kk] = np.ascontiguousarray(inputs[kk]).astype(np.float32)
    return m


_CACHE = {}


def kernel(**inputs):
    S = inputs["x"].shape[1]
    B = inputs["x"].shape[0]
    L = inputs["w_in"].shape[0]
    key = (S, L)
    if key not in _CACHE:
        _CACHE[key] = build_program(S=S, L=L, phases=("A", "SSD", "RET", "DIFF", "NSA", "F"))
    nc = _CACHE[key]
    in_maps = [make_inmap(inputs, b) for b in range(B)]
    res = run_bass_kernel_spmd(nc, in_maps, core_ids=list(range(B)))
    return np.stack([r["y"] for r in res.results], axis=0).astype(np.float32)
```

```python
import math
from contextlib import ExitStack

import numpy as np
import ml_dtypes

import concourse.bass as bass
import concourse.mybir as mybir
from concourse.bass_utils import run_bass_kernel_spmd

F32 = mybir.dt.float32
BF16 = mybir.dt.bfloat16
I32 = mybir.dt.int32
AF = mybir.ActivationFunctionType
ALU = mybir.AluOpType
AX = mybir.AxisListType

D_MODEL = 1024
DEPTH = 4
SEQ = 4096
IN_W = 3984
EPS = 1e-6
BIG = 30000.0
STORE_Q = "pool"
NFM = 18
FMW = NFM * 128
TMW = 1936
TM_VSLC, TM_VWIN, TM_GATE, TM_NSAZ, TM_DV, TM_DZ, TM_RK, TM_RV, TM_RZ, TM_SZ, TM_DT = (
    0, 64, 128, 140, 396, 652, 908, 1164, 1420, 1676, 1932)
FM_NQ, FM_KVC, FM_KSW, FM_DQ, FM_DK, FM_RQ, FM_RK, FM_XBC = 0, 2, 3, 4, 6, 8, 10, 12

W_PIECES = [
    (0, 0, 256),
    (256, 256, 128),
    (384, 384, 64),
    (448, 512, 64),
    (512, 908, 512),
    (1024, 1932, 512),
    (1536, 3212, 768),
    (FMW + 0, 448, 64),
    (FMW + 64, 576, 332),
    (FMW + 396, 1420, 512),
    (FMW + 908, 2188, 1024),
    (FMW + 1932, 3980, 4),
]
WSBW = FMW + TMW


class Prog:
    ENGS = ("pe", "act", "dve", "pool", "sp")
    DMA_RING = 16

    def __init__(self, nc):
        self.nc = nc
        self.q = {e: [] for e in self.ENGS}
        self.buf = {}
        self.seen_e = {e: {} for e in self.ENGS}
        self.seen_d = {e: {} for e in self.ENGS}
        self.ring_uses = {e: [0] * self.DMA_RING for e in self.ENGS}
        self.ring_next = {e: 0 for e in self.ENGS}
        self.ring_last = {e: [None] * self.DMA_RING for e in self.ENGS}
        self.all_dma = []

    def _bs(self, k):
        s = self.buf.get(k)
        if s is None:
            s = {"w": None, "r_e": {}, "r_d": []}
            self.buf[k] = s
        return s

    def _need(self, eng, tok, waits):
        if tok is None:
            return
        if tok[0] == "e":
            _, pe, idx = tok
            if pe == eng and eng in ("pe", "sp"):
                return
            if self.seen_e[eng].get(pe, -1) >= idx:
                return
            self.seen_e[eng][pe] = idx
            self.q[pe][idx]["flag"] = True
            waits.append(tok)
        else:
            _, sid, val = tok
            if self.seen_d[eng].get(sid, 0) >= val:
                return
            self.seen_d[eng][sid] = val
            waits.append(tok)

    def _deps(self, eng, reads, writes):
        waits = []
        for r in reads:
            self._need(eng, self._bs(r)["w"], waits)
        for w in writes:
            s = self._bs(w)
            self._need(eng, s["w"], waits)
            for pe, idx in s["r_e"].items():
                self._need(eng, ("e", pe, idx), waits)
            for t in s["r_d"]:
                self._need(eng, t, waits)
        return waits

    def _commit(self, tok, reads, writes):
        for r in reads:
            s = self._bs(r)
            if tok[0] == "e":
                s["r_e"][tok[1]] = tok[2]
            else:
                s["r_d"].append(tok)
        for w in writes:
            s = self._bs(w)
            s["w"] = tok
            s["r_e"] = {}
            s["r_d"] = []

    def op(self, eng, fn, reads=(), writes=()):
        if eng != "pe":
            extra = [r for r in reads if r.startswith("ps") and r not in writes]
            if extra:
                writes = list(writes) + extra
        waits = self._deps(eng, reads, writes)
        idx = len(self.q[eng])
        self.q[eng].append({"fn": fn, "waits": waits, "flag": False, "dma": None})
        self._commit(("e", eng, idx), reads, writes)

    def dma(self, eng, out, in_, reads=(), writes=()):
        waits = self._deps(eng, reads, writes)
        slot = self.ring_next[eng]
        self.ring_next[eng] = (slot + 1) % self.DMA_RING
        self._need(eng, self.ring_last[eng][slot], waits)
        self.ring_uses[eng][slot] += 1
        tok = ("d", (eng, slot), 16 * self.ring_uses[eng][slot])
        self.ring_last[eng][slot] = tok
        self.all_dma.append(tok)
        self.q[eng].append({"fn": lambda e: e.dma_start(out=out, in_=in_), "waits": waits,
                            "flag": False, "dma": (eng, slot)})
        self._commit(tok, reads, writes)

    def barrier(self):
        lasts = {e: len(self.q[e]) - 1 for e in self.ENGS}
        dmas = []
        for e in self.ENGS:
            dmas += [t for t in self.ring_last[e] if t is not None]
        for e in self.ENGS:
            waits = []
            for pe, idx in lasts.items():
                if pe == e:
                    continue
                j = idx
                while j >= 0 and (self.q[pe][j]["dma"] is not None or self.q[pe][j]["fn"] is None):
                    j -= 1
                if j >= 0:
                    self._need(e, ("e", pe, j), waits)
            for t in dmas:
                self._need(e, t, waits)
            self.q[e].append({"fn": None, "waits": waits, "flag": False, "dma": None})
        self.buf = {}

    def emit(self, es):
        nc = self.nc
        esem = {e: es.enter_context(nc.semaphore("s_" + e)) for e in self.ENGS}
        dsem = {}
        for e in self.ENGS:
            for s in range(self.DMA_RING):
                if self.ring_uses[e][s]:
                    dsem[(e, s)] = es.enter_context(nc.semaphore("d_%s%d" % (e, s)))
        cnt = {}
        for e in self.ENGS:
            c = 0
            arr = []
            for r in self.q[e]:
                if r["flag"]:
                    c += 1
                arr.append(c)
            cnt[e] = arr
        handles = {"pe": "tensor", "act": "scalar", "dve": "vector", "pool": "gpsimd", "sp": "sync"}
        block = es.enter_context(nc.Block())

        def run(ename, e):
            for r in self.q[ename]:
                for t in r["waits"]:
                    if t[0] == "e":
                        e.wait_ge(esem[t[1]], cnt[t[1]][t[2]])
                    else:
                        e.wait_ge(dsem[t[1]], t[2])
                if r["fn"] is None:
                    continue
                ins = r["fn"](e)
                if r["dma"] is not None:
                    ins.then_inc(dsem[r["dma"]], 16)
                elif r["flag"]:
                    ins.then_inc(esem[ename], 1)

        for ename in self.ENGS:
            getattr(block, handles[ename])(lambda e, ename=ename: run(ename, e))


class Arena:
    def __init__(self, t, width):
        self.t = t
        self.width = width
        self.off = 0
        self.marks = []

    def alloc(self, n, shape=None):
        assert self.off + n <= self.width, ("arena overflow", self.off, n, self.width)
        ap = self.t[:, self.off:self.off + n]
        self.off += n
        if shape is not None:
            names = "abcdef"[:len(shape)]
            ap = ap.rearrange("p (%s) -> p %s" % (" ".join(names), " ".join(names)),
                              **{nm: s for nm, s in zip(names, shape)})
        return ap

    def mark(self):
        self.marks.append(self.off)

    def release(self):
        self.off = self.marks.pop()


class Stream:
    def __init__(self):
        self.ops = []

    def op(self, eng, fn, reads=(), writes=()):
        self.ops.append(("op", eng, fn, tuple(reads), tuple(writes)))

    def dma(self, eng, out, in_, reads=(), writes=()):
        self.ops.append(("dma", eng, out, in_, tuple(reads), tuple(writes)))


def merge_streams(P, streams):
    idx = [0] * len(streams)
    while True:
        best, bf = -1, 2.0
        pending = False
        for i, s in enumerate(streams):
            if idx[i] < len(s.ops):
                pending = True
                o = s.ops[idx[i]]
                rds = o[3] if o[0] == "op" else o[4]
                if any(r.startswith("mixR") and (r not in P.buf or P.buf[r]["w"] is None) for r in rds) and len(streams) > 1:
                    continue
                f = idx[i] / len(s.ops)
                if f < bf:
                    best, bf = i, f
        if best < 0:
            assert not pending, "merge deadlock"
            break
        o = streams[best].ops[idx[best]]
        idx[best] += 1
        if o[0] == "op":
            P.op(o[1], o[2], reads=o[3], writes=o[4])
        else:
            P.dma(o[1], o[2], o[3], reads=o[4], writes=o[5])


def build_program(S=SEQ, L=DEPTH, debug=False, phases=("A", "F"), final_norm=True, stop=99):
    T = S // 128
    NSB = S // 512
    nc = bass.Bass("TRN2", target_bir_lowering=False)
    es = ExitStack()
    dkind = "ExternalOutput" if debug else "Internal"

    def din(name, shape, dt=F32):
        return nc.dram_tensor(name, list(shape), dt, kind="ExternalInput").ap()

    x_in = din("x", [S, D_MODEL])
    norm_w = din("norm_w", [L, D_MODEL])
    w_in = din("w_in", [L, D_MODEL, IN_W])
    w_out = din("w_out", [L, D_MODEL, D_MODEL])
    final_norm_w = din("final_norm_w", [1, D_MODEL])
    ident_d = din("c_ident", [128, 128])
    ret_gn_w = din("ret_gn_w", [L, 256])
    ssm_norm_w = din("ssm_norm_w", [L, 256])
    conv_wT = din("conv_wT", [L, 128, 24])
    conv_bT = din("conv_bT", [L, 128, 6])
    ssm_vec = din("ssm_vec", [L, 12])
    c_ret_decT = din("c_ret_decT", [128, 512])
    c_ret_xiT = din("c_ret_xiT", [128, 256])
    c_ret_zeta = din("c_ret_zeta", [128, 256])
    c_ret_cd = din("c_ret_cd", [128, 4])
    c_triu = din("c_triu", [128, 128])
    mixed_d = nc.dram_tensor("mixed_d", [S, D_MODEL], BF16, kind=dkind).ap()
    NC = (S - 32) // 16 + 1
    NBK = (NC + 127) // 128
    NS = S // 64
    nsa_peT = din("nsa_peT", [L, 128, 32])
    w_ck1 = din("nsa_w_ck1", [L, 2048, 256])
    w_cv1 = din("nsa_w_cv1", [L, 2048, 256])
    w_ck2 = din("nsa_w_ck2", [L, 256, 64])
    w_cv2 = din("nsa_w_cv2", [L, 256, 64])
    c_gneg = din("c_gneg", [128, 4096], BF16)
    c_ex2 = din("c_ex2", [128, T * 128], BF16)
    c_low01 = din("c_low01", [128, 128], BF16)
    c_ovl = din("c_ovl", [128, 2 * 64], BF16)
    c_alibi_cmp = din("c_alibi_cmp", [128, 4 * 2 * 32])
    c_topk_add = din("c_topk_add", [128, T * 64])
    diff_vec = din("diff_vec", [L, 192])
    c_alibi = din("c_alibi", [128, 2 * 4 * 35])
    c_tri01 = din("c_tri01", [128, 128])
    y_out = nc.dram_tensor("y", [S, D_MODEL], F32, kind="ExternalOutput").ap()
    xres = nc.dram_tensor("xres", [S, D_MODEL], F32, kind=dkind).ap()
    fm_d = nc.dram_tensor("fm", [FMW, S], BF16, kind=dkind).ap()
    tm_d = nc.dram_tensor("tm", [S, TMW], BF16, kind=dkind).ap()

    A32W = 14000
    A16W = 66000
    a32 = Arena(es.enter_context(nc.sbuf_tensor("a32", [128, A32W], F32)), A32W)
    a16 = Arena(es.enter_context(nc.sbuf_tensor("a16", [128, A16W], BF16)), A16W)
    pbig = es.enter_context(nc.psum_tensor("pbig", [128, 8 * 512], F32))
    psb = [pbig[:, i * 512:(i + 1) * 512] for i in range(8)]

    P = Prog(nc)

    ident_f = a32.alloc(128)
    ident_b = a16.alloc(128)
    small = a32.alloc(T * 16, (T, 16))
    P.dma("sp", ident_f, ident_d, writes=["ident_f"])
    P.op("dve", lambda e: e.tensor_copy(out=ident_b, in_=ident_f), reads=["ident_f"], writes=["ident_b"])

    cp_rr = [0]

    def evac(out, in_, reads, writes, force=None):
        cp_rr[0] ^= 1
        if force == "act" or (force is None and cp_rr[0]):
            P.op("act", lambda e: e.copy(out=out, in_=in_), reads=reads, writes=writes)
        else:
            P.op("dve", lambda e: e.tensor_copy(out=out, in_=in_), reads=reads, writes=writes)

    def rms_rstd(xt, ss, junk, rstd, key):
        P.op("dve", lambda e: e.scalar_tensor_tensor(out=junk, in0=xt, scalar=1.0, in1=xt,
                                                     op0=ALU.mult, op1=ALU.mult, accum_out=ss),
             reads=[key + "x"], writes=[key + "junk", key + "ss"])
        P.op("dve", lambda e: e.tensor_scalar(out=ss, in0=ss, scalar1=1.0 / D_MODEL, scalar2=EPS,
                                              op0=ALU.mult, op1=ALU.add),
             reads=[key + "ss"], writes=[key + "ss"])
        P.op("act", lambda e: e.activation(out=ss, in_=ss, func=AF.Sqrt), reads=[key + "ss"], writes=[key + "ss"])
        P.op("dve", lambda e: e.reciprocal(out=rstd, in_=ss), reads=[key + "ss"], writes=[key + "rstd"])

    def phase_A(l):
        x_src = x_in if l == 0 else xres
        a32.mark()
        a16.mark()
        w_sb = a16.alloc(8 * WSBW, (8, WSBW))
        stage = [a32.alloc(2048) for _ in range(4)]
        nwb = a32.alloc(D_MODEL)
        xts = [a32.alloc(D_MODEL) for _ in range(2)]
        junk = a16.alloc(D_MODEL)
        sss = [a32.alloc(1) for _ in range(2)]
        rstds = [a32.alloc(1) for _ in range(2)]
        hbs = [a16.alloc(D_MODEL) for _ in range(2)]
        hTs = [a16.alloc(8 * 512, (8, 512)) for _ in range(2)]
        tmts = [a16.alloc(TMW) for _ in range(2)]
        fmts = [a16.alloc(512) for _ in range(4)]

        P.dma("sp", nwb, norm_w[l:l + 1, :].partition_broadcast(128), writes=["nwb"])
        pieces = []
        for (dc, sc, wd) in W_PIECES:
            if sc < 2048 < sc + wd:
                pieces.append((dc, sc, 2048 - sc))
                pieces.append((dc + 2048 - sc, 2048, wd - (2048 - sc)))
            else:
                pieces.append((dc, sc, wd))
        for kc in range(8):
            sp_ = (kc % 2) * 2
            P.dma("sp", stage[sp_], w_in[l, kc * 128:(kc + 1) * 128, 0:2048], writes=["stage%d" % sp_])
            P.dma("sp", stage[sp_ + 1][:, 0:IN_W - 2048], w_in[l, kc * 128:(kc + 1) * 128, 2048:IN_W],
                  writes=["stage%d" % (sp_ + 1)])
            for pi_, (dc, sc, wd) in enumerate(pieces):
                hf = 0 if sc < 2048 else 1
                si = sp_ + hf
                ce = ("pool", "dve", "act")[(pi_ + kc) % 3]
                if ce == "act":
                    P.op("act", lambda e, kc=kc, si=si, hf=hf, dc=dc, sc=sc, wd=wd:
                         e.copy(out=w_sb[:, kc, dc:dc + wd], in_=stage[si][:, sc - 2048 * hf:sc - 2048 * hf + wd]),
                         reads=["stage%d" % si], writes=["w_sb%d" % (pi_ % 4)])
                else:
                    P.op(ce, lambda e, kc=kc, si=si, hf=hf, dc=dc, sc=sc, wd=wd:
                         e.tensor_copy(out=w_sb[:, kc, dc:dc + wd], in_=stage[si][:, sc - 2048 * hf:sc - 2048 * hf + wd]),
                         reads=["stage%d" % si], writes=["w_sb%d" % (pi_ % 4)])

        fm_i = 0
        if stop <= 1:
            P.barrier(); a32.release(); a16.release(); return
        def a_stage1(t):
            j = t % 2
            xt, ss, rstd, hb = xts[j], sss[j], rstds[j], hbs[j]
            k = "A%d_" % j
            P.dma("sp", xt, x_src[t * 128:(t + 1) * 128, :],
                  reads=["xres_t%d" % t], writes=[k + "x"])
            rms_rstd(xt, ss, junk, rstd, k)
            P.op("dve", lambda e, xt=xt, rstd=rstd, hb=hb:
                 e.scalar_tensor_tensor(out=hb, in0=xt, scalar=rstd, in1=nwb, op0=ALU.mult, op1=ALU.mult),
                 reads=[k + "x", k + "rstd", "nwb"], writes=[k + "hb"])

        a_stage1(0)
        for sb in range(NSB):
            hT = hTs[sb % 2]
            hk = "hT%d" % (sb % 2)
            for ti in range(4):
                t = sb * 4 + ti
                j = t % 2
                xt, ss, rstd, hb, tmt = xts[j], sss[j], rstds[j], hbs[j], tmts[j]
                k = "A%d_" % j
                pst = psb[j]
                pstb = pst[:, 0:512].bitcast(BF16)
                for kc in range(8):
                    P.op("pe", lambda e, kc=kc, hb=hb, pstb=pstb:
                         e.transpose(out=pstb[:, kc * 128:(kc + 1) * 128], in_=hb[:, kc * 128:(kc + 1) * 128],
                                     identity=ident_b),
                         reads=[k + "hb", "ident_b"], writes=["ps%d" % j])
                evac(hT[:, :, ti * 128:(ti + 1) * 128], pstb.rearrange("p (k t) -> p k t", k=8),
                     reads=["ps%d" % j], writes=[hk])
                if t + 1 < T:
                    a_stage1(t + 1)
                if stop <= 2:
                    continue
                for c in range(4):
                    c0 = c * 512
                    n = min(512, TMW - c0)
                    pi = 2 + (t * 4 + c) % 3
                    ps = psb[pi]
                    for kc in range(8):
                        P.op("pe", lambda e, kc=kc, ps=ps, hT=hT, ti=ti, c0=c0, n=n:
                             e.matmul(ps[:, 0:n], lhsT=hT[:, kc, ti * 128:(ti + 1) * 128],
                                      rhs=w_sb[:, kc, FMW + c0:FMW + c0 + n], start=(kc == 0), stop=(kc == 7)),
                             reads=[hk, "w_sb0", "w_sb1", "w_sb2", "w_sb3"], writes=["ps%d" % pi])
                    evac(tmt[:, c0:c0 + n], ps[:, 0:n], reads=["ps%d" % pi], writes=[k + "tm"],
                         force=("act" if c in (0, 3) else None))
                    if c == 0 and stop != 31:
                        P.op("act", lambda e, ps=ps, t=t:
                             e.copy(out=small[:, t, 0:12], in_=ps[:, TM_GATE:TM_GATE + 12]),
                             reads=["ps%d" % pi], writes=["small"])
                    if c == 3 and stop != 31:
                        P.op("act", lambda e, ps=ps, t=t, c0=c0:
                             e.copy(out=small[:, t, 12:16], in_=ps[:, TM_DT - c0:TM_DT - c0 + 4]),
                             reads=["ps%d" % pi], writes=["small"])
                if stop != 32:
                    P.dma(STORE_Q, tm_d[t * 128:(t + 1) * 128, :], tmt, reads=[k + "tm"], writes=["tm_t%d" % t])
            for r in range(NFM if stop > 3 else 0):
                pi = 5 + r % 3
                ps = psb[pi]
                for kc in range(8):
                    P.op("pe", lambda e, kc=kc, ps=ps, hT=hT, r=r:
                         e.matmul(ps[:, :], lhsT=w_sb[:, kc, r * 128:(r + 1) * 128], rhs=hT[:, kc, :],
                                  start=(kc == 0), stop=(kc == 7)),
                         reads=[hk, "w_sb0", "w_sb1", "w_sb2", "w_sb3"], writes=["ps%d" % pi])
                fmt = fmts[fm_i % 4]
                fk = "fmt%d" % (fm_i % 4)
                fm_i += 1
                evac(fmt, ps[:, :], reads=["ps%d" % pi], writes=[fk])
                P.dma(STORE_Q, fm_d[r * 128:(r + 1) * 128, sb * 512:(sb + 1) * 512], fmt,
                      reads=[fk], writes=["fm_r%d" % r])
        P.barrier()
        a32.release()
        a16.release()

    def load_bcast(dst, src_row, key):
        P.dma("sp", dst, src_row.partition_broadcast(128), writes=[key])

    def head_norm_stats(o_sb, H, Dh, s1, s2, sq, key):
        o3 = o_sb.rearrange("p (h e) -> p h e", h=H)
        P.op("dve", lambda e: e.tensor_reduce(out=s1, in_=o3, axis=AX.X, op=ALU.add),
             reads=[key + "o"], writes=[key + "s1"])
        P.op("dve", lambda e: e.tensor_tensor(out=sq, in0=o_sb, in1=o_sb, op=ALU.mult),
             reads=[key + "o"], writes=[key + "sq"])
        P.op("dve", lambda e: e.tensor_reduce(out=s2, in_=sq.rearrange("p (h e) -> p h e", h=H), axis=AX.X, op=ALU.add),
             reads=[key + "sq"], writes=[key + "s2"])

    def phase_ret(l):
        decT = a32.alloc(512)
        xiT = a32.alloc(256, (2, 128))
        zeta = a32.alloc(256)
        cdt = a32.alloc(4)
        gnw = a32.alloc(256)
        state = a32.alloc(128, (2, 64))
        state_b = a16.alloc(128, (2, 64))
        P.dma("sp", decT, c_ret_decT, writes=["decT"])
        P.dma("sp", xiT, c_ret_xiT.rearrange("p (r q) -> p r q", r=2), writes=["xiT"])
        P.dma("sp", zeta, c_ret_zeta, writes=["zeta"])
        P.dma("sp", cdt, c_ret_cd, writes=["cdt"])
        load_bcast(gnw, ret_gn_w[l:l + 1, :], "gnw")
        P.op("dve", lambda e: e.memset(state, 0.0), writes=["state"])
        P.op("dve", lambda e: e.memset(state_b, 0.0), writes=["state_b"])
        NB = 2
        qTs = [a16.alloc(256, (2, 128)) for _ in range(NB)]
        kTs = [a16.alloc(256, (2, 128)) for _ in range(NB)]
        tms = [a16.alloc(768) for _ in range(NB)]
        ATs = [a16.alloc(512) for _ in range(NB)]
        qxs = [a16.alloc(256, (2, 128)) for _ in range(NB)]
        kzs = [a16.alloc(256) for _ in range(NB)]
        osb = [a32.alloc(256) for _ in range(NB)]
        sqs = [a32.alloc(256) for _ in range(NB)]
        szs = [a32.alloc(256) for _ in range(NB)]
        st1 = [a32.alloc(4) for _ in range(NB)]
        st2 = [a32.alloc(4) for _ in range(NB)]
        mos = [a16.alloc(256) for _ in range(NB)]
        for t in range(T):
            j = t % NB
            k = "R%d_" % j
            qT, kT, tm, AT, qx, kz, o_sb, sq, sz, s1, s2 = (qTs[j], kTs[j], tms[j], ATs[j], qxs[j], kzs[j],
                                                          osb[j], sqs[j], szs[j], st1[j], st2[j])
            c0 = t * 128
            P.dma("sp", qT, fm_d[FM_RQ * 128:(FM_RQ + 2) * 128, c0:c0 + 128].rearrange("(r p) s -> p r s", p=128),
                  writes=[k + "qT"])
            P.dma("sp", kT, fm_d[FM_RK * 128:(FM_RK + 2) * 128, c0:c0 + 128].rearrange("(r p) s -> p r s", p=128),
                  writes=[k + "kT"])
            P.dma("sp", tm, tm_d[c0:c0 + 128, TM_RK:TM_RK + 768], writes=[k + "tm"])
            ps_sg = [psb[6][:, 0:256], psb[7][:, 0:256]]
            ps_xg = [psb[6][:, 256:384], psb[7][:, 256:384]]
            ps_o, ps_kv = psb[6][:, 0:256], psb[7][:, 0:256]
            ks_o, ks_kv = "ps6", "ps7"
            yield
            for h in range(4):
                hp, hr, g = (h % 2) * 64, h // 2, h % 2
                P.op("pe", lambda e, hp=hp, hr=hr, g=g, kT=kT, qT=qT:
                     e.matmul(ps_sg[g][:, hr * 128:(hr + 1) * 128], lhsT=kT[hp:hp + 64, hr, :], rhs=qT[hp:hp + 64, hr, :],
                              start=True, stop=True),
                     reads=[k + "qT", k + "kT"], writes=["ps%d" % (6 + g)])
            yield
            for g in range(2):
                P.op("dve", lambda e, g=g, AT=AT: e.tensor_tensor(out=AT[:, g * 256:(g + 1) * 256], in0=ps_sg[g][:, 0:256],
                                                                  in1=decT[:, g * 256:(g + 1) * 256], op=ALU.mult),
                     reads=["ps%d" % (6 + g), "decT"], writes=[k + "AT"])
            P.op("dve", lambda e, qx=qx, qT=qT: e.tensor_tensor(out=qx, in0=qT, in1=xiT, op=ALU.mult),
                 reads=[k + "qT", "xiT"], writes=[k + "qx"])
            P.op("dve", lambda e, kz=kz, tm=tm: e.tensor_tensor(out=kz, in0=tm[:, 0:256], in1=zeta, op=ALU.mult),
                 reads=[k + "tm", "zeta"], writes=[k + "kz"])
            for h in range(4):
                hp, hr, g = (h % 2) * 64, h // 2, h % 2
                ai = g * 2 + hr
                P.op("pe", lambda e, h=h, ai=ai, ps_o=ps_o, AT=AT, tm=tm:
                     e.matmul(ps_o[:, h * 64:(h + 1) * 64], lhsT=AT[:, ai * 128:(ai + 1) * 128],
                              rhs=tm[:, 256 + h * 64:256 + (h + 1) * 64], start=True, stop=True),
                     reads=[k + "AT", k + "tm"], writes=[ks_o])
                if t > 0:
                    P.op("pe", lambda e, hp=hp, hr=hr, g=g, qx=qx:
                         e.matmul(ps_xg[g][:, hr * 64:(hr + 1) * 64], lhsT=qx[hp:hp + 64, hr, :],
                                  rhs=state_b[hp:hp + 64, hr, :], start=True, stop=True),
                         reads=[k + "qx", "state_b"], writes=["ps%d" % (6 + g)])
            yield
            if t < T - 1:
                for r in range(2):
                    P.op("pe", lambda e, r=r, ps_kv=ps_kv, kz=kz, tm=tm:
                         e.matmul(ps_kv[:, r * 128:(r + 1) * 128], lhsT=kz[:, r * 128:(r + 1) * 128],
                                  rhs=tm[:, 256 + r * 128:256 + (r + 1) * 128], start=True, stop=True),
                         reads=[k + "kz", k + "tm"], writes=[ks_kv])
                for h in range(4):
                    hp, hr = (h % 2) * 64, h // 2
                    P.op("dve", lambda e, h=h, hp=hp, hr=hr, ps_kv=ps_kv:
                         e.scalar_tensor_tensor(out=state[hp:hp + 64, hr, :], in0=state[hp:hp + 64, hr, :],
                                                scalar=cdt[hp:hp + 64, h:h + 1],
                                                in1=ps_kv[hp:hp + 64, hr * 128 + (h % 2) * 64:hr * 128 + (h % 2) * 64 + 64],
                                                op0=ALU.mult, op1=ALU.add),
                         reads=[ks_kv, "cdt", "state"], writes=["state"])
                P.op("dve", lambda e: e.tensor_copy(out=state_b, in_=state), reads=["state"], writes=["state_b"])
            yield
            P.op("act", lambda e, o_sb=o_sb, ps_o=ps_o: e.copy(out=o_sb, in_=ps_o[:, 0:256]),
                 reads=[ks_o], writes=[k + "o"])
            if t > 0:
                for h in range(4):
                    hr, g = h // 2, h % 2
                    P.op("dve", lambda e, h=h, hr=hr, g=g, o_sb=o_sb:
                         e.tensor_tensor(out=o_sb[:, h * 64:(h + 1) * 64], in0=ps_xg[g][:, hr * 64:(hr + 1) * 64],
                                         in1=o_sb[:, h * 64:(h + 1) * 64], op=ALU.add),
                         reads=["ps%d" % (6 + g), k + "o"], writes=[k + "o"])
            P.op("act", lambda e, sz=sz, tm=tm: e.activation(out=sz, in_=tm[:, 512:768], func=AF.Silu),
                 reads=[k + "tm"], writes=[k + "sz"])
            head_norm_stats(o_sb, 4, 64, s1, s2, sq, k)
            P.op("dve", lambda e, s1=s1: e.tensor_scalar(out=s1, in0=s1, scalar1=1.0 / 64, scalar2=None, op0=ALU.mult),
                 reads=[k + "s1"], writes=[k + "s1"])
            P.op("dve", lambda e, s1=s1, s2=s2, sq=sq: e.tensor_tensor(out=sq[:, 0:4], in0=s1, in1=s1, op=ALU.mult),
                 reads=[k + "s1"], writes=[k + "sq"])
            P.op("dve", lambda e, s2=s2, sq=sq: e.scalar_tensor_tensor(out=s2, in0=s2, scalar=1.0 / 64, in1=sq[:, 0:4],
                                                                       op0=ALU.mult, op1=ALU.subtract),
                 reads=[k + "s2", k + "sq"], writes=[k + "s2"])
            P.op("dve", lambda e, s2=s2: e.tensor_scalar(out=s2, in0=s2, scalar1=EPS, scalar2=None, op0=ALU.add),
                 reads=[k + "s2"], writes=[k + "s2"])
            P.op("act", lambda e, s2=s2: e.activation(out=s2, in_=s2, func=AF.Sqrt), reads=[k + "s2"], writes=[k + "s2"])
            P.op("dve", lambda e, s2=s2: e.reciprocal(out=s2, in_=s2), reads=[k + "s2"], writes=[k + "s2"])
            for h in range(4):
                P.op("dve", lambda e, h=h, o_sb=o_sb, s1=s1, s2=s2:
                     e.tensor_scalar(out=o_sb[:, h * 64:(h + 1) * 64], in0=o_sb[:, h * 64:(h + 1) * 64],
                                     scalar1=s1[:, h:h + 1], scalar2=s2[:, h:h + 1], op0=ALU.subtract, op1=ALU.mult),
                     reads=[k + "o", k + "s1", k + "s2"], writes=[k + "o"])
            P.op("dve", lambda e, sz=sz: e.tensor_tensor(out=sz, in0=sz, in1=gnw, op=ALU.mult),
                 reads=[k + "sz", "gnw"], writes=[k + "sz"])
            mo = mos[j]
            P.op("dve", lambda e, mo=mo, o_sb=o_sb, sz=sz:
                 e.tensor_tensor(out=mo, in0=o_sb, in1=sz, op=ALU.mult),
                 reads=[k + "o", k + "sz"], writes=[k + "mo"])
            P.dma(STORE_Q, mixed_d[t * 128:(t + 1) * 128, 512:768], mo, reads=[k + "mo"], writes=["mixR%d" % t])
            yield

    def phase_ssd(l):
        triu = a32.alloc(128)
        ones_f = a32.alloc(128)
        cw = a32.alloc(24)
        cb = a32.alloc(6)
        vec = a32.alloc(12)
        snw = a32.alloc(256)
        dtt = a32.alloc(T * 4)
        dAt = a32.alloc(T * 4)
        cst = a32.alloc(T * 4)
        ecs = a32.alloc(T * 4)
        dsd = a32.alloc(T * 4)
        ecl = a32.alloc(T * 4)
        Sst = a32.alloc(256, (4, 64))
        Sst_b = a16.alloc(256, (4, 64))
        maskT = a32.alloc(128)
        P.dma("sp", triu, c_triu, writes=["triu"])
        P.dma("sp", cw, conv_wT[l], writes=["cw"])
        P.dma("sp", cb, conv_bT[l], writes=["cb"])
        load_bcast(vec, ssm_vec[l:l + 1, :], "vec")
        load_bcast(snw, ssm_norm_w[l:l + 1, :], "snw")
        P.op("dve", lambda e: e.memset(ones_f, 1.0), writes=["ones_f"])
        P.op("dve", lambda e: e.memset(Sst, 0.0), writes=["Sst"])
        P.op("dve", lambda e: e.memset(Sst_b, 0.0), writes=["Sst_b"])
        P.op("dve", lambda e: e.tensor_copy(out=maskT, in_=triu), reads=["triu"], writes=["maskT"])
        sm3 = small[:, :, 12:16]
        dt3 = dtt.rearrange("p (t h) -> p t h", h=4)
        dA3 = dAt.rearrange("p (t h) -> p t h", h=4)
        for h in range(4):
            P.op("dve", lambda e, h=h: e.tensor_scalar(out=dt3[:, :, h], in0=sm3[:, :, h], scalar1=vec[:, h:h + 1],
                                                       scalar2=None, op0=ALU.add),
                 reads=["small", "vec"], writes=["dtt"])
        P.op("act", lambda e: e.activation(out=dtt, in_=dtt, func=AF.Exp), reads=["dtt"], writes=["dtt"])
        P.op("act", lambda e: e.activation(out=dtt, in_=dtt, func=AF.Ln, bias=1.0), reads=["dtt"], writes=["dtt"])
        P.op("act", lambda e: e.activation(out=vec[:, 4:8], in_=vec[:, 4:8], func=AF.Exp), reads=["vec"], writes=["vec"])
        for h in range(4):
            P.op("dve", lambda e, h=h: e.tensor_scalar(out=dA3[:, :, h], in0=dt3[:, :, h], scalar1=vec[:, 4 + h:5 + h],
                                                       scalar2=-1.0, op0=ALU.mult, op1=ALU.mult),
                 reads=["dtt", "vec"], writes=["dAt"])
        psA, psB = psb[6], psb[7]
        P.op("pe", lambda e: e.matmul(psA[:, 0:T * 4], lhsT=triu, rhs=dAt, start=True, stop=True),
             reads=["triu", "dAt"], writes=["ps6"])
        P.op("pe", lambda e: e.matmul(psB[:, 0:T * 4], lhsT=ones_f, rhs=dAt, start=True, stop=True),
             reads=["ones_f", "dAt"], writes=["ps7"])
        P.op("act", lambda e: e.copy(out=cst, in_=psA[:, 0:T * 4]), reads=["ps6"], writes=["cst"])
        P.op("act", lambda e: e.activation(out=ecs, in_=cst, func=AF.Exp), reads=["cst"], writes=["ecs"])
        P.op("act", lambda e: e.activation(out=ecl, in_=psB[:, 0:T * 4], func=AF.Exp), reads=["ps7"], writes=["ecl"])
        P.op("dve", lambda e: e.tensor_tensor(out=dsd, in0=psB[:, 0:T * 4], in1=cst, op=ALU.subtract),
             reads=["ps7", "cst"], writes=["dsd"])
        P.op("act", lambda e: e.activation(out=dsd, in_=dsd, func=AF.Exp), reads=["dsd"], writes=["dsd"])
        P.op("dve", lambda e: e.tensor_tensor(out=dsd, in0=dsd, in1=dtt, op=ALU.mult),
             reads=["dsd", "dtt"], writes=["dsd"])

        xin = [a16.alloc(6 * 516, (6, 516)) for _ in range(2)]
        acc = a32.alloc(512)
        cvo = [a16.alloc(6 * 512, (6, 512)) for _ in range(2)]
        NB = 2
        xtm = [a32.alloc(256) for _ in range(NB)]
        btm = [a16.alloc(256) for _ in range(NB)]
        xdt = [a16.alloc(256) for _ in range(NB)]
        xdd = [a16.alloc(256) for _ in range(NB)]
        cbm = [a32.alloc(256) for _ in range(NB)]
        uda = [a32.alloc(128) for _ in range(NB)]
        dif = [a32.alloc(128) for _ in range(NB)]
        MTs = [a16.alloc(512) for _ in range(NB)]
        ysb = [a32.alloc(256) for _ in range(NB)]
        zts = [a16.alloc(256) for _ in range(NB)]
        szs = [a32.alloc(256) for _ in range(NB)]
        sqs = [a32.alloc(256) for _ in range(NB)]
        sss = [a32.alloc(1) for _ in range(NB)]
        mos = [a16.alloc(256) for _ in range(NB)]
        for sb in range(NSB):
            xi = xin[sb % 2]
            co = cvo[sb % 2]
            kx = "xin%d" % (sb % 2)
            kc_ = "cvo%d" % (sb % 2)
            src = fm_d[FM_XBC * 128:(FM_XBC + 6) * 128, :].rearrange("(r p) s -> p r s", p=128)
            if sb == 0:
                P.op("pool", lambda e, xi=xi: e.memset(xi[:, :, 0:3], 0.0), writes=[kx])
                P.dma("sp", xi[:, :, 3:515], src[:, :, 0:512], writes=[kx])
            else:
                P.dma("sp", xi[:, :, 0:515], src[:, :, sb * 512 - 3:sb * 512 + 512], writes=[kx])
            for r in range(6):
                for jj in range(4):
                    if jj == 0:
                        P.op("pool", lambda e, r=r, xi=xi: e.tensor_scalar(
                            out=acc, in0=xi[:, r, 0:512], scalar1=cw[:, r * 4:r * 4 + 1], scalar2=None, op0=ALU.mult),
                            reads=[kx, "cw"], writes=["acc"])
                    else:
                        P.op("dve", lambda e, r=r, jj=jj, xi=xi: e.scalar_tensor_tensor(
                            out=acc, in0=xi[:, r, jj:jj + 512], scalar=cw[:, r * 4 + jj:r * 4 + jj + 1], in1=acc,
                            op0=ALU.mult, op1=ALU.add),
                            reads=[kx, "cw", "acc"], writes=["acc"])
                P.op("act", lambda e, r=r, co=co: e.activation(out=co[:, r, :], in_=acc, func=AF.Silu, bias=cb[:, r:r + 1]),
                     reads=["acc", "cb"], writes=[kc_])
                yield
            for ti in range(4):
                t = sb * 4 + ti
                j = t % NB
                k = "S%d_" % j
                cs0 = ti * 128
                x_tm, b_tm, xd, xdd_, cbm_, ud, df, MT, y_sb, zt, sz, sq, ss = (
                    xtm[j], btm[j], xdt[j], xdd[j], cbm[j], uda[j], dif[j], MTs[j], ysb[j], zts[j], szs[j], sqs[j], sss[j])
                P.dma("sp", zt, tm_d[t * 128:(t + 1) * 128, TM_SZ:TM_SZ + 256], writes=[k + "z"])
                pTb = psb[6][:, 0:256].bitcast(BF16)
                for r in range(4):
                    P.op("pe", lambda e, r=r, pTb=pTb, co=co, cs0=cs0:
                         e.transpose(out=pTb[:, r * 128:(r + 1) * 128], in_=co[:, r, cs0:cs0 + 128], identity=ident_b),
                         reads=[kc_, "ident_b"], writes=["ps6"])
                yield
                P.op("act", lambda e, x_tm=x_tm, pTb=pTb: e.copy(out=x_tm, in_=pTb[:, 0:256]),
                     reads=["ps6"], writes=[k + "x"])
                P.op("act", lambda e, b_tm=b_tm, pTb=pTb: e.copy(out=b_tm, in_=pTb[:, 256:512]),
                     reads=["ps6"], writes=[k + "b"])
                for h in range(4):
                    P.op("dve", lambda e, h=h, xd=xd, x_tm=x_tm, t=t: e.tensor_scalar(
                        out=xd[:, h * 64:(h + 1) * 64], in0=x_tm[:, h * 64:(h + 1) * 64],
                        scalar1=dtt[:, t * 4 + h:t * 4 + h + 1], scalar2=None, op0=ALU.mult),
                        reads=[k + "x", "dtt"], writes=[k + "xd"])
                    P.op("dve", lambda e, h=h, xdd_=xdd_, x_tm=x_tm, t=t: e.tensor_scalar(
                        out=xdd_[:, h * 64:(h + 1) * 64], in0=x_tm[:, h * 64:(h + 1) * 64],
                        scalar1=dsd[:, t * 4 + h:t * 4 + h + 1], scalar2=None, op0=ALU.mult),
                        reads=[k + "x", "dsd"], writes=[k + "xdd"])
                yield
                pcb = psb[6][:, 256:512]
                kcb = "ps6"
                for g in range(2):
                    P.op("pe", lambda e, g=g, pcb=pcb, co=co, cs0=cs0:
                         e.matmul(pcb[:, g * 128:(g + 1) * 128], lhsT=co[:, 2 + g, cs0:cs0 + 128],
                                  rhs=co[:, 4 + g, cs0:cs0 + 128], start=True, stop=True),
                         reads=[kc_], writes=[kcb])
                for g in range(2):
                    P.op("dve", lambda e, g=g, cbm_=cbm_, pcb=pcb:
                         e.tensor_tensor(out=cbm_[:, g * 128:(g + 1) * 128], in0=pcb[:, g * 128:(g + 1) * 128],
                                         in1=maskT, op=ALU.mult),
                         reads=[kcb, "maskT"], writes=[k + "cbm"])
                pcs = psb[7]
                kcs = "ps7"
                for h in range(4):
                    yield
                    P.op("dve", lambda e, h=h, ud=ud, t=t: e.tensor_scalar(
                        out=ud, in0=triu, scalar1=dAt[:, t * 4 + h:t * 4 + h + 1], scalar2=None, op0=ALU.mult),
                        reads=["triu", "dAt"], writes=[k + "ud"])
                    P.op("pe", lambda e, h=h, pcs=pcs, ud=ud:
                         e.matmul(pcs[:, h * 128:(h + 1) * 128], lhsT=ones_f, rhs=ud, start=True, stop=True),
                         reads=["ones_f", k + "ud"], writes=[kcs])
                    P.op("dve", lambda e, h=h, df=df, pcs=pcs, t=t: e.tensor_scalar(
                        out=df, in0=pcs[:, h * 128:(h + 1) * 128], scalar1=cst[:, t * 4 + h:t * 4 + h + 1],
                        scalar2=0.0, op0=ALU.subtract, op1=ALU.min),
                        reads=[kcs, "cst"], writes=[k + "df"])
                    P.op("act", lambda e, df=df: e.activation(out=df, in_=df, func=AF.Exp),
                         reads=[k + "df"], writes=[k + "df"])
                    P.op("dve", lambda e, h=h, MT=MT, df=df, cbm_=cbm_: e.tensor_tensor(
                        out=MT[:, h * 128:(h + 1) * 128], in0=df, in1=cbm_[:, (h // 2) * 128:(h // 2 + 1) * 128],
                        op=ALU.mult),
                        reads=[k + "df", k + "cbm"], writes=[k + "MT"])
                yield
                py, pyo, pst = psb[6][:, 0:256], psb[6][:, 256:512], psb[7][:, 0:256]
                for h in range(4):
                    P.op("pe", lambda e, h=h, py=py, MT=MT, xd=xd:
                         e.matmul(py[:, h * 64:(h + 1) * 64], lhsT=MT[:, h * 128:(h + 1) * 128],
                                  rhs=xd[:, h * 64:(h + 1) * 64], start=True, stop=True),
                         reads=[k + "MT", k + "xd"], writes=["ps6"])
                if t > 0:
                    for h in range(4):
                        P.op("pe", lambda e, h=h, pyo=pyo, co=co, cs0=cs0:
                             e.matmul(pyo[:, h * 64:(h + 1) * 64], lhsT=co[:, 4 + h // 2, cs0:cs0 + 128],
                                      rhs=Sst_b[:, h, :], start=True, stop=True),
                             reads=[kc_, "Sst_b"], writes=["ps6"])
                yield
                P.op("act", lambda e, y_sb=y_sb, py=py: e.copy(out=y_sb, in_=py[:, 0:256]),
                     reads=["ps6"], writes=[k + "y"])
                for h in range(4):
                    if t > 0:
                        P.op("dve", lambda e, h=h, y_sb=y_sb, pyo=pyo, t=t: e.scalar_tensor_tensor(
                            out=y_sb[:, h * 64:(h + 1) * 64], in0=pyo[:, h * 64:(h + 1) * 64],
                            scalar=ecs[:, t * 4 + h:t * 4 + h + 1], in1=y_sb[:, h * 64:(h + 1) * 64],
                            op0=ALU.mult, op1=ALU.add),
                            reads=["ps6", "ecs", k + "y"], writes=[k + "y"])
                    P.op("dve", lambda e, h=h, y_sb=y_sb, x_tm=x_tm: e.scalar_tensor_tensor(
                        out=y_sb[:, h * 64:(h + 1) * 64], in0=x_tm[:, h * 64:(h + 1) * 64],
                        scalar=vec[:, 8 + h:9 + h], in1=y_sb[:, h * 64:(h + 1) * 64], op0=ALU.mult, op1=ALU.add),
                        reads=[k + "x", "vec", k + "y"], writes=[k + "y"])
                yield
                if t < T - 1:
                    kst = "ps7"
                    for h in range(4):
                        P.op("pe", lambda e, h=h, pst=pst, b_tm=b_tm, xdd_=xdd_:
                             e.matmul(pst[:, h * 64:(h + 1) * 64], lhsT=b_tm[:, (h // 2) * 128:(h // 2 + 1) * 128],
                                      rhs=xdd_[:, h * 64:(h + 1) * 64], start=True, stop=True),
                             reads=[k + "b", k + "xdd"], writes=[kst])
                    for h in range(4):
                        P.op("dve", lambda e, h=h, pst=pst, t=t: e.scalar_tensor_tensor(
                            out=Sst[:, h, :], in0=Sst[:, h, :], scalar=ecl[:, t * 4 + h:t * 4 + h + 1],
                            in1=pst[:, h * 64:(h + 1) * 64], op0=ALU.mult, op1=ALU.add),
                            reads=[kst, "ecl", "Sst"], writes=["Sst"])
                    P.op("dve", lambda e: e.tensor_copy(out=Sst_b, in_=Sst), reads=["Sst"], writes=["Sst_b"])
                yield
                P.op("act", lambda e, sz=sz, zt=zt: e.activation(out=sz, in_=zt, func=AF.Silu),
                     reads=[k + "z"], writes=[k + "sz"])
                P.op("dve", lambda e, y_sb=y_sb, sz=sz: e.tensor_tensor(out=y_sb, in0=y_sb, in1=sz, op=ALU.mult),
                     reads=[k + "y", k + "sz"], writes=[k + "y"])
                P.op("dve", lambda e, y_sb=y_sb, sq=sq, ss=ss: e.scalar_tensor_tensor(
                    out=sq, in0=y_sb, scalar=1.0, in1=y_sb, op0=ALU.mult, op1=ALU.mult, accum_out=ss),
                    reads=[k + "y"], writes=[k + "sq", k + "ss"])
                P.op("dve", lambda e, ss=ss: e.tensor_scalar(out=ss, in0=ss, scalar1=1.0 / 256, scalar2=EPS,
                                                             op0=ALU.mult, op1=ALU.add),
                     reads=[k + "ss"], writes=[k + "ss"])
                P.op("act", lambda e, ss=ss: e.activation(out=ss, in_=ss, func=AF.Sqrt), reads=[k + "ss"], writes=[k + "ss"])
                P.op("dve", lambda e, ss=ss: e.reciprocal(out=ss, in_=ss), reads=[k + "ss"], writes=[k + "ss"])
                mo = mos[j]
                P.op("dve", lambda e, mo=mo, y_sb=y_sb, ss=ss: e.scalar_tensor_tensor(
                    out=mo, in0=y_sb, scalar=ss, in1=snw, op0=ALU.mult, op1=ALU.mult),
                    reads=[k + "y", k + "ss", "snw"], writes=[k + "mo"])
                P.dma(STORE_Q, mixed_d[t * 128:(t + 1) * 128, 768:1024], mo, reads=[k + "mo"], writes=["mixS%d" % t])
                yield

    def phase_diff(l):
        scale = 32 ** -0.5
        lam_init = 0.8 - 0.6 * math.exp(-0.3 * l)
        dslopes = [2.0 ** (-8.0 * (i + 1) / 8) for i in range(8)][1::2]
        alib = a32.alloc(2 * 4 * 35, (2, 4, 35))
        tri01 = a16.alloc(128)
        tri_f = a32.alloc(128)
        dv = a32.alloc(192)
        lamt = a32.alloc(4)
        sw = a32.alloc(256)
        P.dma("sp", alib, c_alibi.rearrange("p (s h d) -> p s h d", s=2, h=4), writes=["alib"])
        P.dma("sp", tri_f, c_tri01, writes=["tri_f"])
        P.op("dve", lambda e: e.tensor_copy(out=tri01, in_=tri_f), reads=["tri_f"], writes=["tri01"])
        load_bcast(dv, diff_vec[l:l + 1, :], "dv")
        P.op("dve", lambda e: e.scalar_tensor_tensor(out=dv[:, 0:32], in0=dv[:, 0:32], scalar=1.0, in1=dv[:, 32:64],
                                                     op0=ALU.mult, op1=ALU.mult, accum_out=lamt[:, 0:1]),
             reads=["dv"], writes=["dv", "lamt"])
        P.op("dve", lambda e: e.scalar_tensor_tensor(out=dv[:, 64:96], in0=dv[:, 64:96], scalar=1.0, in1=dv[:, 96:128],
                                                     op0=ALU.mult, op1=ALU.mult, accum_out=lamt[:, 1:2]),
             reads=["dv", "lamt"], writes=["dv", "lamt"])
        P.op("act", lambda e: e.activation(out=lamt[:, 0:2], in_=lamt[:, 0:2], func=AF.Exp), reads=["lamt"], writes=["lamt"])
        P.op("dve", lambda e: e.tensor_tensor(out=lamt[:, 2:3], in0=lamt[:, 1:2], in1=lamt[:, 0:1], op=ALU.subtract),
             reads=["lamt"], writes=["lamt"])
        P.op("dve", lambda e: e.tensor_scalar(out=lamt[:, 2:3], in0=lamt[:, 2:3], scalar1=-lam_init, scalar2=None,
                                              op0=ALU.add),
             reads=["lamt"], writes=["lamt"])
        for h in range(4):
            P.op("dve", lambda e, h=h: e.tensor_scalar(out=sw[:, h * 64:(h + 1) * 64], in0=dv[:, 128:192],
                                                       scalar1=1.0 - lam_init, scalar2=None, op0=ALU.mult),
                 reads=["dv"], writes=["sw"])
        qT = a16.alloc(2 * S, (2, S))
        kT = a16.alloc(2 * S, (2, S))
        va = a16.alloc(T * 4 * 65, (T, 4, 65))
        for r in range(2):
            P.dma("sp", qT[:, r, :], fm_d[(FM_DQ + r) * 128:(FM_DQ + r + 1) * 128, :], writes=["dqT"])
            P.dma("sp", kT[:, r, :], fm_d[(FM_DK + r) * 128:(FM_DK + r + 1) * 128, :], writes=["dkT"])
        P.op("pool", lambda e: e.memset(va, 1.0), writes=["va%d" % t_ for t_ in range(T)])
        for t in range(T):
            P.dma("sp", va[:, t, :, 0:64], tm_d[t * 128:(t + 1) * 128, TM_DV:TM_DV + 256].rearrange("p (h e) -> p h e", h=4),
                  writes=["va%d" % t])
        ETp = [a16.alloc(1024, (2, 512)) for _ in range(2)]
        et_i = [0]
        oTs = [a32.alloc(512) for _ in range(2)]
        dso = a32.alloc(4 * 256, (4, 256))
        rz = a32.alloc(2)
        zts = [a16.alloc(256) for _ in range(2)]
        szs = [a32.alloc(256) for _ in range(2)]
        sq = a32.alloc(256)
        s2 = a32.alloc(4)
        mos = [a16.alloc(256) for _ in range(2)]
        ot_i = 0
        yield
        for sb in range(NSB):
            q0 = sb * 512
            for h in range(4):
                r, hl = h // 2, h % 2
                nkb = 4 * sb + 4
                W = 256 if dslopes[h] * 511 > 80 else 512
                def qk(kb):
                    par = kb % 2
                    qlo = max(0, kb - 4 * sb) * 128
                    for i in range(2):
                        rg = hl * 2 + i
                        bk = par * 2 + i
                        P.op("pe", lambda e, bk=bk, rg=rg, kb=kb, qlo=qlo, r=r, q0=q0:
                             e.matmul(psb[bk][:, qlo:512], lhsT=kT[rg * 32:(rg + 1) * 32, r, kb * 128:(kb + 1) * 128],
                                      rhs=qT[rg * 32:(rg + 1) * 32, r, q0 + qlo:q0 + 512], start=True, stop=True,
                                      tile_position=(rg * 32, 0)),
                             reads=["dqT", "dkT"], writes=["ps%d" % bk])

                def rest(kb):
                    par = kb % 2
                    rel = kb - 4 * sb
                    qlo = max(0, rel) * 128
                    ET = ETp[et_i[0] % 2]
                    ek = "ETp%d" % (et_i[0] % 2)
                    et_i[0] += 1
                    pair = pbig[:, par * 1024:(par + 1) * 1024].rearrange("p (i c) -> p i c", i=2)
                    for sub in range(512 // W):
                        a, b = max(qlo, sub * W), (sub + 1) * W
                        if a >= b:
                            continue
                        dd = (kb * 128 - (q0 + sub * W)) // 128
                        P.op("act", lambda e, ET=ET, a=a, b=b, dd=dd, pair=pair, h=h:
                             e.activation(out=ET[:, :, a:b], in_=pair[:, :, a:b], func=AF.Exp,
                                          bias=alib[:, 1, h, dd + 31:dd + 32], scale=scale),
                             reads=["ps%d" % (par * 2), "ps%d" % (par * 2 + 1), "alib"], writes=[ek])
                    if rel >= 0:
                        for i in range(2):
                            P.op("pool", lambda e, i=i, ET=ET, qlo=qlo:
                                 e.tensor_tensor(out=ET[:, i, qlo:qlo + 128], in0=ET[:, i, qlo:qlo + 128], in1=tri01, op=ALU.mult),
                                 reads=[ek, "tri01"], writes=[ek])
                    for i in range(2):
                        P.op("pe", lambda e, i=i, ET=ET, kb=kb, qlo=qlo, h=h, nkb=nkb:
                             e.matmul(psb[4 + i][0:65, qlo:512], lhsT=va[:, kb, h, :], rhs=ET[:, i, qlo:512],
                                      start=(kb == 0), stop=(kb == nkb - 1)),
                             reads=[ek, "va%d" % kb], writes=["ps%d" % (4 + i)])

                qk(0)
                for kb in range(nkb):
                    if kb + 1 < nkb:
                        qk(kb + 1)
                    rest(kb)
                    yield
                for i in range(2):
                    oT = oTs[ot_i % 2]
                    ok = "oT%d" % (ot_i % 2)
                    ot_i += 1
                    evac(oT[0:65, :], psb[4 + i][0:65, :], reads=["ps%d" % (4 + i)], writes=[ok])
                    for qt in range(4):
                        cbi = ((qt % 2) * 2 + i) * 65
                        P.op("pe", lambda e, qt=qt, cbi=cbi, oT=oT:
                             e.transpose(out=psb[qt // 2][:, cbi:cbi + 65], in_=oT[0:65, qt * 128:(qt + 1) * 128],
                                         identity=ident_f[0:65, 0:65]),
                             reads=[ok, "ident_f"], writes=["ps%d" % (qt // 2)])
                yield
                for qt in range(4):
                    pq = psb[qt // 2]
                    kq = "ps%d" % (qt // 2)
                    cb0 = (qt % 2) * 130
                    P.op("dve", lambda e, pq=pq, cb0=cb0: e.tensor_scalar(
                        out=rz, in0=pq[:, cb0:cb0 + 130].rearrange("p (s c) -> p s c", c=65)[:, :, 64], scalar1=1e-30,
                        scalar2=None, op0=ALU.max),
                        reads=[kq], writes=["rz"])
                    P.op("dve", lambda e: e.reciprocal(out=rz, in_=rz), reads=["rz"], writes=["rz"])
                    P.op("dve", lambda e: e.tensor_tensor(out=rz[:, 1:2], in0=rz[:, 1:2], in1=lamt[:, 2:3], op=ALU.mult),
                         reads=["rz", "lamt"], writes=["rz"])
                    P.op("dve", lambda e, h=h, qt=qt, pq=pq, cb0=cb0: e.tensor_scalar(
                        out=dso[:, qt, h * 64:(h + 1) * 64], in0=pq[:, cb0:cb0 + 64],
                        scalar1=rz[:, 0:1], scalar2=None, op0=ALU.mult),
                        reads=[kq, "rz"], writes=["dso"])
                    P.op("dve", lambda e, h=h, qt=qt, pq=pq, cb0=cb0: e.scalar_tensor_tensor(
                        out=dso[:, qt, h * 64:(h + 1) * 64], in0=pq[:, cb0 + 65:cb0 + 129],
                        scalar=rz[:, 1:2], in1=dso[:, qt, h * 64:(h + 1) * 64],
                        op0=ALU.mult, op1=ALU.add),
                        reads=[kq, "rz", "dso"], writes=["dso"])
                yield
            for qt in range(4):
                t = sb * 4 + qt
                j = t % 2
                zt, sz = zts[j], szs[j]
                k = "D%d_" % j
                P.dma("sp", zt, tm_d[t * 128:(t + 1) * 128, TM_DZ:TM_DZ + 256], writes=[k + "z"])
                P.op("act", lambda e, sz=sz, zt=zt: e.activation(out=sz, in_=zt, func=AF.Silu), reads=[k + "z"], writes=[k + "sz"])
                P.op("dve", lambda e, sz=sz: e.tensor_tensor(out=sz, in0=sz, in1=sw, op=ALU.mult),
                     reads=[k + "sz", "sw"], writes=[k + "sz"])
                P.op("dve", lambda e, qt=qt: e.tensor_tensor(out=sq, in0=dso[:, qt, :], in1=dso[:, qt, :], op=ALU.mult),
                     reads=["dso"], writes=["dsq"])
                P.op("dve", lambda e: e.tensor_reduce(out=s2, in_=sq.rearrange("p (h e) -> p h e", h=4), axis=AX.X, op=ALU.add),
                     reads=["dsq"], writes=["ds2"])
                P.op("dve", lambda e: e.tensor_scalar(out=s2, in0=s2, scalar1=1.0 / 64, scalar2=EPS, op0=ALU.mult, op1=ALU.add),
                     reads=["ds2"], writes=["ds2"])
                P.op("act", lambda e: e.activation(out=s2, in_=s2, func=AF.Sqrt), reads=["ds2"], writes=["ds2"])
                P.op("dve", lambda e: e.reciprocal(out=s2, in_=s2), reads=["ds2"], writes=["ds2"])
                mo = mos[j]
                for hh in range(4):
                    P.op("dve", lambda e, hh=hh, qt=qt, mo=mo, sz=sz: e.scalar_tensor_tensor(
                        out=mo[:, hh * 64:(hh + 1) * 64], in0=dso[:, qt, hh * 64:(hh + 1) * 64],
                        scalar=s2[:, hh:hh + 1], in1=sz[:, hh * 64:(hh + 1) * 64], op0=ALU.mult, op1=ALU.mult),
                        reads=["dso", "ds2", k + "sz"], writes=[k + "mo"])
                P.dma(STORE_Q, mixed_d[t * 128:(t + 1) * 128, 256:512], mo, reads=[k + "mo"], writes=["mixD%d" % t])
                yield

    def phase_nsa(l):
        a32.mark()
        a16.mark()
        scale = 64 ** -0.5
        nslopes = [2.0 ** (-8.0 * (i + 1) / 8) for i in range(8)][0::2]

        def subw(sl):
            return 128 if sl * 255 > 80 else (256 if sl * 511 > 80 else 512)
        kcT2 = a16.alloc(NBK * 128)
        vca = a16.alloc(NBK * 65, (NBK, 65))
        a32.mark()
        a16.mark()
        kvc = a16.alloc(S)
        P.dma("sp", kvc, fm_d[FM_KVC * 128:(FM_KVC + 1) * 128, :], writes=["kvc"])
        W1 = a16.alloc(32 * 256, (32, 256))
        W2k = a16.alloc(2 * 128, (2, 128))
        W2v = a16.alloc(2 * 64, (2, 64))
        peT = a16.alloc(32)
        hsk = a16.alloc(2 * NC, (2, NC))
        hsv = a16.alloc(2 * NC, (2, NC))
        stg = a32.alloc(8 * 256, (8, 256))
        stg2 = a32.alloc(2 * 64, (2, 64))
        pef = a32.alloc(32)
        hb = a32.alloc(4)
        for ch in range(4):
            P.dma("sp", stg[0:64], w_ck1[l, ch * 512:(ch + 1) * 512, :].rearrange("(l c) h -> c l h", c=64), writes=["stg"])
            P.dma("sp", stg[64:128], w_cv1[l, ch * 512:(ch + 1) * 512, :].rearrange("(l c) h -> c l h", c=64), writes=["stg"])
            P.op("pool", lambda e, ch=ch: e.tensor_copy(out=W1[:, ch * 8:(ch + 1) * 8, :], in_=stg), reads=["stg"], writes=["W1"])
        P.dma("sp", stg2, w_ck2[l].rearrange("(a p) d -> p a d", p=128), writes=["stg2"])
        for dup in range(2):
            P.op("pool", lambda e, dup=dup: e.tensor_copy(out=W2k[:, :, dup * 64:(dup + 1) * 64], in_=stg2), reads=["stg2"], writes=["W2k"])
        P.dma("sp", stg2, w_cv2[l].rearrange("(a p) d -> p a d", p=128), writes=["stg2"])
        P.op("pool", lambda e: e.tensor_copy(out=W2v, in_=stg2), reads=["stg2"], writes=["W2v"])
        P.dma("sp", pef, nsa_peT[l], writes=["pef"])
        P.op("dve", lambda e: e.tensor_copy(out=peT, in_=pef), reads=["pef"], writes=["peT"])
        P.op("pool", lambda e: e.memset(vca, 1.0), writes=["vca"])
        P.op("pool", lambda e: e.memset(kcT2, 0.0), writes=["kcT2"])
        for g, hs in ((0, hsk), (1, hsv)):
            for hh in range(2):
                ph, pbb = psb[2 * g + hh], psb[4 + 2 * g + hh]
                for li in range(32):
                    P.op("pe", lambda e, g=g, hh=hh, li=li, pbb=pbb:
                         e.matmul(pbb[:, 0:1], lhsT=W1[g * 64:(g + 1) * 64, li, hh * 128:(hh + 1) * 128],
                                  rhs=peT[g * 64:(g + 1) * 64, li:li + 1], start=(li == 0), stop=(li == 31)),
                         reads=["W1", "peT"], writes=["ps%d" % (4 + 2 * g + hh)])
                P.op("dve", lambda e, g=g, hh=hh, pbb=pbb: e.tensor_copy(out=hb[:, 2 * g + hh:2 * g + hh + 1], in_=pbb[:, 0:1]),
                     reads=["ps%d" % (4 + 2 * g + hh)], writes=["hb"])
                for li in range(32):
                    P.op("pe", lambda e, g=g, hh=hh, li=li, ph=ph:
                         e.matmul(ph[:, 0:NC], lhsT=W1[g * 64:(g + 1) * 64, li, hh * 128:(hh + 1) * 128],
                                  rhs=kvc[g * 64:(g + 1) * 64, li:li + 16 * (NC - 1) + 1:16], start=(li == 0), stop=(li == 31)),
                         reads=["W1", "kvc"], writes=["ps%d" % (2 * g + hh)])
                P.op("act", lambda e, g=g, hh=hh, ph=ph, hs=hs:
                     e.activation(out=hs[:, hh, :], in_=ph[:, 0:NC], func=AF.Silu, bias=hb[:, 2 * g + hh:2 * g + hh + 1]),
                     reads=["ps%d" % (2 * g + hh), "hb"], writes=["hs%d" % g])
        for hh in range(2):
            P.op("pe", lambda e, hh=hh: e.matmul(psb[6][:, 0:NC], lhsT=W2k[:, hh, :], rhs=hsk[:, hh, :],
                                                 start=(hh == 0), stop=(hh == 1)),
                 reads=["W2k", "hs0"], writes=["ps6"])
        P.op("act", lambda e: e.copy(out=kcT2[:, 0:NC], in_=psb[6][:, 0:NC]), reads=["ps6"], writes=["kcT2"])
        for nb in range(NBK):
            nn = min(128, NC - nb * 128)
            for hh in range(2):
                P.op("pe", lambda e, hh=hh, nb=nb, nn=nn:
                     e.matmul(psb[7][0:nn, 0:64], lhsT=hsv[:, hh, nb * 128:nb * 128 + nn], rhs=W2v[:, hh, :],
                              start=(hh == 0), stop=(hh == 1)),
                     reads=["W2v", "hs1"], writes=["ps7"])
            P.op("act", lambda e, nb=nb, nn=nn: e.copy(out=vca[0:nn, nb, 0:64], in_=psb[7][0:nn, 0:64]),
                 reads=["ps7"], writes=["vca"])
        P.barrier()
        a32.release()
        a16.release()

        alib = a32.alloc(2 * 4 * 35, (2, 4, 35))
        alic = a32.alloc(256, (4, 2, 32))
        tri01 = a16.alloc(128)
        low01 = a16.alloc(128)
        gneg = a16.alloc(4096)
        ex2 = a16.alloc(T * 128, (T, 128))
        ovl = a16.alloc(128, (2, 64))
        tri_f = a32.alloc(128)
        P.dma("sp", alib, c_alibi.rearrange("p (s h d) -> p s h d", s=2, h=4), writes=["alib"])
        P.dma("sp", alic, c_alibi_cmp.rearrange("p (h n q) -> p h n q", h=4, n=2), writes=["alic"])
        P.dma("sp", tri_f, c_tri01, writes=["tri_f"])
        P.op("dve", lambda e: e.tensor_copy(out=tri01, in_=tri_f), reads=["tri_f"], writes=["tri01"])
        P.dma("sp", low01, c_low01, writes=["low01"])
        P.dma("sp", gneg, c_gneg, writes=["gneg"])
        P.dma("sp", ex2, c_ex2.rearrange("p (t k) -> p t k", k=128), writes=["ex2"])
        P.dma("sp", ovl, c_ovl.rearrange("p (n j) -> p n j", n=2), writes=["ovl"])
        if NS > 16:
            tka = a32.alloc(T * 64, (T, 64))
            P.dma("sp", tka, c_topk_add.rearrange("p (t j) -> p t j", j=64), writes=["tka"])
        qT = a16.alloc(2 * S, (2, S))
        ksl = a16.alloc(S)
        kwn = a16.alloc(S)
        vsl = a16.alloc(T * 65, (T, 65))
        vwn = a16.alloc(T * 65, (T, 65))
        for r in range(2):
            P.dma("sp", qT[:, r, :], fm_d[(FM_NQ + r) * 128:(FM_NQ + r + 1) * 128, :], writes=["nqT"])
        for g in range(2):
            P.dma("sp", ksl[g * 64:(g + 1) * 64, :], fm_d[FM_KSW * 128:FM_KSW * 128 + 64, :], writes=["ksl"])
            P.dma("sp", kwn[g * 64:(g + 1) * 64, :], fm_d[FM_KSW * 128 + 64:FM_KSW * 128 + 128, :], writes=["kwn"])
        P.op("pool", lambda e: e.memset(vsl, 1.0), writes=["vsl%d" % t_ for t_ in range(T)])
        P.op("pool", lambda e: e.memset(vwn, 1.0), writes=["vwn%d" % t_ for t_ in range(T)])
        for t in range(T):
            P.dma("sp", vsl[:, t, 0:64], tm_d[t * 128:(t + 1) * 128, TM_VSLC:TM_VSLC + 64], writes=["vsl%d" % t])
            P.dma("sp", vwn[:, t, 0:64], tm_d[t * 128:(t + 1) * 128, TM_VWIN:TM_VWIN + 64], writes=["vwn%d" % t])

        ETs = [[a16.alloc(512) for _ in range(2)] for _ in range(4)]
        et_i = [0, 0, 0, 0]
        smf = [a32.alloc(512) for _ in range(2)]
        oTs = [a32.alloc(512) for _ in range(2)]
        acc = a32.alloc(4 * 256, (4, 256))
        imp = a32.alloc(4 * 64, (4, 64))
        gs = a32.alloc(4 * 12, (4, 12))
        rz = a32.alloc(4)
        coef = a32.alloc(4)
        top8 = a32.alloc(8)
        tmpk = a32.alloc(64)
        nsel = a16.alloc(128)
        negT2 = a16.alloc(512)
        zts = [a16.alloc(256) for _ in range(2)]
        szs = [a32.alloc(256) for _ in range(2)]
        mos = [a16.alloc(256) for _ in range(2)]
        ot_i = [0]
        sm_i = [0]

        def epilogue(sb, b, first):
            for h in range(4):
                oT = oTs[ot_i[0] % 2]
                ok = "noT%d" % (ot_i[0] % 2)
                ot_i[0] += 1
                evac(oT[0:65, :], psb[4 + h][0:65, :], reads=["ps%d" % (4 + h)], writes=[ok])
                for qt in range(4):
                    P.op("pe", lambda e, h=h, qt=qt, oT=oT:
                         e.transpose(out=psb[qt][:, h * 65:(h + 1) * 65], in_=oT[0:65, qt * 128:(qt + 1) * 128],
                                     identity=ident_f[0:65, 0:65]),
                         reads=[ok, "ident_f"], writes=["ps%d" % qt])
            for qt in range(4):
                pq = psb[qt]
                kq = "ps%d" % qt
                P.op("dve", lambda e, pq=pq: e.tensor_scalar(
                    out=rz, in0=pq[:, 0:260].rearrange("p (s c) -> p s c", c=65)[:, :, 64], scalar1=1e-30, scalar2=None,
                    op0=ALU.max), reads=[kq], writes=["nrz"])
                P.op("dve", lambda e: e.reciprocal(out=rz, in_=rz), reads=["nrz"], writes=["nrz"])
                if b == 0:
                    P.op("dve", lambda e, qt=qt: e.tensor_copy(out=rzc[:, qt, :], in_=rz), reads=["nrz"], writes=["rzc"])
                P.op("dve", lambda e, qt=qt: e.tensor_tensor(
                    out=coef, in0=rz, in1=gs[:, qt, :].rearrange("p (h b) -> p h b", b=3)[:, :, b], op=ALU.mult),
                    reads=["nrz", "gs"], writes=["coef"])
                for h in range(4):
                    if first:
                        P.op("dve", lambda e, h=h, qt=qt, pq=pq: e.tensor_scalar(
                            out=acc[:, qt, h * 64:(h + 1) * 64], in0=pq[:, h * 65:h * 65 + 64], scalar1=coef[:, h:h + 1],
                            scalar2=None, op0=ALU.mult), reads=[kq, "coef"], writes=["nacc"])
                    else:
                        P.op("dve", lambda e, h=h, qt=qt, pq=pq: e.scalar_tensor_tensor(
                            out=acc[:, qt, h * 64:(h + 1) * 64], in0=pq[:, h * 65:h * 65 + 64], scalar=coef[:, h:h + 1],
                            in1=acc[:, qt, h * 64:(h + 1) * 64], op0=ALU.mult, op1=ALU.add),
                            reads=[kq, "coef", "nacc"], writes=["nacc"])

        def exp_tile(h, ET, ek, src, src_key, a_lo, a_hi, kb, q0, rows=128):
            W = subw(nslopes[h])
            for sub in range(512 // W):
                a, bb = max(a_lo, sub * W), min(a_hi, (sub + 1) * W)
                if a >= bb:
                    continue
                dd = (kb * 128 - (q0 + sub * W)) // 128
                P.op("act", lambda e, a=a, bb=bb, dd=dd: e.activation(
                    out=ET[0:rows, a:bb], in_=src[0:rows, a:bb], func=AF.Exp, bias=alib[0:rows, 0, h, dd + 31:dd + 32], scale=scale),
                    reads=[src_key, "alib"], writes=[ek])

        rzc = a32.alloc(16, (4, 4))
        for sb in range(NSB):
            q0 = sb * 512
            P.op("act", lambda e, sb=sb: e.activation(out=gs, in_=small[:, sb * 4:sb * 4 + 4, 0:12], func=AF.Sigmoid),
                 reads=["small"], writes=["gs"])
            nbs = []
            for nb in range(NBK):
                nn = min(128, NC - 128 * nb, 32 * sb + 31 - 128 * nb)
                if nn > 0:
                    nbs.append((nb, nn))
            cET = {}
            if nbs:
                for h in range(4):
                    g, r = h % 2, h // 2
                    for (nb, nn) in nbs:
                        P.op("pe", lambda e, h=h, g=g, r=r, nb=nb, nn=nn, q0=q0:
                             e.matmul(psb[h][0:nn, :], lhsT=kcT2[g * 64:(g + 1) * 64, nb * 128:nb * 128 + nn],
                                      rhs=qT[g * 64:(g + 1) * 64, r, q0:q0 + 512], start=True, stop=True),
                             reads=["kcT2", "nqT"], writes=["ps%d" % h])
                        sm = smf[sm_i[0] % 2]
                        sk = "smf%d" % (sm_i[0] % 2)
                        sm_i[0] += 1
                        c0 = q0 - 2048 * nb
                        P.op("dve", lambda e, h=h, nn=nn, sm=sm, c0=c0: e.tensor_tensor(
                            out=sm[0:nn, :], in0=psb[h][0:nn, :], in1=gneg[0:nn, c0:c0 + 512], op=ALU.add),
                            reads=["ps%d" % h, "gneg"], writes=[sk])
                        ET = ETs[h][nb]
                        ek = "ET%d_%d" % (h, nb)
                        cET[(h, nb)] = (ET, ek, nn)
                        W = subw(nslopes[h])
                        for sub in range(512 // W):
                            a, bb = sub * W, (sub + 1) * W
                            qi = (q0 + a) // 128
                            P.op("act", lambda e, h=h, nb=nb, nn=nn, sm=sm, ET=ET, a=a, bb=bb, qi=qi: e.activation(
                                out=ET[0:nn, a:bb], in_=sm[0:nn, a:bb], func=AF.Exp, bias=alic[0:nn, h, nb, qi:qi + 1], scale=scale),
                                reads=[sk, "alic"], writes=[ek])
                    for i, (nb, nn) in enumerate(nbs):
                        ET, ek, _ = cET[(h, nb)]
                        P.op("pe", lambda e, h=h, nb=nb, nn=nn, ET=ET, i=i, n=len(nbs):
                             e.matmul(psb[4 + h][0:65, :], lhsT=vca[0:nn, nb, :], rhs=ET[0:nn, :],
                                      start=(i == 0), stop=(i == n - 1)),
                             reads=[ek, "vca"], writes=["ps%d" % (4 + h)])
                epilogue(sb, 0, True)
                for h in range(4):
                    for qt in range(4):
                        for i, (nb, nn) in enumerate(nbs):
                            ET, ek, _ = cET[(h, nb)]
                            P.op("pe", lambda e, h=h, qt=qt, nb=nb, nn=nn, ET=ET, i=i, n=len(nbs):
                                 e.matmul(psb[h][:, qt * 64:(qt + 1) * 64], lhsT=ET[0:nn, qt * 128:(qt + 1) * 128],
                                          rhs=ovl[0:nn, nb, :], start=(i == 0), stop=(i == n - 1)),
                                 reads=[ek, "ovl"], writes=["ps%d" % h])
                for h in range(4):
                    for qt in range(4):
                        if h == 0:
                            P.op("dve", lambda e, h=h, qt=qt: e.tensor_scalar(
                                out=imp[:, qt, :], in0=psb[h][:, qt * 64:(qt + 1) * 64], scalar1=rzc[:, qt, h:h + 1],
                                scalar2=None, op0=ALU.mult), reads=["ps%d" % h, "rzc"], writes=["imp"])
                        else:
                            P.op("dve", lambda e, h=h, qt=qt: e.scalar_tensor_tensor(
                                out=imp[:, qt, :], in0=psb[h][:, qt * 64:(qt + 1) * 64], scalar=rzc[:, qt, h:h + 1],
                                in1=imp[:, qt, :], op0=ALU.mult, op1=ALU.add), reads=["ps%d" % h, "rzc", "imp"], writes=["imp"])
            else:
                P.op("dve", lambda e: e.memset(acc, 0.0), writes=["nacc"])
                P.op("dve", lambda e: e.memset(imp, 0.0), writes=["imp"])
            for qt in range(4):
                t = sb * 4 + qt
                if NS > 16:
                    P.op("dve", lambda e, qt=qt, t=t: e.tensor_tensor(out=imp[:, qt, :], in0=imp[:, qt, :], in1=tka[:, t, :], op=ALU.add),
                         reads=["imp", "tka"], writes=["imp"])
                    P.op("dve", lambda e, qt=qt: e.max(out=top8, in_=imp[:, qt, :]), reads=["imp"], writes=["top8"])
                    P.op("dve", lambda e, qt=qt: e.match_replace(out=tmpk, in_to_replace=top8, in_values=imp[:, qt, :], imm_value=-1e9),
                         reads=["imp", "top8"], writes=["tmpk"])
                    P.op("dve", lambda e: e.max(out=top8, in_=tmpk), reads=["tmpk"], writes=["top8"])
                    P.op("dve", lambda e, qt=qt: e.tensor_scalar(out=tmpk, in0=imp[:, qt, :], scalar1=top8[:, 7:8], scalar2=-1.0,
                                                                 op0=ALU.is_ge, op1=ALU.add),
                         reads=["imp", "top8"], writes=["tmpk"])
                    for dup in range(2):
                        P.op("dve", lambda e, dup=dup: e.tensor_scalar(out=nsel[:, dup * 64:(dup + 1) * 64], in0=tmpk, scalar1=BIG,
                                                                       scalar2=None, op0=ALU.mult),
                             reads=["tmpk"], writes=["nsel"])
                else:
                    P.op("dve", lambda e: e.memset(nsel, 0.0), writes=["nsel"])
                pt = psb[qt][:, 0:64].bitcast(BF16)
                P.op("pe", lambda e, pt=pt: e.transpose(out=pt, in_=nsel, identity=ident_b), reads=["nsel", "ident_b"], writes=["ps%d" % qt])
                P.op("act", lambda e, qt=qt, pt=pt: e.copy(out=negT2[:, qt * 128:(qt + 1) * 128], in_=pt), reads=["ps%d" % qt], writes=["negT2"])
            for b, kT2, va_, kbs in ((1, ksl, vsl, list(range(0, 4 * sb + 4))),
                                     (2, kwn, vwn, list(range(max(0, 4 * sb - 4), 4 * sb + 4)))):
                kname = "ksl" if b == 1 else "kwn"
                vname = "vsl" if b == 1 else "vwn"
                n_kb = len(kbs)
                for hp in range(2):
                    heads = (2 * hp, 2 * hp + 1)

                    def geom(i, kbs=kbs, b=b, sb=sb):
                        kb = kbs[i]
                        rel = kb - 4 * sb
                        qlo = max(0, rel) * 128
                        qhi = 512 if (b == 1 or rel >= 0) else 128 * (rel + 5)
                        return kb, rel, qlo, qhi

                    def qk(i, heads=heads, kT2=kT2, b=b, q0=q0, kname=kname):
                        kb, rel, qlo, qhi = geom(i)
                        par = i % 2
                        for h in heads:
                            g, r = h % 2, h // 2
                            bk = par * 2 + g
                            P.op("pe", lambda e, bk=bk, g=g, r=r, kb=kb, qlo=qlo, qhi=qhi, kT2=kT2, b=b, q0=q0:
                                 e.matmul(psb[bk][:, qlo:qhi], lhsT=kT2[g * 64:(g + 1) * 64, kb * 128:(kb + 1) * 128],
                                          rhs=qT[g * 64:(g + 1) * 64, r, q0 + qlo:q0 + qhi], start=True, stop=(b != 1)),
                                 reads=[kname, "nqT"], writes=["ps%d" % bk])
                            if b == 1:
                                P.op("pe", lambda e, bk=bk, g=g, kb=kb, qlo=qlo, qhi=qhi:
                                     e.matmul(psb[bk][:, qlo:qhi], lhsT=ex2[g * 64:(g + 1) * 64, kb, :],
                                              rhs=negT2[g * 64:(g + 1) * 64, qlo:qhi], start=False, stop=True),
                                     reads=["ex2", "negT2"], writes=["ps%d" % bk])

                    def rest(i, heads=heads, b=b, q0=q0, va_=va_, vname=vname, n_kb=n_kb):
                        kb, rel, qlo, qhi = geom(i)
                        par = i % 2
                        for h in heads:
                            bk = par * 2 + h % 2
                            ET = ETs[h][et_i[h] % 2]
                            ek = "ET%d_%d" % (h, et_i[h] % 2)
                            et_i[h] += 1
                            exp_tile(h, ET, ek, psb[bk], "ps%d" % bk, qlo, qhi, kb, q0)
                            if rel >= 0:
                                P.op("pool", lambda e, ET=ET, qlo=qlo: e.tensor_tensor(
                                    out=ET[:, qlo:qlo + 128], in0=ET[:, qlo:qlo + 128], in1=tri01, op=ALU.mult),
                                    reads=[ek, "tri01"], writes=[ek])
                            elif b == 2:
                                P.op("pool", lambda e, ET=ET, qhi=qhi: e.tensor_tensor(
                                    out=ET[:, qhi - 128:qhi], in0=ET[:, qhi - 128:qhi], in1=low01, op=ALU.mult),
                                    reads=[ek, "low01"], writes=[ek])
                            P.op("pe", lambda e, h=h, ET=ET, kb=kb, qlo=qlo, qhi=qhi, va_=va_, i=i, n=n_kb:
                                 e.matmul(psb[4 + h][0:65, qlo:qhi], lhsT=va_[:, kb, :], rhs=ET[:, qlo:qhi],
                                          start=(i == 0), stop=(i == n - 1), skip_group_check=True),
                                 reads=[ek, "%s%d" % (vname, kb)], writes=["ps%d" % (4 + h)])

                    qk(0)
                    for i in range(n_kb):
                        if i + 1 < n_kb:
                            qk(i + 1)
                        rest(i)
                epilogue(sb, b, False)
            for qt in range(4):
                t = sb * 4 + qt
                j = t % 2
                zt, sz = zts[j], szs[j]
                k = "N%d_" % j
                P.dma("sp", zt, tm_d[t * 128:(t + 1) * 128, TM_NSAZ:TM_NSAZ + 256], writes=[k + "z"])
                P.op("act", lambda e, sz=sz, zt=zt: e.activation(out=sz, in_=zt, func=AF.Silu), reads=[k + "z"], writes=[k + "sz"])
                mo = mos[j]
                P.op("dve", lambda e, qt=qt, mo=mo, sz=sz: e.tensor_tensor(out=mo, in0=acc[:, qt, :], in1=sz, op=ALU.mult),
                     reads=["nacc", k + "sz"], writes=[k + "mo"])
                P.dma(STORE_Q, mixed_d[t * 128:(t + 1) * 128, 0:256], mo, reads=[k + "mo"], writes=["mixN%d" % t])
        P.barrier()
        a32.release()
        a16.release()

    def phase_dump(l):
        P.barrier()

    def phase_stub(l):
        if True:
            a16.mark()
            tl = [a16.alloc(1024) for _ in range(2)]
            for t in range(T):
                P.dma("sp", tl[t % 2], tm_d[t * 128:(t + 1) * 128, 396:396 + 1024],
                      reads=[], writes=["tl%d" % (t % 2)])
                if stop == 98:
                    continue
                P.dma(STORE_Q, mixed_d[t * 128:(t + 1) * 128, :], tl[t % 2], reads=["tl%d" % (t % 2)], writes=["mixX%d" % t])
            P.barrier()
            a16.release()

    def phase_F(l):
        x_src = x_in if l == 0 else xres
        wo_sb = a16.alloc(8 * 1024, (8, 1024))
        stage = [a32.alloc(1024) for _ in range(2)]
        xts = [a32.alloc(D_MODEL) for _ in range(2)]
        xns = [a32.alloc(D_MODEL) for _ in range(2)]
        mTs = [a16.alloc(8 * 128, (8, 128)) for _ in range(2)]
        mxs = [a16.alloc(1024) for _ in range(2)]
        last = (l == L - 1) and final_norm
        if last:
            fnwb = a32.alloc(D_MODEL)
            junk = a16.alloc(D_MODEL)
            sss = [a32.alloc(1) for _ in range(2)]
            rstds = [a32.alloc(1) for _ in range(2)]
            yts = [a32.alloc(D_MODEL) for _ in range(2)]
            P.dma("sp", fnwb, final_norm_w[0:1, :].partition_broadcast(128), writes=["fnwb"])
        for kc in range(8):
            st = stage[kc % 2]
            P.dma("sp", st, w_out[l, kc * 128:(kc + 1) * 128, :], writes=["stage%d" % (kc % 2)])
            P.op(("pool", "dve")[kc % 2], lambda e, kc=kc, st=st: e.tensor_copy(out=wo_sb[:, kc, :], in_=st),
                 reads=["stage%d" % (kc % 2)], writes=["wo_sb%d" % (kc % 2)])
        def f_loads(t):
            j = t % 2
            k = "F%d_" % j
            P.dma("sp", mxs[j], mixed_d[t * 128:(t + 1) * 128, :], reads=["mixR%d" % t], writes=[k + "mx"])
            P.dma("sp", xts[j], x_src[t * 128:(t + 1) * 128, :], reads=["xres_t%d" % t], writes=[k + "x"])
        f_loads(0)
        for t in range(T):
            j = t % 2
            k = "F%d_" % j
            xt, xn, mT = xts[j], xns[j], mTs[j]
            mx = mxs[j]
            pst = psb[j]
            pstb = pst[:, 0:512].bitcast(BF16)
            for kc in range(8):
                P.op("pe", lambda e, kc=kc, mx=mx, pstb=pstb:
                     e.transpose(out=pstb[:, kc * 128:(kc + 1) * 128], in_=mx[:, kc * 128:(kc + 1) * 128],
                                 identity=ident_b),
                     reads=[k + "mx", "ident_b"], writes=["ps%d" % j])
            evac(mT, pstb.rearrange("p (k t) -> p k t", k=8), reads=["ps%d" % j], writes=[k + "mT"])
            if t + 1 < T:
                f_loads(t + 1)
            for c in range(2):
                pi = 2 + (t * 2 + c) % 4
                ps = psb[pi]
                for kc in range(8):
                    P.op("pe", lambda e, kc=kc, ps=ps, mT=mT, c=c:
                         e.matmul(ps[:, :], lhsT=mT[:, kc, :], rhs=wo_sb[:, kc, c * 512:(c + 1) * 512],
                                  start=(kc == 0), stop=(kc == 7)),
                         reads=[k + "mT", "wo_sb0", "wo_sb1"], writes=["ps%d" % pi])
                P.op("dve", lambda e, ps=ps, xt=xt, xn=xn, c=c:
                     e.tensor_tensor(out=xn[:, c * 512:(c + 1) * 512], in0=ps[:, :],
                                     in1=xt[:, c * 512:(c + 1) * 512], op=ALU.add),
                     reads=["ps%d" % pi, k + "x"], writes=[k + "xn"])
            if not last:
                P.dma(STORE_Q, xres[t * 128:(t + 1) * 128, :], xn, reads=[k + "xn"], writes=["xres_t%d" % t])
            else:
                ss, rstd, yt = sss[j], rstds[j], yts[j]
                P.op("dve", lambda e, xn=xn, ss=ss: e.scalar_tensor_tensor(
                    out=junk, in0=xn, scalar=1.0, in1=xn, op0=ALU.mult, op1=ALU.mult, accum_out=ss),
                    reads=[k + "xn"], writes=["Fjunk", k + "ss"])
                P.op("dve", lambda e, ss=ss: e.tensor_scalar(out=ss, in0=ss, scalar1=1.0 / D_MODEL, scalar2=EPS,
                                                             op0=ALU.mult, op1=ALU.add),
                     reads=[k + "ss"], writes=[k + "ss"])
                P.op("act", lambda e, ss=ss: e.activation(out=ss, in_=ss, func=AF.Sqrt),
                     reads=[k + "ss"], writes=[k + "ss"])
                P.op("dve", lambda e, ss=ss, rstd=rstd: e.reciprocal(out=rstd, in_=ss),
                     reads=[k + "ss"], writes=[k + "rstd"])
                P.op("dve", lambda e, xn=xn, rstd=rstd, yt=yt: e.scalar_tensor_tensor(
                    out=yt, in0=xn, scalar=rstd, in1=fnwb, op0=ALU.mult, op1=ALU.mult),
                    reads=[k + "xn", k + "rstd", "fnwb"], writes=[k + "y"])
                P.dma(STORE_Q, y_out[t * 128:(t + 1) * 128, :], yt, reads=[k + "y"], writes=["y_t%d" % t])

    for l in range(L):
        phase_A(l)
        if "STUB" in phases:
            phase_stub(l)
        realP = P
        for group in ((("DIFF", phase_diff), ("SSD", phase_ssd)),):
            a32.mark()
            a16.mark()
            streams = []
            for nm, ph in group:
                if nm in phases:
                    st_ = Stream()
                    P = st_
                    for _ in ph(l):
                        pass
                    streams.append(st_)
            P = realP
            merge_streams(P, streams)
            P.barrier()
            a32.release()
            a16.release()
        if "NSA" in phases:
            phase_nsa(l)
        if "DUMP" in phases:
            phase_dump(l)
        a32.mark()
        a16.mark()
        streams = []
        if "F" in phases:
            st_ = Stream()
            P = st_
            phase_F(l)
            streams.append(st_)
        if "RET" in phases:
            st_ = Stream()
            P = st_
            for _ in phase_ret(l):
                pass
            streams.append(st_)
        P = realP
        merge_streams(P, streams)
        P.barrier()
        a32.release()
        a16.release()

    P.emit(es)
    es.close()
    return nc


def make_consts(S=SEQ):
    c = {"c_ident": np.eye(128, dtype=np.float32)}
    bf = ml_dtypes.bfloat16
    T = S // 128
    NC = (S - 32) // 16 + 1
    NS = S // 64
    p_ = np.arange(128)
    cc = np.arange(4096)
    c["c_gneg"] = np.where(cc[None, :] - 16 * p_[:, None] >= 31, 0.0, -BIG).astype(bf)
    ex = np.zeros((128, T, 128), np.float32)
    for kb in range(T):
        for pp in range(128):
            j = 2 * kb + pp // 64
            if j < 64:
                ex[j, kb, pp] = 1.0
                ex[64 + j, kb, pp] = 1.0
    c["c_ex2"] = ex.reshape(128, -1).astype(bf)
    c["c_low01"] = (p_[None, :] < p_[:, None]).astype(np.float32).astype(bf)
    n_ = np.arange(256)
    c_start, c_end = n_ * 16, n_ * 16 + 31
    s_start = np.arange(64) * 64
    s_end = s_start + 63
    ov = ((c_start[:, None] <= s_end[None, :]) & (c_end[:, None] >= s_start[None, :]) & (n_[:, None] < NC)).astype(np.float32)
    c["c_ovl"] = ov.reshape(2, 128, 64).transpose(1, 0, 2).reshape(128, 128).astype(bf)
    nsl = np.array([2.0 ** (-8.0 * (i + 1) / 8) for i in range(8)], np.float64)[0::2]
    ac = np.zeros((128, 4, 2, 32), np.float64)
    for h in range(4):
        for nb in range(2):
            for qi in range(32):
                ac[:, h, nb, qi] = nsl[h] * (16.0 * (128 * nb + p_) + 15.5 - 128.0 * qi)
    c["c_alibi_cmp"] = ac.reshape(128, -1).astype(np.float32)
    ta = np.zeros((128, T, 64), np.float32)
    jb = np.arange(64)
    for t in range(T):
        tq = t * 128 + p_
        cur = tq // 64
        forced = (jb[None, :] == 0) | (jb[None, :] == cur[:, None]) | (jb[None, :] == cur[:, None] - 1)
        valid = (jb[None, :] * 64 <= tq[:, None]) & (jb[None, :] < NS)
        ta[:, t, :] = np.where(forced, 1000.0, np.where(valid, 0.0, -1000.0))
    c["c_topk_add"] = ta.reshape(128, -1)
    H, C, dh = 4, 128, 64
    scale = dh ** -0.5
    log_g = np.log(1.0 - 2.0 ** (-5.0 - np.arange(H, dtype=np.float64)))
    pos = np.arange(C, dtype=np.float64)
    rel = pos[None, :] - pos[:, None]
    decT = np.zeros((128, 4 * 128), np.float64)
    xiT = np.zeros((128, 2 * 128), np.float64)
    zeta = np.zeros((128, 256), np.float64)
    cd = np.zeros((128, 4), np.float64)
    for h in range(H):
        ai = (h % 2) * 2 + h // 2
        decT[:, ai * 128:(ai + 1) * 128] = np.where(rel >= 0, np.exp(log_g[h] * np.maximum(rel, 0.0)), 0.0) * scale
        hp, hr = (h % 2) * 64, h // 2
        xiT[hp:hp + 64, hr * 128:(hr + 1) * 128] = (np.exp(log_g[h] * (pos + 1.0)) * scale)[None, :]
        zeta[:, h * 64:(h + 1) * 64] = np.exp(log_g[h] * (C - 1.0 - pos))[:, None]
        cd[:, h] = np.exp(log_g[h] * C)
    c["c_ret_decT"] = decT.astype(np.float32)
    c["c_ret_xiT"] = xiT.astype(np.float32)
    c["c_ret_zeta"] = zeta.astype(np.float32)
    c["c_ret_cd"] = cd.astype(np.float32)
    c["c_triu"] = np.triu(np.ones((128, 128), np.float32))
    c["c_tri01"] = np.triu(np.ones((128, 128), np.float32))
    slopes = np.array([2.0 ** (-8.0 * (i + 1) / 8) for i in range(8)], np.float64)
    al = np.zeros((128, 2, 4, 35), np.float64)
    p = np.arange(128, dtype=np.float64)
    for s, sl in enumerate((slopes[0::2], slopes[1::2])):
        for h in range(4):
            for dd in range(-31, 4):
                al[:, s, h, dd + 31] = sl[h] * (128.0 * dd + p)
    c["c_alibi"] = al.reshape(128, -1).astype(np.float32)
    return c


def layout_params(inputs):
    L = inputs["w_in"].shape[0]
    cw = np.asarray(inputs["ssm_conv_w"])
    conv_wT = np.ascontiguousarray(cw.reshape(L, 4, 6, 128).transpose(0, 3, 2, 1).reshape(L, 128, 24))
    conv_bT = np.ascontiguousarray(np.asarray(inputs["ssm_conv_b"]).reshape(L, 6, 128).transpose(0, 2, 1))
    ssm_vec = np.ascontiguousarray(np.concatenate([np.asarray(inputs["ssm_dt_bias"]), np.asarray(inputs["ssm_A_log"]),
                                                   np.asarray(inputs["ssm_D"])], axis=1))
    return {"conv_wT": conv_wT.astype(np.float32), "conv_bT": conv_bT.astype(np.float32),
            "ssm_vec": ssm_vec.astype(np.float32),
            "nsa_peT": np.ascontiguousarray(np.concatenate(
                [np.asarray(inputs["nsa_pe_k"]).transpose(0, 2, 1), np.asarray(inputs["nsa_pe_v"]).transpose(0, 2, 1)],
                axis=1)).astype(np.float32),
            "diff_vec": np.ascontiguousarray(np.concatenate(
                [np.asarray(inputs[k]) for k in ("diff_lam_q1", "diff_lam_k1", "diff_lam_q2", "diff_lam_k2",
                                                 "diff_subln_w")], axis=1)).astype(np.float32),
            "ret_gn_w": np.ascontiguousarray(inputs["ret_gn_w"]).astype(np.float32),
            "ssm_norm_w": np.ascontiguousarray(inputs["ssm_norm_w"]).astype(np.float32)}


def make_inmap(inputs, b):
    m = {
        "x": np.ascontiguousarray(inputs["x"][b]).astype(np.float32),
        "norm_w": np.ascontiguousarray(inputs["norm_w"]).astype(np.float32),
        "w_in": np.ascontiguousarray(inputs["w_in"]).astype(np.float32),
        "w_out": np.ascontiguousarray(inputs["w_out"]).astype(np.float32),
        "final_norm_w": np.ascontiguousarray(inputs["final_norm_w"]).reshape(1, -1).astype(np.float32),
    }
    m.update(make_consts(S=m["x"].shape[0]))
    m.update(layout_params(inputs))
    for kk in ("nsa_w_ck1", "nsa_w_cv1", "nsa_w_ck2", "nsa_w_cv2"):
        m[kk] = np.ascontiguousarray(inputs[kk]).astype(np.float32)
    return m


_CACHE = {}


def kernel(**inputs):
    S = inputs["x"].shape[1]
    B = inputs["x"].shape[0]
    L = inputs["w_in"].shape[0]
    key = (S, L)
    if key not in _CACHE:
        _CACHE[key] = build_program(S=S, L=L, phases=("A", "SSD", "RET", "DIFF", "NSA", "F"))
    nc = _CACHE[key]
    in_maps = [make_inmap(inputs, b) for b in range(B)]
    res = run_bass_kernel_spmd(nc, in_maps, core_ids=list(range(B)))
    return np.stack([r["y"] for r in res.results], axis=0).astype(np.float32)
```

```python
import math
from contextlib import ExitStack

import numpy as np
import ml_dtypes

import concourse.bass as bass
import concourse.mybir as mybir
from concourse.bass_utils import run_bass_kernel_spmd

F32 = mybir.dt.float32
BF16 = mybir.dt.bfloat16
I32 = mybir.dt.int32
AF = mybir.ActivationFunctionType
ALU = mybir.AluOpType
AX = mybir.AxisListType

D_MODEL = 1024
DEPTH = 4
SEQ = 4096
IN_W = 3984
EPS = 1e-6
BIG = 30000.0
STORE_Q = "pool"
NFM = 18
FMW = NFM * 128
TMW = 1936
TM_VSLC, TM_VWIN, TM_GATE, TM_NSAZ, TM_DV, TM_DZ, TM_RK, TM_RV, TM_RZ, TM_SZ, TM_DT = (
    0, 64, 128, 140, 396, 652, 908, 1164, 1420, 1676, 1932)
FM_NQ, FM_KVC, FM_KSW, FM_DQ, FM_DK, FM_RQ, FM_RK, FM_XBC = 0, 2, 3, 4, 6, 8, 10, 12

W_PIECES = [
    (0, 0, 256),
    (256, 256, 128),
    (384, 384, 64),
    (448, 512, 64),
    (512, 908, 512),
    (1024, 1932, 512),
    (1536, 3212, 768),
    (FMW + 0, 448, 64),
    (FMW + 64, 576, 332),
    (FMW + 396, 1420, 512),
    (FMW + 908, 2188, 1024),
    (FMW + 1932, 3980, 4),
]
WSBW = FMW + TMW


class Prog:
    ENGS = ("pe", "act", "dve", "pool", "sp")
    DMA_RING = 8

    def __init__(self, nc):
        self.nc = nc
        self.q = {e: [] for e in self.ENGS}
        self.buf = {}
        self.seen_e = {e: {} for e in self.ENGS}
        self.seen_d = {e: {} for e in self.ENGS}
        self.ring_uses = {e: [0] * self.DMA_RING for e in self.ENGS}
        self.ring_next = {e: 0 for e in self.ENGS}
        self.ring_last = {e: [None] * self.DMA_RING for e in self.ENGS}
        self.all_dma = []

    def _bs(self, k):
        s = self.buf.get(k)
        if s is None:
            s = {"w": None, "r_e": {}, "r_d": []}
            self.buf[k] = s
        return s

    def _need(self, eng, tok, waits):
        if tok is None:
            return
        if tok[0] == "e":
            _, pe, idx = tok
            if pe == eng and eng in ("pe", "sp"):
                return
            if self.seen_e[eng].get(pe, -1) >= idx:
                return
            self.seen_e[eng][pe] = idx
            self.q[pe][idx]["flag"] = True
            waits.append(tok)
        else:
            _, sid, val = tok
            if self.seen_d[eng].get(sid, 0) >= val:
                return
            self.seen_d[eng][sid] = val
            waits.append(tok)

    def _deps(self, eng, reads, writes):
        waits = []
        for r in reads:
            self._need(eng, self._bs(r)["w"], waits)
        for w in writes:
            s = self._bs(w)
            self._need(eng, s["w"], waits)
            for pe, idx in s["r_e"].items():
                self._need(eng, ("e", pe, idx), waits)
            for t in s["r_d"]:
                self._need(eng, t, waits)
        return waits

    def _commit(self, tok, reads, writes):
        for r in reads:
            s = self._bs(r)
            if tok[0] == "e":
                s["r_e"][tok[1]] = tok[2]
            else:
                s["r_d"].append(tok)
        for w in writes:
            s = self._bs(w)
            s["w"] = tok
            s["r_e"] = {}
            s["r_d"] = []

    def op(self, eng, fn, reads=(), writes=()):
        if eng != "pe":
            extra = [r for r in reads if r.startswith("ps") and r not in writes]
            if extra:
                writes = list(writes) + extra
        waits = self._deps(eng, reads, writes)
        idx = len(self.q[eng])
        self.q[eng].append({"fn": fn, "waits": waits, "flag": False, "dma": None})
        self._commit(("e", eng, idx), reads, writes)

    def dma(self, eng, out, in_, reads=(), writes=()):
        waits = self._deps(eng, reads, writes)
        slot = self.ring_next[eng]
        self.ring_next[eng] = (slot + 1) % self.DMA_RING
        self._need(eng, self.ring_last[eng][slot], waits)
        self.ring_uses[eng][slot] += 1
        tok = ("d", (eng, slot), 16 * self.ring_uses[eng][slot])
        self.ring_last[eng][slot] = tok
        self.all_dma.append(tok)
        self.q[eng].append({"fn": lambda e: e.dma_start(out=out, in_=in_), "waits": waits,
                            "flag": False, "dma": (eng, slot)})
        self._commit(tok, reads, writes)

    def barrier(self):
        lasts = {e: len(self.q[e]) - 1 for e in self.ENGS}
        dmas = []
        for e in self.ENGS:
            dmas += [t for t in self.ring_last[e] if t is not None]
        for e in self.ENGS:
            waits = []
            for pe, idx in lasts.items():
                if pe == e:
                    continue
                j = idx
                while j >= 0 and (self.q[pe][j]["dma"] is not None or self.q[pe][j]["fn"] is None):
                    j -= 1
                if j >= 0:
                    self._need(e, ("e", pe, j), waits)
            for t in dmas:
                self._need(e, t, waits)
            self.q[e].append({"fn": None, "waits": waits, "flag": False, "dma": None})
        self.buf = {}

    def emit(self, es):
        nc = self.nc
        esem = {e: es.enter_context(nc.semaphore("s_" + e)) for e in self.ENGS}
        dsem = {}
        for e in self.ENGS:
            for s in range(self.DMA_RING):
                if self.ring_uses[e][s]:
                    dsem[(e, s)] = es.enter_context(nc.semaphore("d_%s%d" % (e, s)))
        cnt = {}
        for e in self.ENGS:
            c = 0
            arr = []
            for r in self.q[e]:
                if r["flag"]:
                    c += 1
                arr.append(c)
            cnt[e] = arr
        handles = {"pe": "tensor", "act": "scalar", "dve": "vector", "pool": "gpsimd", "sp": "sync"}
        block = es.enter_context(nc.Block())

        def run(ename, e):
            for r in self.q[ename]:
                for t in r["waits"]:
                    if t[0] == "e":
                        e.wait_ge(esem[t[1]], cnt[t[1]][t[2]])
                    else:
                        e.wait_ge(dsem[t[1]], t[2])
                if r["fn"] is None:
                    continue
                ins = r["fn"](e)
                if r["dma"] is not None:
                    ins.then_inc(dsem[r["dma"]], 16)
                elif r["flag"]:
                    ins.then_inc(esem[ename], 1)

        for ename in self.ENGS:
            getattr(block, handles[ename])(lambda e, ename=ename: run(ename, e))


class Arena:
    def __init__(self, t, width):
        self.t = t
        self.width = width
        self.off = 0
        self.marks = []

    def alloc(self, n, shape=None):
        assert self.off + n <= self.width, ("arena overflow", self.off, n, self.width)
        ap = self.t[:, self.off:self.off + n]
        self.off += n
        if shape is not None:
            names = "abcdef"[:len(shape)]
            ap = ap.rearrange("p (%s) -> p %s" % (" ".join(names), " ".join(names)),
                              **{nm: s for nm, s in zip(names, shape)})
        return ap

    def mark(self):
        self.marks.append(self.off)

    def release(self):
        self.off = self.marks.pop()


class Stream:
    def __init__(self):
        self.ops = []

    def op(self, eng, fn, reads=(), writes=()):
        self.ops.append(("op", eng, fn, tuple(reads), tuple(writes)))

    def dma(self, eng, out, in_, reads=(), writes=()):
        self.ops.append(("dma", eng, out, in_, tuple(reads), tuple(writes)))


def merge_streams(P, streams):
    idx = [0] * len(streams)
    while True:
        best, bf = -1, 2.0
        pending = False
        for i, s in enumerate(streams):
            if idx[i] < len(s.ops):
                pending = True
                o = s.ops[idx[i]]
                rds = o[3] if o[0] == "op" else o[4]
                if any(r.startswith("mixR") and (r not in P.buf or P.buf[r]["w"] is None) for r in rds) and len(streams) > 1:
                    continue
                f = idx[i] / len(s.ops)
                if f < bf:
                    best, bf = i, f
        if best < 0:
            assert not pending, "merge deadlock"
            break
        o = streams[best].ops[idx[best]]
        idx[best] += 1
        if o[0] == "op":
            P.op(o[1], o[2], reads=o[3], writes=o[4])
        else:
            P.dma(o[1], o[2], o[3], reads=o[4], writes=o[5])


def build_program(S=SEQ, L=DEPTH, debug=False, phases=("A", "F"), final_norm=True, stop=99):
    T = S // 128
    NSB = S // 512
    nc = bass.Bass("TRN2", target_bir_lowering=False)
    es = ExitStack()
    dkind = "ExternalOutput" if debug else "Internal"

    def din(name, shape, dt=F32):
        return nc.dram_tensor(name, list(shape), dt, kind="ExternalInput").ap()

    x_in = din("x", [S, D_MODEL])
    norm_w = din("norm_w", [L, D_MODEL])
    w_in = din("w_in", [L, D_MODEL, IN_W])
    w_out = din("w_out", [L, D_MODEL, D_MODEL])
    final_norm_w = din("final_norm_w", [1, D_MODEL])
    ident_d = din("c_ident", [128, 128])
    ret_gn_w = din("ret_gn_w", [L, 256])
    ssm_norm_w = din("ssm_norm_w", [L, 256])
    conv_wT = din("conv_wT", [L, 128, 24])
    conv_bT = din("conv_bT", [L, 128, 6])
    ssm_vec = din("ssm_vec", [L, 12])
    c_ret_decT = din("c_ret_decT", [128, 512])
    c_ret_xiT = din("c_ret_xiT", [128, 256])
    c_ret_zeta = din("c_ret_zeta", [128, 256])
    c_ret_cd = din("c_ret_cd", [128, 4])
    c_triu = din("c_triu", [128, 128])
    mixed_d = nc.dram_tensor("mixed_d", [S, D_MODEL], BF16, kind=dkind).ap()
    NC = (S - 32) // 16 + 1
    NBK = (NC + 127) // 128
    NS = S // 64
    nsa_peT = din("nsa_peT", [L, 128, 32])
    w_ck1 = din("nsa_w_ck1", [L, 2048, 256])
    w_cv1 = din("nsa_w_cv1", [L, 2048, 256])
    w_ck2 = din("nsa_w_ck2", [L, 256, 64])
    w_cv2 = din("nsa_w_cv2", [L, 256, 64])
    c_gneg = din("c_gneg", [128, 4096], BF16)
    c_ex2 = din("c_ex2", [128, T * 128], BF16)
    c_low01 = din("c_low01", [128, 128], BF16)
    c_ovl = din("c_ovl", [128, 2 * 64], BF16)
    c_alibi_cmp = din("c_alibi_cmp", [128, 4 * 2 * 32])
    c_topk_add = din("c_topk_add", [128, T * 64])
    diff_vec = din("diff_vec", [L, 192])
    c_alibi = din("c_alibi", [128, 2 * 4 * 35])
    c_tri01 = din("c_tri01", [128, 128])
    y_out = nc.dram_tensor("y", [S, D_MODEL], F32, kind="ExternalOutput").ap()
    xres = nc.dram_tensor("xres", [S, D_MODEL], F32, kind=dkind).ap()
    fm_d = nc.dram_tensor("fm", [FMW, S], BF16, kind=dkind).ap()
    tm_d = nc.dram_tensor("tm", [S, TMW], BF16, kind=dkind).ap()

    A32W = 14000
    A16W = 66000
    a32 = Arena(es.enter_context(nc.sbuf_tensor("a32", [128, A32W], F32)), A32W)
    a16 = Arena(es.enter_context(nc.sbuf_tensor("a16", [128, A16W], BF16)), A16W)
    pbig = es.enter_context(nc.psum_tensor("pbig", [128, 8 * 512], F32))
    psb = [pbig[:, i * 512:(i + 1) * 512] for i in range(8)]

    P = Prog(nc)

    ident_f = a32.alloc(128)
    ident_b = a16.alloc(128)
    small = a32.alloc(T * 16, (T, 16))
    P.dma("sp", ident_f, ident_d, writes=["ident_f"])
    P.op("dve", lambda e: e.tensor_copy(out=ident_b, in_=ident_f), reads=["ident_f"], writes=["ident_b"])

    cp_rr = [0]

    def evac(out, in_, reads, writes, force=None):
        cp_rr[0] ^= 1
        if force == "act" or (force is None and cp_rr[0]):
            P.op("act", lambda e: e.copy(out=out, in_=in_), reads=reads, writes=writes)
        else:
            P.op("dve", lambda e: e.tensor_copy(out=out, in_=in_), reads=reads, writes=writes)

    def rms_rstd(xt, ss, junk, rstd, key):
        P.op("dve", lambda e: e.scalar_tensor_tensor(out=junk, in0=xt, scalar=1.0, in1=xt,
                                                     op0=ALU.mult, op1=ALU.mult, accum_out=ss),
             reads=[key + "x"], writes=[key + "junk", key + "ss"])
        P.op("dve", lambda e: e.tensor_scalar(out=ss, in0=ss, scalar1=1.0 / D_MODEL, scalar2=EPS,
                                              op0=ALU.mult, op1=ALU.add),
             reads=[key + "ss"], writes=[key + "ss"])
        P.op("act", lambda e: e.activation(out=ss, in_=ss, func=AF.Sqrt), reads=[key + "ss"], writes=[key + "ss"])
        P.op("dve", lambda e: e.reciprocal(out=rstd, in_=ss), reads=[key + "ss"], writes=[key + "rstd"])

    def phase_A(l):
        x_src = x_in if l == 0 else xres
        a32.mark()
        a16.mark()
        w_sb = a16.alloc(8 * WSBW, (8, WSBW))
        stage = [a32.alloc(2048) for _ in range(4)]
        nwb = a32.alloc(D_MODEL)
        xts = [a32.alloc(D_MODEL) for _ in range(2)]
        junk = a16.alloc(D_MODEL)
        sss = [a32.alloc(1) for _ in range(2)]
        rstds = [a32.alloc(1) for _ in range(2)]
        hbs = [a16.alloc(D_MODEL) for _ in range(2)]
        hTs = [a16.alloc(8 * 512, (8, 512)) for _ in range(2)]
        tmts = [a16.alloc(TMW) for _ in range(2)]
        fmts = [a16.alloc(512) for _ in range(4)]

        P.dma("sp", nwb, norm_w[l:l + 1, :].partition_broadcast(128), writes=["nwb"])
        pieces = []
        for (dc, sc, wd) in W_PIECES:
            if sc < 2048 < sc + wd:
                pieces.append((dc, sc, 2048 - sc))
                pieces.append((dc + 2048 - sc, 2048, wd - (2048 - sc)))
            else:
                pieces.append((dc, sc, wd))
        for kc in range(8):
            sp_ = (kc % 2) * 2
            P.dma("sp", stage[sp_], w_in[l, kc * 128:(kc + 1) * 128, 0:2048], writes=["stage%d" % sp_])
            P.dma("sp", stage[sp_ + 1][:, 0:IN_W - 2048], w_in[l, kc * 128:(kc + 1) * 128, 2048:IN_W],
                  writes=["stage%d" % (sp_ + 1)])
            for pi_, (dc, sc, wd) in enumerate(pieces):
                hf = 0 if sc < 2048 else 1
                si = sp_ + hf
                ce = ("pool", "dve", "act")[(pi_ + kc) % 3]
                if ce == "act":
                    P.op("act", lambda e, kc=kc, si=si, hf=hf, dc=dc, sc=sc, wd=wd:
                         e.copy(out=w_sb[:, kc, dc:dc + wd], in_=stage[si][:, sc - 2048 * hf:sc - 2048 * hf + wd]),
                         reads=["stage%d" % si], writes=["w_sb%d" % (pi_ % 4)])
                else:
                    P.op(ce, lambda e, kc=kc, si=si, hf=hf, dc=dc, sc=sc, wd=wd:
                         e.tensor_copy(out=w_sb[:, kc, dc:dc + wd], in_=stage[si][:, sc - 2048 * hf:sc - 2048 * hf + wd]),
                         reads=["stage%d" % si], writes=["w_sb%d" % (pi_ % 4)])

        fm_i = 0
        if stop <= 1:
            P.barrier(); a32.release(); a16.release(); return
        def a_stage1(t):
            j = t % 2
            xt, ss, rstd, hb = xts[j], sss[j], rstds[j], hbs[j]
            k = "A%d_" % j
            P.dma("sp", xt, x_src[t * 128:(t + 1) * 128, :],
                  reads=["xres_t%d" % t], writes=[k + "x"])
            rms_rstd(xt, ss, junk, rstd, k)
            P.op("dve", lambda e, xt=xt, rstd=rstd, hb=hb:
                 e.scalar_tensor_tensor(out=hb, in0=xt, scalar=rstd, in1=nwb, op0=ALU.mult, op1=ALU.mult),
                 reads=[k + "x", k + "rstd", "nwb"], writes=[k + "hb"])

        a_stage1(0)
        for sb in range(NSB):
            hT = hTs[sb % 2]
            hk = "hT%d" % (sb % 2)
            for ti in range(4):
                t = sb * 4 + ti
                j = t % 2
                xt, ss, rstd, hb, tmt = xts[j], sss[j], rstds[j], hbs[j], tmts[j]
                k = "A%d_" % j
                pst = psb[j]
                pstb = pst[:, 0:512].bitcast(BF16)
                for kc in range(8):
                    P.op("pe", lambda e, kc=kc, hb=hb, pstb=pstb:
                         e.transpose(out=pstb[:, kc * 128:(kc + 1) * 128], in_=hb[:, kc * 128:(kc + 1) * 128],
                                     identity=ident_b),
                         reads=[k + "hb", "ident_b"], writes=["ps%d" % j])
                evac(hT[:, :, ti * 128:(ti + 1) * 128], pstb.rearrange("p (k t) -> p k t", k=8),
                     reads=["ps%d" % j], writes=[hk])
                if t + 1 < T:
                    a_stage1(t + 1)
                if stop <= 2:
                    continue
                for c in range(4):
                    c0 = c * 512
                    n = min(512, TMW - c0)
                    pi = 2 + (t * 4 + c) % 3
                    ps = psb[pi]
                    for kc in range(8):
                        P.op("pe", lambda e, kc=kc, ps=ps, hT=hT, ti=ti, c0=c0, n=n:
                             e.matmul(ps[:, 0:n], lhsT=hT[:, kc, ti * 128:(ti + 1) * 128],
                                      rhs=w_sb[:, kc, FMW + c0:FMW + c0 + n], start=(kc == 0), stop=(kc == 7)),
                             reads=[hk, "w_sb0", "w_sb1", "w_sb2", "w_sb3"], writes=["ps%d" % pi])
                    evac(tmt[:, c0:c0 + n], ps[:, 0:n], reads=["ps%d" % pi], writes=[k + "tm"],
                         force=("act" if c in (0, 3) else None))
                    if c == 0 and stop != 31:
                        P.op("act", lambda e, ps=ps, t=t:
                             e.copy(out=small[:, t, 0:12], in_=ps[:, TM_GATE:TM_GATE + 12]),
                             reads=["ps%d" % pi], writes=["small"])
                    if c == 3 and stop != 31:
                        P.op("act", lambda e, ps=ps, t=t, c0=c0:
                             e.copy(out=small[:, t, 12:16], in_=ps[:, TM_DT - c0:TM_DT - c0 + 4]),
                             reads=["ps%d" % pi], writes=["small"])
                if stop != 32:
                    P.dma(STORE_Q, tm_d[t * 128:(t + 1) * 128, :], tmt, reads=[k + "tm"], writes=["tm_t%d" % t])
            for r in range(NFM if stop > 3 else 0):
                pi = 5 + r % 3
                ps = psb[pi]
                for kc in range(8):
                    P.op("pe", lambda e, kc=kc, ps=ps, hT=hT, r=r:
                         e.matmul(ps[:, :], lhsT=w_sb[:, kc, r * 128:(r + 1) * 128], rhs=hT[:, kc, :],
                                  start=(kc == 0), stop=(kc == 7)),
                         reads=[hk, "w_sb0", "w_sb1", "w_sb2", "w_sb3"], writes=["ps%d" % pi])
                fmt = fmts[fm_i % 4]
                fk = "fmt%d" % (fm_i % 4)
                fm_i += 1
                evac(fmt, ps[:, :], reads=["ps%d" % pi], writes=[fk])
                P.dma(STORE_Q, fm_d[r * 128:(r + 1) * 128, sb * 512:(sb + 1) * 512], fmt,
                      reads=[fk], writes=["fm_r%d" % r])
        P.barrier()
        a32.release()
        a16.release()

    def load_bcast(dst, src_row, key):
        P.dma("sp", dst, src_row.partition_broadcast(128), writes=[key])

    def head_norm_stats(o_sb, H, Dh, s1, s2, sq, key):
        o3 = o_sb.rearrange("p (h e) -> p h e", h=H)
        P.op("dve", lambda e: e.tensor_reduce(out=s1, in_=o3, axis=AX.X, op=ALU.add),
             reads=[key + "o"], writes=[key + "s1"])
        P.op("dve", lambda e: e.tensor_tensor(out=sq, in0=o_sb, in1=o_sb, op=ALU.mult),
             reads=[key + "o"], writes=[key + "sq"])
        P.op("dve", lambda e: e.tensor_reduce(out=s2, in_=sq.rearrange("p (h e) -> p h e", h=H), axis=AX.X, op=ALU.add),
             reads=[key + "sq"], writes=[key + "s2"])

    def phase_ret(l):
        decT = a32.alloc(512)
        xiT = a32.alloc(256, (2, 128))
        zeta = a32.alloc(256)
        cdt = a32.alloc(4)
        gnw = a32.alloc(256)
        state = a32.alloc(128, (2, 64))
        state_b = a16.alloc(128, (2, 64))
        P.dma("sp", decT, c_ret_decT, writes=["decT"])
        P.dma("sp", xiT, c_ret_xiT.rearrange("p (r q) -> p r q", r=2), writes=["xiT"])
        P.dma("sp", zeta, c_ret_zeta, writes=["zeta"])
        P.dma("sp", cdt, c_ret_cd, writes=["cdt"])
        load_bcast(gnw, ret_gn_w[l:l + 1, :], "gnw")
        P.op("dve", lambda e: e.memset(state, 0.0), writes=["state"])
        P.op("dve", lambda e: e.memset(state_b, 0.0), writes=["state_b"])
        NB = 2
        qTs = [a16.alloc(256, (2, 128)) for _ in range(NB)]
        kTs = [a16.alloc(256, (2, 128)) for _ in range(NB)]
        tms = [a16.alloc(768) for _ in range(NB)]
        ATs = [a16.alloc(512) for _ in range(NB)]
        qxs = [a16.alloc(256, (2, 128)) for _ in range(NB)]
        kzs = [a16.alloc(256) for _ in range(NB)]
        osb = [a32.alloc(256) for _ in range(NB)]
        sqs = [a32.alloc(256) for _ in range(NB)]
        szs = [a32.alloc(256) for _ in range(NB)]
        st1 = [a32.alloc(4) for _ in range(NB)]
        st2 = [a32.alloc(4) for _ in range(NB)]
        mos = [a16.alloc(256) for _ in range(NB)]
        for t in range(T):
            j = t % NB
            k = "R%d_" % j
            qT, kT, tm, AT, qx, kz, o_sb, sq, sz, s1, s2 = (qTs[j], kTs[j], tms[j], ATs[j], qxs[j], kzs[j],
                                                          osb[j], sqs[j], szs[j], st1[j], st2[j])
            c0 = t * 128
            P.dma("sp", qT, fm_d[FM_RQ * 128:(FM_RQ + 2) * 128, c0:c0 + 128].rearrange("(r p) s -> p r s", p=128),
                  writes=[k + "qT"])
            P.dma("sp", kT, fm_d[FM_RK * 128:(FM_RK + 2) * 128, c0:c0 + 128].rearrange("(r p) s -> p r s", p=128),
                  writes=[k + "kT"])
            P.dma("sp", tm, tm_d[c0:c0 + 128, TM_RK:TM_RK + 768], writes=[k + "tm"])
            ps_sg = [psb[6][:, 0:256], psb[7][:, 0:256]]
            ps_xg = [psb[6][:, 256:384], psb[7][:, 256:384]]
            ps_o, ps_kv = psb[6][:, 0:256], psb[7][:, 0:256]
            ks_o, ks_kv = "ps6", "ps7"
            yield
            for h in range(4):
                hp, hr, g = (h % 2) * 64, h // 2, h % 2
                P.op("pe", lambda e, hp=hp, hr=hr, g=g, kT=kT, qT=qT:
                     e.matmul(ps_sg[g][:, hr * 128:(hr + 1) * 128], lhsT=kT[hp:hp + 64, hr, :], rhs=qT[hp:hp + 64, hr, :],
                              start=True, stop=True),
                     reads=[k + "qT", k + "kT"], writes=["ps%d" % (6 + g)])
            yield
            for g in range(2):
                P.op("dve", lambda e, g=g, AT=AT: e.tensor_tensor(out=AT[:, g * 256:(g + 1) * 256], in0=ps_sg[g][:, 0:256],
                                                                  in1=decT[:, g * 256:(g + 1) * 256], op=ALU.mult),
                     reads=["ps%d" % (6 + g), "decT"], writes=[k + "AT"])
            P.op("dve", lambda e, qx=qx, qT=qT: e.tensor_tensor(out=qx, in0=qT, in1=xiT, op=ALU.mult),
                 reads=[k + "qT", "xiT"], writes=[k + "qx"])
            P.op("dve", lambda e, kz=kz, tm=tm: e.tensor_tensor(out=kz, in0=tm[:, 0:256], in1=zeta, op=ALU.mult),
                 reads=[k + "tm", "zeta"], writes=[k + "kz"])
            for h in range(4):
                hp, hr, g = (h % 2) * 64, h // 2, h % 2
                ai = g * 2 + hr
                P.op("pe", lambda e, h=h, ai=ai, ps_o=ps_o, AT=AT, tm=tm:
                     e.matmul(ps_o[:, h * 64:(h + 1) * 64], lhsT=AT[:, ai * 128:(ai + 1) * 128],
                              rhs=tm[:, 256 + h * 64:256 + (h + 1) * 64], start=True, stop=True),
                     reads=[k + "AT", k + "tm"], writes=[ks_o])
                if t > 0:
                    P.op("pe", lambda e, hp=hp, hr=hr, g=g, qx=qx:
                         e.matmul(ps_xg[g][:, hr * 64:(hr + 1) * 64], lhsT=qx[hp:hp + 64, hr, :],
                                  rhs=state_b[hp:hp + 64, hr, :], start=True, stop=True),
                         reads=[k + "qx", "state_b"], writes=["ps%d" % (6 + g)])
            yield
            if t < T - 1:
                for r in range(2):
                    P.op("pe", lambda e, r=r, ps_kv=ps_kv, kz=kz, tm=tm:
                         e.matmul(ps_kv[:, r * 128:(r + 1) * 128], lhsT=kz[:, r * 128:(r + 1) * 128],
                                  rhs=tm[:, 256 + r * 128:256 + (r + 1) * 128], start=True, stop=True),
                         reads=[k + "kz", k + "tm"], writes=[ks_kv])
                for h in range(4):
                    hp, hr = (h % 2) * 64, h // 2
                    P.op("dve", lambda e, h=h, hp=hp, hr=hr, ps_kv=ps_kv:
                         e.scalar_tensor_tensor(out=state[hp:hp + 64, hr, :], in0=state[hp:hp + 64, hr, :],
                                                scalar=cdt[hp:hp + 64, h:h + 1],
                                                in1=ps_kv[hp:hp + 64, hr * 128 + (h % 2) * 64:hr * 128 + (h % 2) * 64 + 64],
                                                op0=ALU.mult, op1=ALU.add),
                         reads=[ks_kv, "cdt", "state"], writes=["state"])
                P.op("dve", lambda e: e.tensor_copy(out=state_b, in_=state), reads=["state"], writes=["state_b"])
            yield
            P.op("act", lambda e, o_sb=o_sb, ps_o=ps_o: e.copy(out=o_sb, in_=ps_o[:, 0:256]),
                 reads=[ks_o], writes=[k + "o"])
            if t > 0:
                for h in range(4):
                    hr, g = h // 2, h % 2
                    P.op("dve", lambda e, h=h, hr=hr, g=g, o_sb=o_sb:
                         e.tensor_tensor(out=o_sb[:, h * 64:(h + 1) * 64], in0=ps_xg[g][:, hr * 64:(hr + 1) * 64],
                                         in1=o_sb[:, h * 64:(h + 1) * 64], op=ALU.add),
                         reads=["ps%d" % (6 + g), k + "o"], writes=[k + "o"])
            P.op("act", lambda e, sz=sz, tm=tm: e.activation(out=sz, in_=tm[:, 512:768], func=AF.Silu),
                 reads=[k + "tm"], writes=[k + "sz"])
            head_norm_stats(o_sb, 4, 64, s1, s2, sq, k)
            P.op("dve", lambda e, s1=s1: e.tensor_scalar(out=s1, in0=s1, scalar1=1.0 / 64, scalar2=None, op0=ALU.mult),
                 reads=[k + "s1"], writes=[k + "s1"])
            P.op("dve", lambda e, s1=s1, s2=s2, sq=sq: e.tensor_tensor(out=sq[:, 0:4], in0=s1, in1=s1, op=ALU.mult),
                 reads=[k + "s1"], writes=[k + "sq"])
            P.op("dve", lambda e, s2=s2, sq=sq: e.scalar_tensor_tensor(out=s2, in0=s2, scalar=1.0 / 64, in1=sq[:, 0:4],
                                                                       op0=ALU.mult, op1=ALU.subtract),
                 reads=[k + "s2", k + "sq"], writes=[k + "s2"])
            P.op("dve", lambda e, s2=s2: e.tensor_scalar(out=s2, in0=s2, scalar1=EPS, scalar2=None, op0=ALU.add),
                 reads=[k + "s2"], writes=[k + "s2"])
            P.op("act", lambda e, s2=s2: e.activation(out=s2, in_=s2, func=AF.Sqrt), reads=[k + "s2"], writes=[k + "s2"])
            P.op("dve", lambda e, s2=s2: e.reciprocal(out=s2, in_=s2), reads=[k + "s2"], writes=[k + "s2"])
            for h in range(4):
                P.op("dve", lambda e, h=h, o_sb=o_sb, s1=s1, s2=s2:
                     e.tensor_scalar(out=o_sb[:, h * 64:(h + 1) * 64], in0=o_sb[:, h * 64:(h + 1) * 64],
                                     scalar1=s1[:, h:h + 1], scalar2=s2[:, h:h + 1], op0=ALU.subtract, op1=ALU.mult),
                     reads=[k + "o", k + "s1", k + "s2"], writes=[k + "o"])
            P.op("dve", lambda e, sz=sz: e.tensor_tensor(out=sz, in0=sz, in1=gnw, op=ALU.mult),
                 reads=[k + "sz", "gnw"], writes=[k + "sz"])
            mo = mos[j]
            P.op("dve", lambda e, mo=mo, o_sb=o_sb, sz=sz:
                 e.tensor_tensor(out=mo, in0=o_sb, in1=sz, op=ALU.mult),
                 reads=[k + "o", k + "sz"], writes=[k + "mo"])
            P.dma(STORE_Q, mixed_d[t * 128:(t + 1) * 128, 512:768], mo, reads=[k + "mo"], writes=["mixR%d" % t])
            yield

    def phase_ssd(l):
        triu = a32.alloc(128)
        ones_f = a32.alloc(128)
        cw = a32.alloc(24)
        cb = a32.alloc(6)
        vec = a32.alloc(12)
        snw = a32.alloc(256)
        dtt = a32.alloc(T * 4)
        dAt = a32.alloc(T * 4)
        cst = a32.alloc(T * 4)
        ecs = a32.alloc(T * 4)
        dsd = a32.alloc(T * 4)
        ecl = a32.alloc(T * 4)
        Sst = a32.alloc(256, (4, 64))
        Sst_b = a16.alloc(256, (4, 64))
        maskT = a32.alloc(128)
        P.dma("sp", triu, c_triu, writes=["triu"])
        P.dma("sp", cw, conv_wT[l], writes=["cw"])
        P.dma("sp", cb, conv_bT[l], writes=["cb"])
        load_bcast(vec, ssm_vec[l:l + 1, :], "vec")
        load_bcast(snw, ssm_norm_w[l:l + 1, :], "snw")
        P.op("dve", lambda e: e.memset(ones_f, 1.0), writes=["ones_f"])
        P.op("dve", lambda e: e.memset(Sst, 0.0), writes=["Sst"])
        P.op("dve", lambda e: e.memset(Sst_b, 0.0), writes=["Sst_b"])
        P.op("dve", lambda e: e.tensor_copy(out=maskT, in_=triu), reads=["triu"], writes=["maskT"])
        sm3 = small[:, :, 12:16]
        dt3 = dtt.rearrange("p (t h) -> p t h", h=4)
        dA3 = dAt.rearrange("p (t h) -> p t h", h=4)
        for h in range(4):
            P.op("dve", lambda e, h=h: e.tensor_scalar(out=dt3[:, :, h], in0=sm3[:, :, h], scalar1=vec[:, h:h + 1],
                                                       scalar2=None, op0=ALU.add),
                 reads=["small", "vec"], writes=["dtt"])
        P.op("act", lambda e: e.activation(out=dtt, in_=dtt, func=AF.Exp), reads=["dtt"], writes=["dtt"])
        P.op("act", lambda e: e.activation(out=dtt, in_=dtt, func=AF.Ln, bias=1.0), reads=["dtt"], writes=["dtt"])
        P.op("act", lambda e: e.activation(out=vec[:, 4:8], in_=vec[:, 4:8], func=AF.Exp), reads=["vec"], writes=["vec"])
        for h in range(4):
            P.op("dve", lambda e, h=h: e.tensor_scalar(out=dA3[:, :, h], in0=dt3[:, :, h], scalar1=vec[:, 4 + h:5 + h],
                                                       scalar2=-1.0, op0=ALU.mult, op1=ALU.mult),
                 reads=["dtt", "vec"], writes=["dAt"])
        psA, psB = psb[6], psb[7]
        P.op("pe", lambda e: e.matmul(psA[:, 0:T * 4], lhsT=triu, rhs=dAt, start=True, stop=True),
             reads=["triu", "dAt"], writes=["ps6"])
        P.op("pe", lambda e: e.matmul(psB[:, 0:T * 4], lhsT=ones_f, rhs=dAt, start=True, stop=True),
             reads=["ones_f", "dAt"], writes=["ps7"])
        P.op("act", lambda e: e.copy(out=cst, in_=psA[:, 0:T * 4]), reads=["ps6"], writes=["cst"])
        P.op("act", lambda e: e.activation(out=ecs, in_=cst, func=AF.Exp), reads=["cst"], writes=["ecs"])
        P.op("act", lambda e: e.activation(out=ecl, in_=psB[:, 0:T * 4], func=AF.Exp), reads=["ps7"], writes=["ecl"])
        P.op("dve", lambda e: e.tensor_tensor(out=dsd, in0=psB[:, 0:T * 4], in1=cst, op=ALU.subtract),
             reads=["ps7", "cst"], writes=["dsd"])
        P.op("act", lambda e: e.activation(out=dsd, in_=dsd, func=AF.Exp), reads=["dsd"], writes=["dsd"])
        P.op("dve", lambda e: e.tensor_tensor(out=dsd, in0=dsd, in1=dtt, op=ALU.mult),
             reads=["dsd", "dtt"], writes=["dsd"])

        xin = [a16.alloc(6 * 516, (6, 516)) for _ in range(2)]
        acc = a32.alloc(512)
        cvo = [a16.alloc(6 * 512, (6, 512)) for _ in range(2)]
        NB = 2
        xtm = [a32.alloc(256) for _ in range(NB)]
        btm = [a16.alloc(256) for _ in range(NB)]
        xdt = [a16.alloc(256) for _ in range(NB)]
        xdd = [a16.alloc(256) for _ in range(NB)]
        cbm = [a32.alloc(256) for _ in range(NB)]
        uda = [a32.alloc(128) for _ in range(NB)]
        dif = [a32.alloc(128) for _ in range(NB)]
        MTs = [a16.alloc(512) for _ in range(NB)]
        ysb = [a32.alloc(256) for _ in range(NB)]
        zts = [a16.alloc(256) for _ in range(NB)]
        szs = [a32.alloc(256) for _ in range(NB)]
        sqs = [a32.alloc(256) for _ in range(NB)]
        sss = [a32.alloc(1) for _ in range(NB)]
        mos = [a16.alloc(256) for _ in range(NB)]
        for sb in range(NSB):
            xi = xin[sb % 2]
            co = cvo[sb % 2]
            kx = "xin%d" % (sb % 2)
            kc_ = "cvo%d" % (sb % 2)
            src = fm_d[FM_XBC * 128:(FM_XBC + 6) * 128, :].rearrange("(r p) s -> p r s", p=128)
            if sb == 0:
                P.op("pool", lambda e, xi=xi: e.memset(xi[:, :, 0:3], 0.0), writes=[kx])
                P.dma("sp", xi[:, :, 3:515], src[:, :, 0:512], writes=[kx])
            else:
                P.dma("sp", xi[:, :, 0:515], src[:, :, sb * 512 - 3:sb * 512 + 512], writes=[kx])
            for r in range(6):
                for jj in range(4):
                    if jj == 0:
                        P.op("pool", lambda e, r=r, xi=xi: e.tensor_scalar(
                            out=acc, in0=xi[:, r, 0:512], scalar1=cw[:, r * 4:r * 4 + 1], scalar2=1.0, op0=ALU.mult,
                            op1=ALU.mult),
                            reads=[kx, "cw"], writes=["acc"])
                    else:
                        P.op("dve", lambda e, r=r, jj=jj, xi=xi: e.scalar_tensor_tensor(
                            out=acc, in0=xi[:, r, jj:jj + 512], scalar=cw[:, r * 4 + jj:r * 4 + jj + 1], in1=acc,
                            op0=ALU.mult, op1=ALU.add),
                            reads=[kx, "cw", "acc"], writes=["acc"])
                P.op("act", lambda e, r=r, co=co: e.activation(out=co[:, r, :], in_=acc, func=AF.Silu, bias=cb[:, r:r + 1]),
                     reads=["acc", "cb"], writes=[kc_])
                yield
            for ti in range(4):
                t = sb * 4 + ti
                j = t % NB
                k = "S%d_" % j
                cs0 = ti * 128
                x_tm, b_tm, xd, xdd_, cbm_, ud, df, MT, y_sb, zt, sz, sq, ss = (
                    xtm[j], btm[j], xdt[j], xdd[j], cbm[j], uda[j], dif[j], MTs[j], ysb[j], zts[j], szs[j], sqs[j], sss[j])
                P.dma("sp", zt, tm_d[t * 128:(t + 1) * 128, TM_SZ:TM_SZ + 256], writes=[k + "z"])
                pTb = psb[6][:, 0:256].bitcast(BF16)
                for r in range(4):
                    P.op("pe", lambda e, r=r, pTb=pTb, co=co, cs0=cs0:
                         e.transpose(out=pTb[:, r * 128:(r + 1) * 128], in_=co[:, r, cs0:cs0 + 128], identity=ident_b),
                         reads=[kc_, "ident_b"], writes=["ps6"])
                yield
                P.op("act", lambda e, x_tm=x_tm, pTb=pTb: e.copy(out=x_tm, in_=pTb[:, 0:256]),
                     reads=["ps6"], writes=[k + "x"])
                P.op("act", lambda e, b_tm=b_tm, pTb=pTb: e.copy(out=b_tm, in_=pTb[:, 256:512]),
                     reads=["ps6"], writes=[k + "b"])
                for h in range(4):
                    P.op("dve", lambda e, h=h, xd=xd, x_tm=x_tm, t=t: e.tensor_scalar(
                        out=xd[:, h * 64:(h + 1) * 64], in0=x_tm[:, h * 64:(h + 1) * 64],
                        scalar1=dtt[:, t * 4 + h:t * 4 + h + 1], scalar2=None, op0=ALU.mult),
                        reads=[k + "x", "dtt"], writes=[k + "xd"])
                    P.op("dve", lambda e, h=h, xdd_=xdd_, x_tm=x_tm, t=t: e.tensor_scalar(
                        out=xdd_[:, h * 64:(h + 1) * 64], in0=x_tm[:, h * 64:(h + 1) * 64],
                        scalar1=dsd[:, t * 4 + h:t * 4 + h + 1], scalar2=None, op0=ALU.mult),
                        reads=[k + "x", "dsd"], writes=[k + "xdd"])
                yield
                pcb = psb[6][:, 256:512]
                kcb = "ps6"
                for g in range(2):
                    P.op("pe", lambda e, g=g, pcb=pcb, co=co, cs0=cs0:
                         e.matmul(pcb[:, g * 128:(g + 1) * 128], lhsT=co[:, 2 + g, cs0:cs0 + 128],
                                  rhs=co[:, 4 + g, cs0:cs0 + 128], start=True, stop=True),
                         reads=[kc_], writes=[kcb])
                for g in range(2):
                    P.op("dve", lambda e, g=g, cbm_=cbm_, pcb=pcb:
                         e.tensor_tensor(out=cbm_[:, g * 128:(g + 1) * 128], in0=pcb[:, g * 128:(g + 1) * 128],
                                         in1=maskT, op=ALU.mult),
                         reads=[kcb, "maskT"], writes=[k + "cbm"])
                pcs = psb[7]
                kcs = "ps7"
                for h in range(4):
                    yield
                    P.op("dve", lambda e, h=h, ud=ud, t=t: e.tensor_scalar(
                        out=ud, in0=triu, scalar1=dAt[:, t * 4 + h:t * 4 + h + 1], scalar2=None, op0=ALU.mult),
                        reads=["triu", "dAt"], writes=[k + "ud"])
                    P.op("pe", lambda e, h=h, pcs=pcs, ud=ud:
                         e.matmul(pcs[:, h * 128:(h + 1) * 128], lhsT=ones_f, rhs=ud, start=True, stop=True),
                         reads=["ones_f", k + "ud"], writes=[kcs])
                    P.op("dve", lambda e, h=h, df=df, pcs=pcs, t=t: e.tensor_scalar(
                        out=df, in0=pcs[:, h * 128:(h + 1) * 128], scalar1=cst[:, t * 4 + h:t * 4 + h + 1],
                        scalar2=0.0, op0=ALU.subtract, op1=ALU.min),
                        reads=[kcs, "cst"], writes=[k + "df"])
                    P.op("act", lambda e, df=df: e.activation(out=df, in_=df, func=AF.Exp),
                         reads=[k + "df"], writes=[k + "df"])
                    P.op("dve", lambda e, h=h, MT=MT, df=df, cbm_=cbm_: e.tensor_tensor(
                        out=MT[:, h * 128:(h + 1) * 128], in0=df, in1=cbm_[:, (h // 2) * 128:(h // 2 + 1) * 128],
                        op=ALU.mult),
                        reads=[k + "df", k + "cbm"], writes=[k + "MT"])
                yield
                py, pyo, pst = psb[6][:, 0:256], psb[6][:, 256:512], psb[7][:, 0:256]
                for h in range(4):
                    P.op("pe", lambda e, h=h, py=py, MT=MT, xd=xd:
                         e.matmul(py[:, h * 64:(h + 1) * 64], lhsT=MT[:, h * 128:(h + 1) * 128],
                                  rhs=xd[:, h * 64:(h + 1) * 64], start=True, stop=True),
                         reads=[k + "MT", k + "xd"], writes=["ps6"])
                if t > 0:
                    for h in range(4):
                        P.op("pe", lambda e, h=h, pyo=pyo, co=co, cs0=cs0:
                             e.matmul(pyo[:, h * 64:(h + 1) * 64], lhsT=co[:, 4 + h // 2, cs0:cs0 + 128],
                                      rhs=Sst_b[:, h, :], start=True, stop=True),
                             reads=[kc_, "Sst_b"], writes=["ps6"])
                yield
                P.op("act", lambda e, y_sb=y_sb, py=py: e.copy(out=y_sb, in_=py[:, 0:256]),
                     reads=["ps6"], writes=[k + "y"])
                for h in range(4):
                    if t > 0:
                        P.op("dve", lambda e, h=h, y_sb=y_sb, pyo=pyo, t=t: e.scalar_tensor_tensor(
                            out=y_sb[:, h * 64:(h + 1) * 64], in0=pyo[:, h * 64:(h + 1) * 64],
                            scalar=ecs[:, t * 4 + h:t * 4 + h + 1], in1=y_sb[:, h * 64:(h + 1) * 64],
                            op0=ALU.mult, op1=ALU.add),
                            reads=["ps6", "ecs", k + "y"], writes=[k + "y"])
                    P.op("dve", lambda e, h=h, y_sb=y_sb, x_tm=x_tm: e.scalar_tensor_tensor(
                        out=y_sb[:, h * 64:(h + 1) * 64], in0=x_tm[:, h * 64:(h + 1) * 64],
                        scalar=vec[:, 8 + h:9 + h], in1=y_sb[:, h * 64:(h + 1) * 64], op0=ALU.mult, op1=ALU.add),
                        reads=[k + "x", "vec", k + "y"], writes=[k + "y"])
                yield
                if t < T - 1:
                    kst = "ps7"
                    for h in range(4):
                        P.op("pe", lambda e, h=h, pst=pst, b_tm=b_tm, xdd_=xdd_:
                             e.matmul(pst[:, h * 64:(h + 1) * 64], lhsT=b_tm[:, (h // 2) * 128:(h // 2 + 1) * 128],
                                      rhs=xdd_[:, h * 64:(h + 1) * 64], start=True, stop=True),
                             reads=[k + "b", k + "xdd"], writes=[kst])
                    for h in range(4):
                        P.op("dve", lambda e, h=h, pst=pst, t=t: e.scalar_tensor_tensor(
                            out=Sst[:, h, :], in0=Sst[:, h, :], scalar=ecl[:, t * 4 + h:t * 4 + h + 1],
                            in1=pst[:, h * 64:(h + 1) * 64], op0=ALU.mult, op1=ALU.add),
                            reads=[kst, "ecl", "Sst"], writes=["Sst"])
                    P.op("dve", lambda e: e.tensor_copy(out=Sst_b, in_=Sst), reads=["Sst"], writes=["Sst_b"])
                yield
                P.op("act", lambda e, sz=sz, zt=zt: e.activation(out=sz, in_=zt, func=AF.Silu),
                     reads=[k + "z"], writes=[k + "sz"])
                P.op("dve", lambda e, y_sb=y_sb, sz=sz: e.tensor_tensor(out=y_sb, in0=y_sb, in1=sz, op=ALU.mult),
                     reads=[k + "y", k + "sz"], writes=[k + "y"])
                P.op("dve", lambda e, y_sb=y_sb, sq=sq, ss=ss: e.scalar_tensor_tensor(
                    out=sq, in0=y_sb, scalar=1.0, in1=y_sb, op0=ALU.mult, op1=ALU.mult, accum_out=ss),
                    reads=[k + "y"], writes=[k + "sq", k + "ss"])
                P.op("dve", lambda e, ss=ss: e.tensor_scalar(out=ss, in0=ss, scalar1=1.0 / 256, scalar2=EPS,
                                                             op0=ALU.mult, op1=ALU.add),
                     reads=[k + "ss"], writes=[k + "ss"])
                P.op("act", lambda e, ss=ss: e.activation(out=ss, in_=ss, func=AF.Sqrt), reads=[k + "ss"], writes=[k + "ss"])
                P.op("dve", lambda e, ss=ss: e.reciprocal(out=ss, in_=ss), reads=[k + "ss"], writes=[k + "ss"])
                mo = mos[j]
                P.op("dve", lambda e, mo=mo, y_sb=y_sb, ss=ss: e.scalar_tensor_tensor(
                    out=mo, in0=y_sb, scalar=ss, in1=snw, op0=ALU.mult, op1=ALU.mult),
                    reads=[k + "y", k + "ss", "snw"], writes=[k + "mo"])
                P.dma(STORE_Q, mixed_d[t * 128:(t + 1) * 128, 768:1024], mo, reads=[k + "mo"], writes=["mixS%d" % t])
                yield

    def phase_diff(l):
        scale = 32 ** -0.5
        lam_init = 0.8 - 0.6 * math.exp(-0.3 * l)
        dslopes = [2.0 ** (-8.0 * (i + 1) / 8) for i in range(8)][1::2]
        alib = a32.alloc(2 * 4 * 35, (2, 4, 35))
        tri01 = a16.alloc(128)
        tri_f = a32.alloc(128)
        dv = a32.alloc(192)
        lamt = a32.alloc(4)
        sw = a32.alloc(256)
        P.dma("sp", alib, c_alibi.rearrange("p (s h d) -> p s h d", s=2, h=4), writes=["alib"])
        P.dma("sp", tri_f, c_tri01, writes=["tri_f"])
        P.op("dve", lambda e: e.tensor_copy(out=tri01, in_=tri_f), reads=["tri_f"], writes=["tri01"])
        load_bcast(dv, diff_vec[l:l + 1, :], "dv")
        P.op("dve", lambda e: e.scalar_tensor_tensor(out=dv[:, 0:32], in0=dv[:, 0:32], scalar=1.0, in1=dv[:, 32:64],
                                                     op0=ALU.mult, op1=ALU.mult, accum_out=lamt[:, 0:1]),
             reads=["dv"], writes=["dv", "lamt"])
        P.op("dve", lambda e: e.scalar_tensor_tensor(out=dv[:, 64:96], in0=dv[:, 64:96], scalar=1.0, in1=dv[:, 96:128],
                                                     op0=ALU.mult, op1=ALU.mult, accum_out=lamt[:, 1:2]),
             reads=["dv", "lamt"], writes=["dv", "lamt"])
        P.op("act", lambda e: e.activation(out=lamt[:, 0:2], in_=lamt[:, 0:2], func=AF.Exp), reads=["lamt"], writes=["lamt"])
        P.op("dve", lambda e: e.tensor_tensor(out=lamt[:, 2:3], in0=lamt[:, 1:2], in1=lamt[:, 0:1], op=ALU.subtract),
             reads=["lamt"], writes=["lamt"])
        P.op("dve", lambda e: e.tensor_scalar(out=lamt[:, 2:3], in0=lamt[:, 2:3], scalar1=-lam_init, scalar2=None,
                                              op0=ALU.add),
             reads=["lamt"], writes=["lamt"])
        for h in range(4):
            P.op("dve", lambda e, h=h: e.tensor_scalar(out=sw[:, h * 64:(h + 1) * 64], in0=dv[:, 128:192],
                                                       scalar1=1.0 - lam_init, scalar2=None, op0=ALU.mult),
                 reads=["dv"], writes=["sw"])
        qT = a16.alloc(2 * S, (2, S))
        kT = a16.alloc(2 * S, (2, S))
        va = a16.alloc(T * 4 * 65, (T, 4, 65))
        for r in range(2):
            P.dma("sp", qT[:, r, :], fm_d[(FM_DQ + r) * 128:(FM_DQ + r + 1) * 128, :], writes=["dqT"])
            P.dma("sp", kT[:, r, :], fm_d[(FM_DK + r) * 128:(FM_DK + r + 1) * 128, :], writes=["dkT"])
        P.op("pool", lambda e: e.memset(va, 1.0), writes=["va%d" % t_ for t_ in range(T)])
        for t in range(T):
            P.dma("sp", va[:, t, :, 0:64], tm_d[t * 128:(t + 1) * 128, TM_DV:TM_DV + 256].rearrange("p (h e) -> p h e", h=4),
                  writes=["va%d" % t])
        ETp = [a16.alloc(1024, (2, 512)) for _ in range(2)]
        et_i = [0]
        oTs = [a32.alloc(512) for _ in range(2)]
        dso = a32.alloc(4 * 256, (4, 256))
        rz = a32.alloc(2)
        zts = [a16.alloc(256) for _ in range(2)]
        szs = [a32.alloc(256) for _ in range(2)]
        sq = a32.alloc(256)
        s2 = a32.alloc(4)
        mos = [a16.alloc(256) for _ in range(2)]
        ot_i = 0
        yield
        for sb in range(NSB):
            q0 = sb * 512
            for h in range(4):
                r, hl = h // 2, h % 2
                nkb = 4 * sb + 4
                W = 256 if dslopes[h] * 511 > 80 else 512
                def qk(kb):
                    par = kb % 2
                    qlo = max(0, kb - 4 * sb) * 128
                    for i in range(2):
                        rg = hl * 2 + i
                        bk = par * 2 + i
                        P.op("pe", lambda e, bk=bk, rg=rg, kb=kb, qlo=qlo, r=r, q0=q0:
                             e.matmul(psb[bk][:, qlo:512], lhsT=kT[rg * 32:(rg + 1) * 32, r, kb * 128:(kb + 1) * 128],
                                      rhs=qT[rg * 32:(rg + 1) * 32, r, q0 + qlo:q0 + 512], start=True, stop=True,
                                      tile_position=(rg * 32, 0)),
                             reads=["dqT", "dkT"], writes=["ps%d" % bk])

                def rest(kb):
                    par = kb % 2
                    rel = kb - 4 * sb
                    qlo = max(0, rel) * 128
                    ET = ETp[et_i[0] % 2]
                    ek = "ETp%d" % (et_i[0] % 2)
                    et_i[0] += 1
                    pair = pbig[:, par * 1024:(par + 1) * 1024].rearrange("p (i c) -> p i c", i=2)
                    for sub in range(512 // W):
                        a, b = max(qlo, sub * W), (sub + 1) * W
                        if a >= b:
                            continue
                        dd = (kb * 128 - (q0 + sub * W)) // 128
                        P.op("act", lambda e, ET=ET, a=a, b=b, dd=dd, pair=pair, h=h:
                             e.activation(out=ET[:, :, a:b], in_=pair[:, :, a:b], func=AF.Exp,
                                          bias=alib[:, 1, h, dd + 31:dd + 32], scale=scale),
                             reads=["ps%d" % (par * 2), "ps%d" % (par * 2 + 1), "alib"], writes=[ek])
                    if rel >= 0:
                        for i in range(2):
                            P.op("pool", lambda e, i=i, ET=ET, qlo=qlo:
                                 e.tensor_tensor(out=ET[:, i, qlo:qlo + 128], in0=ET[:, i, qlo:qlo + 128], in1=tri01, op=ALU.mult),
                                 reads=[ek, "tri01"], writes=[ek])
                    for i in range(2):
                        P.op("pe", lambda e, i=i, ET=ET, kb=kb, qlo=qlo, h=h, nkb=nkb:
                             e.matmul(psb[4 + i][0:65, qlo:512], lhsT=va[:, kb, h, :], rhs=ET[:, i, qlo:512],
                                      start=(kb == 0), stop=(kb == nkb - 1)),
                             reads=[ek, "va%d" % kb], writes=["ps%d" % (4 + i)])

                qk(0)
                for kb in range(nkb):
                    if kb + 1 < nkb:
                        qk(kb + 1)
                    rest(kb)
                    yield
                for i in range(2):
                    oT = oTs[ot_i % 2]
                    ok = "oT%d" % (ot_i % 2)
                    ot_i += 1
                    evac(oT[0:65, :], psb[4 + i][0:65, :], reads=["ps%d" % (4 + i)], writes=[ok])
                    for qt in range(4):
                        cbi = ((qt % 2) * 2 + i) * 65
                        P.op("pe", lambda e, qt=qt, cbi=cbi, oT=oT:
                             e.transpose(out=psb[qt // 2][:, cbi:cbi + 65], in_=oT[0:65, qt * 128:(qt + 1) * 128],
                                         identity=ident_f[0:65, 0:65]),
                             reads=[ok, "ident_f"], writes=["ps%d" % (qt // 2)])
                yield
                for qt in range(4):
                    pq = psb[qt // 2]
                    kq = "ps%d" % (qt // 2)
                    cb0 = (qt % 2) * 130
                    P.op("dve", lambda e, pq=pq, cb0=cb0: e.tensor_scalar(
                        out=rz, in0=pq[:, cb0:cb0 + 130].rearrange("p (s c) -> p s c", c=65)[:, :, 64], scalar1=1e-30,
                        scalar2=None, op0=ALU.max),
                        reads=[kq], writes=["rz"])
                    P.op("dve", lambda e: e.reciprocal(out=rz, in_=rz), reads=["rz"], writes=["rz"])
                    P.op("dve", lambda e: e.tensor_tensor(out=rz[:, 1:2], in0=rz[:, 1:2], in1=lamt[:, 2:3], op=ALU.mult),
                         reads=["rz", "lamt"], writes=["rz"])
                    P.op("dve", lambda e, h=h, qt=qt, pq=pq, cb0=cb0: e.tensor_scalar(
                        out=dso[:, qt, h * 64:(h + 1) * 64], in0=pq[:, cb0:cb0 + 64],
                        scalar1=rz[:, 0:1], scalar2=None, op0=ALU.mult),
                        reads=[kq, "rz"], writes=["dso"])
                    P.op("dve", lambda e, h=h, qt=qt, pq=pq, cb0=cb0: e.scalar_tensor_tensor(
                        out=dso[:, qt, h * 64:(h + 1) * 64], in0=pq[:, cb0 + 65:cb0 + 129],
                        scalar=rz[:, 1:2], in1=dso[:, qt, h * 64:(h + 1) * 64],
                        op0=ALU.mult, op1=ALU.add),
                        reads=[kq, "rz", "dso"], writes=["dso"])
                yield
            for qt in range(4):
                t = sb * 4 + qt
                j = t % 2
                zt, sz = zts[j], szs[j]
                k = "D%d_" % j
                P.dma("sp", zt, tm_d[t * 128:(t + 1) * 128, TM_DZ:TM_DZ + 256], writes=[k + "z"])
                P.op("act", lambda e, sz=sz, zt=zt: e.activation(out=sz, in_=zt, func=AF.Silu), reads=[k + "z"], writes=[k + "sz"])
                P.op("dve", lambda e, sz=sz: e.tensor_tensor(out=sz, in0=sz, in1=sw, op=ALU.mult),
                     reads=[k + "sz", "sw"], writes=[k + "sz"])
                P.op("dve", lambda e, qt=qt: e.tensor_tensor(out=sq, in0=dso[:, qt, :], in1=dso[:, qt, :], op=ALU.mult),
                     reads=["dso"], writes=["dsq"])
                P.op("dve", lambda e: e.tensor_reduce(out=s2, in_=sq.rearrange("p (h e) -> p h e", h=4), axis=AX.X, op=ALU.add),
                     reads=["dsq"], writes=["ds2"])
                P.op("dve", lambda e: e.tensor_scalar(out=s2, in0=s2, scalar1=1.0 / 64, scalar2=EPS, op0=ALU.mult, op1=ALU.add),
                     reads=["ds2"], writes=["ds2"])
                P.op("act", lambda e: e.activation(out=s2, in_=s2, func=AF.Sqrt), reads=["ds2"], writes=["ds2"])
                P.op("dve", lambda e: e.reciprocal(out=s2, in_=s2), reads=["ds2"], writes=["ds2"])
                mo = mos[j]
                for hh in range(4):
                    P.op("dve", lambda e, hh=hh, qt=qt, mo=mo, sz=sz: e.scalar_tensor_tensor(
                        out=mo[:, hh * 64:(hh + 1) * 64], in0=dso[:, qt, hh * 64:(hh + 1) * 64],
                        scalar=s2[:, hh:hh + 1], in1=sz[:, hh * 64:(hh + 1) * 64], op0=ALU.mult, op1=ALU.mult),
                        reads=["dso", "ds2", k + "sz"], writes=[k + "mo"])
                P.dma(STORE_Q, mixed_d[t * 128:(t + 1) * 128, 256:512], mo, reads=[k + "mo"], writes=["mixD%d" % t])
                yield

    def phase_nsa(l):
        a32.mark()
        a16.mark()
        scale = 64 ** -0.5
        nslopes = [2.0 ** (-8.0 * (i + 1) / 8) for i in range(8)][0::2]

        def subw(sl):
            return 128 if sl * 255 > 80 else (256 if sl * 511 > 80 else 512)
        kcT2 = a16.alloc(NBK * 128)
        vca = a16.alloc(NBK * 65, (NBK, 65))
        a32.mark()
        a16.mark()
        kvc = a16.alloc(S)
        P.dma("sp", kvc, fm_d[FM_KVC * 128:(FM_KVC + 1) * 128, :], writes=["kvc"])
        W1 = a16.alloc(32 * 256, (32, 256))
        W2k = a16.alloc(2 * 128, (2, 128))
        W2v = a16.alloc(2 * 64, (2, 64))
        peT = a16.alloc(32)
        hsk = a16.alloc(2 * NC, (2, NC))
        hsv = a16.alloc(2 * NC, (2, NC))
        stg = a32.alloc(8 * 256, (8, 256))
        stg2 = a32.alloc(2 * 64, (2, 64))
        pef = a32.alloc(32)
        hb = a32.alloc(4)
        for ch in range(4):
            P.dma("sp", stg[0:64], w_ck1[l, ch * 512:(ch + 1) * 512, :].rearrange("(l c) h -> c l h", c=64), writes=["stg"])
            P.dma("sp", stg[64:128], w_cv1[l, ch * 512:(ch + 1) * 512, :].rearrange("(l c) h -> c l h", c=64), writes=["stg"])
            P.op("pool", lambda e, ch=ch: e.tensor_copy(out=W1[:, ch * 8:(ch + 1) * 8, :], in_=stg), reads=["stg"], writes=["W1"])
        P.dma("sp", stg2, w_ck2[l].rearrange("(a p) d -> p a d", p=128), writes=["stg2"])
        for dup in range(2):
            P.op("pool", lambda e, dup=dup: e.tensor_copy(out=W2k[:, :, dup * 64:(dup + 1) * 64], in_=stg2), reads=["stg2"], writes=["W2k"])
        P.dma("sp", stg2, w_cv2[l].rearrange("(a p) d -> p a d", p=128), writes=["stg2"])
        P.op("pool", lambda e: e.tensor_copy(out=W2v, in_=stg2), reads=["stg2"], writes=["W2v"])
        P.dma("sp", pef, nsa_peT[l], writes=["pef"])
        P.op("dve", lambda e: e.tensor_copy(out=peT, in_=pef), reads=["pef"], writes=["peT"])
        P.op("pool", lambda e: e.memset(vca, 1.0), writes=["vca"])
        P.op("pool", lambda e: e.memset(kcT2, 0.0), writes=["kcT2"])
        for g, hs in ((0, hsk), (1, hsv)):
            for hh in range(2):
                ph, pbb = psb[2 * g + hh], psb[4 + 2 * g + hh]
                for li in range(32):
                    P.op("pe", lambda e, g=g, hh=hh, li=li, pbb=pbb:
                         e.matmul(pbb[:, 0:1], lhsT=W1[g * 64:(g + 1) * 64, li, hh * 128:(hh + 1) * 128],
                                  rhs=peT[g * 64:(g + 1) * 64, li:li + 1], start=(li == 0), stop=(li == 31)),
                         reads=["W1", "peT"], writes=["ps%d" % (4 + 2 * g + hh)])
                P.op("dve", lambda e, g=g, hh=hh, pbb=pbb: e.tensor_copy(out=hb[:, 2 * g + hh:2 * g + hh + 1], in_=pbb[:, 0:1]),
                     reads=["ps%d" % (4 + 2 * g + hh)], writes=["hb"])
                for li in range(32):
                    P.op("pe", lambda e, g=g, hh=hh, li=li, ph=ph:
                         e.matmul(ph[:, 0:NC], lhsT=W1[g * 64:(g + 1) * 64, li, hh * 128:(hh + 1) * 128],
                                  rhs=kvc[g * 64:(g + 1) * 64, li:li + 16 * (NC - 1) + 1:16], start=(li == 0), stop=(li == 31)),
                         reads=["W1", "kvc"], writes=["ps%d" % (2 * g + hh)])
                P.op("act", lambda e, g=g, hh=hh, ph=ph, hs=hs:
                     e.activation(out=hs[:, hh, :], in_=ph[:, 0:NC], func=AF.Silu, bias=hb[:, 2 * g + hh:2 * g + hh + 1]),
                     reads=["ps%d" % (2 * g + hh), "hb"], writes=["hs%d" % g])
        for hh in range(2):
            P.op("pe", lambda e, hh=hh: e.matmul(psb[6][:, 0:NC], lhsT=W2k[:, hh, :], rhs=hsk[:, hh, :],
                                                 start=(hh == 0), stop=(hh == 1)),
                 reads=["W2k", "hs0"], writes=["ps6"])
        P.op("act", lambda e: e.copy(out=kcT2[:, 0:NC], in_=psb[6][:, 0:NC]), reads=["ps6"], writes=["kcT2"])
        for nb in range(NBK):
            nn = min(128, NC - nb * 128)
            for hh in range(2):
                P.op("pe", lambda e, hh=hh, nb=nb, nn=nn:
                     e.matmul(psb[7][0:nn, 0:64], lhsT=hsv[:, hh, nb * 128:nb * 128 + nn], rhs=W2v[:, hh, :],
                              start=(hh == 0), stop=(hh == 1)),
                     reads=["W2v", "hs1"], writes=["ps7"])
            P.op("act", lambda e, nb=nb, nn=nn: e.copy(out=vca[0:nn, nb, 0:64], in_=psb[7][0:nn, 0:64]),
                 reads=["ps7"], writes=["vca"])
        P.barrier()
        a32.release()
        a16.release()

        alib = a32.alloc(2 * 4 * 35, (2, 4, 35))
        alic = a32.alloc(256, (4, 2, 32))
        tri01 = a16.alloc(128)
        low01 = a16.alloc(128)
        gneg = a16.alloc(4096)
        ex2 = a16.alloc(T * 128, (T, 128))
        ovl = a16.alloc(128, (2, 64))
        tri_f = a32.alloc(128)
        P.dma("sp", alib, c_alibi.rearrange("p (s h d) -> p s h d", s=2, h=4), writes=["alib"])
        P.dma("sp", alic, c_alibi_cmp.rearrange("p (h n q) -> p h n q", h=4, n=2), writes=["alic"])
        P.dma("sp", tri_f, c_tri01, writes=["tri_f"])
        P.op("dve", lambda e: e.tensor_copy(out=tri01, in_=tri_f), reads=["tri_f"], writes=["tri01"])
        P.dma("sp", low01, c_low01, writes=["low01"])
        P.dma("sp", gneg, c_gneg, writes=["gneg"])
        P.dma("sp", ex2, c_ex2.rearrange("p (t k) -> p t k", k=128), writes=["ex2"])
        P.dma("sp", ovl, c_ovl.rearrange("p (n j) -> p n j", n=2), writes=["ovl"])
        if NS > 16:
            tka = a32.alloc(T * 64, (T, 64))
            P.dma("sp", tka, c_topk_add.rearrange("p (t j) -> p t j", j=64), writes=["tka"])
        qT = a16.alloc(2 * S, (2, S))
        ksl = a16.alloc(S)
        kwn = a16.alloc(S)
        vsl = a16.alloc(T * 65, (T, 65))
        vwn = a16.alloc(T * 65, (T, 65))
        for r in range(2):
            P.dma("sp", qT[:, r, :], fm_d[(FM_NQ + r) * 128:(FM_NQ + r + 1) * 128, :], writes=["nqT"])
        for g in range(2):
            P.dma("sp", ksl[g * 64:(g + 1) * 64, :], fm_d[FM_KSW * 128:FM_KSW * 128 + 64, :], writes=["ksl"])
            P.dma("sp", kwn[g * 64:(g + 1) * 64, :], fm_d[FM_KSW * 128 + 64:FM_KSW * 128 + 128, :], writes=["kwn"])
        P.op("pool", lambda e: e.memset(vsl, 1.0), writes=["vsl%d" % t_ for t_ in range(T)])
        P.op("pool", lambda e: e.memset(vwn, 1.0), writes=["vwn%d" % t_ for t_ in range(T)])
        for t in range(T):
            P.dma("sp", vsl[:, t, 0:64], tm_d[t * 128:(t + 1) * 128, TM_VSLC:TM_VSLC + 64], writes=["vsl%d" % t])
            P.dma("sp", vwn[:, t, 0:64], tm_d[t * 128:(t + 1) * 128, TM_VWIN:TM_VWIN + 64], writes=["vwn%d" % t])

        ETs = [[a16.alloc(512) for _ in range(2)] for _ in range(4)]
        et_i = [0, 0, 0, 0]
        smf = [a32.alloc(512) for _ in range(2)]
        oTs = [a32.alloc(512) for _ in range(2)]
        acc = a32.alloc(4 * 256, (4, 256))
        imp = a32.alloc(4 * 64, (4, 64))
        gs = a32.alloc(4 * 12, (4, 12))
        rz = a32.alloc(4)
        coef = a32.alloc(4)
        top8 = a32.alloc(8)
        tmpk = a32.alloc(64)
        nsel = a16.alloc(128)
        negT2 = a16.alloc(512)
        zts = [a16.alloc(256) for _ in range(2)]
        szs = [a32.alloc(256) for _ in range(2)]
        mos = [a16.alloc(256) for _ in range(2)]
        ot_i = [0]
        sm_i = [0]

        def epilogue(sb, b, first):
            for h in range(4):
                oT = oTs[ot_i[0] % 2]
                ok = "noT%d" % (ot_i[0] % 2)
                ot_i[0] += 1
                evac(oT[0:65, :], psb[4 + h][0:65, :], reads=["ps%d" % (4 + h)], writes=[ok])
                for qt in range(4):
                    P.op("pe", lambda e, h=h, qt=qt, oT=oT:
                         e.transpose(out=psb[qt][:, h * 65:(h + 1) * 65], in_=oT[0:65, qt * 128:(qt + 1) * 128],
                                     identity=ident_f[0:65, 0:65]),
                         reads=[ok, "ident_f"], writes=["ps%d" % qt])
            for qt in range(4):
                pq = psb[qt]
                kq = "ps%d" % qt
                P.op("dve", lambda e, pq=pq: e.tensor_scalar(
                    out=rz, in0=pq[:, 0:260].rearrange("p (s c) -> p s c", c=65)[:, :, 64], scalar1=1e-30, scalar2=None,
                    op0=ALU.max), reads=[kq], writes=["nrz"])
                P.op("dve", lambda e: e.reciprocal(out=rz, in_=rz), reads=["nrz"], writes=["nrz"])
                if b == 0:
                    P.op("dve", lambda e, qt=qt: e.tensor_copy(out=rzc[:, qt, :], in_=rz), reads=["nrz"], writes=["rzc"])
                P.op("dve", lambda e, qt=qt: e.tensor_tensor(
                    out=coef, in0=rz, in1=gs[:, qt, :].rearrange("p (h b) -> p h b", b=3)[:, :, b], op=ALU.mult),
                    reads=["nrz", "gs"], writes=["coef"])
                for h in range(4):
                    if first:
                        P.op("dve", lambda e, h=h, qt=qt, pq=pq: e.tensor_scalar(
                            out=acc[:, qt, h * 64:(h + 1) * 64], in0=pq[:, h * 65:h * 65 + 64], scalar1=coef[:, h:h + 1],
                            scalar2=None, op0=ALU.mult), reads=[kq, "coef"], writes=["nacc"])
                    else:
                        P.op("dve", lambda e, h=h, qt=qt, pq=pq: e.scalar_tensor_tensor(
                            out=acc[:, qt, h * 64:(h + 1) * 64], in0=pq[:, h * 65:h * 65 + 64], scalar=coef[:, h:h + 1],
                            in1=acc[:, qt, h * 64:(h + 1) * 64], op0=ALU.mult, op1=ALU.add),
                            reads=[kq, "coef", "nacc"], writes=["nacc"])

        def exp_tile(h, ET, ek, src, src_key, a_lo, a_hi, kb, q0, rows=128):
            W = subw(nslopes[h])
            for sub in range(512 // W):
                a, bb = max(a_lo, sub * W), min(a_hi, (sub + 1) * W)
                if a >= bb:
                    continue
                dd = (kb * 128 - (q0 + sub * W)) // 128
                P.op("act", lambda e, a=a, bb=bb, dd=dd: e.activation(
                    out=ET[0:rows, a:bb], in_=src[0:rows, a:bb], func=AF.Exp, bias=alib[0:rows, 0, h, dd + 31:dd + 32], scale=scale),
                    reads=[src_key, "alib"], writes=[ek])

        rzc = a32.alloc(16, (4, 4))
        for sb in range(NSB):
            q0 = sb * 512
            P.op("act", lambda e, sb=sb: e.activation(out=gs, in_=small[:, sb * 4:sb * 4 + 4, 0:12], func=AF.Sigmoid),
                 reads=["small"], writes=["gs"])
            nbs = []
            for nb in range(NBK):
                nn = min(128, NC - 128 * nb, 32 * sb + 31 - 128 * nb)
                if nn > 0:
                    nbs.append((nb, nn))
            cET = {}
            if nbs:
                for h in range(4):
                    g, r = h % 2, h // 2
                    for (nb, nn) in nbs:
                        P.op("pe", lambda e, h=h, g=g, r=r, nb=nb, nn=nn, q0=q0:
                             e.matmul(psb[h][0:nn, :], lhsT=kcT2[g * 64:(g + 1) * 64, nb * 128:nb * 128 + nn],
                                      rhs=qT[g * 64:(g + 1) * 64, r, q0:q0 + 512], start=True, stop=True),
                             reads=["kcT2", "nqT"], writes=["ps%d" % h])
                        sm = smf[sm_i[0] % 2]
                        sk = "smf%d" % (sm_i[0] % 2)
                        sm_i[0] += 1
                        c0 = q0 - 2048 * nb
                        P.op("dve", lambda e, h=h, nn=nn, sm=sm, c0=c0: e.tensor_tensor(
                            out=sm[0:nn, :], in0=psb[h][0:nn, :], in1=gneg[0:nn, c0:c0 + 512], op=ALU.add),
                            reads=["ps%d" % h, "gneg"], writes=[sk])
                        ET = ETs[h][nb]
                        ek = "ET%d_%d" % (h, nb)
                        cET[(h, nb)] = (ET, ek, nn)
                        W = subw(nslopes[h])
                        for sub in range(512 // W):
                            a, bb = sub * W, (sub + 1) * W
                            qi = (q0 + a) // 128
                            P.op("act", lambda e, h=h, nb=nb, nn=nn, sm=sm, ET=ET, a=a, bb=bb, qi=qi: e.activation(
                                out=ET[0:nn, a:bb], in_=sm[0:nn, a:bb], func=AF.Exp, bias=alic[0:nn, h, nb, qi:qi + 1], scale=scale),
                                reads=[sk, "alic"], writes=[ek])
                    for i, (nb, nn) in enumerate(nbs):
                        ET, ek, _ = cET[(h, nb)]
                        P.op("pe", lambda e, h=h, nb=nb, nn=nn, ET=ET, i=i, n=len(nbs):
                             e.matmul(psb[4 + h][0:65, :], lhsT=vca[0:nn, nb, :], rhs=ET[0:nn, :],
                                      start=(i == 0), stop=(i == n - 1)),
                             reads=[ek, "vca"], writes=["ps%d" % (4 + h)])
                epilogue(sb, 0, True)
                for h in range(4):
                    for qt in range(4):
                        for i, (nb, nn) in enumerate(nbs):
                            ET, ek, _ = cET[(h, nb)]
                            P.op("pe", lambda e, h=h, qt=qt, nb=nb, nn=nn, ET=ET, i=i, n=len(nbs):
                                 e.matmul(psb[h][:, qt * 64:(qt + 1) * 64], lhsT=ET[0:nn, qt * 128:(qt + 1) * 128],
                                          rhs=ovl[0:nn, nb, :], start=(i == 0), stop=(i == n - 1)),
                                 reads=[ek, "ovl"], writes=["ps%d" % h])
                for h in range(4):
                    for qt in range(4):
                        if h == 0:
                            P.op("dve", lambda e, h=h, qt=qt: e.tensor_scalar(
                                out=imp[:, qt, :], in0=psb[h][:, qt * 64:(qt + 1) * 64], scalar1=rzc[:, qt, h:h + 1],
                                scalar2=None, op0=ALU.mult), reads=["ps%d" % h, "rzc"], writes=["imp"])
                        else:
                            P.op("dve", lambda e, h=h, qt=qt: e.scalar_tensor_tensor(
                                out=imp[:, qt, :], in0=psb[h][:, qt * 64:(qt + 1) * 64], scalar=rzc[:, qt, h:h + 1],
                                in1=imp[:, qt, :], op0=ALU.mult, op1=ALU.add), reads=["ps%d" % h, "rzc", "imp"], writes=["imp"])
            else:
                P.op("dve", lambda e: e.memset(acc, 0.0), writes=["nacc"])
                P.op("dve", lambda e: e.memset(imp, 0.0), writes=["imp"])
            for qt in range(4):
                t = sb * 4 + qt
                if NS > 16:
                    P.op("dve", lambda e, qt=qt, t=t: e.tensor_tensor(out=imp[:, qt, :], in0=imp[:, qt, :], in1=tka[:, t, :], op=ALU.add),
                         reads=["imp", "tka"], writes=["imp"])
                    P.op("dve", lambda e, qt=qt: e.max(out=top8, in_=imp[:, qt, :]), reads=["imp"], writes=["top8"])
                    P.op("dve", lambda e, qt=qt: e.match_replace(out=tmpk, in_to_replace=top8, in_values=imp[:, qt, :], imm_value=-1e9),
                         reads=["imp", "top8"], writes=["tmpk"])
                    P.op("dve", lambda e: e.max(out=top8, in_=tmpk), reads=["tmpk"], writes=["top8"])
                    P.op("dve", lambda e, qt=qt: e.tensor_scalar(out=tmpk, in0=imp[:, qt, :], scalar1=top8[:, 7:8], scalar2=-1.0,
                                                                 op0=ALU.is_ge, op1=ALU.add),
                         reads=["imp", "top8"], writes=["tmpk"])
                    for dup in range(2):
                        P.op("dve", lambda e, dup=dup: e.tensor_scalar(out=nsel[:, dup * 64:(dup + 1) * 64], in0=tmpk, scalar1=BIG,
                                                                       scalar2=None, op0=ALU.mult),
                             reads=["tmpk"], writes=["nsel"])
                else:
                    P.op("dve", lambda e: e.memset(nsel, 0.0), writes=["nsel"])
                pt = psb[qt][:, 0:64].bitcast(BF16)
                P.op("pe", lambda e, pt=pt: e.transpose(out=pt, in_=nsel, identity=ident_b), reads=["nsel", "ident_b"], writes=["ps%d" % qt])
                P.op("act", lambda e, qt=qt, pt=pt: e.copy(out=negT2[:, qt * 128:(qt + 1) * 128], in_=pt), reads=["ps%d" % qt], writes=["negT2"])
            for b, kT2, va_, kbs in ((1, ksl, vsl, list(range(0, 4 * sb + 4))),
                                     (2, kwn, vwn, list(range(max(0, 4 * sb - 4), 4 * sb + 4)))):
                kname = "ksl" if b == 1 else "kwn"
                vname = "vsl" if b == 1 else "vwn"
                n_kb = len(kbs)
                for hp in range(2):
                    heads = (2 * hp, 2 * hp + 1)

                    def geom(i, kbs=kbs, b=b, sb=sb):
                        kb = kbs[i]
                        rel = kb - 4 * sb
                        qlo = max(0, rel) * 128
                        qhi = 512 if (b == 1 or rel >= 0) else 128 * (rel + 5)
                        return kb, rel, qlo, qhi

                    def qk(i, heads=heads, kT2=kT2, b=b, q0=q0, kname=kname):
                        kb, rel, qlo, qhi = geom(i)
                        par = i % 2
                        for h in heads:
                            g, r = h % 2, h // 2
                            bk = par * 2 + g
                            P.op("pe", lambda e, bk=bk, g=g, r=r, kb=kb, qlo=qlo, qhi=qhi, kT2=kT2, b=b, q0=q0:
                                 e.matmul(psb[bk][:, qlo:qhi], lhsT=kT2[g * 64:(g + 1) * 64, kb * 128:(kb + 1) * 128],
                                          rhs=qT[g * 64:(g + 1) * 64, r, q0 + qlo:q0 + qhi], start=True, stop=(b != 1)),
                                 reads=[kname, "nqT"], writes=["ps%d" % bk])
                            if b == 1:
                                P.op("pe", lambda e, bk=bk, g=g, kb=kb, qlo=qlo, qhi=qhi:
                                     e.matmul(psb[bk][:, qlo:qhi], lhsT=ex2[g * 64:(g + 1) * 64, kb, :],
                                              rhs=negT2[g * 64:(g + 1) * 64, qlo:qhi], start=False, stop=True),
                                     reads=["ex2", "negT2"], writes=["ps%d" % bk])

                    def rest(i, heads=heads, b=b, q0=q0, va_=va_, vname=vname, n_kb=n_kb):
                        kb, rel, qlo, qhi = geom(i)
                        par = i % 2
                        for h in heads:
                            bk = par * 2 + h % 2
                            ET = ETs[h][et_i[h] % 2]
                            ek = "ET%d_%d" % (h, et_i[h] % 2)
                            et_i[h] += 1
                            exp_tile(h, ET, ek, psb[bk], "ps%d" % bk, qlo, qhi, kb, q0)
                            if rel >= 0:
                                P.op("pool", lambda e, ET=ET, qlo=qlo: e.tensor_tensor(
                                    out=ET[:, qlo:qlo + 128], in0=ET[:, qlo:qlo + 128], in1=tri01, op=ALU.mult),
                                    reads=[ek, "tri01"], writes=[ek])
                            elif b == 2:
                                P.op("pool", lambda e, ET=ET, qhi=qhi: e.tensor_tensor(
                                    out=ET[:, qhi - 128:qhi], in0=ET[:, qhi - 128:qhi], in1=low01, op=ALU.mult),
                                    reads=[ek, "low01"], writes=[ek])
                            P.op("pe", lambda e, h=h, ET=ET, kb=kb, qlo=qlo, qhi=qhi, va_=va_, i=i, n=n_kb:
                                 e.matmul(psb[4 + h][0:65, qlo:qhi], lhsT=va_[:, kb, :], rhs=ET[:, qlo:qhi],
                                          start=(i == 0), stop=(i == n - 1), skip_group_check=True),
                                 reads=[ek, "%s%d" % (vname, kb)], writes=["ps%d" % (4 + h)])

                    qk(0)
                    for i in range(n_kb):
                        if i + 1 < n_kb:
                            qk(i + 1)
                        rest(i)
                epilogue(sb, b, False)
            for qt in range(4):
                t = sb * 4 + qt
                j = t % 2
                zt, sz = zts[j], szs[j]
                k = "N%d_" % j
                P.dma("sp", zt, tm_d[t * 128:(t + 1) * 128, TM_NSAZ:TM_NSAZ + 256], writes=[k + "z"])
                P.op("act", lambda e, sz=sz, zt=zt: e.activation(out=sz, in_=zt, func=AF.Silu), reads=[k + "z"], writes=[k + "sz"])
                mo = mos[j]
                P.op("dve", lambda e, qt=qt, mo=mo, sz=sz: e.tensor_tensor(out=mo, in0=acc[:, qt, :], in1=sz, op=ALU.mult),
                     reads=["nacc", k + "sz"], writes=[k + "mo"])
                P.dma(STORE_Q, mixed_d[t * 128:(t + 1) * 128, 0:256], mo, reads=[k + "mo"], writes=["mixN%d" % t])
        P.barrier()
        a32.release()
        a16.release()

    def phase_dump(l):
        P.barrier()

    def phase_stub(l):
        if True:
            a16.mark()
            tl = [a16.alloc(1024) for _ in range(2)]
            for t in range(T):
                P.dma("sp", tl[t % 2], tm_d[t * 128:(t + 1) * 128, 396:396 + 1024],
                      reads=[], writes=["tl%d" % (t % 2)])
                if stop == 98:
                    continue
                P.dma(STORE_Q, mixed_d[t * 128:(t + 1) * 128, :], tl[t % 2], reads=["tl%d" % (t % 2)], writes=["mixX%d" % t])
            P.barrier()
            a16.release()

    def phase_F(l):
        x_src = x_in if l == 0 else xres
        wo_sb = a16.alloc(8 * 1024, (8, 1024))
        stage = [a32.alloc(1024) for _ in range(2)]
        xts = [a32.alloc(D_MODEL) for _ in range(2)]
        xns = [a32.alloc(D_MODEL) for _ in range(2)]
        mTs = [a16.alloc(8 * 128, (8, 128)) for _ in range(2)]
        mxs = [a16.alloc(1024) for _ in range(2)]
        last = (l == L - 1) and final_norm
        if last:
            fnwb = a32.alloc(D_MODEL)
            junk = a16.alloc(D_MODEL)
            sss = [a32.alloc(1) for _ in range(2)]
            rstds = [a32.alloc(1) for _ in range(2)]
            yts = [a32.alloc(D_MODEL) for _ in range(2)]
            P.dma("sp", fnwb, final_norm_w[0:1, :].partition_broadcast(128), writes=["fnwb"])
        for kc in range(8):
            st = stage[kc % 2]
            P.dma("sp", st, w_out[l, kc * 128:(kc + 1) * 128, :], writes=["stage%d" % (kc % 2)])
            P.op(("pool", "dve")[kc % 2], lambda e, kc=kc, st=st: e.tensor_copy(out=wo_sb[:, kc, :], in_=st),
                 reads=["stage%d" % (kc % 2)], writes=["wo_sb%d" % (kc % 2)])
        def f_loads(t):
            j = t % 2
            k = "F%d_" % j
            P.dma("sp", mxs[j], mixed_d[t * 128:(t + 1) * 128, :], reads=["mixR%d" % t], writes=[k + "mx"])
            P.dma("sp", xts[j], x_src[t * 128:(t + 1) * 128, :], reads=["xres_t%d" % t], writes=[k + "x"])
        f_loads(0)
        for t in range(T):
            j = t % 2
            k = "F%d_" % j
            xt, xn, mT = xts[j], xns[j], mTs[j]
            mx = mxs[j]
            pst = psb[j]
            pstb = pst[:, 0:512].bitcast(BF16)
            for kc in range(8):
                P.op("pe", lambda e, kc=kc, mx=mx, pstb=pstb:
                     e.transpose(out=pstb[:, kc * 128:(kc + 1) * 128], in_=mx[:, kc * 128:(kc + 1) * 128],
                                 identity=ident_b),
                     reads=[k + "mx", "ident_b"], writes=["ps%d" % j])
            evac(mT, pstb.rearrange("p (k t) -> p k t", k=8), reads=["ps%d" % j], writes=[k + "mT"])
            if t + 1 < T:
                f_loads(t + 1)
            for c in range(2):
                pi = 2 + (t * 2 + c) % 4
                ps = psb[pi]
                for kc in range(8):
                    P.op("pe", lambda e, kc=kc, ps=ps, mT=mT, c=c:
                         e.matmul(ps[:, :], lhsT=mT[:, kc, :], rhs=wo_sb[:, kc, c * 512:(c + 1) * 512],
                                  start=(kc == 0), stop=(kc == 7)),
                         reads=[k + "mT", "wo_sb0", "wo_sb1"], writes=["ps%d" % pi])
                P.op("dve", lambda e, ps=ps, xt=xt, xn=xn, c=c:
                     e.tensor_tensor(out=xn[:, c * 512:(c + 1) * 512], in0=ps[:, :],
                                     in1=xt[:, c * 512:(c + 1) * 512], op=ALU.add),
                     reads=["ps%d" % pi, k + "x"], writes=[k + "xn"])
            if not last:
                P.dma(STORE_Q, xres[t * 128:(t + 1) * 128, :], xn, reads=[k + "xn"], writes=["xres_t%d" % t])
            else:
                ss, rstd, yt = sss[j], rstds[j], yts[j]
                P.op("dve", lambda e, xn=xn, ss=ss: e.scalar_tensor_tensor(
                    out=junk, in0=xn, scalar=1.0, in1=xn, op0=ALU.mult, op1=ALU.mult, accum_out=ss),
                    reads=[k + "xn"], writes=["Fjunk", k + "ss"])
                P.op("dve", lambda e, ss=ss: e.tensor_scalar(out=ss, in0=ss, scalar1=1.0 / D_MODEL, scalar2=EPS,
                                                             op0=ALU.mult, op1=ALU.add),
                     reads=[k + "ss"], writes=[k + "ss"])
                P.op("act", lambda e, ss=ss: e.activation(out=ss, in_=ss, func=AF.Sqrt),
                     reads=[k + "ss"], writes=[k + "ss"])
                P.op("dve", lambda e, ss=ss, rstd=rstd: e.reciprocal(out=rstd, in_=ss),
                     reads=[k + "ss"], writes=[k + "rstd"])
                P.op("dve", lambda e, xn=xn, rstd=rstd, yt=yt: e.scalar_tensor_tensor(
                    out=yt, in0=xn, scalar=rstd, in1=fnwb, op0=ALU.mult, op1=ALU.mult),
                    reads=[k + "xn", k + "rstd", "fnwb"], writes=[k + "y"])
                P.dma(STORE_Q, y_out[t * 128:(t + 1) * 128, :], yt, reads=[k + "y"], writes=["y_t%d" % t])

    for l in range(L):
        phase_A(l)
        if "STUB" in phases:
            phase_stub(l)
        realP = P
        for group in ((("DIFF", phase_diff), ("SSD", phase_ssd)),):
            a32.mark()
            a16.mark()
            streams = []
            for nm, ph in group:
                if nm in phases:
                    st_ = Stream()
                    P = st_
                    for _ in ph(l):
                        pass
                    streams.append(st_)
            P = realP
            merge_streams(P, streams)
            P.barrier()
            a32.release()
            a16.release()
        if "NSA" in phases:
            phase_nsa(l)
        if "DUMP" in phases:
            phase_dump(l)
        a32.mark()
        a16.mark()
        streams = []
        if "F" in phases:
            st_ = Stream()
            P = st_
            phase_F(l)
            streams.append(st_)
        if "RET" in phases:
            st_ = Stream()
            P = st_
            for _ in phase_ret(l):
                pass
            streams.append(st_)
        P = realP
        merge_streams(P, streams)
        P.barrier()
        a32.release()
        a16.release()

    P.emit(es)
    es.close()
    return nc


def make_consts(S=SEQ):
    c = {"c_ident": np.eye(128, dtype=np.float32)}
    bf = ml_dtypes.bfloat16
    T = S // 128
    NC = (S - 32) // 16 + 1
    NS = S // 64
    p_ = np.arange(128)
    cc = np.arange(4096)
    c["c_gneg"] = np.where(cc[None, :] - 16 * p_[:, None] >= 31, 0.0, -BIG).astype(bf)
    ex = np.zeros((128, T, 128), np.float32)
    for kb in range(T):
        for pp in range(128):
            j = 2 * kb + pp // 64
            if j < 64:
                ex[j, kb, pp] = 1.0
                ex[64 + j, kb, pp] = 1.0
    c["c_ex2"] = ex.reshape(128, -1).astype(bf)
    c["c_low01"] = (p_[None, :] < p_[:, None]).astype(np.float32).astype(bf)
    n_ = np.arange(256)
    c_start, c_end = n_ * 16, n_ * 16 + 31
    s_start = np.arange(64) * 64
    s_end = s_start + 63
    ov = ((c_start[:, None] <= s_end[None, :]) & (c_end[:, None] >= s_start[None, :]) & (n_[:, None] < NC)).astype(np.float32)
    c["c_ovl"] = ov.reshape(2, 128, 64).transpose(1, 0, 2).reshape(128, 128).astype(bf)
    nsl = np.array([2.0 ** (-8.0 * (i + 1) / 8) for i in range(8)], np.float64)[0::2]
    ac = np.zeros((128, 4, 2, 32), np.float64)
    for h in range(4):
        for nb in range(2):
            for qi in range(32):
                ac[:, h, nb, qi] = nsl[h] * (16.0 * (128 * nb + p_) + 15.5 - 128.0 * qi)
    c["c_alibi_cmp"] = ac.reshape(128, -1).astype(np.float32)
    ta = np.zeros((128, T, 64), np.float32)
    jb = np.arange(64)
    for t in range(T):
        tq = t * 128 + p_
        cur = tq // 64
        forced = (jb[None, :] == 0) | (jb[None, :] == cur[:, None]) | (jb[None, :] == cur[:, None] - 1)
        valid = (jb[None, :] * 64 <= tq[:, None]) & (jb[None, :] < NS)
        ta[:, t, :] = np.where(forced, 1000.0, np.where(valid, 0.0, -1000.0))
    c["c_topk_add"] = ta.reshape(128, -1)
    H, C, dh = 4, 128, 64
    scale = dh ** -0.5
    log_g = np.log(1.0 - 2.0 ** (-5.0 - np.arange(H, dtype=np.float64)))
    pos = np.arange(C, dtype=np.float64)
    rel = pos[None, :] - pos[:, None]
    decT = np.zeros((128, 4 * 128), np.float64)
    xiT = np.zeros((128, 2 * 128), np.float64)
    zeta = np.zeros((128, 256), np.float64)
    cd = np.zeros((128, 4), np.float64)
    for h in range(H):
        ai = (h % 2) * 2 + h // 2
        decT[:, ai * 128:(ai + 1) * 128] = np.where(rel >= 0, np.exp(log_g[h] * np.maximum(rel, 0.0)), 0.0) * scale
        hp, hr = (h % 2) * 64, h // 2
        xiT[hp:hp + 64, hr * 128:(hr + 1) * 128] = (np.exp(log_g[h] * (pos + 1.0)) * scale)[None, :]
        zeta[:, h * 64:(h + 1) * 64] = np.exp(log_g[h] * (C - 1.0 - pos))[:, None]
        cd[:, h] = np.exp(log_g[h] * C)
    c["c_ret_decT"] = decT.astype(np.float32)
    c["c_ret_xiT"] = xiT.astype(np.float32)
    c["c_ret_zeta"] = zeta.astype(np.float32)
    c["c_ret_cd"] = cd.astype(np.float32)
    c["c_triu"] = np.triu(np.ones((128, 128), np.float32))
    c["c_tri01"] = np.triu(np.ones((128, 128), np.float32))
    slopes = np.array([2.0 ** (-8.0 * (i + 1) / 8) for i in range(8)], np.float64)
    al = np.zeros((128, 2, 4, 35), np.float64)
    p = np.arange(128, dtype=np.float64)
    for s, sl in enumerate((slopes[0::2], slopes[1::2])):
        for h in range(4):
            for dd in range(-31, 4):
                al[:, s, h, dd + 31] = sl[h] * (128.0 * dd + p)
    c["c_alibi"] = al.reshape(128, -1).astype(np.float32)
    return c


def layout_params(inputs):
    L = inputs["w_in"].shape[0]
    cw = np.asarray(inputs["ssm_conv_w"])
    conv_wT = np.ascontiguousarray(cw.reshape(L, 4, 6, 128).transpose(0, 3, 2, 1).reshape(L, 128, 24))
    conv_bT = np.ascontiguousarray(np.asarray(inputs["ssm_conv_b"]).reshape(L, 6, 128).transpose(0, 2, 1))
    ssm_vec = np.ascontiguousarray(np.concatenate([np.asarray(inputs["ssm_dt_bias"]), np.asarray(inputs["ssm_A_log"]),
                                                   np.asarray(inputs["ssm_D"])], axis=1))
    return {"conv_wT": conv_wT.astype(np.float32), "conv_bT": conv_bT.astype(np.float32),
            "ssm_vec": ssm_vec.astype(np.float32),
            "nsa_peT": np.ascontiguousarray(np.concatenate(
                [np.asarray(inputs["nsa_pe_k"]).transpose(0, 2, 1), np.asarray(inputs["nsa_pe_v"]).transpose(0, 2, 1)],
                axis=1)).astype(np.float32),
            "diff_vec": np.ascontiguousarray(np.concatenate(
                [np.asarray(inputs[k]) for k in ("diff_lam_q1", "diff_lam_k1", "diff_lam_q2", "diff_lam_k2",
                                                 "diff_subln_w")], axis=1)).astype(np.float32),
            "ret_gn_w": np.ascontiguousarray(inputs["ret_gn_w"]).astype(np.float32),
            "ssm_norm_w": np.ascontiguousarray(inputs["ssm_norm_w"]).astype(np.float32)}


def make_inmap(inputs, b):
    m = {
        "x": np.ascontiguousarray(inputs["x"][b]).astype(np.float32),
        "norm_w": np.ascontiguousarray(inputs["norm_w"]).astype(np.float32),
        "w_in": np.ascontiguousarray(inputs["w_in"]).astype(np.float32),
        "w_out": np.ascontiguousarray(inputs["w_out"]).astype(np.float32),
        "final_norm_w": np.ascontiguousarray(inputs["final_norm_w"]).reshape(1, -1).astype(np.float32),
    }
    m.update(make_consts(S=m["x"].shape[0]))
    m.update(layout_params(inputs))
    for kk in ("nsa_w_ck1", "nsa_w_cv1", "nsa_w_ck2", "nsa_w_cv2"):
        m[kk] = np.ascontiguousarray(inputs[kk]).astype(np.float32)
    return m


_CACHE = {}


def kernel(**inputs):
    S = inputs["x"].shape[1]
    B = inputs["x"].shape[0]
    L = inputs["w_in"].shape[0]
    key = (S, L)
    if key not in _CACHE:
        _CACHE[key] = build_program(S=S, L=L, phases=("A", "SSD", "RET", "DIFF", "NSA", "F"))
    nc = _CACHE[key]
    in_maps = [make_inmap(inputs, b) for b in range(B)]
    res = run_bass_kernel_spmd(nc, in_maps, core_ids=list(range(B)))
    return np.stack([r["y"] for r in res.results], axis=0).astype(np.float32)
```

```python
import math
from contextlib import ExitStack

import numpy as np
import ml_dtypes

import concourse.bass as bass
import concourse.mybir as mybir
from concourse.bass_utils import run_bass_kernel_spmd

F32 = mybir.dt.float32
BF16 = mybir.dt.bfloat16
I32 = mybir.dt.int32
AF = mybir.ActivationFunctionType
ALU = mybir.AluOpType
AX = mybir.AxisListType

D_MODEL = 1024
DEPTH = 4
SEQ = 4096
IN_W = 3984
EPS = 1e-6
BIG = 30000.0
STORE_Q = "pool"
NFM = 18
FMW = NFM * 128
TMW = 1936
TM_VSLC, TM_VWIN, TM_GATE, TM_NSAZ, TM_DV, TM_DZ, TM_RK, TM_RV, TM_RZ, TM_SZ, TM_DT = (
    0, 64, 128, 140, 396, 652, 908, 1164, 1420, 1676, 1932)
FM_NQ, FM_KVC, FM_KSW, FM_DQ, FM_DK, FM_RQ, FM_RK, FM_XBC = 0, 2, 3, 4, 6, 8, 10, 12

W_PIECES = [
    (0, 0, 256),
    (256, 256, 128),
    (384, 384, 64),
    (448, 512, 64),
    (512, 908, 512),
    (1024, 1932, 512),
    (1536, 3212, 768),
    (FMW + 0, 448, 64),
    (FMW + 64, 576, 332),
    (FMW + 396, 1420, 512),
    (FMW + 908, 2188, 1024),
    (FMW + 1932, 3980, 4),
]
WSBW = FMW + TMW


class Prog:
    ENGS = ("pe", "act", "dve", "pool", "sp")
    DMA_RING = 8

    def __init__(self, nc):
        self.nc = nc
        self.q = {e: [] for e in self.ENGS}
        self.buf = {}
        self.seen_e = {e: {} for e in self.ENGS}
        self.seen_d = {e: {} for e in self.ENGS}
        self.ring_uses = {e: [0] * self.DMA_RING for e in self.ENGS}
        self.ring_next = {e: 0 for e in self.ENGS}
        self.ring_last = {e: [None] * self.DMA_RING for e in self.ENGS}
        self.all_dma = []

    def _bs(self, k):
        s = self.buf.get(k)
        if s is None:
            s = {"w": None, "r_e": {}, "r_d": []}
            self.buf[k] = s
        return s

    def _need(self, eng, tok, waits):
        if tok is None:
            return
        if tok[0] == "e":
            _, pe, idx = tok
            if pe == eng and eng in ("pe", "sp"):
                return
            if self.seen_e[eng].get(pe, -1) >= idx:
                return
            self.seen_e[eng][pe] = idx
            self.q[pe][idx]["flag"] = True
            waits.append(tok)
        else:
            _, sid, val = tok
            if self.seen_d[eng].get(sid, 0) >= val:
                return
            self.seen_d[eng][sid] = val
            waits.append(tok)

    def _deps(self, eng, reads, writes):
        waits = []
        for r in reads:
            self._need(eng, self._bs(r)["w"], waits)
        for w in writes:
            s = self._bs(w)
            self._need(eng, s["w"], waits)
            for pe, idx in s["r_e"].items():
                self._need(eng, ("e", pe, idx), waits)
            for t in s["r_d"]:
                self._need(eng, t, waits)
        return waits

    def _commit(self, tok, reads, writes):
        for r in reads:
            s = self._bs(r)
            if tok[0] == "e":
                s["r_e"][tok[1]] = tok[2]
            else:
                s["r_d"].append(tok)
        for w in writes:
            s = self._bs(w)
            s["w"] = tok
            s["r_e"] = {}
            s["r_d"] = []

    def op(self, eng, fn, reads=(), writes=()):
        if eng != "pe":
            extra = [r for r in reads if r.startswith("ps") and r not in writes]
            if extra:
                writes = list(writes) + extra
        waits = self._deps(eng, reads, writes)
        idx = len(self.q[eng])
        self.q[eng].append({"fn": fn, "waits": waits, "flag": False, "dma": None})
        self._commit(("e", eng, idx), reads, writes)

    def dma(self, eng, out, in_, reads=(), writes=()):
        waits = self._deps(eng, reads, writes)
        slot = self.ring_next[eng]
        self.ring_next[eng] = (slot + 1) % self.DMA_RING
        self._need(eng, self.ring_last[eng][slot], waits)
        self.ring_uses[eng][slot] += 1
        tok = ("d", (eng, slot), 16 * self.ring_uses[eng][slot])
        self.ring_last[eng][slot] = tok
        self.all_dma.append(tok)
        self.q[eng].append({"fn": lambda e: e.dma_start(out=out, in_=in_), "waits": waits,
                            "flag": False, "dma": (eng, slot)})
        self._commit(tok, reads, writes)

    def barrier(self):
        lasts = {e: len(self.q[e]) - 1 for e in self.ENGS}
        dmas = []
        for e in self.ENGS:
            dmas += [t for t in self.ring_last[e] if t is not None]
        for e in self.ENGS:
            waits = []
            for pe, idx in lasts.items():
                if pe == e:
                    continue
                j = idx
                while j >= 0 and (self.q[pe][j]["dma"] is not None or self.q[pe][j]["fn"] is None):
                    j -= 1
                if j >= 0:
                    self._need(e, ("e", pe, j), waits)
            for t in dmas:
                self._need(e, t, waits)
            self.q[e].append({"fn": None, "waits": waits, "flag": False, "dma": None})
        self.buf = {}

    def emit(self, es):
        nc = self.nc
        esem = {e: es.enter_context(nc.semaphore("s_" + e)) for e in self.ENGS}
        dsem = {}
        for e in self.ENGS:
            for s in range(self.DMA_RING):
                if self.ring_uses[e][s]:
                    dsem[(e, s)] = es.enter_context(nc.semaphore("d_%s%d" % (e, s)))
        cnt = {}
        for e in self.ENGS:
            c = 0
            arr = []
            for r in self.q[e]:
                if r["flag"]:
                    c += 1
                arr.append(c)
            cnt[e] = arr
        handles = {"pe": "tensor", "act": "scalar", "dve": "vector", "pool": "gpsimd", "sp": "sync"}
        block = es.enter_context(nc.Block())

        def run(ename, e):
            for r in self.q[ename]:
                for t in r["waits"]:
                    if t[0] == "e":
                        e.wait_ge(esem[t[1]], cnt[t[1]][t[2]])
                    else:
                        e.wait_ge(dsem[t[1]], t[2])
                if r["fn"] is None:
                    continue
                ins = r["fn"](e)
                if r["dma"] is not None:
                    ins.then_inc(dsem[r["dma"]], 16)
                elif r["flag"]:
                    ins.then_inc(esem[ename], 1)

        for ename in self.ENGS:
            getattr(block, handles[ename])(lambda e, ename=ename: run(ename, e))


class Arena:
    def __init__(self, t, width):
        self.t = t
        self.width = width
        self.off = 0
        self.marks = []

    def alloc(self, n, shape=None):
        assert self.off + n <= self.width, ("arena overflow", self.off, n, self.width)
        ap = self.t[:, self.off:self.off + n]
        self.off += n
        if shape is not None:
            names = "abcdef"[:len(shape)]
            ap = ap.rearrange("p (%s) -> p %s" % (" ".join(names), " ".join(names)),
                              **{nm: s for nm, s in zip(names, shape)})
        return ap

    def mark(self):
        self.marks.append(self.off)

    def release(self):
        self.off = self.marks.pop()


class Stream:
    def __init__(self):
        self.ops = []

    def op(self, eng, fn, reads=(), writes=()):
        self.ops.append(("op", eng, fn, tuple(reads), tuple(writes)))

    def dma(self, eng, out, in_, reads=(), writes=()):
        self.ops.append(("dma", eng, out, in_, tuple(reads), tuple(writes)))


def merge_streams(P, streams):
    idx = [0] * len(streams)
    while True:
        best, bf = -1, 2.0
        pending = False
        for i, s in enumerate(streams):
            if idx[i] < len(s.ops):
                pending = True
                o = s.ops[idx[i]]
                rds = o[3] if o[0] == "op" else o[4]
                if any(r.startswith("mixR") and (r not in P.buf or P.buf[r]["w"] is None) for r in rds) and len(streams) > 1:
                    continue
                f = idx[i] / len(s.ops)
                if f < bf:
                    best, bf = i, f
        if best < 0:
            assert not pending, "merge deadlock"
            break
        o = streams[best].ops[idx[best]]
        idx[best] += 1
        if o[0] == "op":
            P.op(o[1], o[2], reads=o[3], writes=o[4])
        else:
            P.dma(o[1], o[2], o[3], reads=o[4], writes=o[5])


def build_program(S=SEQ, L=DEPTH, debug=False, phases=("A", "F"), final_norm=True, stop=99):
    T = S // 128
    NSB = S // 512
    nc = bass.Bass("TRN2", target_bir_lowering=False)
    es = ExitStack()
    dkind = "ExternalOutput" if debug else "Internal"

    def din(name, shape, dt=F32):
        return nc.dram_tensor(name, list(shape), dt, kind="ExternalInput").ap()

    x_in = din("x", [S, D_MODEL])
    norm_w = din("norm_w", [L, D_MODEL])
    w_in = din("w_in", [L, D_MODEL, IN_W])
    w_out = din("w_out", [L, D_MODEL, D_MODEL])
    final_norm_w = din("final_norm_w", [1, D_MODEL])
    ident_d = din("c_ident", [128, 128])
    ret_gn_w = din("ret_gn_w", [L, 256])
    ssm_norm_w = din("ssm_norm_w", [L, 256])
    conv_wT = din("conv_wT", [L, 128, 24])
    conv_bT = din("conv_bT", [L, 128, 6])
    ssm_vec = din("ssm_vec", [L, 12])
    c_ret_decT = din("c_ret_decT", [128, 512])
    c_ret_xiT = din("c_ret_xiT", [128, 256])
    c_ret_zeta = din("c_ret_zeta", [128, 256])
    c_ret_cd = din("c_ret_cd", [128, 4])
    c_triu = din("c_triu", [128, 128])
    mixed_d = nc.dram_tensor("mixed_d", [S, D_MODEL], BF16, kind=dkind).ap()
    NC = (S - 32) // 16 + 1
    NBK = (NC + 127) // 128
    NS = S // 64
    nsa_peT = din("nsa_peT", [L, 128, 32])
    w_ck1 = din("nsa_w_ck1", [L, 2048, 256])
    w_cv1 = din("nsa_w_cv1", [L, 2048, 256])
    w_ck2 = din("nsa_w_ck2", [L, 256, 64])
    w_cv2 = din("nsa_w_cv2", [L, 256, 64])
    c_gneg = din("c_gneg", [128, 4096], BF16)
    c_ex2 = din("c_ex2", [128, T * 128], BF16)
    c_low01 = din("c_low01", [128, 128], BF16)
    c_ovl = din("c_ovl", [128, 2 * 64], BF16)
    c_alibi_cmp = din("c_alibi_cmp", [128, 4 * 2 * 32])
    c_topk_add = din("c_topk_add", [128, T * 64])
    diff_vec = din("diff_vec", [L, 192])
    c_alibi = din("c_alibi", [128, 2 * 4 * 35])
    c_tri01 = din("c_tri01", [128, 128])
    y_out = nc.dram_tensor("y", [S, D_MODEL], F32, kind="ExternalOutput").ap()
    xres = nc.dram_tensor("xres", [S, D_MODEL], F32, kind=dkind).ap()
    fm_d = nc.dram_tensor("fm", [FMW, S], BF16, kind=dkind).ap()
    tm_d = nc.dram_tensor("tm", [S, TMW], BF16, kind=dkind).ap()

    A32W = 14000
    A16W = 66000
    a32 = Arena(es.enter_context(nc.sbuf_tensor("a32", [128, A32W], F32)), A32W)
    a16 = Arena(es.enter_context(nc.sbuf_tensor("a16", [128, A16W], BF16)), A16W)
    pbig = es.enter_context(nc.psum_tensor("pbig", [128, 8 * 512], F32))
    psb = [pbig[:, i * 512:(i + 1) * 512] for i in range(8)]

    P = Prog(nc)

    ident_f = a32.alloc(128)
    ident_b = a16.alloc(128)
    small = a32.alloc(T * 16, (T, 16))
    P.dma("sp", ident_f, ident_d, writes=["ident_f"])
    P.op("dve", lambda e: e.tensor_copy(out=ident_b, in_=ident_f), reads=["ident_f"], writes=["ident_b"])

    cp_rr = [0]

    def evac(out, in_, reads, writes, force=None):
        cp_rr[0] ^= 1
        if force == "act" or (force is None and cp_rr[0]):
            P.op("act", lambda e: e.copy(out=out, in_=in_), reads=reads, writes=writes)
        else:
            P.op("dve", lambda e: e.tensor_copy(out=out, in_=in_), reads=reads, writes=writes)

    def rms_rstd(xt, ss, junk, rstd, key):
        P.op("dve", lambda e: e.scalar_tensor_tensor(out=junk, in0=xt, scalar=1.0, in1=xt,
                                                     op0=ALU.mult, op1=ALU.mult, accum_out=ss),
             reads=[key + "x"], writes=[key + "junk", key + "ss"])
        P.op("dve", lambda e: e.tensor_scalar(out=ss, in0=ss, scalar1=1.0 / D_MODEL, scalar2=EPS,
                                              op0=ALU.mult, op1=ALU.add),
             reads=[key + "ss"], writes=[key + "ss"])
        P.op("act", lambda e: e.activation(out=ss, in_=ss, func=AF.Sqrt), reads=[key + "ss"], writes=[key + "ss"])
        P.op("dve", lambda e: e.reciprocal(out=rstd, in_=ss), reads=[key + "ss"], writes=[key + "rstd"])

    def phase_A(l):
        x_src = x_in if l == 0 else xres
        a32.mark()
        a16.mark()
        w_sb = a16.alloc(8 * WSBW, (8, WSBW))
        stage = [a32.alloc(2048) for _ in range(4)]
        nwb = a32.alloc(D_MODEL)
        xts = [a32.alloc(D_MODEL) for _ in range(2)]
        junk = a16.alloc(D_MODEL)
        sss = [a32.alloc(1) for _ in range(2)]
        rstds = [a32.alloc(1) for _ in range(2)]
        hbs = [a16.alloc(D_MODEL) for _ in range(2)]
        hTs = [a16.alloc(8 * 512, (8, 512)) for _ in range(2)]
        tmts = [a16.alloc(TMW) for _ in range(2)]
        fmts = [a16.alloc(512) for _ in range(4)]

        P.dma("sp", nwb, norm_w[l:l + 1, :].partition_broadcast(128), writes=["nwb"])
        pieces = []
        for (dc, sc, wd) in W_PIECES:
            if sc < 2048 < sc + wd:
                pieces.append((dc, sc, 2048 - sc))
                pieces.append((dc + 2048 - sc, 2048, wd - (2048 - sc)))
            else:
                pieces.append((dc, sc, wd))
        for kc in range(8):
            sp_ = (kc % 2) * 2
            P.dma("sp", stage[sp_], w_in[l, kc * 128:(kc + 1) * 128, 0:2048], writes=["stage%d" % sp_])
            P.dma("sp", stage[sp_ + 1][:, 0:IN_W - 2048], w_in[l, kc * 128:(kc + 1) * 128, 2048:IN_W],
                  writes=["stage%d" % (sp_ + 1)])
            for pi_, (dc, sc, wd) in enumerate(pieces):
                hf = 0 if sc < 2048 else 1
                si = sp_ + hf
                ce = ("pool", "dve", "act")[(pi_ + kc) % 3]
                if ce == "act":
                    P.op("act", lambda e, kc=kc, si=si, hf=hf, dc=dc, sc=sc, wd=wd:
                         e.copy(out=w_sb[:, kc, dc:dc + wd], in_=stage[si][:, sc - 2048 * hf:sc - 2048 * hf + wd]),
                         reads=["stage%d" % si], writes=["w_sb%d" % (pi_ % 4)])
                else:
                    P.op(ce, lambda e, kc=kc, si=si, hf=hf, dc=dc, sc=sc, wd=wd:
                         e.tensor_copy(out=w_sb[:, kc, dc:dc + wd], in_=stage[si][:, sc - 2048 * hf:sc - 2048 * hf + wd]),
                         reads=["stage%d" % si], writes=["w_sb%d" % (pi_ % 4)])

        fm_i = 0
        if stop <= 1:
            P.barrier(); a32.release(); a16.release(); return
        def a_stage1(t):
            j = t % 2
            xt, ss, rstd, hb = xts[j], sss[j], rstds[j], hbs[j]
            k = "A%d_" % j
            P.dma("sp", xt, x_src[t * 128:(t + 1) * 128, :],
                  reads=["xres_t%d" % t], writes=[k + "x"])
            rms_rstd(xt, ss, junk, rstd, k)
            P.op("dve", lambda e, xt=xt, rstd=rstd, hb=hb:
                 e.scalar_tensor_tensor(out=hb, in0=xt, scalar=rstd, in1=nwb, op0=ALU.mult, op1=ALU.mult),
                 reads=[k + "x", k + "rstd", "nwb"], writes=[k + "hb"])

        a_stage1(0)
        for sb in range(NSB):
            hT = hTs[sb % 2]
            hk = "hT%d" % (sb % 2)
            for ti in range(4):
                t = sb * 4 + ti
                j = t % 2
                xt, ss, rstd, hb, tmt = xts[j], sss[j], rstds[j], hbs[j], tmts[j]
                k = "A%d_" % j
                pst = psb[j]
                pstb = pst[:, 0:512].bitcast(BF16)
                for kc in range(8):
                    P.op("pe", lambda e, kc=kc, hb=hb, pstb=pstb:
                         e.transpose(out=pstb[:, kc * 128:(kc + 1) * 128], in_=hb[:, kc * 128:(kc + 1) * 128],
                                     identity=ident_b),
                         reads=[k + "hb", "ident_b"], writes=["ps%d" % j])
                evac(hT[:, :, ti * 128:(ti + 1) * 128], pstb.rearrange("p (k t) -> p k t", k=8),
                     reads=["ps%d" % j], writes=[hk])
                if t + 1 < T:
                    a_stage1(t + 1)
                if stop <= 2:
                    continue
                for c in range(4):
                    c0 = c * 512
                    n = min(512, TMW - c0)
                    pi = 2 + (t * 4 + c) % 3
                    ps = psb[pi]
                    for kc in range(8):
                        P.op("pe", lambda e, kc=kc, ps=ps, hT=hT, ti=ti, c0=c0, n=n:
                             e.matmul(ps[:, 0:n], lhsT=hT[:, kc, ti * 128:(ti + 1) * 128],
                                      rhs=w_sb[:, kc, FMW + c0:FMW + c0 + n], start=(kc == 0), stop=(kc == 7)),
                             reads=[hk, "w_sb0", "w_sb1", "w_sb2", "w_sb3"], writes=["ps%d" % pi])
                    evac(tmt[:, c0:c0 + n], ps[:, 0:n], reads=["ps%d" % pi], writes=[k + "tm"],
                         force=("act" if c in (0, 3) else None))
                    if c == 0 and stop != 31:
                        P.op("act", lambda e, ps=ps, t=t:
                             e.copy(out=small[:, t, 0:12], in_=ps[:, TM_GATE:TM_GATE + 12]),
                             reads=["ps%d" % pi], writes=["small"])
                    if c == 3 and stop != 31:
                        P.op("act", lambda e, ps=ps, t=t, c0=c0:
                             e.copy(out=small[:, t, 12:16], in_=ps[:, TM_DT - c0:TM_DT - c0 + 4]),
                             reads=["ps%d" % pi], writes=["small"])
                if stop != 32:
                    P.dma(STORE_Q, tm_d[t * 128:(t + 1) * 128, :], tmt, reads=[k + "tm"], writes=["tm_t%d" % t])
            for r in range(NFM if stop > 3 else 0):
                pi = 5 + r % 3
                ps = psb[pi]
                for kc in range(8):
                    P.op("pe", lambda e, kc=kc, ps=ps, hT=hT, r=r:
                         e.matmul(ps[:, :], lhsT=w_sb[:, kc, r * 128:(r + 1) * 128], rhs=hT[:, kc, :],
                                  start=(kc == 0), stop=(kc == 7)),
                         reads=[hk, "w_sb0", "w_sb1", "w_sb2", "w_sb3"], writes=["ps%d" % pi])
                fmt = fmts[fm_i % 4]
                fk = "fmt%d" % (fm_i % 4)
                fm_i += 1
                evac(fmt, ps[:, :], reads=["ps%d" % pi], writes=[fk])
                P.dma(STORE_Q, fm_d[r * 128:(r + 1) * 128, sb * 512:(sb + 1) * 512], fmt,
                      reads=[fk], writes=["fm_r%d" % r])
        P.barrier()
        a32.release()
        a16.release()

    def load_bcast(dst, src_row, key):
        P.dma("sp", dst, src_row.partition_broadcast(128), writes=[key])

    def head_norm_stats(o_sb, H, Dh, s1, s2, sq, key):
        o3 = o_sb.rearrange("p (h e) -> p h e", h=H)
        P.op("dve", lambda e: e.tensor_reduce(out=s1, in_=o3, axis=AX.X, op=ALU.add),
             reads=[key + "o"], writes=[key + "s1"])
        P.op("dve", lambda e: e.tensor_tensor(out=sq, in0=o_sb, in1=o_sb, op=ALU.mult),
             reads=[key + "o"], writes=[key + "sq"])
        P.op("dve", lambda e: e.tensor_reduce(out=s2, in_=sq.rearrange("p (h e) -> p h e", h=H), axis=AX.X, op=ALU.add),
             reads=[key + "sq"], writes=[key + "s2"])

    def phase_ret(l):
        decT = a32.alloc(512)
        xiT = a32.alloc(256, (2, 128))
        zeta = a32.alloc(256)
        cdt = a32.alloc(4)
        gnw = a32.alloc(256)
        state = a32.alloc(128, (2, 64))
        state_b = a16.alloc(128, (2, 64))
        P.dma("sp", decT, c_ret_decT, writes=["decT"])
        P.dma("sp", xiT, c_ret_xiT.rearrange("p (r q) -> p r q", r=2), writes=["xiT"])
        P.dma("sp", zeta, c_ret_zeta, writes=["zeta"])
        P.dma("sp", cdt, c_ret_cd, writes=["cdt"])
        load_bcast(gnw, ret_gn_w[l:l + 1, :], "gnw")
        P.op("dve", lambda e: e.memset(state, 0.0), writes=["state"])
        P.op("dve", lambda e: e.memset(state_b, 0.0), writes=["state_b"])
        NB = 2
        qTs = [a16.alloc(256, (2, 128)) for _ in range(NB)]
        kTs = [a16.alloc(256, (2, 128)) for _ in range(NB)]
        tms = [a16.alloc(768) for _ in range(NB)]
        ATs = [a16.alloc(512) for _ in range(NB)]
        qxs = [a16.alloc(256, (2, 128)) for _ in range(NB)]
        kzs = [a16.alloc(256) for _ in range(NB)]
        osb = [a32.alloc(256) for _ in range(NB)]
        sqs = [a32.alloc(256) for _ in range(NB)]
        szs = [a32.alloc(256) for _ in range(NB)]
        st1 = [a32.alloc(4) for _ in range(NB)]
        st2 = [a32.alloc(4) for _ in range(NB)]
        mos = [a16.alloc(256) for _ in range(NB)]
        for t in range(T):
            j = t % NB
            k = "R%d_" % j
            qT, kT, tm, AT, qx, kz, o_sb, sq, sz, s1, s2 = (qTs[j], kTs[j], tms[j], ATs[j], qxs[j], kzs[j],
                                                          osb[j], sqs[j], szs[j], st1[j], st2[j])
            c0 = t * 128
            P.dma("sp", qT, fm_d[FM_RQ * 128:(FM_RQ + 2) * 128, c0:c0 + 128].rearrange("(r p) s -> p r s", p=128),
                  writes=[k + "qT"])
            P.dma("sp", kT, fm_d[FM_RK * 128:(FM_RK + 2) * 128, c0:c0 + 128].rearrange("(r p) s -> p r s", p=128),
                  writes=[k + "kT"])
            P.dma("sp", tm, tm_d[c0:c0 + 128, TM_RK:TM_RK + 768], writes=[k + "tm"])
            ps_sg = [psb[6][:, 0:256], psb[7][:, 0:256]]
            ps_xg = [psb[6][:, 256:384], psb[7][:, 256:384]]
            ps_o, ps_kv = psb[6][:, 0:256], psb[7][:, 0:256]
            ks_o, ks_kv = "ps6", "ps7"
            yield
            for h in range(4):
                hp, hr, g = (h % 2) * 64, h // 2, h % 2
                P.op("pe", lambda e, hp=hp, hr=hr, g=g, kT=kT, qT=qT:
                     e.matmul(ps_sg[g][:, hr * 128:(hr + 1) * 128], lhsT=kT[hp:hp + 64, hr, :], rhs=qT[hp:hp + 64, hr, :],
                              start=True, stop=True),
                     reads=[k + "qT", k + "kT"], writes=["ps%d" % (6 + g)])
            yield
            for g in range(2):
                P.op("dve", lambda e, g=g, AT=AT: e.tensor_tensor(out=AT[:, g * 256:(g + 1) * 256], in0=ps_sg[g][:, 0:256],
                                                                  in1=decT[:, g * 256:(g + 1) * 256], op=ALU.mult),
                     reads=["ps%d" % (6 + g), "decT"], writes=[k + "AT"])
            P.op("dve", lambda e, qx=qx, qT=qT: e.tensor_tensor(out=qx, in0=qT, in1=xiT, op=ALU.mult),
                 reads=[k + "qT", "xiT"], writes=[k + "qx"])
            P.op("dve", lambda e, kz=kz, tm=tm: e.tensor_tensor(out=kz, in0=tm[:, 0:256], in1=zeta, op=ALU.mult),
                 reads=[k + "tm", "zeta"], writes=[k + "kz"])
            for h in range(4):
                hp, hr, g = (h % 2) * 64, h // 2, h % 2
                ai = g * 2 + hr
                P.op("pe", lambda e, h=h, ai=ai, ps_o=ps_o, AT=AT, tm=tm:
                     e.matmul(ps_o[:, h * 64:(h + 1) * 64], lhsT=AT[:, ai * 128:(ai + 1) * 128],
                              rhs=tm[:, 256 + h * 64:256 + (h + 1) * 64], start=True, stop=True),
                     reads=[k + "AT", k + "tm"], writes=[ks_o])
                if t > 0:
                    P.op("pe", lambda e, hp=hp, hr=hr, g=g, qx=qx:
                         e.matmul(ps_xg[g][:, hr * 64:(hr + 1) * 64], lhsT=qx[hp:hp + 64, hr, :],
                                  rhs=state_b[hp:hp + 64, hr, :], start=True, stop=True),
                         reads=[k + "qx", "state_b"], writes=["ps%d" % (6 + g)])
            yield
            if t < T - 1:
                for r in range(2):
                    P.op("pe", lambda e, r=r, ps_kv=ps_kv, kz=kz, tm=tm:
                         e.matmul(ps_kv[:, r * 128:(r + 1) * 128], lhsT=kz[:, r * 128:(r + 1) * 128],
                                  rhs=tm[:, 256 + r * 128:256 + (r + 1) * 128], start=True, stop=True),
                         reads=[k + "kz", k + "tm"], writes=[ks_kv])
                for h in range(4):
                    hp, hr = (h % 2) * 64, h // 2
                    P.op("dve", lambda e, h=h, hp=hp, hr=hr, ps_kv=ps_kv:
                         e.scalar_tensor_tensor(out=state[hp:hp + 64, hr, :], in0=state[hp:hp + 64, hr, :],
                                                scalar=cdt[hp:hp + 64, h:h + 1],
                                                in1=ps_kv[hp:hp + 64, hr * 128 + (h % 2) * 64:hr * 128 + (h % 2) * 64 + 64],
                                                op0=ALU.mult, op1=ALU.add),
                         reads=[ks_kv, "cdt", "state"], writes=["state"])
                P.op("dve", lambda e: e.tensor_copy(out=state_b, in_=state), reads=["state"], writes=["state_b"])
            yield
            P.op("act", lambda e, o_sb=o_sb, ps_o=ps_o: e.copy(out=o_sb, in_=ps_o[:, 0:256]),
                 reads=[ks_o], writes=[k + "o"])
            if t > 0:
                for h in range(4):
                    hr, g = h // 2, h % 2
                    P.op("dve", lambda e, h=h, hr=hr, g=g, o_sb=o_sb:
                         e.tensor_tensor(out=o_sb[:, h * 64:(h + 1) * 64], in0=ps_xg[g][:, hr * 64:(hr + 1) * 64],
                                         in1=o_sb[:, h * 64:(h + 1) * 64], op=ALU.add),
                         reads=["ps%d" % (6 + g), k + "o"], writes=[k + "o"])
            P.op("act", lambda e, sz=sz, tm=tm: e.activation(out=sz, in_=tm[:, 512:768], func=AF.Silu),
                 reads=[k + "tm"], writes=[k + "sz"])
            head_norm_stats(o_sb, 4, 64, s1, s2, sq, k)
            P.op("dve", lambda e, s1=s1: e.tensor_scalar(out=s1, in0=s1, scalar1=1.0 / 64, scalar2=None, op0=ALU.mult),
                 reads=[k + "s1"], writes=[k + "s1"])
            P.op("dve", lambda e, s1=s1, s2=s2, sq=sq: e.tensor_tensor(out=sq[:, 0:4], in0=s1, in1=s1, op=ALU.mult),
                 reads=[k + "s1"], writes=[k + "sq"])
            P.op("dve", lambda e, s2=s2, sq=sq: e.scalar_tensor_tensor(out=s2, in0=s2, scalar=1.0 / 64, in1=sq[:, 0:4],
                                                                       op0=ALU.mult, op1=ALU.subtract),
                 reads=[k + "s2", k + "sq"], writes=[k + "s2"])
            P.op("dve", lambda e, s2=s2: e.tensor_scalar(out=s2, in0=s2, scalar1=EPS, scalar2=None, op0=ALU.add),
                 reads=[k + "s2"], writes=[k + "s2"])
            P.op("act", lambda e, s2=s2: e.activation(out=s2, in_=s2, func=AF.Sqrt), reads=[k + "s2"], writes=[k + "s2"])
            P.op("dve", lambda e, s2=s2: e.reciprocal(out=s2, in_=s2), reads=[k + "s2"], writes=[k + "s2"])
            for h in range(4):
                P.op("dve", lambda e, h=h, o_sb=o_sb, s1=s1, s2=s2:
                     e.tensor_scalar(out=o_sb[:, h * 64:(h + 1) * 64], in0=o_sb[:, h * 64:(h + 1) * 64],
                                     scalar1=s1[:, h:h + 1], scalar2=s2[:, h:h + 1], op0=ALU.subtract, op1=ALU.mult),
                     reads=[k + "o", k + "s1", k + "s2"], writes=[k + "o"])
            P.op("dve", lambda e, sz=sz: e.tensor_tensor(out=sz, in0=sz, in1=gnw, op=ALU.mult),
                 reads=[k + "sz", "gnw"], writes=[k + "sz"])
            mo = mos[j]
            P.op("dve", lambda e, mo=mo, o_sb=o_sb, sz=sz:
                 e.tensor_tensor(out=mo, in0=o_sb, in1=sz, op=ALU.mult),
                 reads=[k + "o", k + "sz"], writes=[k + "mo"])
            P.dma(STORE_Q, mixed_d[t * 128:(t + 1) * 128, 512:768], mo, reads=[k + "mo"], writes=["mixR%d" % t])
            yield

    def phase_ssd(l):
        triu = a32.alloc(128)
        ones_f = a32.alloc(128)
        cw = a32.alloc(24)
        cb = a32.alloc(6)
        vec = a32.alloc(12)
        snw = a32.alloc(256)
        dtt = a32.alloc(T * 4)
        dAt = a32.alloc(T * 4)
        cst = a32.alloc(T * 4)
        ecs = a32.alloc(T * 4)
        dsd = a32.alloc(T * 4)
        ecl = a32.alloc(T * 4)
        Sst = a32.alloc(256, (4, 64))
        Sst_b = a16.alloc(256, (4, 64))
        maskT = a32.alloc(128)
        P.dma("sp", triu, c_triu, writes=["triu"])
        P.dma("sp", cw, conv_wT[l], writes=["cw"])
        P.dma("sp", cb, conv_bT[l], writes=["cb"])
        load_bcast(vec, ssm_vec[l:l + 1, :], "vec")
        load_bcast(snw, ssm_norm_w[l:l + 1, :], "snw")
        P.op("dve", lambda e: e.memset(ones_f, 1.0), writes=["ones_f"])
        P.op("dve", lambda e: e.memset(Sst, 0.0), writes=["Sst"])
        P.op("dve", lambda e: e.memset(Sst_b, 0.0), writes=["Sst_b"])
        P.op("dve", lambda e: e.tensor_copy(out=maskT, in_=triu), reads=["triu"], writes=["maskT"])
        sm3 = small[:, :, 12:16]
        dt3 = dtt.rearrange("p (t h) -> p t h", h=4)
        dA3 = dAt.rearrange("p (t h) -> p t h", h=4)
        for h in range(4):
            P.op("dve", lambda e, h=h: e.tensor_scalar(out=dt3[:, :, h], in0=sm3[:, :, h], scalar1=vec[:, h:h + 1],
                                                       scalar2=None, op0=ALU.add),
                 reads=["small", "vec"], writes=["dtt"])
        P.op("act", lambda e: e.activation(out=dtt, in_=dtt, func=AF.Exp), reads=["dtt"], writes=["dtt"])
        P.op("act", lambda e: e.activation(out=dtt, in_=dtt, func=AF.Ln, bias=1.0), reads=["dtt"], writes=["dtt"])
        P.op("act", lambda e: e.activation(out=vec[:, 4:8], in_=vec[:, 4:8], func=AF.Exp), reads=["vec"], writes=["vec"])
        for h in range(4):
            P.op("dve", lambda e, h=h: e.tensor_scalar(out=dA3[:, :, h], in0=dt3[:, :, h], scalar1=vec[:, 4 + h:5 + h],
                                                       scalar2=-1.0, op0=ALU.mult, op1=ALU.mult),
                 reads=["dtt", "vec"], writes=["dAt"])
        psA, psB = psb[6], psb[7]
        P.op("pe", lambda e: e.matmul(psA[:, 0:T * 4], lhsT=triu, rhs=dAt, start=True, stop=True),
             reads=["triu", "dAt"], writes=["ps6"])
        P.op("pe", lambda e: e.matmul(psB[:, 0:T * 4], lhsT=ones_f, rhs=dAt, start=True, stop=True),
             reads=["ones_f", "dAt"], writes=["ps7"])
        P.op("act", lambda e: e.copy(out=cst, in_=psA[:, 0:T * 4]), reads=["ps6"], writes=["cst"])
        P.op("act", lambda e: e.activation(out=ecs, in_=cst, func=AF.Exp), reads=["cst"], writes=["ecs"])
        P.op("act", lambda e: e.activation(out=ecl, in_=psB[:, 0:T * 4], func=AF.Exp), reads=["ps7"], writes=["ecl"])
        P.op("dve", lambda e: e.tensor_tensor(out=dsd, in0=psB[:, 0:T * 4], in1=cst, op=ALU.subtract),
             reads=["ps7", "cst"], writes=["dsd"])
        P.op("act", lambda e: e.activation(out=dsd, in_=dsd, func=AF.Exp), reads=["dsd"], writes=["dsd"])
        P.op("dve", lambda e: e.tensor_tensor(out=dsd, in0=dsd, in1=dtt, op=ALU.mult),
             reads=["dsd", "dtt"], writes=["dsd"])

        xin = [a16.alloc(6 * 516, (6, 516)) for _ in range(2)]
        acc = a32.alloc(512)
        thb = a32.alloc(512)
        cbh = a32.alloc(6)
        P.op("dve", lambda e: e.tensor_scalar(out=cbh, in0=cb, scalar1=0.5, scalar2=None, op0=ALU.mult),
             reads=["cb"], writes=["cbh"])
        cvo = [a16.alloc(6 * 512, (6, 512)) for _ in range(2)]
        NB = 2
        xtm = [a32.alloc(256) for _ in range(NB)]
        btm = [a16.alloc(256) for _ in range(NB)]
        xdt = [a16.alloc(256) for _ in range(NB)]
        xdd = [a16.alloc(256) for _ in range(NB)]
        cbm = [a32.alloc(256) for _ in range(NB)]
        uda = [a32.alloc(128) for _ in range(NB)]
        dif = [a32.alloc(128) for _ in range(NB)]
        MTs = [a16.alloc(512) for _ in range(NB)]
        ysb = [a32.alloc(256) for _ in range(NB)]
        zts = [a16.alloc(256) for _ in range(NB)]
        szs = [a32.alloc(256) for _ in range(NB)]
        sqs = [a32.alloc(256) for _ in range(NB)]
        sss = [a32.alloc(1) for _ in range(NB)]
        mos = [a16.alloc(256) for _ in range(NB)]
        for sb in range(NSB):
            xi = xin[sb % 2]
            co = cvo[sb % 2]
            kx = "xin%d" % (sb % 2)
            kc_ = "cvo%d" % (sb % 2)
            src = fm_d[FM_XBC * 128:(FM_XBC + 6) * 128, :].rearrange("(r p) s -> p r s", p=128)
            if sb == 0:
                P.op("pool", lambda e, xi=xi: e.memset(xi[:, :, 0:3], 0.0), writes=[kx])
                P.dma("sp", xi[:, :, 3:515], src[:, :, 0:512], writes=[kx])
            else:
                P.dma("sp", xi[:, :, 0:515], src[:, :, sb * 512 - 3:sb * 512 + 512], writes=[kx])
            for r in range(6):
                for jj in range(4):
                    if jj == 0:
                        P.op("pool", lambda e, r=r, xi=xi: e.tensor_scalar(
                            out=acc, in0=xi[:, r, 0:512], scalar1=cw[:, r * 4:r * 4 + 1], scalar2=1.0, op0=ALU.mult,
                            op1=ALU.mult),
                            reads=[kx, "cw"], writes=["acc"])
                    else:
                        P.op("dve", lambda e, r=r, jj=jj, xi=xi: e.scalar_tensor_tensor(
                            out=acc, in0=xi[:, r, jj:jj + 512], scalar=cw[:, r * 4 + jj:r * 4 + jj + 1], in1=acc,
                            op0=ALU.mult, op1=ALU.add),
                            reads=[kx, "cw", "acc"], writes=["acc"])
                P.op("act", lambda e, r=r: e.activation(out=thb, in_=acc, func=AF.Tanh, bias=cbh[:, r:r + 1], scale=0.5),
                     reads=["acc", "cbh"], writes=["thb"])
                P.op("dve", lambda e: e.tensor_scalar(out=thb, in0=thb, scalar1=1.0, scalar2=0.5, op0=ALU.add, op1=ALU.mult),
                     reads=["thb"], writes=["thb"])
                P.op("dve", lambda e, r=r, co=co: e.scalar_tensor_tensor(out=co[:, r, :], in0=acc, scalar=cb[:, r:r + 1], in1=thb,
                                                                         op0=ALU.add, op1=ALU.mult),
                     reads=["acc", "cb", "thb"], writes=[kc_])
                yield
            for ti in range(4):
                t = sb * 4 + ti
                j = t % NB
                k = "S%d_" % j
                cs0 = ti * 128
                x_tm, b_tm, xd, xdd_, cbm_, ud, df, MT, y_sb, zt, sz, sq, ss = (
                    xtm[j], btm[j], xdt[j], xdd[j], cbm[j], uda[j], dif[j], MTs[j], ysb[j], zts[j], szs[j], sqs[j], sss[j])
                P.dma("sp", zt, tm_d[t * 128:(t + 1) * 128, TM_SZ:TM_SZ + 256], writes=[k + "z"])
                pTb = psb[6][:, 0:256].bitcast(BF16)
                for r in range(4):
                    P.op("pe", lambda e, r=r, pTb=pTb, co=co, cs0=cs0:
                         e.transpose(out=pTb[:, r * 128:(r + 1) * 128], in_=co[:, r, cs0:cs0 + 128], identity=ident_b),
                         reads=[kc_, "ident_b"], writes=["ps6"])
                yield
                P.op("act", lambda e, x_tm=x_tm, pTb=pTb: e.copy(out=x_tm, in_=pTb[:, 0:256]),
                     reads=["ps6"], writes=[k + "x"])
                P.op("act", lambda e, b_tm=b_tm, pTb=pTb: e.copy(out=b_tm, in_=pTb[:, 256:512]),
                     reads=["ps6"], writes=[k + "b"])
                for h in range(4):
                    P.op("dve", lambda e, h=h, xd=xd, x_tm=x_tm, t=t: e.tensor_scalar(
                        out=xd[:, h * 64:(h + 1) * 64], in0=x_tm[:, h * 64:(h + 1) * 64],
                        scalar1=dtt[:, t * 4 + h:t * 4 + h + 1], scalar2=None, op0=ALU.mult),
                        reads=[k + "x", "dtt"], writes=[k + "xd"])
                    P.op("dve", lambda e, h=h, xdd_=xdd_, x_tm=x_tm, t=t: e.tensor_scalar(
                        out=xdd_[:, h * 64:(h + 1) * 64], in0=x_tm[:, h * 64:(h + 1) * 64],
                        scalar1=dsd[:, t * 4 + h:t * 4 + h + 1], scalar2=None, op0=ALU.mult),
                        reads=[k + "x", "dsd"], writes=[k + "xdd"])
                yield
                pcb = psb[6][:, 256:512]
                kcb = "ps6"
                for g in range(2):
                    P.op("pe", lambda e, g=g, pcb=pcb, co=co, cs0=cs0:
                         e.matmul(pcb[:, g * 128:(g + 1) * 128], lhsT=co[:, 2 + g, cs0:cs0 + 128],
                                  rhs=co[:, 4 + g, cs0:cs0 + 128], start=True, stop=True),
                         reads=[kc_], writes=[kcb])
                for g in range(2):
                    P.op("dve", lambda e, g=g, cbm_=cbm_, pcb=pcb:
                         e.tensor_tensor(out=cbm_[:, g * 128:(g + 1) * 128], in0=pcb[:, g * 128:(g + 1) * 128],
                                         in1=maskT, op=ALU.mult),
                         reads=[kcb, "maskT"], writes=[k + "cbm"])
                pcs = psb[7]
                kcs = "ps7"
                for h in range(4):
                    yield
                    P.op("dve", lambda e, h=h, ud=ud, t=t: e.tensor_scalar(
                        out=ud, in0=triu, scalar1=dAt[:, t * 4 + h:t * 4 + h + 1], scalar2=None, op0=ALU.mult),
                        reads=["triu", "dAt"], writes=[k + "ud"])
                    P.op("pe", lambda e, h=h, pcs=pcs, ud=ud:
                         e.matmul(pcs[:, h * 128:(h + 1) * 128], lhsT=ones_f, rhs=ud, start=True, stop=True),
                         reads=["ones_f", k + "ud"], writes=[kcs])
                    P.op("dve", lambda e, h=h, df=df, pcs=pcs, t=t: e.tensor_scalar(
                        out=df, in0=pcs[:, h * 128:(h + 1) * 128], scalar1=cst[:, t * 4 + h:t * 4 + h + 1],
                        scalar2=0.0, op0=ALU.subtract, op1=ALU.min),
                        reads=[kcs, "cst"], writes=[k + "df"])
                    P.op("act", lambda e, df=df: e.activation(out=df, in_=df, func=AF.Exp),
                         reads=[k + "df"], writes=[k + "df"])
                    P.op("dve", lambda e, h=h, MT=MT, df=df, cbm_=cbm_: e.tensor_tensor(
                        out=MT[:, h * 128:(h + 1) * 128], in0=df, in1=cbm_[:, (h // 2) * 128:(h // 2 + 1) * 128],
                        op=ALU.mult),
                        reads=[k + "df", k + "cbm"], writes=[k + "MT"])
                yield
                py, pyo, pst = psb[6][:, 0:256], psb[6][:, 256:512], psb[7][:, 0:256]
                for h in range(4):
                    P.op("pe", lambda e, h=h, py=py, MT=MT, xd=xd:
                         e.matmul(py[:, h * 64:(h + 1) * 64], lhsT=MT[:, h * 128:(h + 1) * 128],
                                  rhs=xd[:, h * 64:(h + 1) * 64], start=True, stop=True),
                         reads=[k + "MT", k + "xd"], writes=["ps6"])
                if t > 0:
                    for h in range(4):
                        P.op("pe", lambda e, h=h, pyo=pyo, co=co, cs0=cs0:
                             e.matmul(pyo[:, h * 64:(h + 1) * 64], lhsT=co[:, 4 + h // 2, cs0:cs0 + 128],
                                      rhs=Sst_b[:, h, :], start=True, stop=True),
                             reads=[kc_, "Sst_b"], writes=["ps6"])
                yield
                P.op("act", lambda e, y_sb=y_sb, py=py: e.copy(out=y_sb, in_=py[:, 0:256]),
                     reads=["ps6"], writes=[k + "y"])
                for h in range(4):
                    if t > 0:
                        P.op("dve", lambda e, h=h, y_sb=y_sb, pyo=pyo, t=t: e.scalar_tensor_tensor(
                            out=y_sb[:, h * 64:(h + 1) * 64], in0=pyo[:, h * 64:(h + 1) * 64],
                            scalar=ecs[:, t * 4 + h:t * 4 + h + 1], in1=y_sb[:, h * 64:(h + 1) * 64],
                            op0=ALU.mult, op1=ALU.add),
                            reads=["ps6", "ecs", k + "y"], writes=[k + "y"])
                    P.op("dve", lambda e, h=h, y_sb=y_sb, x_tm=x_tm: e.scalar_tensor_tensor(
                        out=y_sb[:, h * 64:(h + 1) * 64], in0=x_tm[:, h * 64:(h + 1) * 64],
                        scalar=vec[:, 8 + h:9 + h], in1=y_sb[:, h * 64:(h + 1) * 64], op0=ALU.mult, op1=ALU.add),
                        reads=[k + "x", "vec", k + "y"], writes=[k + "y"])
                yield
                if t < T - 1:
                    kst = "ps7"
                    for h in range(4):
                        P.op("pe", lambda e, h=h, pst=pst, b_tm=b_tm, xdd_=xdd_:
                             e.matmul(pst[:, h * 64:(h + 1) * 64], lhsT=b_tm[:, (h // 2) * 128:(h // 2 + 1) * 128],
                                      rhs=xdd_[:, h * 64:(h + 1) * 64], start=True, stop=True),
                             reads=[k + "b", k + "xdd"], writes=[kst])
                    for h in range(4):
                        P.op("dve", lambda e, h=h, pst=pst, t=t: e.scalar_tensor_tensor(
                            out=Sst[:, h, :], in0=Sst[:, h, :], scalar=ecl[:, t * 4 + h:t * 4 + h + 1],
                            in1=pst[:, h * 64:(h + 1) * 64], op0=ALU.mult, op1=ALU.add),
                            reads=[kst, "ecl", "Sst"], writes=["Sst"])
                    P.op("dve", lambda e: e.tensor_copy(out=Sst_b, in_=Sst), reads=["Sst"], writes=["Sst_b"])
                yield
                P.op("act", lambda e, sz=sz, zt=zt: e.activation(out=sz, in_=zt, func=AF.Tanh, scale=0.5),
                     reads=[k + "z"], writes=[k + "sz"])
                P.op("dve", lambda e, sz=sz: e.tensor_scalar(out=sz, in0=sz, scalar1=1.0, scalar2=0.5, op0=ALU.add, op1=ALU.mult),
                     reads=[k + "sz"], writes=[k + "sz"])
                P.op("dve", lambda e, sz=sz, zt=zt: e.tensor_tensor(out=sz, in0=sz, in1=zt, op=ALU.mult),
                     reads=[k + "sz", k + "z"], writes=[k + "sz"])
                P.op("dve", lambda e, y_sb=y_sb, sz=sz: e.tensor_tensor(out=y_sb, in0=y_sb, in1=sz, op=ALU.mult),
                     reads=[k + "y", k + "sz"], writes=[k + "y"])
                P.op("dve", lambda e, y_sb=y_sb, sq=sq, ss=ss: e.scalar_tensor_tensor(
                    out=sq, in0=y_sb, scalar=1.0, in1=y_sb, op0=ALU.mult, op1=ALU.mult, accum_out=ss),
                    reads=[k + "y"], writes=[k + "sq", k + "ss"])
                P.op("dve", lambda e, ss=ss: e.tensor_scalar(out=ss, in0=ss, scalar1=1.0 / 256, scalar2=EPS,
                                                             op0=ALU.mult, op1=ALU.add),
                     reads=[k + "ss"], writes=[k + "ss"])
                P.op("act", lambda e, ss=ss: e.activation(out=ss, in_=ss, func=AF.Sqrt), reads=[k + "ss"], writes=[k + "ss"])
                P.op("dve", lambda e, ss=ss: e.reciprocal(out=ss, in_=ss), reads=[k + "ss"], writes=[k + "ss"])
                mo = mos[j]
                P.op("dve", lambda e, mo=mo, y_sb=y_sb, ss=ss: e.scalar_tensor_tensor(
                    out=mo, in0=y_sb, scalar=ss, in1=snw, op0=ALU.mult, op1=ALU.mult),
                    reads=[k + "y", k + "ss", "snw"], writes=[k + "mo"])
                P.dma(STORE_Q, mixed_d[t * 128:(t + 1) * 128, 768:1024], mo, reads=[k + "mo"], writes=["mixS%d" % t])
                yield

    def phase_diff(l):
        scale = 32 ** -0.5
        lam_init = 0.8 - 0.6 * math.exp(-0.3 * l)
        dslopes = [2.0 ** (-8.0 * (i + 1) / 8) for i in range(8)][1::2]
        alib = a32.alloc(2 * 4 * 35, (2, 4, 35))
        tri01 = a16.alloc(128)
        tri_f = a32.alloc(128)
        dv = a32.alloc(192)
        lamt = a32.alloc(4)
        sw = a32.alloc(256)
        P.dma("sp", alib, c_alibi.rearrange("p (s h d) -> p s h d", s=2, h=4), writes=["alib"])
        P.dma("sp", tri_f, c_tri01, writes=["tri_f"])
        P.op("dve", lambda e: e.tensor_copy(out=tri01, in_=tri_f), reads=["tri_f"], writes=["tri01"])
        load_bcast(dv, diff_vec[l:l + 1, :], "dv")
        P.op("dve", lambda e: e.scalar_tensor_tensor(out=dv[:, 0:32], in0=dv[:, 0:32], scalar=1.0, in1=dv[:, 32:64],
                                                     op0=ALU.mult, op1=ALU.mult, accum_out=lamt[:, 0:1]),
             reads=["dv"], writes=["dv", "lamt"])
        P.op("dve", lambda e: e.scalar_tensor_tensor(out=dv[:, 64:96], in0=dv[:, 64:96], scalar=1.0, in1=dv[:, 96:128],
                                                     op0=ALU.mult, op1=ALU.mult, accum_out=lamt[:, 1:2]),
             reads=["dv", "lamt"], writes=["dv", "lamt"])
        P.op("act", lambda e: e.activation(out=lamt[:, 0:2], in_=lamt[:, 0:2], func=AF.Exp), reads=["lamt"], writes=["lamt"])
        P.op("dve", lambda e: e.tensor_tensor(out=lamt[:, 2:3], in0=lamt[:, 1:2], in1=lamt[:, 0:1], op=ALU.subtract),
             reads=["lamt"], writes=["lamt"])
        P.op("dve", lambda e: e.tensor_scalar(out=lamt[:, 2:3], in0=lamt[:, 2:3], scalar1=-lam_init, scalar2=None,
                                              op0=ALU.add),
             reads=["lamt"], writes=["lamt"])
        for h in range(4):
            P.op("dve", lambda e, h=h: e.tensor_scalar(out=sw[:, h * 64:(h + 1) * 64], in0=dv[:, 128:192],
                                                       scalar1=1.0 - lam_init, scalar2=None, op0=ALU.mult),
                 reads=["dv"], writes=["sw"])
        qT = a16.alloc(2 * S, (2, S))
        kT = a16.alloc(2 * S, (2, S))
        va = a16.alloc(T * 4 * 65, (T, 4, 65))
        for r in range(2):
            P.dma("sp", qT[:, r, :], fm_d[(FM_DQ + r) * 128:(FM_DQ + r + 1) * 128, :], writes=["dqT"])
            P.dma("sp", kT[:, r, :], fm_d[(FM_DK + r) * 128:(FM_DK + r + 1) * 128, :], writes=["dkT"])
        P.op("pool", lambda e: e.memset(va, 1.0), writes=["va%d" % t_ for t_ in range(T)])
        for t in range(T):
            P.dma("sp", va[:, t, :, 0:64], tm_d[t * 128:(t + 1) * 128, TM_DV:TM_DV + 256].rearrange("p (h e) -> p h e", h=4),
                  writes=["va%d" % t])
        ETp = [a16.alloc(1024, (2, 512)) for _ in range(2)]
        et_i = [0]
        oTs = [a32.alloc(512) for _ in range(2)]
        dso = a32.alloc(4 * 256, (4, 256))
        rz = a32.alloc(2)
        zts = [a16.alloc(256) for _ in range(2)]
        szs = [a32.alloc(256) for _ in range(2)]
        sq = a32.alloc(256)
        s2 = a32.alloc(4)
        mos = [a16.alloc(256) for _ in range(2)]
        ot_i = 0
        yield
        for sb in range(NSB):
            q0 = sb * 512
            for h in range(4):
                r, hl = h // 2, h % 2
                nkb = 4 * sb + 4
                W = 256 if dslopes[h] * 511 > 80 else 512
                def qk(kb):
                    par = kb % 2
                    qlo = max(0, kb - 4 * sb) * 128
                    for i in range(2):
                        rg = hl * 2 + i
                        bk = par * 2 + i
                        P.op("pe", lambda e, bk=bk, rg=rg, kb=kb, qlo=qlo, r=r, q0=q0:
                             e.matmul(psb[bk][:, qlo:512], lhsT=kT[rg * 32:(rg + 1) * 32, r, kb * 128:(kb + 1) * 128],
                                      rhs=qT[rg * 32:(rg + 1) * 32, r, q0 + qlo:q0 + 512], start=True, stop=True,
                                      tile_position=(rg * 32, 0)),
                             reads=["dqT", "dkT"], writes=["ps%d" % bk])

                def rest(kb):
                    par = kb % 2
                    rel = kb - 4 * sb
                    qlo = max(0, rel) * 128
                    ET = ETp[et_i[0] % 2]
                    ek = "ETp%d" % (et_i[0] % 2)
                    et_i[0] += 1
                    pair = pbig[:, par * 1024:(par + 1) * 1024].rearrange("p (i c) -> p i c", i=2)
                    for sub in range(512 // W):
                        a, b = max(qlo, sub * W), (sub + 1) * W
                        if a >= b:
                            continue
                        dd = (kb * 128 - (q0 + sub * W)) // 128
                        P.op("act", lambda e, ET=ET, a=a, b=b, dd=dd, pair=pair, h=h:
                             e.activation(out=ET[:, :, a:b], in_=pair[:, :, a:b], func=AF.Exp,
                                          bias=alib[:, 1, h, dd + 31:dd + 32], scale=scale),
                             reads=["ps%d" % (par * 2), "ps%d" % (par * 2 + 1), "alib"], writes=[ek])
                    if rel >= 0:
                        for i in range(2):
                            P.op("pool", lambda e, i=i, ET=ET, qlo=qlo:
                                 e.tensor_tensor(out=ET[:, i, qlo:qlo + 128], in0=ET[:, i, qlo:qlo + 128], in1=tri01, op=ALU.mult),
                                 reads=[ek, "tri01"], writes=[ek])
                    for i in range(2):
                        P.op("pe", lambda e, i=i, ET=ET, kb=kb, qlo=qlo, h=h, nkb=nkb:
                             e.matmul(psb[4 + i][0:65, qlo:512], lhsT=va[:, kb, h, :], rhs=ET[:, i, qlo:512],
                                      start=(kb == 0), stop=(kb == nkb - 1)),
                             reads=[ek, "va%d" % kb], writes=["ps%d" % (4 + i)])

                qk(0)
                for kb in range(nkb):
                    if kb + 1 < nkb:
                        qk(kb + 1)
                    rest(kb)
                    yield
                for i in range(2):
                    oT = oTs[ot_i % 2]
                    ok = "oT%d" % (ot_i % 2)
                    ot_i += 1
                    evac(oT[0:65, :], psb[4 + i][0:65, :], reads=["ps%d" % (4 + i)], writes=[ok])
                    for qt in range(4):
                        cbi = ((qt % 2) * 2 + i) * 65
                        P.op("pe", lambda e, qt=qt, cbi=cbi, oT=oT:
                             e.transpose(out=psb[qt // 2][:, cbi:cbi + 65], in_=oT[0:65, qt * 128:(qt + 1) * 128],
                                         identity=ident_f[0:65, 0:65]),
                             reads=[ok, "ident_f"], writes=["ps%d" % (qt // 2)])
                yield
                for qt in range(4):
                    pq = psb[qt // 2]
                    kq = "ps%d" % (qt // 2)
                    cb0 = (qt % 2) * 130
                    P.op("dve", lambda e, pq=pq, cb0=cb0: e.tensor_scalar(
                        out=rz, in0=pq[:, cb0:cb0 + 130].rearrange("p (s c) -> p s c", c=65)[:, :, 64], scalar1=1e-30,
                        scalar2=None, op0=ALU.max),
                        reads=[kq], writes=["rz"])
                    P.op("dve", lambda e: e.reciprocal(out=rz, in_=rz), reads=["rz"], writes=["rz"])
                    P.op("dve", lambda e: e.tensor_tensor(out=rz[:, 1:2], in0=rz[:, 1:2], in1=lamt[:, 2:3], op=ALU.mult),
                         reads=["rz", "lamt"], writes=["rz"])
                    P.op("dve", lambda e, h=h, qt=qt, pq=pq, cb0=cb0: e.tensor_scalar(
                        out=dso[:, qt, h * 64:(h + 1) * 64], in0=pq[:, cb0:cb0 + 64],
                        scalar1=rz[:, 0:1], scalar2=None, op0=ALU.mult),
                        reads=[kq, "rz"], writes=["dso"])
                    P.op("dve", lambda e, h=h, qt=qt, pq=pq, cb0=cb0: e.scalar_tensor_tensor(
                        out=dso[:, qt, h * 64:(h + 1) * 64], in0=pq[:, cb0 + 65:cb0 + 129],
                        scalar=rz[:, 1:2], in1=dso[:, qt, h * 64:(h + 1) * 64],
                        op0=ALU.mult, op1=ALU.add),
                        reads=[kq, "rz", "dso"], writes=["dso"])
                yield
            for qt in range(4):
                t = sb * 4 + qt
                j = t % 2
                zt, sz = zts[j], szs[j]
                k = "D%d_" % j
                P.dma("sp", zt, tm_d[t * 128:(t + 1) * 128, TM_DZ:TM_DZ + 256], writes=[k + "z"])
                P.op("act", lambda e, sz=sz, zt=zt: e.activation(out=sz, in_=zt, func=AF.Tanh, scale=0.5), reads=[k + "z"], writes=[k + "sz"])
                P.op("dve", lambda e, sz=sz: e.tensor_scalar(out=sz, in0=sz, scalar1=1.0, scalar2=0.5, op0=ALU.add, op1=ALU.mult),
                     reads=[k + "sz"], writes=[k + "sz"])
                P.op("dve", lambda e, sz=sz, zt=zt: e.tensor_tensor(out=sz, in0=sz, in1=zt, op=ALU.mult),
                     reads=[k + "sz", k + "z"], writes=[k + "sz"])
                P.op("dve", lambda e, sz=sz: e.tensor_tensor(out=sz, in0=sz, in1=sw, op=ALU.mult),
                     reads=[k + "sz", "sw"], writes=[k + "sz"])
                P.op("dve", lambda e, qt=qt: e.tensor_tensor(out=sq, in0=dso[:, qt, :], in1=dso[:, qt, :], op=ALU.mult),
                     reads=["dso"], writes=["dsq"])
                P.op("dve", lambda e: e.tensor_reduce(out=s2, in_=sq.rearrange("p (h e) -> p h e", h=4), axis=AX.X, op=ALU.add),
                     reads=["dsq"], writes=["ds2"])
                P.op("dve", lambda e: e.tensor_scalar(out=s2, in0=s2, scalar1=1.0 / 64, scalar2=EPS, op0=ALU.mult, op1=ALU.add),
                     reads=["ds2"], writes=["ds2"])
                P.op("act", lambda e: e.activation(out=s2, in_=s2, func=AF.Sqrt), reads=["ds2"], writes=["ds2"])
                P.op("dve", lambda e: e.reciprocal(out=s2, in_=s2), reads=["ds2"], writes=["ds2"])
                mo = mos[j]
                for hh in range(4):
                    P.op("dve", lambda e, hh=hh, qt=qt, mo=mo, sz=sz: e.scalar_tensor_tensor(
                        out=mo[:, hh * 64:(hh + 1) * 64], in0=dso[:, qt, hh * 64:(hh + 1) * 64],
                        scalar=s2[:, hh:hh + 1], in1=sz[:, hh * 64:(hh + 1) * 64], op0=ALU.mult, op1=ALU.mult),
                        reads=["dso", "ds2", k + "sz"], writes=[k + "mo"])
                P.dma(STORE_Q, mixed_d[t * 128:(t + 1) * 128, 256:512], mo, reads=[k + "mo"], writes=["mixD%d" % t])
                yield

    def phase_nsa(l):
        a32.mark()
        a16.mark()
        scale = 64 ** -0.5
        nslopes = [2.0 ** (-8.0 * (i + 1) / 8) for i in range(8)][0::2]

        def subw(sl):
            return 128 if sl * 255 > 80 else (256 if sl * 511 > 80 else 512)
        kcT2 = a16.alloc(NBK * 128)
        vca = a16.alloc(NBK * 65, (NBK, 65))
        a32.mark()
        a16.mark()
        kvc = a16.alloc(S)
        P.dma("sp", kvc, fm_d[FM_KVC * 128:(FM_KVC + 1) * 128, :], writes=["kvc"])
        W1 = a16.alloc(32 * 256, (32, 256))
        W2k = a16.alloc(2 * 128, (2, 128))
        W2v = a16.alloc(2 * 64, (2, 64))
        peT = a16.alloc(32)
        hsk = a16.alloc(2 * NC, (2, NC))
        hsv = a16.alloc(2 * NC, (2, NC))
        stg = a32.alloc(8 * 256, (8, 256))
        stg2 = a32.alloc(2 * 64, (2, 64))
        pef = a32.alloc(32)
        hb = a32.alloc(4)
        for ch in range(4):
            P.dma("sp", stg[0:64], w_ck1[l, ch * 512:(ch + 1) * 512, :].rearrange("(l c) h -> c l h", c=64), writes=["stg"])
            P.dma("sp", stg[64:128], w_cv1[l, ch * 512:(ch + 1) * 512, :].rearrange("(l c) h -> c l h", c=64), writes=["stg"])
            P.op("pool", lambda e, ch=ch: e.tensor_copy(out=W1[:, ch * 8:(ch + 1) * 8, :], in_=stg), reads=["stg"], writes=["W1"])
        P.dma("sp", stg2, w_ck2[l].rearrange("(a p) d -> p a d", p=128), writes=["stg2"])
        for dup in range(2):
            P.op("pool", lambda e, dup=dup: e.tensor_copy(out=W2k[:, :, dup * 64:(dup + 1) * 64], in_=stg2), reads=["stg2"], writes=["W2k"])
        P.dma("sp", stg2, w_cv2[l].rearrange("(a p) d -> p a d", p=128), writes=["stg2"])
        P.op("pool", lambda e: e.tensor_copy(out=W2v, in_=stg2), reads=["stg2"], writes=["W2v"])
        P.dma("sp", pef, nsa_peT[l], writes=["pef"])
        P.op("dve", lambda e: e.tensor_copy(out=peT, in_=pef), reads=["pef"], writes=["peT"])
        P.op("pool", lambda e: e.memset(vca, 1.0), writes=["vca"])
        P.op("pool", lambda e: e.memset(kcT2, 0.0), writes=["kcT2"])
        for g, hs in ((0, hsk), (1, hsv)):
            for hh in range(2):
                ph, pbb = psb[2 * g + hh], psb[4 + 2 * g + hh]
                for li in range(32):
                    P.op("pe", lambda e, g=g, hh=hh, li=li, pbb=pbb:
                         e.matmul(pbb[:, 0:1], lhsT=W1[g * 64:(g + 1) * 64, li, hh * 128:(hh + 1) * 128],
                                  rhs=peT[g * 64:(g + 1) * 64, li:li + 1], start=(li == 0), stop=(li == 31)),
                         reads=["W1", "peT"], writes=["ps%d" % (4 + 2 * g + hh)])
                P.op("dve", lambda e, g=g, hh=hh, pbb=pbb: e.tensor_copy(out=hb[:, 2 * g + hh:2 * g + hh + 1], in_=pbb[:, 0:1]),
                     reads=["ps%d" % (4 + 2 * g + hh)], writes=["hb"])
                for li in range(32):
                    P.op("pe", lambda e, g=g, hh=hh, li=li, ph=ph:
                         e.matmul(ph[:, 0:NC], lhsT=W1[g * 64:(g + 1) * 64, li, hh * 128:(hh + 1) * 128],
                                  rhs=kvc[g * 64:(g + 1) * 64, li:li + 16 * (NC - 1) + 1:16], start=(li == 0), stop=(li == 31)),
                         reads=["W1", "kvc"], writes=["ps%d" % (2 * g + hh)])
                P.op("act", lambda e, g=g, hh=hh, ph=ph, hs=hs:
                     e.activation(out=hs[:, hh, :], in_=ph[:, 0:NC], func=AF.Silu, bias=hb[:, 2 * g + hh:2 * g + hh + 1]),
                     reads=["ps%d" % (2 * g + hh), "hb"], writes=["hs%d" % g])
        for hh in range(2):
            P.op("pe", lambda e, hh=hh: e.matmul(psb[6][:, 0:NC], lhsT=W2k[:, hh, :], rhs=hsk[:, hh, :],
                                                 start=(hh == 0), stop=(hh == 1)),
                 reads=["W2k", "hs0"], writes=["ps6"])
        P.op("act", lambda e: e.copy(out=kcT2[:, 0:NC], in_=psb[6][:, 0:NC]), reads=["ps6"], writes=["kcT2"])
        for nb in range(NBK):
            nn = min(128, NC - nb * 128)
            for hh in range(2):
                P.op("pe", lambda e, hh=hh, nb=nb, nn=nn:
                     e.matmul(psb[7][0:nn, 0:64], lhsT=hsv[:, hh, nb * 128:nb * 128 + nn], rhs=W2v[:, hh, :],
                              start=(hh == 0), stop=(hh == 1)),
                     reads=["W2v", "hs1"], writes=["ps7"])
            P.op("act", lambda e, nb=nb, nn=nn: e.copy(out=vca[0:nn, nb, 0:64], in_=psb[7][0:nn, 0:64]),
                 reads=["ps7"], writes=["vca"])
        P.barrier()
        a32.release()
        a16.release()

        alib = a32.alloc(2 * 4 * 35, (2, 4, 35))
        alic = a32.alloc(256, (4, 2, 32))
        tri01 = a16.alloc(128)
        low01 = a16.alloc(128)
        gneg = a16.alloc(4096)
        ex2 = a16.alloc(T * 128, (T, 128))
        ovl = a16.alloc(128, (2, 64))
        tri_f = a32.alloc(128)
        P.dma("sp", alib, c_alibi.rearrange("p (s h d) -> p s h d", s=2, h=4), writes=["alib"])
        P.dma("sp", alic, c_alibi_cmp.rearrange("p (h n q) -> p h n q", h=4, n=2), writes=["alic"])
        P.dma("sp", tri_f, c_tri01, writes=["tri_f"])
        P.op("dve", lambda e: e.tensor_copy(out=tri01, in_=tri_f), reads=["tri_f"], writes=["tri01"])
        P.dma("sp", low01, c_low01, writes=["low01"])
        P.dma("sp", gneg, c_gneg, writes=["gneg"])
        P.dma("sp", ex2, c_ex2.rearrange("p (t k) -> p t k", k=128), writes=["ex2"])
        P.dma("sp", ovl, c_ovl.rearrange("p (n j) -> p n j", n=2), writes=["ovl"])
        if NS > 16:
            tka = a32.alloc(T * 64, (T, 64))
            P.dma("sp", tka, c_topk_add.rearrange("p (t j) -> p t j", j=64), writes=["tka"])
        qT = a16.alloc(2 * S, (2, S))
        ksl = a16.alloc(S)
        kwn = a16.alloc(S)
        vsl = a16.alloc(T * 65, (T, 65))
        vwn = a16.alloc(T * 65, (T, 65))
        for r in range(2):
            P.dma("sp", qT[:, r, :], fm_d[(FM_NQ + r) * 128:(FM_NQ + r + 1) * 128, :], writes=["nqT"])
        for g in range(2):
            P.dma("sp", ksl[g * 64:(g + 1) * 64, :], fm_d[FM_KSW * 128:FM_KSW * 128 + 64, :], writes=["ksl"])
            P.dma("sp", kwn[g * 64:(g + 1) * 64, :], fm_d[FM_KSW * 128 + 64:FM_KSW * 128 + 128, :], writes=["kwn"])
        P.op("pool", lambda e: e.memset(vsl, 1.0), writes=["vsl%d" % t_ for t_ in range(T)])
        P.op("pool", lambda e: e.memset(vwn, 1.0), writes=["vwn%d" % t_ for t_ in range(T)])
        for t in range(T):
            P.dma("sp", vsl[:, t, 0:64], tm_d[t * 128:(t + 1) * 128, TM_VSLC:TM_VSLC + 64], writes=["vsl%d" % t])
            P.dma("sp", vwn[:, t, 0:64], tm_d[t * 128:(t + 1) * 128, TM_VWIN:TM_VWIN + 64], writes=["vwn%d" % t])

        ETs = [[a16.alloc(512) for _ in range(2)] for _ in range(4)]
        et_i = [0, 0, 0, 0]
        smf = [a32.alloc(512) for _ in range(2)]
        oTs = [a32.alloc(512) for _ in range(2)]
        acc = a32.alloc(4 * 256, (4, 256))
        imp = a32.alloc(4 * 64, (4, 64))
        gs = a32.alloc(4 * 12, (4, 12))
        rz = a32.alloc(4)
        coef = a32.alloc(4)
        top8 = a32.alloc(8)
        tmpk = a32.alloc(64)
        nsel = a16.alloc(128)
        negT2 = a16.alloc(512)
        zts = [a16.alloc(256) for _ in range(2)]
        szs = [a32.alloc(256) for _ in range(2)]
        mos = [a16.alloc(256) for _ in range(2)]
        ot_i = [0]
        sm_i = [0]

        def epilogue(sb, b, first):
            for h in range(4):
                oT = oTs[ot_i[0] % 2]
                ok = "noT%d" % (ot_i[0] % 2)
                ot_i[0] += 1
                evac(oT[0:65, :], psb[4 + h][0:65, :], reads=["ps%d" % (4 + h)], writes=[ok])
                for qt in range(4):
                    P.op("pe", lambda e, h=h, qt=qt, oT=oT:
                         e.transpose(out=psb[qt][:, h * 65:(h + 1) * 65], in_=oT[0:65, qt * 128:(qt + 1) * 128],
                                     identity=ident_f[0:65, 0:65]),
                         reads=[ok, "ident_f"], writes=["ps%d" % qt])
            for qt in range(4):
                pq = psb[qt]
                kq = "ps%d" % qt
                P.op("dve", lambda e, pq=pq: e.tensor_scalar(
                    out=rz, in0=pq[:, 0:260].rearrange("p (s c) -> p s c", c=65)[:, :, 64], scalar1=1e-30, scalar2=None,
                    op0=ALU.max), reads=[kq], writes=["nrz"])
                P.op("dve", lambda e: e.reciprocal(out=rz, in_=rz), reads=["nrz"], writes=["nrz"])
                if b == 0:
                    P.op("dve", lambda e, qt=qt: e.tensor_copy(out=rzc[:, qt, :], in_=rz), reads=["nrz"], writes=["rzc"])
                P.op("dve", lambda e, qt=qt: e.tensor_tensor(
                    out=coef, in0=rz, in1=gs[:, qt, :].rearrange("p (h b) -> p h b", b=3)[:, :, b], op=ALU.mult),
                    reads=["nrz", "gs"], writes=["coef"])
                for h in range(4):
                    if first:
                        P.op("dve", lambda e, h=h, qt=qt, pq=pq: e.tensor_scalar(
                            out=acc[:, qt, h * 64:(h + 1) * 64], in0=pq[:, h * 65:h * 65 + 64], scalar1=coef[:, h:h + 1],
                            scalar2=None, op0=ALU.mult), reads=[kq, "coef"], writes=["nacc"])
                    else:
                        P.op("dve", lambda e, h=h, qt=qt, pq=pq: e.scalar_tensor_tensor(
                            out=acc[:, qt, h * 64:(h + 1) * 64], in0=pq[:, h * 65:h * 65 + 64], scalar=coef[:, h:h + 1],
                            in1=acc[:, qt, h * 64:(h + 1) * 64], op0=ALU.mult, op1=ALU.add),
                            reads=[kq, "coef", "nacc"], writes=["nacc"])

        def exp_tile(h, ET, ek, src, src_key, a_lo, a_hi, kb, q0, rows=128):
            W = subw(nslopes[h])
            for sub in range(512 // W):
                a, bb = max(a_lo, sub * W), min(a_hi, (sub + 1) * W)
                if a >= bb:
                    continue
                dd = (kb * 128 - (q0 + sub * W)) // 128
                P.op("act", lambda e, a=a, bb=bb, dd=dd: e.activation(
                    out=ET[0:rows, a:bb], in_=src[0:rows, a:bb], func=AF.Exp, bias=alib[0:rows, 0, h, dd + 31:dd + 32], scale=scale),
                    reads=[src_key, "alib"], writes=[ek])

        rzc = a32.alloc(16, (4, 4))
        for sb in range(NSB):
            q0 = sb * 512
            P.op("act", lambda e, sb=sb: e.activation(out=gs, in_=small[:, sb * 4:sb * 4 + 4, 0:12], func=AF.Sigmoid),
                 reads=["small"], writes=["gs"])
            nbs = []
            for nb in range(NBK):
                nn = min(128, NC - 128 * nb, 32 * sb + 31 - 128 * nb)
                if nn > 0:
                    nbs.append((nb, nn))
            cET = {}
            if nbs:
                for h in range(4):
                    g, r = h % 2, h // 2
                    for (nb, nn) in nbs:
                        P.op("pe", lambda e, h=h, g=g, r=r, nb=nb, nn=nn, q0=q0:
                             e.matmul(psb[h][0:nn, :], lhsT=kcT2[g * 64:(g + 1) * 64, nb * 128:nb * 128 + nn],
                                      rhs=qT[g * 64:(g + 1) * 64, r, q0:q0 + 512], start=True, stop=True),
                             reads=["kcT2", "nqT"], writes=["ps%d" % h])
                        sm = smf[sm_i[0] % 2]
                        sk = "smf%d" % (sm_i[0] % 2)
                        sm_i[0] += 1
                        c0 = q0 - 2048 * nb
                        P.op("dve", lambda e, h=h, nn=nn, sm=sm, c0=c0: e.tensor_tensor(
                            out=sm[0:nn, :], in0=psb[h][0:nn, :], in1=gneg[0:nn, c0:c0 + 512], op=ALU.add),
                            reads=["ps%d" % h, "gneg"], writes=[sk])
                        ET = ETs[h][nb]
                        ek = "ET%d_%d" % (h, nb)
                        cET[(h, nb)] = (ET, ek, nn)
                        W = subw(nslopes[h])
                        for sub in range(512 // W):
                            a, bb = sub * W, (sub + 1) * W
                            qi = (q0 + a) // 128
                            P.op("act", lambda e, h=h, nb=nb, nn=nn, sm=sm, ET=ET, a=a, bb=bb, qi=qi: e.activation(
                                out=ET[0:nn, a:bb], in_=sm[0:nn, a:bb], func=AF.Exp, bias=alic[0:nn, h, nb, qi:qi + 1], scale=scale),
                                reads=[sk, "alic"], writes=[ek])
                    for i, (nb, nn) in enumerate(nbs):
                        ET, ek, _ = cET[(h, nb)]
                        P.op("pe", lambda e, h=h, nb=nb, nn=nn, ET=ET, i=i, n=len(nbs):
                             e.matmul(psb[4 + h][0:65, :], lhsT=vca[0:nn, nb, :], rhs=ET[0:nn, :],
                                      start=(i == 0), stop=(i == n - 1)),
                             reads=[ek, "vca"], writes=["ps%d" % (4 + h)])
                epilogue(sb, 0, True)
                for h in range(4):
                    for qt in range(4):
                        for i, (nb, nn) in enumerate(nbs):
                            ET, ek, _ = cET[(h, nb)]
                            P.op("pe", lambda e, h=h, qt=qt, nb=nb, nn=nn, ET=ET, i=i, n=len(nbs):
                                 e.matmul(psb[h][:, qt * 64:(qt + 1) * 64], lhsT=ET[0:nn, qt * 128:(qt + 1) * 128],
                                          rhs=ovl[0:nn, nb, :], start=(i == 0), stop=(i == n - 1)),
                                 reads=[ek, "ovl"], writes=["ps%d" % h])
                for h in range(4):
                    for qt in range(4):
                        if h == 0:
                            P.op("dve", lambda e, h=h, qt=qt: e.tensor_scalar(
                                out=imp[:, qt, :], in0=psb[h][:, qt * 64:(qt + 1) * 64], scalar1=rzc[:, qt, h:h + 1],
                                scalar2=None, op0=ALU.mult), reads=["ps%d" % h, "rzc"], writes=["imp"])
                        else:
                            P.op("dve", lambda e, h=h, qt=qt: e.scalar_tensor_tensor(
                                out=imp[:, qt, :], in0=psb[h][:, qt * 64:(qt + 1) * 64], scalar=rzc[:, qt, h:h + 1],
                                in1=imp[:, qt, :], op0=ALU.mult, op1=ALU.add), reads=["ps%d" % h, "rzc", "imp"], writes=["imp"])
            else:
                P.op("dve", lambda e: e.memset(acc, 0.0), writes=["nacc"])
                P.op("dve", lambda e: e.memset(imp, 0.0), writes=["imp"])
            for qt in range(4):
                t = sb * 4 + qt
                if NS > 16:
                    P.op("dve", lambda e, qt=qt, t=t: e.tensor_tensor(out=imp[:, qt, :], in0=imp[:, qt, :], in1=tka[:, t, :], op=ALU.add),
                         reads=["imp", "tka"], writes=["imp"])
                    P.op("dve", lambda e, qt=qt: e.max(out=top8, in_=imp[:, qt, :]), reads=["imp"], writes=["top8"])
                    P.op("dve", lambda e, qt=qt: e.match_replace(out=tmpk, in_to_replace=top8, in_values=imp[:, qt, :], imm_value=-1e9),
                         reads=["imp", "top8"], writes=["tmpk"])
                    P.op("dve", lambda e: e.max(out=top8, in_=tmpk), reads=["tmpk"], writes=["top8"])
                    P.op("dve", lambda e, qt=qt: e.tensor_scalar(out=tmpk, in0=imp[:, qt, :], scalar1=top8[:, 7:8], scalar2=-1.0,
                                                                 op0=ALU.is_ge, op1=ALU.add),
                         reads=["imp", "top8"], writes=["tmpk"])
                    for dup in range(2):
                        P.op("dve", lambda e, dup=dup: e.tensor_scalar(out=nsel[:, dup * 64:(dup + 1) * 64], in0=tmpk, scalar1=BIG,
                                                                       scalar2=None, op0=ALU.mult),
                             reads=["tmpk"], writes=["nsel"])
                else:
                    P.op("dve", lambda e: e.memset(nsel, 0.0), writes=["nsel"])
                pt = psb[qt][:, 0:64].bitcast(BF16)
                P.op("pe", lambda e, pt=pt: e.transpose(out=pt, in_=nsel, identity=ident_b), reads=["nsel", "ident_b"], writes=["ps%d" % qt])
                P.op("act", lambda e, qt=qt, pt=pt: e.copy(out=negT2[:, qt * 128:(qt + 1) * 128], in_=pt), reads=["ps%d" % qt], writes=["negT2"])
            for b, kT2, va_, kbs in ((1, ksl, vsl, list(range(0, 4 * sb + 4))),
                                     (2, kwn, vwn, list(range(max(0, 4 * sb - 4), 4 * sb + 4)))):
                kname = "ksl" if b == 1 else "kwn"
                vname = "vsl" if b == 1 else "vwn"
                n_kb = len(kbs)
                for hp in range(2):
                    heads = (2 * hp, 2 * hp + 1)

                    def geom(i, kbs=kbs, b=b, sb=sb):
                        kb = kbs[i]
                        rel = kb - 4 * sb
                        qlo = max(0, rel) * 128
                        qhi = 512 if (b == 1 or rel >= 0) else 128 * (rel + 5)
                        return kb, rel, qlo, qhi

                    def qk(i, heads=heads, kT2=kT2, b=b, q0=q0, kname=kname):
                        kb, rel, qlo, qhi = geom(i)
                        par = i % 2
                        for h in heads:
                            g, r = h % 2, h // 2
                            bk = par * 2 + g
                            P.op("pe", lambda e, bk=bk, g=g, r=r, kb=kb, qlo=qlo, qhi=qhi, kT2=kT2, b=b, q0=q0:
                                 e.matmul(psb[bk][:, qlo:qhi], lhsT=kT2[g * 64:(g + 1) * 64, kb * 128:(kb + 1) * 128],
                                          rhs=qT[g * 64:(g + 1) * 64, r, q0 + qlo:q0 + qhi], start=True, stop=(b != 1)),
                                 reads=[kname, "nqT"], writes=["ps%d" % bk])
                            if b == 1:
                                P.op("pe", lambda e, bk=bk, g=g, kb=kb, qlo=qlo, qhi=qhi:
                                     e.matmul(psb[bk][:, qlo:qhi], lhsT=ex2[g * 64:(g + 1) * 64, kb, :],
                                              rhs=negT2[g * 64:(g + 1) * 64, qlo:qhi], start=False, stop=True),
                                     reads=["ex2", "negT2"], writes=["ps%d" % bk])

                    def rest(i, heads=heads, b=b, q0=q0, va_=va_, vname=vname, n_kb=n_kb):
                        kb, rel, qlo, qhi = geom(i)
                        par = i % 2
                        for h in heads:
                            bk = par * 2 + h % 2
                            ET = ETs[h][et_i[h] % 2]
                            ek = "ET%d_%d" % (h, et_i[h] % 2)
                            et_i[h] += 1
                            exp_tile(h, ET, ek, psb[bk], "ps%d" % bk, qlo, qhi, kb, q0)
                            if rel >= 0:
                                P.op("pool", lambda e, ET=ET, qlo=qlo: e.tensor_tensor(
                                    out=ET[:, qlo:qlo + 128], in0=ET[:, qlo:qlo + 128], in1=tri01, op=ALU.mult),
                                    reads=[ek, "tri01"], writes=[ek])
                            elif b == 2:
                                P.op("pool", lambda e, ET=ET, qhi=qhi: e.tensor_tensor(
                                    out=ET[:, qhi - 128:qhi], in0=ET[:, qhi - 128:qhi], in1=low01, op=ALU.mult),
                                    reads=[ek, "low01"], writes=[ek])
                            P.op("pe", lambda e, h=h, ET=ET, kb=kb, qlo=qlo, qhi=qhi, va_=va_, i=i, n=n_kb:
                                 e.matmul(psb[4 + h][0:65, qlo:qhi], lhsT=va_[:, kb, :], rhs=ET[:, qlo:qhi],
                                          start=(i == 0), stop=(i == n - 1), skip_group_check=True),
                                 reads=[ek, "%s%d" % (vname, kb)], writes=["ps%d" % (4 + h)])

                    qk(0)
                    for i in range(n_kb):
                        if i + 1 < n_kb:
                            qk(i + 1)
                        rest(i)
                epilogue(sb, b, False)
            for qt in range(4):
                t = sb * 4 + qt
                j = t % 2
                zt, sz = zts[j], szs[j]
                k = "N%d_" % j
                P.dma("sp", zt, tm_d[t * 128:(t + 1) * 128, TM_NSAZ:TM_NSAZ + 256], writes=[k + "z"])
                P.op("act", lambda e, sz=sz, zt=zt: e.activation(out=sz, in_=zt, func=AF.Silu), reads=[k + "z"], writes=[k + "sz"])
                mo = mos[j]
                P.op("dve", lambda e, qt=qt, mo=mo, sz=sz: e.tensor_tensor(out=mo, in0=acc[:, qt, :], in1=sz, op=ALU.mult),
                     reads=["nacc", k + "sz"], writes=[k + "mo"])
                P.dma(STORE_Q, mixed_d[t * 128:(t + 1) * 128, 0:256], mo, reads=[k + "mo"], writes=["mixN%d" % t])
        P.barrier()
        a32.release()
        a16.release()

    def phase_dump(l):
        P.barrier()

    def phase_stub(l):
        if True:
            a16.mark()
            tl = [a16.alloc(1024) for _ in range(2)]
            for t in range(T):
                P.dma("sp", tl[t % 2], tm_d[t * 128:(t + 1) * 128, 396:396 + 1024],
                      reads=[], writes=["tl%d" % (t % 2)])
                if stop == 98:
                    continue
                P.dma(STORE_Q, mixed_d[t * 128:(t + 1) * 128, :], tl[t % 2], reads=["tl%d" % (t % 2)], writes=["mixX%d" % t])
            P.barrier()
            a16.release()

    def phase_F(l):
        x_src = x_in if l == 0 else xres
        wo_sb = a16.alloc(8 * 1024, (8, 1024))
        stage = [a32.alloc(1024) for _ in range(2)]
        xts = [a32.alloc(D_MODEL) for _ in range(2)]
        xns = [a32.alloc(D_MODEL) for _ in range(2)]
        mTs = [a16.alloc(8 * 128, (8, 128)) for _ in range(2)]
        mxs = [a16.alloc(1024) for _ in range(2)]
        last = (l == L - 1) and final_norm
        if last:
            fnwb = a32.alloc(D_MODEL)
            junk = a16.alloc(D_MODEL)
            sss = [a32.alloc(1) for _ in range(2)]
            rstds = [a32.alloc(1) for _ in range(2)]
            yts = [a32.alloc(D_MODEL) for _ in range(2)]
            P.dma("sp", fnwb, final_norm_w[0:1, :].partition_broadcast(128), writes=["fnwb"])
        for kc in range(8):
            st = stage[kc % 2]
            P.dma("sp", st, w_out[l, kc * 128:(kc + 1) * 128, :], writes=["stage%d" % (kc % 2)])
            P.op(("pool", "dve")[kc % 2], lambda e, kc=kc, st=st: e.tensor_copy(out=wo_sb[:, kc, :], in_=st),
                 reads=["stage%d" % (kc % 2)], writes=["wo_sb%d" % (kc % 2)])
        def f_loads(t):
            j = t % 2
            k = "F%d_" % j
            P.dma("sp", mxs[j], mixed_d[t * 128:(t + 1) * 128, :], reads=["mixR%d" % t], writes=[k + "mx"])
            P.dma("sp", xts[j], x_src[t * 128:(t + 1) * 128, :], reads=["xres_t%d" % t], writes=[k + "x"])
        f_loads(0)
        for t in range(T):
            j = t % 2
            k = "F%d_" % j
            xt, xn, mT = xts[j], xns[j], mTs[j]
            mx = mxs[j]
            pst = psb[j]
            pstb = pst[:, 0:512].bitcast(BF16)
            for kc in range(8):
                P.op("pe", lambda e, kc=kc, mx=mx, pstb=pstb:
                     e.transpose(out=pstb[:, kc * 128:(kc + 1) * 128], in_=mx[:, kc * 128:(kc + 1) * 128],
                                 identity=ident_b),
                     reads=[k + "mx", "ident_b"], writes=["ps%d" % j])
            evac(mT, pstb.rearrange("p (k t) -> p k t", k=8), reads=["ps%d" % j], writes=[k + "mT"])
            if t + 1 < T:
                f_loads(t + 1)
            for c in range(2):
                pi = 2 + (t * 2 + c) % 4
                ps = psb[pi]
                for kc in range(8):
                    P.op("pe", lambda e, kc=kc, ps=ps, mT=mT, c=c:
                         e.matmul(ps[:, :], lhsT=mT[:, kc, :], rhs=wo_sb[:, kc, c * 512:(c + 1) * 512],
                                  start=(kc == 0), stop=(kc == 7)),
                         reads=[k + "mT", "wo_sb0", "wo_sb1"], writes=["ps%d" % pi])
                P.op("dve", lambda e, ps=ps, xt=xt, xn=xn, c=c:
                     e.tensor_tensor(out=xn[:, c * 512:(c + 1) * 512], in0=ps[:, :],
                                     in1=xt[:, c * 512:(c + 1) * 512], op=ALU.add),
                     reads=["ps%d" % pi, k + "x"], writes=[k + "xn"])
            if not last:
                P.dma(STORE_Q, xres[t * 128:(t + 1) * 128, :], xn, reads=[k + "xn"], writes=["xres_t%d" % t])
            else:
                ss, rstd, yt = sss[j], rstds[j], yts[j]
                P.op("dve", lambda e, xn=xn, ss=ss: e.scalar_tensor_tensor(
                    out=junk, in0=xn, scalar=1.0, in1=xn, op0=ALU.mult, op1=ALU.mult, accum_out=ss),
                    reads=[k + "xn"], writes=["Fjunk", k + "ss"])
                P.op("dve", lambda e, ss=ss: e.tensor_scalar(out=ss, in0=ss, scalar1=1.0 / D_MODEL, scalar2=EPS,
                                                             op0=ALU.mult, op1=ALU.add),
                     reads=[k + "ss"], writes=[k + "ss"])
                P.op("act", lambda e, ss=ss: e.activation(out=ss, in_=ss, func=AF.Sqrt),
                     reads=[k + "ss"], writes=[k + "ss"])
                P.op("dve", lambda e, ss=ss, rstd=rstd: e.reciprocal(out=rstd, in_=ss),
                     reads=[k + "ss"], writes=[k + "rstd"])
                P.op("dve", lambda e, xn=xn, rstd=rstd, yt=yt: e.scalar_tensor_tensor(
                    out=yt, in0=xn, scalar=rstd, in1=fnwb, op0=ALU.mult, op1=ALU.mult),
                    reads=[k + "xn", k + "rstd", "fnwb"], writes=[k + "y"])
                P.dma(STORE_Q, y_out[t * 128:(t + 1) * 128, :], yt, reads=[k + "y"], writes=["y_t%d" % t])

    for l in range(L):
        phase_A(l)
        if "STUB" in phases:
            phase_stub(l)
        realP = P
        for group in ((("DIFF", phase_diff), ("SSD", phase_ssd)),):
            a32.mark()
            a16.mark()
            streams = []
            for nm, ph in group:
                if nm in phases:
                    st_ = Stream()
                    P = st_
                    for _ in ph(l):
                        pass
                    streams.append(st_)
            P = realP
            merge_streams(P, streams)
            P.barrier()
            a32.release()
            a16.release()
        if "NSA" in phases:
            phase_nsa(l)
        if "DUMP" in phases:
            phase_dump(l)
        a32.mark()
        a16.mark()
        streams = []
        if "F" in phases:
            st_ = Stream()
            P = st_
            phase_F(l)
            streams.append(st_)
        if "RET" in phases:
            st_ = Stream()
            P = st_
            for _ in phase_ret(l):
                pass
            streams.append(st_)
        P = realP
        merge_streams(P, streams)
        P.barrier()
        a32.release()
        a16.release()

    P.emit(es)
    es.close()
    return nc


def make_consts(S=SEQ):
    c = {"c_ident": np.eye(128, dtype=np.float32)}
    bf = ml_dtypes.bfloat16
    T = S // 128
    NC = (S - 32) // 16 + 1
    NS = S // 64
    p_ = np.arange(128)
    cc = np.arange(4096)
    c["c_gneg"] = np.where(cc[None, :] - 16 * p_[:, None] >= 31, 0.0, -BIG).astype(bf)
    ex = np.zeros((128, T, 128), np.float32)
    for kb in range(T):
        for pp in range(128):
            j = 2 * kb + pp // 64
            if j < 64:
                ex[j, kb, pp] = 1.0
                ex[64 + j, kb, pp] = 1.0
    c["c_ex2"] = ex.reshape(128, -1).astype(bf)
    c["c_low01"] = (p_[None, :] < p_[:, None]).astype(np.float32).astype(bf)
    n_ = np.arange(256)
    c_start, c_end = n_ * 16, n_ * 16 + 31
    s_start = np.arange(64) * 64
    s_end = s_start + 63
    ov = ((c_start[:, None] <= s_end[None, :]) & (c_end[:, None] >= s_start[None, :]) & (n_[:, None] < NC)).astype(np.float32)
    c["c_ovl"] = ov.reshape(2, 128, 64).transpose(1, 0, 2).reshape(128, 128).astype(bf)
    nsl = np.array([2.0 ** (-8.0 * (i + 1) / 8) for i in range(8)], np.float64)[0::2]
    ac = np.zeros((128, 4, 2, 32), np.float64)
    for h in range(4):
        for nb in range(2):
            for qi in range(32):
                ac[:, h, nb, qi] = nsl[h] * (16.0 * (128 * nb + p_) + 15.5 - 128.0 * qi)
    c["c_alibi_cmp"] = ac.reshape(128, -1).astype(np.float32)
    ta = np.zeros((128, T, 64), np.float32)
    jb = np.arange(64)
    for t in range(T):
        tq = t * 128 + p_
        cur = tq // 64
        forced = (jb[None, :] == 0) | (jb[None, :] == cur[:, None]) | (jb[None, :] == cur[:, None] - 1)
        valid = (jb[None, :] * 64 <= tq[:, None]) & (jb[None, :] < NS)
        ta[:, t, :] = np.where(forced, 1000.0, np.where(valid, 0.0, -1000.0))
    c["c_topk_add"] = ta.reshape(128, -1)
    H, C, dh = 4, 128, 64
    scale = dh ** -0.5
    log_g = np.log(1.0 - 2.0 ** (-5.0 - np.arange(H, dtype=np.float64)))
    pos = np.arange(C, dtype=np.float64)
    rel = pos[None, :] - pos[:, None]
    decT = np.zeros((128, 4 * 128), np.float64)
    xiT = np.zeros((128, 2 * 128), np.float64)
    zeta = np.zeros((128, 256), np.float64)
    cd = np.zeros((128, 4), np.float64)
    for h in range(H):
        ai = (h % 2) * 2 + h // 2
        decT[:, ai * 128:(ai + 1) * 128] = np.where(rel >= 0, np.exp(log_g[h] * np.maximum(rel, 0.0)), 0.0) * scale
        hp, hr = (h % 2) * 64, h // 2
        xiT[hp:hp + 64, hr * 128:(hr + 1) * 128] = (np.exp(log_g[h] * (pos + 1.0)) * scale)[None, :]
        zeta[:, h * 64:(h + 1) * 64] = np.exp(log_g[h] * (C - 1.0 - pos))[:, None]
        cd[:, h] = np.exp(log_g[h] * C)
    c["c_ret_decT"] = decT.astype(np.float32)
    c["c_ret_xiT"] = xiT.astype(np.float32)
    c["c_ret_zeta"] = zeta.astype(np.float32)
    c["c_ret_cd"] = cd.astype(np.float32)
    c["c_triu"] = np.triu(np.ones((128, 128), np.float32))
    c["c_tri01"] = np.triu(np.ones((128, 128), np.float32))
    slopes = np.array([2.0 ** (-8.0 * (i + 1) / 8) for i in range(8)], np.float64)
    al = np.zeros((128, 2, 4, 35), np.float64)
    p = np.arange(128, dtype=np.float64)
    for s, sl in enumerate((slopes[0::2], slopes[1::2])):
        for h in range(4):
            for dd in range(-31, 4):
                al[:, s, h, dd + 31] = sl[h] * (128.0 * dd + p)
    c["c_alibi"] = al.reshape(128, -1).astype(np.float32)
    return c


def layout_params(inputs):
    L = inputs["w_in"].shape[0]
    cw = np.asarray(inputs["ssm_conv_w"])
    conv_wT = np.ascontiguousarray(cw.reshape(L, 4, 6, 128).transpose(0, 3, 2, 1).reshape(L, 128, 24))
    conv_bT = np.ascontiguousarray(np.asarray(inputs["ssm_conv_b"]).reshape(L, 6, 128).transpose(0, 2, 1))
    ssm_vec = np.ascontiguousarray(np.concatenate([np.asarray(inputs["ssm_dt_bias"]), np.asarray(inputs["ssm_A_log"]),
                                                   np.asarray(inputs["ssm_D"])], axis=1))
    return {"conv_wT": conv_wT.astype(np.float32), "conv_bT": conv_bT.astype(np.float32),
            "ssm_vec": ssm_vec.astype(np.float32),
            "nsa_peT": np.ascontiguousarray(np.concatenate(
                [np.asarray(inputs["nsa_pe_k"]).transpose(0, 2, 1), np.asarray(inputs["nsa_pe_v"]).transpose(0, 2, 1)],
                axis=1)).astype(np.float32),
            "diff_vec": np.ascontiguousarray(np.concatenate(
                [np.asarray(inputs[k]) for k in ("diff_lam_q1", "diff_lam_k1", "diff_lam_q2", "diff_lam_k2",
                                                 "diff_subln_w")], axis=1)).astype(np.float32),
            "ret_gn_w": np.ascontiguousarray(inputs["ret_gn_w"]).astype(np.float32),
            "ssm_norm_w": np.ascontiguousarray(inputs["ssm_norm_w"]).astype(np.float32)}


def make_inmap(inputs, b):
    m = {
        "x": np.ascontiguousarray(inputs["x"][b]).astype(np.float32),
        "norm_w": np.ascontiguousarray(inputs["norm_w"]).astype(np.float32),
        "w_in": np.ascontiguousarray(inputs["w_in"]).astype(np.float32),
        "w_out": np.ascontiguousarray(inputs["w_out"]).astype(np.float32),
        "final_norm_w": np.ascontiguousarray(inputs["final_norm_w"]).reshape(1, -1).astype(np.float32),
    }
    m.update(make_consts(S=m["x"].shape[0]))
    m.update(layout_params(inputs))
    for kk in ("nsa_w_ck1", "nsa_w_cv1", "nsa_w_ck2", "nsa_w_cv2"):
        m[kk] = np.ascontiguousarray(inputs[kk]).astype(np.float32)
    return m


_CACHE = {}


def kernel(**inputs):
    S = inputs["x"].shape[1]
    B = inputs["x"].shape[0]
    L = inputs["w_in"].shape[0]
    key = (S, L)
    if key not in _CACHE:
        _CACHE[key] = build_program(S=S, L=L, phases=("A", "SSD", "RET", "DIFF", "NSA", "F"))
    nc = _CACHE[key]
    in_maps = [make_inmap(inputs, b) for b in range(B)]
    res = run_bass_kernel_spmd(nc, in_maps, core_ids=list(range(B)))
    return np.stack([r["y"] for r in res.results], axis=0).astype(np.float32)
```
